# Optimizing a Trainium2 kernel written in Bass

```python
import math
import jax, jax.numpy as jnp
from jax import lax
import numpy as np

D_MODEL = 1024
BATCH = 16
SEQ = 256
DEPTH = 4
DEC_BATCH = 2
DEC_SEQ = 2048
PAST_LEN = 256

GRID_W = 64
N_ATTN_LAYERS = (DEPTH + 1) // 2
N_SSM_LAYERS = DEPTH // 2
MLA_HEADS = 8
Q_LORA = 384
KV_LORA = 256
MLA_NOPE = 64
MLA_ROPE = 32
MLA_QK = MLA_NOPE + MLA_ROPE
MLA_V = 64
GQA_HEADS = 8
GQA_KV_HEADS = 2
GQA_DIM = 64
GQA_REP = GQA_HEADS // GQA_KV_HEADS
IN_DIM = Q_LORA + KV_LORA + MLA_ROPE + GQA_HEADS * GQA_DIM + 2 * GQA_KV_HEADS * GQA_DIM
IN_OFFSETS = (Q_LORA,
              Q_LORA + KV_LORA,
              Q_LORA + KV_LORA + MLA_ROPE,
              Q_LORA + KV_LORA + MLA_ROPE + GQA_HEADS * GQA_DIM,
              Q_LORA + KV_LORA + MLA_ROPE + GQA_HEADS * GQA_DIM + GQA_KV_HEADS * GQA_DIM)
MIX_DIM = MLA_HEADS * MLA_V + GQA_HEADS * GQA_DIM
Q_BLOCK = 128
ROPE_THETA = 10000.0
SSM_GROUP = 16
SSM_GROUPS = D_MODEL // SSM_GROUP
SSM_STATE = 64
N_DIR = 2
DT_MIN = 0.001
DT_MAX = 0.1
D_FF = 2816
CONV_W = 3
EPS = 1e-6

kernel_name = "hybrid_mla_gqa_s5_prefix_diffusion_step"


def rms_norm(x, g):
    xf = x.astype(jnp.float32)
    y = xf * lax.rsqrt(jnp.mean(xf * xf, axis=-1, keepdims=True) + EPS)
    return (y * g.astype(jnp.float32)).astype(x.dtype)


def modulation(cond, w_mod, b_mod):
    m = jax.nn.silu(cond.astype(jnp.float32)) @ w_mod.astype(jnp.float32) + b_mod.astype(jnp.float32)
    return [t[:, None, :].astype(cond.dtype) for t in jnp.split(m, 6, axis=-1)]


def modulate(h, shift, scale):
    return h * (1 + scale) + shift


def grid_angles(rows, rot_dim):
    row = jnp.repeat(jnp.arange(rows, dtype=jnp.float32), GRID_W)
    col = jnp.tile(jnp.arange(GRID_W, dtype=jnp.float32), rows)
    n_freq = rot_dim // 4
    inv_freq = ROPE_THETA ** (-jnp.arange(n_freq, dtype=jnp.float32) / n_freq)
    return row[:, None] * inv_freq, col[:, None] * inv_freq


def rotate(v, ang):
    c = jnp.cos(ang)[None, :, None, :]
    s = jnp.sin(ang)[None, :, None, :]
    v1, v2 = jnp.split(v.astype(jnp.float32), 2, axis=-1)
    return jnp.concatenate([v1 * c - v2 * s, v2 * c + v1 * s], axis=-1)


def axial_rope(x, angles):
    ang_row, ang_col = angles
    x_row, x_col = jnp.split(x, 2, axis=-1)
    return jnp.concatenate([rotate(x_row, ang_row), rotate(x_col, ang_col)], axis=-1).astype(x.dtype)


def rope_tail(x, angles):
    return jnp.concatenate([x[..., :MLA_NOPE], axial_rope(x[..., MLA_NOPE:], angles)], axis=-1)


def attend(q, k, v):
    b, t, g, r, dq = q.shape
    n_blk = t // Q_BLOCK
    qb = q.reshape(b, n_blk, Q_BLOCK, g, r, dq).swapaxes(0, 1)
    kf = k.astype(jnp.float32)
    vf = v.astype(jnp.float32)
    scale = 1.0 / math.sqrt(dq)

    def one_block(qi):
        s = jnp.einsum('bqgrd,bkgd->bgrqk', qi.astype(jnp.float32), kf) * scale
        p = jax.nn.softmax(s, axis=-1)
        return jnp.einsum('bgrqk,bkgd->bqgrd', p, vf)

    out = lax.map(one_block, qb)
    return out.swapaxes(0, 1).reshape(b, t, g * r * v.shape[-1]).astype(v.dtype)


def attn_inputs(h, w_in, g_qa, g_kva, w_uq, g_mq, g_gq, g_gk):
    b, t, _ = h.shape
    q_c, ckv, k_rope, gq, gk, gv = jnp.split(h @ w_in, IN_OFFSETS, axis=-1)
    q_c = rms_norm(q_c, g_qa)
    ckv = rms_norm(ckv, g_kva)
    qm = rms_norm((q_c @ w_uq).reshape(b, t, MLA_HEADS, MLA_QK), g_mq)
    gq = rms_norm(gq.reshape(b, t, GQA_HEADS, GQA_DIM), g_gq)
    gk = rms_norm(gk.reshape(b, t, GQA_KV_HEADS, GQA_DIM), g_gk)
    gv = gv.reshape(b, t, GQA_KV_HEADS, GQA_DIM)
    return qm, ckv, k_rope, gq, gk, gv


def mla_kv(ckv, k_rope, w_ukv, g_mk):
    b, s, _ = ckv.shape
    k_nope, v = jnp.split((ckv @ w_ukv).reshape(b, s, MLA_HEADS, MLA_NOPE + MLA_V), [MLA_NOPE], axis=-1)
    k_pe = jnp.broadcast_to(k_rope[:, :, None, :], (b, s, MLA_HEADS, MLA_ROPE))
    return rms_norm(jnp.concatenate([k_nope, k_pe], axis=-1), g_mk), v


def attn_merge(qm, km, vm, gq, gk, gv, w_out):
    b, t = qm.shape[:2]
    o_m = attend(qm[:, :, :, None, :], km, vm)
    o_g = attend(gq.reshape(b, t, GQA_KV_HEADS, GQA_REP, GQA_DIM), gk, gv)
    return jnp.concatenate([o_m, o_g], axis=-1) @ w_out


def ssm_discretize(a_re, a_im, log_dt, b_re, b_im):
    a_re = a_re.astype(jnp.float32)
    a_im = a_im.astype(jnp.float32)
    dt = jnp.exp(log_dt.astype(jnp.float32))[:, None]
    mag = jnp.exp(a_re * dt)
    ang = a_im * dt
    ab_re, ab_im = mag * jnp.cos(ang), mag * jnp.sin(ang)
    den = a_re * a_re + a_im * a_im
    n_re, n_im = ab_re - 1.0, ab_im
    k_re = ((n_re * a_re + n_im * a_im) / den)[..., None]
    k_im = ((n_im * a_re - n_re * a_im) / den)[..., None]
    b_re = b_re.astype(jnp.float32)
    b_im = b_im.astype(jnp.float32)
    return ab_re, ab_im, k_re * b_re - k_im * b_im, k_re * b_im + k_im * b_re


def scan_combine(e1, e2):
    a1r, a1i, b1r, b1i = e1
    a2r, a2i, b2r, b2i = e2
    return (a2r * a1r - a2i * a1i, a2r * a1i + a2i * a1r,
            a2r * b1r - a2i * b1i + b2r, a2r * b1i + a2i * b1r + b2i)


def ssm_direction(u_g, h0, a_re, a_im, log_dt, b_re, b_im, c_re, c_im, reverse):
    ab_re, ab_im, bb_re, bb_im = ssm_discretize(a_re, a_im, log_dt, b_re, b_im)
    bu_re = jnp.einsum('btgh,gph->btgp', u_g, bb_re)
    bu_im = jnp.einsum('btgh,gph->btgp', u_g, bb_im)
    t_first, t_last = (-1, 0) if reverse else (0, -1)
    if h0 is not None:
        h_re = h0[..., 0].astype(jnp.float32)
        h_im = h0[..., 1].astype(jnp.float32)
        bu_re = bu_re.at[:, t_first].add(ab_re * h_re - ab_im * h_im)
        bu_im = bu_im.at[:, t_first].add(ab_re * h_im + ab_im * h_re)
    a_full_re = jnp.broadcast_to(ab_re, bu_re.shape)
    a_full_im = jnp.broadcast_to(ab_im, bu_im.shape)
    _, _, x_re, x_im = lax.associative_scan(scan_combine, (a_full_re, a_full_im, bu_re, bu_im),
                                            reverse=reverse, axis=1)
    y = (jnp.einsum('btgp,ghp->btgh', x_re, c_re.astype(jnp.float32))
         - jnp.einsum('btgp,ghp->btgh', x_im, c_im.astype(jnp.float32)))
    return y, jnp.stack([x_re[:, t_last], x_im[:, t_last]], axis=-1)


def ssm_mixer(h, h0, a_re, a_im, log_dt, b_re, b_im, c_re, c_im, d_skip, w_glu, b_glu):
    b, t, _ = h.shape
    u_g = h.astype(jnp.float32).reshape(b, t, SSM_GROUPS, SSM_GROUP)
    y = h.astype(jnp.float32) * d_skip.astype(jnp.float32)
    finals = []
    for dr in range(N_DIR):
        y_d, fin = ssm_direction(u_g, None if h0 is None else h0[:, dr], a_re[dr], a_im[dr], log_dt[dr],
                                 b_re[dr], b_im[dr], c_re[dr], c_im[dr], reverse=(dr == 1))
        y = y + y_d.reshape(b, t, D_MODEL)
        finals.append(fin)
    g = jax.nn.gelu(y).astype(h.dtype)
    za, zb = jnp.split(g @ w_glu + b_glu, 2, axis=-1)
    return za * jax.nn.sigmoid(zb), jnp.stack(finals, axis=1)


def conv_ffn(h, w_up, conv_w, conv_b, w_down):
    z = h @ w_up
    z = lax.conv_general_dilated(z, conv_w[:, None, :].astype(z.dtype), window_strides=(1,),
                                 padding=((CONV_W // 2, CONV_W // 2),),
                                 dimension_numbers=('NWC', 'WIO', 'NWC'),
                                 feature_group_count=2 * D_FF) + conv_b
    za, zb = jnp.split(z, 2, axis=-1)
    return (jax.nn.silu(za) * zb) @ w_down


def setup_inputs(seed: int = 0) -> dict:
    key = jax.random.key(seed)
    ks = iter(jax.random.split(key, 48))
    f32 = jnp.float32

    def nrm(shape, fan_in, mult=1.0):
        return jax.random.normal(next(ks), shape, f32) * (mult * fan_in ** -0.5)

    def gain(shape):
        return 1.0 + 0.02 * jax.random.normal(next(ks), shape, f32)

    def small(shape):
        return 0.02 * jax.random.normal(next(ks), shape, f32)

    na, ns = N_ATTN_LAYERS, N_SSM_LAYERS
    ssm_shape = (ns, N_DIR, SSM_GROUPS, SSM_STATE)
    return {
        "x_prompt": jax.random.normal(next(ks), (BATCH, SEQ, D_MODEL), f32),
        "x_sample": jax.random.normal(next(ks), (DEC_BATCH, DEC_SEQ, D_MODEL), f32),
        "c": jax.random.normal(next(ks), (DEC_BATCH, D_MODEL), f32),
        "cache_mla_ckv": jax.random.normal(next(ks), (DEC_BATCH, na, PAST_LEN, KV_LORA), f32),
        "cache_mla_krope": jax.random.normal(next(ks), (DEC_BATCH, na, PAST_LEN, MLA_ROPE), f32),
        "cache_gqa_k": jax.random.normal(next(ks), (DEC_BATCH, na, PAST_LEN, GQA_KV_HEADS, GQA_DIM), f32),
        "cache_gqa_v": jax.random.normal(next(ks), (DEC_BATCH, na, PAST_LEN, GQA_KV_HEADS, GQA_DIM), f32),
        "state_ssm": 0.5 * jax.random.normal(next(ks), (DEC_BATCH, ns, N_DIR, SSM_GROUPS, SSM_STATE, 2), f32),
        "c_ctx": jax.random.normal(next(ks), (D_MODEL,), f32),
        "norm1_g": gain((DEPTH, D_MODEL)),
        "norm2_g": gain((DEPTH, D_MODEL)),
        "w_mod": nrm((DEPTH, D_MODEL, 6 * D_MODEL), D_MODEL, 0.5),
        "b_mod": small((DEPTH, 6 * D_MODEL)),
        "attn_w_in": nrm((na, D_MODEL, IN_DIM), D_MODEL),
        "attn_qa_norm_g": gain((na, Q_LORA)),
        "attn_kva_norm_g": gain((na, KV_LORA)),
        "attn_w_uq": nrm((na, Q_LORA, MLA_HEADS * MLA_QK), Q_LORA),
        "attn_w_ukv": nrm((na, KV_LORA, MLA_HEADS * (MLA_NOPE + MLA_V)), KV_LORA),
        "attn_mla_q_norm_g": gain((na, MLA_QK)),
        "attn_mla_k_norm_g": gain((na, MLA_QK)),
        "attn_gqa_q_norm_g": gain((na, GQA_DIM)),
        "attn_gqa_k_norm_g": gain((na, GQA_DIM)),
        "attn_w_out": nrm((na, MIX_DIM, D_MODEL), MIX_DIM),
        "ssm_a_re": -0.5 + 0.01 * jax.random.normal(next(ks), ssm_shape, f32),
        "ssm_a_im": math.pi * jnp.arange(SSM_STATE, dtype=f32) + 0.01 * jax.random.normal(next(ks), ssm_shape, f32),
        "ssm_log_dt": jax.random.uniform(next(ks), (ns, N_DIR, SSM_GROUPS), f32,
                                         math.log(DT_MIN), math.log(DT_MAX)),
        "ssm_b_re": nrm(ssm_shape + (SSM_GROUP,), 2 * SSM_GROUP),
        "ssm_b_im": nrm(ssm_shape + (SSM_GROUP,), 2 * SSM_GROUP),
        "ssm_c_re": nrm((ns, N_DIR, SSM_GROUPS, SSM_GROUP, SSM_STATE), SSM_STATE),
        "ssm_c_im": nrm((ns, N_DIR, SSM_GROUPS, SSM_GROUP, SSM_STATE), SSM_STATE),
        "ssm_d": jax.random.normal(next(ks), (ns, D_MODEL), f32),
        "ssm_w_glu": nrm((ns, D_MODEL, 2 * D_MODEL), D_MODEL),
        "ssm_b_glu": small((ns, 2 * D_MODEL)),
        "ffn_w_up": nrm((DEPTH, D_MODEL, 2 * D_FF), D_MODEL),
        "ffn_conv_w": nrm((DEPTH, CONV_W, 2 * D_FF), CONV_W),
        "ffn_conv_b": small((DEPTH, 2 * D_FF)),
        "ffn_w_down": nrm((DEPTH, D_FF, D_MODEL), D_FF),
    }


def reference(x_prompt, x_sample, c, cache_mla_ckv, cache_mla_krope, cache_gqa_k, cache_gqa_v, state_ssm, c_ctx,
              norm1_g, norm2_g, w_mod, b_mod,
              attn_w_in, attn_qa_norm_g, attn_kva_norm_g, attn_w_uq, attn_w_ukv,
              attn_mla_q_norm_g, attn_mla_k_norm_g, attn_gqa_q_norm_g, attn_gqa_k_norm_g, attn_w_out,
              ssm_a_re, ssm_a_im, ssm_log_dt, ssm_b_re, ssm_b_im, ssm_c_re, ssm_c_im, ssm_d, ssm_w_glu, ssm_b_glu,
              ffn_w_up, ffn_conv_w, ffn_conv_b, ffn_w_down):
    rows = x_sample.shape[1] // GRID_W
    ang_mla = grid_angles(rows, MLA_ROPE)
    ang_gqa = grid_angles(rows, GQA_DIM)
    xp, xs = x_prompt, x_sample
    new_ckv, new_krope, new_gk, new_gv, new_ssm = [], [], [], [], []

    for i in range(DEPTH):
        j = i // 2
        sh1p, sc1p, gt1p, sh2p, sc2p, gt2p = modulation(c_ctx[None, :], w_mod[i], b_mod[i])
        sh1s, sc1s, gt1s, sh2s, sc2s, gt2s = modulation(c, w_mod[i], b_mod[i])
        hp = modulate(rms_norm(xp, norm1_g[i]), sh1p, sc1p)
        hs = modulate(rms_norm(xs, norm1_g[i]), sh1s, sc1s)

        if i % 2 == 0:
            proj = (attn_w_in[j], attn_qa_norm_g[j], attn_kva_norm_g[j], attn_w_uq[j],
                    attn_mla_q_norm_g[j], attn_gqa_q_norm_g[j], attn_gqa_k_norm_g[j])
            qm, ckv, krope, gq, gk, gv = attn_inputs(hp, *proj)
            km, vm = mla_kv(ckv, krope, attn_w_ukv[j], attn_mla_k_norm_g[j])
            mp = attn_merge(qm, km, vm, gq, gk, gv, attn_w_out[j])
            new_ckv.append(ckv)
            new_krope.append(krope)
            new_gk.append(gk)
            new_gv.append(gv)
            qm, ckv, krope, gq, gk, gv = attn_inputs(hs, *proj)
            km, vm = mla_kv(ckv, krope, attn_w_ukv[j], attn_mla_k_norm_g[j])
            qm, km = rope_tail(qm, ang_mla), rope_tail(km, ang_mla)
            gq, gk = axial_rope(gq, ang_gqa), axial_rope(gk, ang_gqa)
            km_c, vm_c = mla_kv(cache_mla_ckv[:, j], cache_mla_krope[:, j], attn_w_ukv[j], attn_mla_k_norm_g[j])
            ms = attn_merge(qm, jnp.concatenate([km_c, km], axis=1), jnp.concatenate([vm_c, vm], axis=1),
                            gq, jnp.concatenate([cache_gqa_k[:, j], gk], axis=1),
                            jnp.concatenate([cache_gqa_v[:, j], gv], axis=1), attn_w_out[j])
        else:
            sp = (ssm_a_re[j], ssm_a_im[j], ssm_log_dt[j], ssm_b_re[j], ssm_b_im[j], ssm_c_re[j], ssm_c_im[j],
                  ssm_d[j], ssm_w_glu[j], ssm_b_glu[j])
            mp, fin = ssm_mixer(hp, None, *sp)
            new_ssm.append(fin)
            ms, _ = ssm_mixer(hs, state_ssm[:, j], *sp)

        xp = xp + gt1p * mp
        xs = xs + gt1s * ms
        fp = (ffn_w_up[i], ffn_conv_w[i], ffn_conv_b[i], ffn_w_down[i])
        xp = xp + gt2p * conv_ffn(modulate(rms_norm(xp, norm2_g[i]), sh2p, sc2p), *fp)
        xs = xs + gt2s * conv_ffn(modulate(rms_norm(xs, norm2_g[i]), sh2s, sc2s), *fp)

    new_mla_ckv = jnp.stack(new_ckv, axis=1)
    new_mla_krope = jnp.stack(new_krope, axis=1)
    new_gqa_k = jnp.stack(new_gk, axis=1)
    new_gqa_v = jnp.stack(new_gv, axis=1)
    new_state_ssm = jnp.stack(new_ssm, axis=1)
    return (xp, xs, new_mla_ckv, new_mla_krope, new_gqa_k, new_gqa_v, new_state_ssm)
```

```python
import math
import contextlib
import numpy as np
import concourse.bass as bass
import concourse.mybir as mybir
from concourse.bass_utils import run_bass_kernel_spmd

F32 = mybir.dt.float32
BF16 = mybir.dt.bfloat16
I32 = mybir.dt.int32
AF = mybir.ActivationFunctionType
ALU = mybir.AluOpType
AX = mybir.AxisListType

D = 1024
DFF = 2816
EPS = 1e-6
NCORES = 8


class Obj:
    __slots__ = ("name", "last_w", "readers", "excl")

    def __init__(self, name, excl=False):
        self.name = name
        self.last_w = None
        self.readers = []
        self.excl = excl


class Prog:
    ENGS = ("pe", "act", "dve", "pool", "sp")

    def __init__(self, nc):
        self.nc = nc
        self.q = {e: [] for e in self.ENGS}
        self.cnt = {e: 0 for e in self.ENGS}
        self.dmacnt = {}
        self.waited = {e: {} for e in self.ENGS}
        self.semkeys = []
        self.out_events = []

    def _semkey(self, k):
        if k not in self.semkeys:
            self.semkeys.append(k)
        return k

    def _deps(self, eng, reads, writes, pe_accum=False):
        need = {}

        def add(ev, same_ok=False):
            if ev is None:
                return
            k, v = ev
            if same_ok and k == eng:
                return
            if need.get(k, 0) < v:
                need[k] = v
        for o in reads:
            add(o.last_w)
            if o.excl:
                for r in o.readers:
                    add(r, same_ok=True)
        for o in writes:
            add(o.last_w, same_ok=pe_accum)
            for r in o.readers:
                add(r, same_ok=True)
        waits = []
        for k, v in need.items():
            if self.waited[eng].get(k, 0) < v:
                self.waited[eng][k] = v
                waits.append((k, v))
        return waits

    def _commit(self, ev, reads, writes):
        for o in reads:
            o.readers.append(ev)
            if len(o.readers) > 64:
                best = {}
                for k, v in o.readers:
                    if best.get(k, 0) < v:
                        best[k] = v
                o.readers = list(best.items())
        for o in writes:
            o.last_w = ev
            o.readers = []

    def op(self, eng, fn, reads=(), writes=(), pe_accum=False, same_ok=False):
        reads = [o for o in reads if o is not None]
        writes = [o for o in writes if o is not None]
        waits = self._deps(eng, reads, writes, pe_accum)
        if same_ok:
            waits = [(k, v) for (k, v) in waits if k != eng]
        self.cnt[eng] += 1
        ev = (self._semkey(eng), self.cnt[eng])
        self.q[eng].append((waits, fn, [(eng, 1)]))
        self._commit(ev, reads, writes)
        return ev

    def dma(self, eng, fn, key, reads=(), writes=(), is_output=False, inc=16):
        reads = [o for o in reads if o is not None]
        writes = [o for o in writes if o is not None]
        waits = self._deps(eng, reads, writes)
        sk = self._semkey(("dma", key))
        self.dmacnt[key] = self.dmacnt.get(key, 0) + 1
        ev = (sk, inc * self.dmacnt[key])
        self.q[eng].append((waits, fn, [(sk, inc)]))
        self._commit(ev, reads, writes)
        if is_output:
            self.out_events.append(ev)
        return ev

    def finish(self, eng="sp"):
        need = {}
        for k, v in self.out_events:
            need[k] = max(need.get(k, 0), v)
        for e in ("pe", "act", "dve", "pool"):
            if self.cnt[e]:
                need[e] = self.cnt[e]
        waits = [(k, v) for k, v in need.items() if self.waited[eng].get(k, 0) < v]
        self.q[eng].append((waits, None, []))

    def emit(self):
        nc = self.nc
        comp = ("pe", "act", "dve", "pool")
        miles = {e: set() for e in comp}
        for e in self.ENGS:
            for waits, fn, incs in self.q[e]:
                for k, v in waits:
                    if k in miles:
                        miles[k].add(v)
        rank = {e: {v: i + 1 for i, v in enumerate(sorted(miles[e]))} for e in comp}
        with contextlib.ExitStack() as st:
            sems = {}
            for i, k in enumerate(self.semkeys):
                sems[k] = st.enter_context(nc.semaphore("s%d" % i))
            block = st.enter_context(nc.Block())
            q = self.q

            def run(e, name):
                n = 0
                for waits, fn, incs in q[name]:
                    for k, v in waits:
                        if k in rank:
                            v = rank[k][v]
                        e.wait_ge(sems[k], v)
                    if fn is None:
                        continue
                    ins = fn(e)
                    for k, inc in incs:
                        if k in rank:
                            n += 1
                            if n in rank[k]:
                                ins.then_inc(sems[k], 1)
                        else:
                            ins.then_inc(sems[k], inc)

            @block.tensor
            def _(e):
                run(e, "pe")

            @block.scalar
            def _(e):
                run(e, "act")

            @block.vector
            def _(e):
                run(e, "dve")

            @block.gpsimd
            def _(e):
                run(e, "pool")

            @block.sync
            def _(e):
                run(e, "sp")


class Reg:
    def __init__(self, arena, off, nbytes, name, req=None):
        self.arena = arena
        self.off = off
        self.nbytes = nbytes
        self.req = req or nbytes
        self.o = Obj(name)
        self.name = name

    def ap(self, dt=F32, pat=None, **kw):
        a = self.arena[:, self.off // 4:(self.off + self.req) // 4]
        if dt != F32:
            a = a.bitcast(dt)
        if pat is not None:
            a = a.rearrange(pat, **kw)
        return a


class Arena:
    def __init__(self, arena_ap, nbytes):
        self.arena = arena_ap
        self.free = [(0, nbytes)]
        self.hist = []

    def alloc(self, name, nbytes):
        req = nbytes
        nbytes = (nbytes + 63) // 64 * 64
        cands = [(b - a, idx) for idx, (a, b) in enumerate(self.free) if b - a >= nbytes]
        for _, idx in sorted(cands)[:1]:
            a, b = self.free[idx]
            if b - a >= nbytes:
                self.free[idx] = (a + nbytes, b)
                if self.free[idx][0] == self.free[idx][1]:
                    self.free.pop(idx)
                r = Reg(self.arena, a, nbytes, name, req)
                keep = []
                for (ha, hb, ho) in self.hist:
                    if ha < a + nbytes and hb > a:
                        if ho.last_w is not None:
                            r.o.readers.append(ho.last_w)
                        r.o.readers.extend(ho.readers)
                        if ha < a:
                            keep.append((ha, a, ho))
                        if hb > a + nbytes:
                            keep.append((a + nbytes, hb, ho))
                    else:
                        keep.append((ha, hb, ho))
                self.hist = keep
                return r
        raise RuntimeError("SBUF arena full allocating %s (%d bytes); free=%s" % (name, nbytes, self.free))

    def release(self, *regs):
        for r in regs:
            self.hist.append((r.off, r.off + r.nbytes, r.o))
            self.free.append((r.off, r.off + r.nbytes))
        self.free.sort()
        merged = []
        for a, b in self.free:
            if merged and merged[-1][1] == a:
                merged[-1] = (merged[-1][0], b)
            else:
                merged.append((a, b))
        self.free = merged


IN_SPECS = [
    ("xp", [512, D]), ("xs", [512, D]), ("cvec", [2, D]),
    ("cache_ckv", [2, 256, 256]), ("cache_krope", [2, 256, 32]), ("cache_gk", [2, 256, 128]), ("cache_gv", [2, 256, 128]),
    ("state_ssm", [2, 2, 64, 64, 2]),
    ("norm1_g", [4, D]), ("norm2_g", [4, D]), ("w_mod", [4, D, 1536]), ("b_mod", [4, 1536]),
    ("attn_w_in", [2, D, 1440]), ("attn_qa_norm_g", [2, 384]), ("attn_kva_norm_g", [2, 256]),
    ("attn_w_uq", [2, 384, 768]), ("attn_w_ukv", [2, 256, 1024]),
    ("attn_mla_q_norm_g", [2, 96]), ("attn_mla_k_norm_g", [2, 96]), ("attn_gqa_q_norm_g", [2, 64]), ("attn_gqa_k_norm_g", [2, 64]),
    ("attn_w_out", [2, D, D]),
    ("ssm_a_re", [2, 2, 64, 64]), ("ssm_a_im", [2, 2, 64, 64]), ("ssm_log_dt", [2, 2, 64]),
    ("ssm_b_re", [2, 2, 64, 64, 16]), ("ssm_b_im", [2, 2, 64, 64, 16]),
    ("ssm_c_re", [2, 2, 64, 16, 64]), ("ssm_c_im", [2, 2, 64, 16, 64]),
    ("ssm_d", [2, D]), ("ssm_w_glu", [2, D, 2 * D]), ("ssm_b_glu", [2, 2 * D]),
    ("ffn_w_up", [4, D, 2 * DFF]), ("ffn_conv_w", [4, 3, 2 * DFF]), ("ffn_conv_b", [4, 2 * DFF]), ("ffn_w_down", [4, DFF, D]),
    ("qpos", [128, 8, 2]), ("kpos", [128, 16, 2]), ("selhalo", [8, 2]), ("selr", [128, 8]),
]
OUT_SPECS = [
    ("yp", [512, D]), ("ys", [512, D]),
    ("ockv", [2, 2, 256, 256]), ("okrope", [2, 2, 256, 32]), ("ogk", [2, 2, 256, 128]), ("ogv", [2, 2, 256, 128]),
    ("ossm", [2, 2, 2, 64, 64, 2]),
]


class Builder:
    def __init__(self, cfg=None):
        self.cfg = cfg or {}
        self.nc = bass.Bass("TRN2", target_bir_lowering=False)
        nc = self.nc
        self.small = bool(self.cfg.get("small"))
        big = ("w_mod", "ffn_w_up", "ffn_w_down", "ssm_w_glu")
        self.di = {n: nc.dram_tensor(n, s, F32, kind="ExternalInput").ap() for n, s in IN_SPECS if not (self.small and n in big)}
        self.do = {n: nc.dram_tensor(n, s, F32, kind="ExternalOutput").ap() for n, s in OUT_SPECS}
        self.dobj = {}
        self.modd = nc.dram_tensor("modd", [4, 2, 6 * D], F32, kind="Internal").ap()
        self.msend = nc.dram_tensor("msend", [8, 1536], F32, kind="Internal").ap()
        self.mrecv = nc.dram_tensor("mrecv", [32, 1536], F32, kind="Internal").ap()
        self.dobj["modd"] = [Obj("modd%d" % l) for l in range(4)]
        self.hsend = [nc.dram_tensor("hsend%d" % l, [2, D], BF16, kind="Internal").ap() for l in range(4)]
        self.hrecv = [nc.dram_tensor("hrecv%d" % l, [8, D], BF16, kind="Internal").ap() for l in range(4)]
        self.kvsend = [[nc.dram_tensor("kvsend%d_%d" % (l, hf), [256, 544], F32, kind="Internal").ap() for hf in range(2)] for l in range(2)]
        self.kvrecv = [[nc.dram_tensor("kvrecv%d_%d" % (l, hf), [1024, 544], F32, kind="Internal").ap() for hf in range(2)] for l in range(2)]
        self.ssend = [nc.dram_tensor("ssend%d" % l, [128, 128], F32, kind="Internal").ap() for l in range(2)]
        self.srecv = [nc.dram_tensor("srecv%d" % l, [512, 128], F32, kind="Internal").ap() for l in range(2)]

    def mm(self, out, lhsT, rhs, start, stop, reads, writes, **kw):
        self.P.op("pe", lambda e: e.matmul(out, lhsT=lhsT, rhs=rhs, start=start, stop=stop, **kw),
                  reads=reads, writes=writes, pe_accum=True)

    def tr(self, out, in_, ident, reads, writes):
        self.P.op("pe", lambda e: e.transpose(out, in_, ident), reads=reads, writes=writes, pe_accum=True)

    def act(self, out, in_, func, reads, writes, **kw):
        self.P.op("act", lambda e: e.activation(out=out, in_=in_, func=func, **kw), reads=reads, writes=writes)

    def cp(self, eng, out, in_, reads, writes):
        if eng == "act":
            self.P.op("act", lambda e: e.copy(out=out, in_=in_), reads=reads, writes=writes)
        else:
            self.P.op(eng, lambda e: e.tensor_copy(out=out, in_=in_), reads=reads, writes=writes)

    def tt(self, eng, out, in0, in1, op, reads, writes, same_ok=False):
        self.P.op(eng, lambda e: e.tensor_tensor(out=out, in0=in0, in1=in1, op=op), reads=reads, writes=writes, same_ok=same_ok)

    def stt(self, eng, out, in0, scalar, in1, op0, op1, reads, writes, same_ok=False):
        self.P.op(eng, lambda e: e.scalar_tensor_tensor(out=out, in0=in0, scalar=scalar, in1=in1, op0=op0, op1=op1),
                  reads=reads, writes=writes, same_ok=same_ok)

    def ts(self, eng, out, in0, s1, s2, op0, op1, reads, writes):
        if op1 is None:
            self.P.op(eng, lambda e: e.tensor_scalar(out=out, in0=in0, scalar1=s1, scalar2=None, op0=op0), reads=reads, writes=writes)
        else:
            self.P.op(eng, lambda e: e.tensor_scalar(out=out, in0=in0, scalar1=s1, scalar2=s2, op0=op0, op1=op1), reads=reads, writes=writes)

    def recip(self, out, in_, reads, writes):
        self.P.op("dve", lambda e: e.reciprocal(out=out, in_=in_), reads=reads, writes=writes)

    def memset(self, eng, out, val, writes):
        self.P.op(eng, lambda e: e.memset(out, val), writes=writes)

    def dma(self, eng, out, in_, key, reads, writes, is_output=False):
        self.P.dma(eng, lambda e: e.dma_start(out=out, in_=in_), key, reads=reads, writes=writes, is_output=is_output)

    def bank(self, k):
        if k < 6:
            return self.PB[k // 2][:, (k % 2) * 512:(k % 2 + 1) * 512]
        return self.PS[k - 6][:, :]

    def build(self):
        nc = self.nc
        with contextlib.ExitStack() as st:
            AW = 53000
            arena_t = st.enter_context(nc.sbuf_tensor("arena", [128, AW], F32))
            self.PB = [st.enter_context(nc.psum_tensor("pb%d" % i, [128, 1024], F32)) for i in range(3)]
            self.PS = [st.enter_context(nc.psum_tensor("ps%d" % i, [128, 512], F32)) for i in range(2)]
            self.BK = [Obj("bank%d" % i, excl=True) for i in range(8)]
            self.A = Arena(arena_t, AW * 4)
            self.P = Prog(nc)
            self.consts()
            self.load_x()
            if not self.small:
                self.preamble()
            for l in range(4):
                if l >= self.cfg.get("nlayers", 4):
                    break
                self.layer_mod(l)
                if self.cfg.get("mixers", True):
                    if l % 2 == 0:
                        if not self.cfg.get("only_ssm"):
                            self.attention(l)
                    else:
                        self.ssm(l)
                if self.cfg.get("ffn", True):
                    self.ffn(l)
            self.store_x()
            self.P.finish()
            with nc.allow_non_contiguous_dma("small strided parameter loads"):
                self.P.emit()
        return nc

    def consts(self):
        A = self.A
        self.identf = A.alloc("identf", 512)
        self.identb = A.alloc("identb", 256)
        self.onesb = A.alloc("onesb", 256)
        self.epsc = A.alloc("epsc", 64)
        idf = self.identf.ap()
        self.memset("pool", idf, 0.0, [self.identf.o])
        self.P.op("pool", lambda e: e.affine_select(out=idf, in_=idf, pattern=[[-1, 128]], compare_op=ALU.not_equal,
                                                    fill=1.0, base=0, channel_multiplier=1),
                  reads=[self.identf.o], writes=[self.identf.o])
        self.cp("dve", self.identb.ap(BF16), idf, [self.identf.o], [self.identb.o])
        self.memset("dve", self.onesb.ap(BF16), 1.0, [self.onesb.o])
        self.onesf = A.alloc("onesf", 256)
        self.memset("dve", self.onesf.ap(), 1.0, [self.onesf.o])
        self.memset("dve", self.epsc.ap()[:, 0:1], EPS, [self.epsc.o])
        self.memset("dve", self.epsc.ap()[:, 1:2], -math.pi, [self.epsc.o])

    def load_x(self):
        self.X = self.A.alloc("X", 8 * D * 4)
        X = self.X.ap(F32, "p (i d) -> p i d", i=8)
        self.XO = [Obj("X%d" % i) for i in range(8)]
        self.dma("sp", X[0:64], self.di["xp"].rearrange("(q i) d -> q i d", i=8), "x0", [], self.XO)
        self.dma("sp", X[64:128], self.di["xs"].rearrange("(q i) d -> q i d", i=8), "x1", [], self.XO)

    def store_x(self):
        X = self.X.ap(F32, "p (i d) -> p i d", i=8)
        self.dma("sp", self.do["yp"].rearrange("(q i) d -> q i d", i=8), X[0:64], "y0", self.XO, [], is_output=True)
        self.dma("sp", self.do["ys"].rearrange("(q i) d -> q i d", i=8), X[64:128], "y1", self.XO, [], is_output=True)

    def preamble(self):
        A, P = self.A, self.P
        cnd = A.alloc("cnd", 64)
        cndb = A.alloc("cndb", 64)
        c3 = cnd.ap(F32, "p (k c) -> p k c", c=2)
        for r in range(2):
            self.P.dma("sp", (lambda r: lambda e: e.dma_start(out=c3[:, :, r], in_=self.di["cvec"][r].rearrange("(kc k) -> k kc", k=128)))(r),
                       "cnd", writes=[cnd.o])
        self.act(cnd.ap()[:, 0:16], cnd.ap()[:, 0:16], AF.Silu, [cnd.o], [cnd.o])
        cb3 = cndb.ap(BF16)[:, 0:16].rearrange("p (k c) -> p k c", c=2)
        self.cp("dve", cndb.ap(BF16)[:, 0:16], cnd.ap()[:, 0:16], [cnd.o], [cndb.o])
        NS = 1536
        bm = A.alloc("bm", 4 * NS * 4)
        mp = A.alloc("mp", 4 * NS * 4)
        ws = [A.alloc("wmod%d" % i, 8 * 512 * 2) for i in range(3)]
        self.dma("sp", bm.ap()[0:2, :], self.di["b_mod"].rearrange("l n -> (l n)").partition_broadcast(2), "bm", [], [bm.o])
        cnt = 0
        for l in range(4):
            for nt in range(3):
                w = ws[cnt % 3]
                wv = w.ap(BF16, "p (k n) -> p k n", k=8)
                self.dma("pool", wv, self.di["w_mod"][l][:, nt * 512:(nt + 1) * 512].rearrange("(kc k) n -> k kc n", k=128),
                         "wmod%d" % (cnt % 3), [], [w.o])
                bk = cnt % 2
                for kc in range(8):
                    self.mm(self.bank(bk)[0:2, :], cb3[:, kc, :], wv[:, kc, :], kc == 0, kc == 7, [cndb.o, w.o], [self.BK[bk]])
                c0 = l * NS + nt * 512
                self.tt("dve", mp.ap()[0:2, c0:c0 + 512], self.bank(bk)[0:2, :], bm.ap()[0:2, c0:c0 + 512], ALU.add, [self.BK[bk], bm.o], [mp.o])
                cnt += 1
        osd, orc = Obj("msend"), Obj("mrecv")
        self.dma("sp", self.msend.rearrange("(c l) n -> c (l n)", c=2), mp.ap()[0:2, :], "msd", [mp.o], [osd])
        msd, mrc = self.msend, self.mrecv
        P.dma("pool", lambda e: e.collective_compute("AllGather", ALU.bypass, replica_groups=[[0, 1, 2, 3], [4, 5, 6, 7]], ins=[msd.opt()], outs=[mrc.opt()]),
              "cc_mod", reads=[osd], writes=[orc], inc=1)
        for r in range(4):
            self.P.dma("sp", (lambda r: lambda e: e.dma_start(out=self.modd[:, :, r * NS:(r + 1) * NS].rearrange("l c j -> c l j"),
                                                                in_=mrc[r * 8:(r + 1) * 8, :].rearrange("(c l) j -> c l j", c=2)))(r),
                       "m3", reads=[orc], writes=self.dobj["modd"])
        A.release(cnd, cndb, bm, mp, *ws)

    def layer_mod(self, l):
        self.cur_l = l

    def load_mod(self, idx, name):
        l = self.cur_l
        r = self.A.alloc(name, D * 4)
        M = r.ap()
        self.dma("sp", M[0:64, :], self.modd[l, 0, idx * D:(idx + 1) * D].partition_broadcast(64), "mod" + name, [self.dobj["modd"][l]], [r.o])
        self.dma("sp", M[64:128, :], self.modd[l, 1, idx * D:(idx + 1) * D].partition_broadcast(64), "mod" + name, [self.dobj["modd"][l]], [r.o])
        return r

    def load_ab(self, which):
        l = self.cur_l
        base = 0 if which == 1 else 3
        Bt = self.load_mod(base, "modB")
        At = self.load_mod(base + 1, "modA")
        gb = self.A.alloc("gb", D * 4)
        self.dma("sp", gb.ap(), self.di["norm1_g" if which == 1 else "norm2_g"][l].partition_broadcast(128), "gb", [], [gb.o])
        self.stt("dve", At.ap(), At.ap(), 1.0, gb.ap(), ALU.add, ALU.mult, [At.o, gb.o], [At.o])
        self.A.release(gb)
        return At, Bt

    def norm_mod(self, which, dest, dobj, view=None):
        A = self.A
        At, Bt = self.load_ab(which)
        Aap, Bap = At.ap(), Bt.ap()
        X = self.X.ap(F32, "p (i d) -> p i d", i=8)
        st = A.alloc("nm_st", 64)
        junk = A.alloc("nm_junk", D * 4)
        tmp = [A.alloc("nm_tmp%d" % i, D * 4) for i in range(2)]
        ss = st.ap()[:, 0:8]
        rs = st.ap()[:, 8:16]
        self.memset("dve", ss, 0.0, [st.o])
        for i in range(8):
            self.act(junk.ap(), X[:, i, :], AF.Square, [self.XO[i], st.o], [junk.o, st.o], accum_out=ss[:, i:i + 1])
        self.act(rs, ss, AF.Sqrt, [st.o, self.epsc.o], [st.o], bias=self.epsc.ap()[:, 0:1], scale=1.0 / D)
        self.P.op("dve", lambda e: e.reciprocal(out=rs, in_=rs), reads=[st.o], writes=[st.o])
        for i in range(8):
            t = tmp[i % 2]
            self.stt("dve", t.ap(), X[:, i, :], rs[:, i:i + 1], Aap, ALU.mult, ALU.mult, [self.XO[i], st.o, At.o], [t.o])
            if view is None:
                self.tt("pool", dest(i), t.ap(), Bap, ALU.add, [t.o, Bt.o], [dobj])
            else:
                self.tt("pool", dest(i), view(t.ap()), view(Bap), ALU.add, [t.o, Bt.o], [dobj])
        A.release(st, junk, At, Bt, *tmp)

    def transposeT(self, H, HT):
        Hv = H.ap(BF16, "p (i d) -> p i d", i=8)
        HTv = HT.ap(BF16, "p (k t) -> p k t", k=8)
        idb = self.identb.ap(BF16)
        for kc in range(8):
            bk = 6 + kc % 2
            psb = self.bank(bk).bitcast(BF16)
            for i in range(8):
                self.tr(psb[:, i * 128:(i + 1) * 128], Hv[:, i, kc * 128:(kc + 1) * 128], idb, [H.o, self.identb.o], [self.BK[bk]])
            self.cp("act" if kc % 2 else "dve", HTv[:, kc, :], psb, [self.BK[bk]], [HT.o])

    def ffn(self, l):
        A = self.A
        P = self.P
        X = self.X.ap(F32, "p (i d) -> p i d", i=8)
        WA = [A.alloc("wua%d" % i, 4096) for i in range(3)]
        WB = [A.alloc("wub%d" % i, 4096) for i in range(3)]
        wup = self.di["ffn_w_up"][l]

        def load_wu(mq):
            sq = mq % 3
            self.dma("pool", WA[sq].ap(BF16, "p (k n) -> p k n", k=8), wup[:, mq * 256:(mq + 1) * 256].rearrange("(kc k) n -> k kc n", k=128),
                     "wua%d" % sq, [], [WA[sq].o])
            self.dma("pool", WB[sq].ap(BF16, "p (k n) -> p k n", k=8), wup[:, DFF + mq * 256:DFF + (mq + 1) * 256].rearrange("(kc k) n -> k kc n", k=128),
                     "wub%d" % sq, [], [WB[sq].o])
        load_wu(0)
        load_wu(1)
        NWD, NPRE = 6, 4
        WD = [A.alloc("wd%d" % i, 2048) for i in range(NWD)]
        wdn = self.di["ffn_w_down"][l]

        def load_wd(q):
            nh_, mg_ = q // 11, q % 11
            sq = q % NWD
            self.dma("pool", WD[sq].ap(BF16, "p (a n) -> p a n", a=2),
                     wdn[mg_ * 256:(mg_ + 1) * 256, nh_ * 512:(nh_ + 1) * 512].rearrange("(a f) n -> f a n", f=128), "wd%d" % sq, [], [WD[sq].o])
        H = A.alloc("H2", 16384)
        Hv = H.ap(BF16, "p (i d) -> p i d", i=8)
        self.norm_mod(2, lambda i: Hv[:, i, :], H.o)
        hs, hr_d = self.hsend[l], self.hrecv[l]
        ohs, ohr = Obj("hsend"), Obj("hrecv")
        self.dma("sp", hs[0:1, :], Hv[64:65, 0, :], "hs", [H.o], [ohs])
        self.dma("sp", hs[1:2, :], Hv[127:128, 7, :], "hs", [H.o], [ohs])
        P.dma("pool", lambda e: e.collective_compute("AllGather", ALU.bypass, replica_groups=[[0, 1, 2, 3], [4, 5, 6, 7]],
                                                     ins=[hs.opt()], outs=[hr_d.opt()]),
              "cc_h%d" % l, reads=[ohs], writes=[ohr], inc=1)
        HT = A.alloc("HT2", 16384)
        self.transposeT(H, HT)
        A.release(H)
        HTv = HT.ap(BF16, "p (k t) -> p k t", k=8)
        hr = A.alloc("hr", 2048)
        sel = A.alloc("sel", 64)
        hth = A.alloc("hth", 64)
        self.dma("sp", hr.ap(BF16)[0:8, :], hr_d, "hr", [ohr], [hr.o])
        self.dma("sp", sel.ap()[0:8, 0:2], self.di["selhalo"], "sel", [], [sel.o])
        selb = sel.ap(BF16)[:, 8:16]
        self.cp("dve", selb[0:8, 0:2], sel.ap()[0:8, 0:2], [sel.o], [sel.o])
        for kc in range(8):
            self.mm(self.bank(7)[:, kc * 2:kc * 2 + 2], hr.ap(BF16)[0:8, kc * 128:(kc + 1) * 128], selb[0:8, 0:2], True, True,
                    [hr.o, sel.o], [self.BK[7]])
        hthv = hth.ap(BF16)[:, 0:16].rearrange("p (k c) -> p k c", c=2)
        self.cp("dve", hth.ap(BF16)[:, 0:16], self.bank(7)[:, 0:16], [self.BK[7]], [hth.o])
        cw = A.alloc("cw", 44 * 3 * 4)
        cb = A.alloc("cb", 44 * 4)
        cwv = cw.ap(F32, "p (m t) -> p m t", t=3)
        cbv = cb.ap()
        for t in range(3):
            self.dma("sp", cwv[:, :, t], self.di["ffn_conv_w"][l, t].rearrange("(m f) -> f m", f=128), "cw", [], [cw.o])
        self.dma("sp", cbv[:, 0:44], self.di["ffn_conv_b"][l].rearrange("(m f) -> f m", f=128), "cb", [], [cb.o])
        AT = A.alloc("AT", 22 * 1024 * 2)
        ATv = AT.ap(BF16, "p (m t) -> p m t", m=22)
        zc = {z: [A.alloc("zc%s%d" % (z, i), 4096) for i in range(2)] for z in "ab"}
        sA = [A.alloc("sA%d" % i, 4096) for i in range(2)]
        pairc = 0
        hcnt = 0
        for mp in range(11):
            sl = mp % 3
            wav = WA[sl].ap(BF16, "p (k n) -> p k n", k=8)
            wbv = WB[sl].ap(BF16, "p (k n) -> p k n", k=8)
            if mp + 2 < 11:
                load_wu(mp + 2)
            elif mp == 9:
                for q in range(NPRE):
                    load_wd(q)
            for ml in range(2):
                m = mp * 2 + ml
                par = m % 2
                for z, wv, wreg, cidx in (("a", wav, WA[sl], m), ("b", wbv, WB[sl], 22 + m)):
                    pr = pairc % 3
                    pairc += 1
                    Z = self.PB[pr]
                    zobjs = [self.BK[2 * pr], self.BK[2 * pr + 1]]
                    for nt in range(2):
                        for kc in range(8):
                            self.mm(Z[:, nt * 512:(nt + 1) * 512], wv[:, kc, ml * 128:(ml + 1) * 128], HTv[:, kc, nt * 512:(nt + 1) * 512],
                                    kc == 0, kc == 7, [wreg.o, HT.o], [zobjs[nt]])
                    hc = 32 + 2 * (hcnt % 16)
                    hcnt += 1
                    Zh = self.bank(7)[:, hc:hc + 2]
                    for kc in range(8):
                        self.mm(Zh, wv[:, kc, ml * 128:(ml + 1) * 128], hthv[:, kc, :], kc == 0, kc == 7, [wreg.o, hth.o], [self.BK[7]])
                    zr = zc[z][par]
                    zv = zr.ap()
                    w0, w1, w2 = cwv[:, cidx, 0:1], cwv[:, cidx, 1:2], cwv[:, cidx, 2:3]
                    self.act(zv, Z[:, :], AF.Identity, zobjs + [cw.o, cb.o], [zr.o], scale=w1, bias=cbv[:, cidx:cidx + 1])
                    rd = zobjs + [cw.o, zr.o]

                    def tap(dst, src, w, extra=()):
                        self.stt("dve", dst, src, w, dst, ALU.mult, ALU.add, rd + list(extra), [zr.o], same_ok=True)
                    tap(zv[:, 128:1024], Z[:, 0:896], w0)
                    for (a, b) in ((1, 32), (33, 64), (65, 128)):
                        tap(zv[:, a:b], Z[:, 896 + a - 1:896 + b - 1], w0)
                    tap(zv[:, 64:65], Zh[:, 0:1], w0, [self.BK[7]])
                    tap(zv[:, 0:896], Z[:, 128:1024], w2)
                    for (a, b) in ((0, 31), (32, 63), (64, 127)):
                        tap(zv[:, 896 + a:896 + b], Z[:, a + 1:b + 1], w2)
                    tap(zv[:, 1023:1024], Zh[:, 1:2], w2, [self.BK[7]])
                self.act(sA[par].ap(), zc["a"][par].ap(), AF.Silu, [zc["a"][par].o], [sA[par].o])
                self.tt("pool", ATv[:, m, :], sA[par].ap(), zc["b"][par].ap(), ALU.mult, [sA[par].o, zc["b"][par].o], [AT.o])
        A.release(HT, hr, sel, hth, cw, cb, *WA, *WB, *zc["a"], *zc["b"], *sA)
        G2 = self.load_mod(5, "modG")
        tmp = [A.alloc("dtmp%d" % i, 2048) for i in range(2)]
        cnt = 0
        for nh in range(2):
            for mg in range(11):
                sl = cnt % NWD
                if cnt + NPRE < 22:
                    load_wd(cnt + NPRE)
                cnt += 1
                wv = WD[sl].ap(BF16, "p (a n) -> p a n", a=2)
                for ml in range(2):
                    m = mg * 2 + ml
                    for i in range(8):
                        self.mm(self.bank(i), ATv[:, m, i * 128:(i + 1) * 128], wv[:, ml, :], m == 0, m == 21, [AT.o, WD[sl].o], [self.BK[i]])
            for i in range(8):
                t = tmp[i % 2]
                cols = slice(nh * 512, (nh + 1) * 512)
                self.tt("dve", t.ap(), self.bank(i), G2.ap()[:, cols], ALU.mult, [self.BK[i], G2.o], [t.o])
                self.tt("pool", X[:, i, cols], X[:, i, cols], t.ap(), ALU.add, [self.XO[i], t.o], [self.XO[i]])
        A.release(AT, G2, *WD, *tmp)


    def gen_rope(self, pos_reg, n, R, name):
        A = self.A
        nf = R // 4
        m = n * 2 * nf
        invf = A.alloc("invf", nf * 4)
        ii = A.alloc("iota", nf * 4)
        iiv = ii.ap(I32)
        self.P.op("pool", lambda e: e.iota(iiv, pattern=[[1, nf]], base=0, channel_multiplier=0), writes=[ii.o])
        self.cp("dve", invf.ap(), iiv, [ii.o], [invf.o])
        self.act(invf.ap(), invf.ap(), AF.Exp, [invf.o], [invf.o], scale=-math.log(10000.0) / nf)
        ang = A.alloc("ang", m * 4)
        angv = ang.ap(F32, "p (a f) -> p a f", f=nf)
        posv = pos_reg.ap()[:, 0:2 * n]
        self.tt("dve", angv, posv.unsqueeze(2).to_broadcast([128, 2 * n, nf]),
                invf.ap().unsqueeze(1).to_broadcast([128, 2 * n, nf]), ALU.mult, [pos_reg.o, invf.o], [ang.o])
        COS = A.alloc(name + "cos", n * R * 4)
        SIN = A.alloc(name + "sin", n * R * 4)
        t = A.alloc("rt", m * 4)
        ti = A.alloc("rti", m * 4)
        sc = A.alloc("rsc", m * 4)
        for kind, shift, dst in (("sin", 0.5, SIN), ("cos", 0.75, COS)):
            self.ts("dve", t.ap(), ang.ap(), 1.0 / (2 * math.pi), shift, ALU.mult, ALU.add, [ang.o], [t.o])
            self.cp("dve", ti.ap(I32), t.ap(), [t.o], [ti.o])
            self.cp("dve", sc.ap(), ti.ap(I32), [ti.o], [sc.o])
            self.tt("dve", t.ap(), t.ap(), sc.ap(), ALU.subtract, [t.o, sc.o], [t.o])
            self.stt("dve", t.ap(), t.ap(), 0.0, t.ap(), ALU.is_lt, ALU.add, [t.o], [t.o])
            self.act(sc.ap(), t.ap(), AF.Sin, [t.o, self.epsc.o], [sc.o], scale=2 * math.pi, bias=self.epsc.ap()[:, 1:2])
            sv = sc.ap(F32, "p (n c f) -> p n c f", c=2, f=nf)
            dv = dst.ap(F32, "p (n c h f) -> p n c h f", c=2, h=2, f=nf)
            for hf in range(2):
                if kind == "sin" and hf == 0:
                    self.ts("dve", dv[:, :, :, hf, :], sv, -1.0, None, ALU.mult, None, [sc.o], [dst.o])
                else:
                    self.cp("dve", dv[:, :, :, hf, :], sv, [sc.o], [dst.o])
        A.release(invf, ii, ang, t, ti, sc)
        return COS, SIN

    def rope(self, x, H, R, cosap, sinap, cso, t1r, t2r, xo):
        nf = R // 4
        t1 = t1r.ap()[:, 0:H * R].rearrange("p (h r) -> p h r", h=H)
        t2 = t2r.ap()[:, 0:H * R].rearrange("p (h r) -> p h r", h=H)
        cb = cosap.unsqueeze(1).to_broadcast([128, H, R])
        self.tt("pool", t1, x, cb, ALU.mult, [xo] + cso, [t1r.o])
        x5 = x.rearrange("p h (c g f) -> p h c g f", c=2, g=2, f=nf)
        t5 = t2.rearrange("p h (c g f) -> p h c g f", c=2, g=2, f=nf)
        s4 = sinap.rearrange("p (c g f) -> p c g f", c=2, g=2, f=nf)
        for hf in range(2):
            sb = s4[:, :, hf, :].unsqueeze(1).to_broadcast([128, H, 2, nf])
            self.tt("dve", t5[:, :, :, hf, :], x5[:, :, :, 1 - hf, :], sb, ALU.mult, [xo] + cso, [t2r.o], )
        self.tt("dve", x, t1, t2, ALU.add, [t1r.o, t2r.o], [xo])

    def head_norm(self, src, H, d, gain, out, reads, oobj, tmpr, str_, gobj):
        sq = tmpr.ap()[:, 0:H * d].rearrange("p (h d) -> p h d", h=H)
        self.tt("pool", sq, src, src, ALU.mult, reads, [tmpr.o])
        ss = str_.ap()[:, 0:H]
        self.P.op("dve", lambda e: e.tensor_reduce(out=ss, in_=sq, axis=AX.X, op=ALU.add), reads=[tmpr.o], writes=[str_.o])
        self.act(ss, ss, AF.Sqrt, [str_.o, self.epsc.o], [str_.o], bias=self.epsc.ap()[:, 0:1], scale=1.0 / d)
        self.P.op("dve", lambda e: e.reciprocal(out=ss, in_=ss), reads=[str_.o], writes=[str_.o])
        self.tt("dve", out, src, ss.unsqueeze(2).to_broadcast([128, H, d]), ALU.mult, reads + [str_.o], [oobj])
        self.tt("pool", out, out, gain.unsqueeze(1).to_broadcast([128, H, d]), ALU.mult, [oobj, gobj], [oobj])

    def kv_tile(self, S, So, rope, dKT, dV, dKTg, dVg, dobjs, W, pair, tcols=128, tb=0):
        idb = self.identb.ap(BF16)
        sc = W["sc"]
        cb = sc["ckvb"]
        self.cp("act", cb.ap(BF16)[:, 0:256], S[:, 0:256], [So], [cb.o])
        b6 = self.bank(6).bitcast(BF16)
        for kk in range(2):
            self.tr(b6[:, 384 + kk * 128:384 + (kk + 1) * 128], cb.ap(BF16)[:, kk * 128:(kk + 1) * 128], idb, [cb.o, self.identb.o], [self.BK[6]])
        ct = sc["ckvT"]
        self.cp("dve", ct.ap(BF16)[:, 0:256], b6[:, 384:640], [self.BK[6]], [ct.o])
        wk = W["wukv"]
        wkv = wk.ap(BF16, "p (k n) -> p k n", k=2)
        KV = self.PB[pair]
        kobjs = [self.BK[2 * pair], self.BK[2 * pair + 1]]
        for nt in range(2):
            for kk in range(2):
                self.mm(KV[:, nt * 512:(nt + 1) * 512], ct.ap(BF16)[:, kk * 128:(kk + 1) * 128], wkv[:, kk, nt * 512:(nt + 1) * 512],
                        kk == 0, kk == 1, [ct.o, wk.o], [kobjs[nt]])
        kv3 = KV[:, :].rearrange("p (h e) -> p h e", h=8)
        kc = sc["kcat"]
        kcv = kc.ap()[:, 0:768].rearrange("p (h e) -> p h e", h=8)
        self.cp("act", kcv[:, :, 0:64], kv3[:, :, 0:64], kobjs, [kc.o])
        self.cp("pool", kcv[:, :, 64:96], S[:, 256:288].unsqueeze(1).to_broadcast([128, 8, 32]), [So, kc.o], [kc.o])
        self.cp("dve", dV[:, :, 0:64], kv3[:, :, 64:128], kobjs, [dobjs["V"]])
        self.head_norm(kcv, 8, 96, W["gn"].ap()[:, 736:832], kcv, [kc.o], kc.o, sc["t1"], sc["st"], W["gn"].o)
        gk = sc["gk"]
        gkv = gk.ap()[:, 0:128].rearrange("p (h e) -> p h e", h=2)
        self.cp("pool", gk.ap()[:, 0:128], S[:, 288:416], [So], [gk.o])
        if rope is not None:
            c32, s32, c64, s64, ro = rope
            self.rope(kcv[:, :, 64:96], 8, 32, c32, s32, ro, sc["t1"], sc["t2"], kc.o)
            self.rope(gkv, 2, 64, c64, s64, ro, sc["t1"], sc["t2"], gk.o)
        kb = sc["kb"]
        kbv = kb.ap(BF16)[:, 0:768].rearrange("p (h e) -> p h e", h=8)
        self.cp("act", kbv, kcv, [kc.o], [kb.o])
        b7 = self.bank(7).bitcast(BF16)
        for h in range(8):
            self.tr(b7[0:96, h * 128:(h + 1) * 128], kbv[:, h, :], idb, [kb.o, self.identb.o], [self.BK[7]])
        self.cp("dve", dKT, b7[0:96, :].rearrange("p (h t) -> p h t", h=8)[:, :, 0:tcols], [self.BK[7]], [dobjs["KT"]])
        gb = sc["gkb"]
        gbv = gb.ap(BF16)[:, 0:128].rearrange("p (h e) -> p h e", h=2)
        self.cp("act", gbv, gkv, [gk.o], [gb.o])
        for h in range(2):
            self.tr(b6[0:64, 640 + h * 128:640 + (h + 1) * 128], gbv[:, h, :], idb, [gb.o, self.identb.o], [self.BK[6]])
        self.cp("dve", dKTg, b6[0:64, 640:896].rearrange("p (h t) -> p h t", h=2)[:, :, 0:tcols], [self.BK[6]], [dobjs["KTg"]])
        self.cp("pool", dVg[:, :, 0:64], S[:, 416:544].rearrange("p (h e) -> p h e", h=2), [So], [dobjs["Vg"]])

    def attention(self, l):
        jj = l // 2
        A, P, di = self.A, self.P, self.di
        X = self.X.ap(F32, "p (i d) -> p i d", i=8)
        idb = self.identb.ap(BF16)
        gn = A.alloc("gn", 960 * 4)
        for nm, a, b in (("attn_qa_norm_g", 0, 384), ("attn_kva_norm_g", 384, 640), ("attn_mla_q_norm_g", 640, 736),
                         ("attn_mla_k_norm_g", 736, 832), ("attn_gqa_q_norm_g", 832, 896), ("attn_gqa_k_norm_g", 896, 960)):
            self.dma("sp", gn.ap()[:, a:b], di[nm][jj].partition_broadcast(128), "gn", [], [gn.o])
        G = gn.ap()
        qp = A.alloc("qpos", 64)
        self.dma("sp", qp.ap()[:, 0:16], di["qpos"].rearrange("p i c -> p (i c)"), "qpos", [], [qp.o])
        C32, S32 = self.gen_rope(qp, 8, 32, "o32")
        C64, S64 = self.gen_rope(qp, 8, 64, "o64")
        A.release(qp)
        c32v, s32v = C32.ap(F32, "p (n r) -> p n r", n=8), S32.ap(F32, "p (n r) -> p n r", n=8)
        c64v, s64v = C64.ap(F32, "p (n r) -> p n r", n=8), S64.ap(F32, "p (n r) -> p n r", n=8)
        ropeobjs = [C32.o, S32.o, C64.o, S64.o]
        OUT = A.alloc("outst", 8 * 544 * 4)
        OUTv = OUT.ap(F32, "p (i e) -> p i e", i=8)
        QTm = [A.alloc("qtm%d" % k, 8 * 512 * 2) for k in range(2)]
        QTg = [A.alloc("qtg%d" % k, 8 * 512 * 2) for k in range(2)]
        KTo = A.alloc("kto", 8 * 8 * 64 * 2)
        Vo = A.alloc("vo", 8 * 8 * 65 * 2)
        KTgo = A.alloc("ktgo", 2 * 8 * 64 * 2)
        Vgo = A.alloc("vgo", 8 * 2 * 65 * 2)
        QTmv = [q.ap(BF16, "p (h i c) -> p h i c", h=8, i=8) for q in QTm]
        QTgv = [q.ap(BF16, "p (h i c) -> p h i c", h=8, i=8) for q in QTg]
        KTov = KTo.ap(BF16, "p (h i c) -> p h i c", h=8, i=8)
        Vov = Vo.ap(BF16, "p (i h e) -> p i h e", i=8, h=8)
        KTgov = KTgo.ap(BF16, "p (h i c) -> p h i c", h=2, i=8)
        Vgov = Vgo.ap(BF16, "p (i h e) -> p i h e", i=8, h=2)
        self.memset("pool", Vo.ap(BF16), 1.0, [Vo.o])
        self.memset("pool", Vgo.ap(BF16), 1.0, [Vgo.o])
        if self.cfg.get("attn_stop", 9) <= 1:
            return
        win = A.alloc("win", 8 * 1440 * 2)
        winv = win.ap(BF16, "p (k n) -> p k n", k=8)
        self.dma("pool", winv[:, :, 0:720], di["attn_w_in"][jj][:, 0:720].rearrange("(kc k) n -> k kc n", k=128), "win", [], [win.o])
        self.dma("pool", winv[:, :, 720:1440], di["attn_w_in"][jj][:, 720:1440].rearrange("(kc k) n -> k kc n", k=128), "win", [], [win.o])
        wuq = A.alloc("wuq", 3 * 768 * 2)
        wuqv = wuq.ap(BF16, "p (k n) -> p k n", k=3)
        self.dma("pool", wuqv, di["attn_w_uq"][jj].rearrange("(kc k) n -> k kc n", k=128), "wuq", [], [wuq.o])
        wukv = A.alloc("wukv", 2 * 1024 * 2)
        self.dma("pool", wukv.ap(BF16, "p (k n) -> p k n", k=2), di["attn_w_ukv"][jj].rearrange("(kc k) n -> k kc n", k=128), "wukv", [], [wukv.o])
        H = A.alloc("H1", 16384)
        Hv = H.ap(BF16, "p (i d) -> p i d", i=8)
        self.norm_mod(1, lambda i: Hv[:, i, :], H.o)
        HT = A.alloc("HT1", 16384)
        self.transposeT(H, HT)
        A.release(H)
        HTv = HT.ap(BF16, "p (k t) -> p k t", k=8)
        sc = {k: A.alloc("sc_" + k, n) for k, n in (("ckvb", 512), ("ckvT", 512), ("kcat", 3072), ("t1", 3072), ("t2", 3072),
                                                   ("st", 64), ("gk", 512), ("kb", 1536), ("gkb", 256))}
        W = {"sc": sc, "wukv": wukv, "gn": gn}
        qs = A.alloc("qs", 3072)
        qb = A.alloc("qb", 1536)
        st2 = A.alloc("st2", 64)
        junk = A.alloc("junk", 2048)
        qcb = A.alloc("qcb", 768)
        qcT = A.alloc("qcT", 768)
        splits = ((0, 384), (384, 672), (672, 1184), (1184, 1440))
        stp = self.cfg.get("attn_stop", 9)
        if stp <= 1.2:
            return
        for i in range(8):
            for bk, (c0, c1) in enumerate(splits):
                for kc in range(8):
                    self.mm(self.bank(bk)[:, 0:c1 - c0], HTv[:, kc, i * 128:(i + 1) * 128], winv[:, kc, c0:c1], kc == 0, kc == 7,
                            [HT.o, win.o], [self.BK[bk]])
            if stp <= 1.4:
                continue
            ss = st2.ap()[:, 0:1]
            self.memset("dve", st2.ap()[:, 0:2], 0.0, [st2.o])
            self.act(junk.ap()[:, 0:384], self.bank(0)[:, 0:384], AF.Square, [self.BK[0], st2.o], [junk.o, st2.o], accum_out=ss)
            self.act(ss, ss, AF.Sqrt, [st2.o, self.epsc.o], [st2.o], bias=self.epsc.ap()[:, 0:1], scale=1.0 / 384)
            self.P.op("dve", lambda e: e.reciprocal(out=st2.ap()[:, 0:1], in_=st2.ap()[:, 0:1]), reads=[st2.o], writes=[st2.o])
            self.stt("dve", qcb.ap(BF16)[:, 0:384], self.bank(0)[:, 0:384], ss, G[:, 0:384], ALU.mult, ALU.mult, [self.BK[0], st2.o, gn.o], [qcb.o])
            if stp <= 1.45:
                continue
            b6 = self.bank(6).bitcast(BF16)
            for kk in range(3):
                self.tr(b6[:, kk * 128:(kk + 1) * 128], qcb.ap(BF16)[:, kk * 128:(kk + 1) * 128], idb, [qcb.o, self.identb.o], [self.BK[6]])
            self.cp("dve", qcT.ap(BF16)[:, 0:384], b6[:, 0:384], [self.BK[6]], [qcT.o])
            QM = self.PB[2]
            for nt in range(2):
                for kk in range(3):
                    self.mm(QM[:, nt * 512:nt * 512 + 384], qcT.ap(BF16)[:, kk * 128:(kk + 1) * 128], wuqv[:, kk, nt * 384:(nt + 1) * 384],
                            kk == 0, kk == 2, [qcT.o, wuq.o], [self.BK[4 + nt]])
            qsv = qs.ap()[:, 0:768].rearrange("p (h e) -> p h e", h=8)
            self.cp("act", qs.ap()[:, 0:768].rearrange("p (a e) -> p a e", a=2), QM[:, :].rearrange("p (a e) -> p a e", a=2)[:, :, 0:384],
                    [self.BK[4], self.BK[5]], [qs.o])
            if stp <= 1.5:
                continue
            self.head_norm(qsv, 8, 96, G[:, 640:736], qsv, [qs.o], qs.o, sc["t1"], sc["st"], gn.o)
            if stp <= 1.55:
                continue
            self.rope(qsv[:, :, 64:96], 8, 32, c32v[:, i, :], s32v[:, i, :], ropeobjs, sc["t1"], sc["t2"], qs.o)
            if stp <= 1.57:
                continue
            qbv = qb.ap(BF16)[:, 0:768].rearrange("p (h e) -> p h e", h=8)
            self.cp("act", qbv, qsv, [qs.o], [qb.o])
            b7 = self.bank(7).bitcast(BF16)
            if stp <= 1.58:
                continue
            for h in range(8):
                self.tr(b7[0:96, h * 128:(h + 1) * 128], qbv[:, h, :], idb, [qb.o, self.identb.o], [self.BK[7]])
            b73 = b7[0:96, :].rearrange("p (h t) -> p h t", h=8)
            if stp <= 1.59:
                continue
            self.cp("dve", QTmv[0][0:96, :, i, :], b73[:, :, 0:64], [self.BK[7]], [QTm[0].o])
            if stp <= 1.595:
                continue
            self.cp("act", QTmv[1][0:96, :, i, :], b73[:, :, 64:128], [self.BK[7]], [QTm[1].o])
            if stp <= 1.6:
                continue
            ss2 = st2.ap()[:, 1:2]
            self.act(junk.ap()[:, 0:256], self.bank(1)[:, 0:256], AF.Square, [self.BK[1], st2.o], [junk.o, st2.o], accum_out=ss2)
            self.act(ss2, ss2, AF.Sqrt, [st2.o, self.epsc.o], [st2.o], bias=self.epsc.ap()[:, 0:1], scale=1.0 / 256)
            self.P.op("dve", lambda e: e.reciprocal(out=st2.ap()[:, 1:2], in_=st2.ap()[:, 1:2]), reads=[st2.o], writes=[st2.o])
            self.stt("dve", OUTv[:, i, 0:256], self.bank(1)[:, 0:256], ss2, G[:, 384:640], ALU.mult, ALU.mult, [self.BK[1], st2.o, gn.o], [OUT.o])
            self.cp("act", OUTv[:, i, 256:288], self.bank(1)[:, 256:288], [self.BK[1]], [OUT.o])
            self.cp("act", OUTv[:, i, 416:544], self.bank(3)[:, 128:256], [self.BK[3]], [OUT.o])
            gks = sc["gk"]
            self.cp("act", gks.ap()[:, 0:128], self.bank(3)[:, 0:128], [self.BK[3]], [gks.o])
            gk3 = gks.ap()[:, 0:128].rearrange("p (h e) -> p h e", h=2)
            self.head_norm(gk3, 2, 64, G[:, 896:960], OUTv[:, i, 288:416].rearrange("p (h e) -> p h e", h=2), [gks.o], OUT.o, sc["t1"], sc["st"], gn.o)
            if stp <= 1.7:
                continue
            self.cp("act", qs.ap()[:, 0:512], self.bank(2)[:, 0:512], [self.BK[2]], [qs.o])
            gq3 = qs.ap()[:, 0:512].rearrange("p (h e) -> p h e", h=8)
            self.head_norm(gq3, 8, 64, G[:, 832:896], gq3, [qs.o], qs.o, sc["t1"], sc["st"], gn.o)
            self.rope(gq3, 8, 64, c64v[:, i, :], s64v[:, i, :], ropeobjs, sc["t1"], sc["t2"], qs.o)
            gqb = qb.ap(BF16)[:, 0:512].rearrange("p (h e) -> p h e", h=8)
            self.cp("act", gqb, gq3, [qs.o], [qb.o])
            for h in range(8):
                self.tr(b7[0:64, h * 128:(h + 1) * 128], gqb[:, h, :], idb, [qb.o, self.identb.o], [self.BK[7]])
            b74 = b7[0:64, :].rearrange("p (h t) -> p h t", h=8)
            self.cp("dve", QTgv[0][0:64, :, i, :], b74[:, :, 0:64], [self.BK[7]], [QTg[0].o])
            self.cp("act", QTgv[1][0:64, :, i, :], b74[:, :, 64:128], [self.BK[7]], [QTg[1].o])
            if stp <= 1.8:
                continue
            self.kv_tile(OUTv[:, i, :], OUT.o, (c32v[:, i, :], s32v[:, i, :], c64v[:, i, :], s64v[:, i, :], ropeobjs),
                         KTov[0:96, :, i, :], Vov[:, i, :, :], KTgov[0:64, :, i, :], Vgov[:, i, :, :],
                         {"KT": KTo.o, "V": Vo.o, "KTg": KTgo.o, "Vg": Vgo.o}, W, 2, tcols=64)
        A.release(HT, win, wuq, qs, qb, st2, junk, qcb, qcT, C32, S32, C64, S64)
        for s_ in range(2):
            rows = slice(32 * s_, 32 * s_ + 32)
            for nm, a, b in (("ockv", 0, 256), ("okrope", 256, 288), ("ogk", 288, 416), ("ogv", 416, 544)):
                self.dma("sp", self.do[nm][s_, jj].rearrange("(c i) d -> c i d", i=8), OUTv[rows, :, a:b], "o" + nm, [OUT.o], [], is_output=True)
        if self.cfg.get("attn_stop", 9) <= 2:
            return
        orecv = []
        for hf in range(2):
            osend, orc = Obj("kvsend%d" % hf), Obj("kvrecv%d" % hf)
            ks, kr = self.kvsend[jj][hf], self.kvrecv[jj][hf]
            self.dma("sp", ks.rearrange("(c i) d -> c i d", i=8), OUTv[64 + 32 * hf:96 + 32 * hf, :, :], "kvs", [OUT.o], [osend])
            P.dma("pool", (lambda ks, kr: lambda e: e.collective_compute("AllGather", ALU.bypass, replica_groups=[[0, 1, 2, 3], [4, 5, 6, 7]],
                                                                         ins=[ks.opt()], outs=[kr.opt()]))(ks, kr),
                  "cc_kv%d_%d" % (jj, hf), reads=[osend], writes=[orc], inc=1)
            orecv.append(orc)
        A.release(OUT)
        kp = A.alloc("kpos", 128)
        self.dma("sp", kp.ap()[:, 0:32], di["kpos"].rearrange("p u c -> p (u c)"), "kpos", [], [kp.o])
        KC32, KS32 = self.gen_rope(kp, 16, 32, "k32")
        KC64, KS64 = self.gen_rope(kp, 16, 64, "k64")
        A.release(kp)
        kro = [KC32.o, KS32.o, KC64.o, KS64.o]
        kc32, ks32 = KC32.ap(F32, "p (n r) -> p n r", n=16), KS32.ap(F32, "p (n r) -> p n r", n=16)
        kc64, ks64 = KC64.ap(F32, "p (n r) -> p n r", n=16), KS64.ap(F32, "p (n r) -> p n r", n=16)
        KT = A.alloc("KT", 8 * 2304 * 2)
        VM = A.alloc("VM", 18 * 8 * 65 * 2)
        KTg = A.alloc("KTg", 2 * 2304 * 2)
        VG = A.alloc("VG", 18 * 2 * 65 * 2)
        KTv = KT.ap(BF16, "p (h t) -> p h t", h=8)
        VMv = VM.ap(BF16, "p (u h e) -> p u h e", u=18, h=8)
        KTgv = KTg.ap(BF16, "p (h t) -> p h t", h=2)
        VGv = VG.ap(BF16, "p (u h e) -> p u h e", u=18, h=2)
        self.memset("pool", VM.ap(BF16), 1.0, [VM.o])
        self.memset("pool", VG.ap(BF16), 1.0, [VG.o])
        stg = [A.alloc("stg%d" % k, 544 * 4) for k in range(2)]
        dob = {"KT": KT.o, "V": VM.o, "KTg": KTg.o, "Vg": VG.o}
        sc2 = {k: A.alloc("sc2_" + k, r_.req) for k, r_ in sc.items()}
        W2 = {"sc": sc2, "wukv": wukv, "gn": gn}
        for u in range(18):
            sg = stg[u % 2]
            if u < 2:
                rows = slice(128 * u, 128 * u + 128)
                self.dma("sp", sg.ap()[:, 0:256], di["cache_ckv"][jj, rows, :], "stg%d" % (u % 2), [], [sg.o])
                self.dma("sp", sg.ap()[:, 256:288], di["cache_krope"][jj, rows, :], "stg%d" % (u % 2), [], [sg.o])
                self.dma("sp", sg.ap()[:, 288:416], di["cache_gk"][jj, rows, :], "stg%d" % (u % 2), [], [sg.o])
                self.dma("sp", sg.ap()[:, 416:544], di["cache_gv"][jj, rows, :], "stg%d" % (u % 2), [], [sg.o])
                rp = None
            else:
                v = u - 2
                rk, w4 = v // 4, v % 4
                src = self.kvrecv[jj][w4 // 2][256 * rk + 128 * (w4 % 2):256 * rk + 128 * (w4 % 2) + 128, :]
                self.dma("sp", sg.ap()[:, 0:544], src, "stg%d" % (u % 2), [orecv[w4 // 2]], [sg.o])
                rp = (kc32[:, v, :], ks32[:, v, :], kc64[:, v, :], ks64[:, v, :], kro)
            self.kv_tile(sg.ap()[:, 0:544], sg.o, rp, KTv[0:96, :, u * 128:(u + 1) * 128], VMv[:, u, :, :],
                         KTgv[0:64, :, u * 128:(u + 1) * 128], VGv[:, u, :, :], dob, W if u % 2 == 0 else W2, u % 3, tb=u % 2)
        A.release(wukv, KC32, KS32, KC64, KS64, *stg, *sc.values(), *sc2.values())
        if self.cfg.get("attn_stop", 9) <= 3:
            return
        G1 = self.load_mod(2, "modG")
        OT = A.alloc("OT", 8 * 1024 * 2)
        OTv = OT.ap(BF16, "p (h i c) -> p h i c", h=8, i=8)
        PT = [A.alloc("PT%d" % k, 1024) for k in range(3)]
        PTp = A.alloc("PTp", 8 * 256 * 2)
        PTpv = PTp.ap(BF16, "p (g i c) -> p g i c", g=8, i=8)
        rsr = A.alloc("rsr", 2048)
        bcs = A.alloc("bcs", 2048)
        wo = [A.alloc("wo%d" % k, 4 * 512 * 2) for k in range(2)]
        wov = [w_.ap(BF16, "p (h n) -> p h n", h=4) for w_ in wo]
        dtmp = [A.alloc("atmp%d" % k, 2048) for k in range(2)]
        onesf = self.onesf.ap()
        ptc = 0
        sbc = 0
        for grp in range(2):
            dq = 96 if grp == 0 else 64
            scale = 1.0 / math.sqrt(dq)
            for h in range(8):
                hk = h if grp == 0 else h // 4
                if grp == 0:
                    qts, qtp, qo_s, qo_p = QTmv[1][0:96, h], QTmv[0][0:96, h], QTm[1].o, QTm[0].o
                    kt_all, kto_, kobj, koobj = KTv[0:96, hk], KTov[0:96, hk], KT.o, KTo.o
                    v_all, v_own, vobj, voobj = VMv, Vov, VM.o, Vo.o
                else:
                    qts, qtp, qo_s, qo_p = QTgv[1][0:64, h], QTgv[0][0:64, h], QTg[1].o, QTg[0].o
                    kt_all, kto_, kobj, koobj = KTgv[0:64, hk], KTgov[0:64, hk], KTg.o, KTgo.o
                    v_all, v_own, vobj, voobj = VGv, Vgov, VG.o, Vgo.o
                ob = 4 + h % 2
                OB = self.bank(ob)

                pend = []
                for u in range(18 + 2):
                    if u < 18:
                        sb = sbc % 4
                        sbc += 1
                        self.mm(self.bank(sb), kt_all[:, u * 128:(u + 1) * 128], qts.rearrange("p i c -> p (i c)"), True, True, [kobj, qo_s], [self.BK[sb]])
                        pt = PT[ptc % 3]
                        ptc += 1
                        self.act(pt.ap(BF16), self.bank(sb), AF.Exp, [self.BK[sb]], [pt.o], scale=scale)
                        pend.append((u, pt))
                    if u >= 2:
                        uu, pt2 = pend.pop(0)
                        self.mm(OB[0:65, :], v_all[:, uu, hk, :], pt2.ap(BF16), uu == 0, uu == 17, [vobj, pt2.o], [self.BK[ob]])
                self.act(rsr.ap()[64:65, 0:512], OB[64:65, 0:512], AF.Ln, [self.BK[ob]], [rsr.o])
                self.act(rsr.ap()[64:65, 0:512], rsr.ap()[64:65, 0:512], AF.Exp, [rsr.o], [rsr.o], scale=-1.0)
                self.mm(self.bank(6)[0:64, 0:512], onesf[64:65, 0:64], rsr.ap()[64:65, 0:512], True, True, [rsr.o, self.onesf.o], [self.BK[6]])
                self.cp("act", bcs.ap()[0:64, 0:512], self.bank(6)[0:64, 0:512], [self.BK[6]], [bcs.o])
                self.tt("dve", OTv[0:64, h, :, 64:128], OB[0:64, 0:512].rearrange("p (i c) -> p i c", i=8),
                        bcs.ap()[0:64, 0:512].rearrange("p (i c) -> p i c", i=8), ALU.mult, [self.BK[ob], bcs.o], [OT.o])
                for ig in range(8):
                    sb = sbc % 4
                    sbc += 1
                    self.mm(self.bank(sb)[0:64, :], kto_[:, ig, 0:64], qtp.rearrange("p i c -> p (i c)"), True, True, [koobj, qo_p], [self.BK[sb]])
                    s3 = self.bank(sb)[0:64, :].rearrange("p (i c) -> p i c", i=8)
                    for s_ in range(2):
                        rows = slice(32 * s_, 32 * s_ + 32)
                        self.act(PTpv[rows, ig, :, 0:32], s3[rows, :, 32 * s_:32 * s_ + 32], AF.Exp, [self.BK[sb]], [PTp.o], scale=scale)
                for s_ in range(2):
                    rows = slice(32 * s_, 32 * s_ + 32)
                    ob2 = 7
                    OB2 = self.bank(ob2)
                    for ig in range(8):
                        self.mm(OB2[0:65, 0:256], v_own[rows, ig, hk, :], PTpv[rows, ig, :, 0:32].rearrange("p i c -> p (i c)"), ig == 0, ig == 7,
                                [voobj, PTp.o], [self.BK[ob2]])
                    self.act(rsr.ap()[64:65, 0:256], OB2[64:65, 0:256], AF.Ln, [self.BK[ob2]], [rsr.o])
                    self.act(rsr.ap()[64:65, 0:256], rsr.ap()[64:65, 0:256], AF.Exp, [rsr.o], [rsr.o], scale=-1.0)
                    self.mm(self.bank(6)[0:64, 0:256], onesf[64:65, 0:64], rsr.ap()[64:65, 0:256], True, True, [rsr.o, self.onesf.o], [self.BK[6]])
                    self.cp("act", bcs.ap()[0:64, 0:256], self.bank(6)[0:64, 0:256], [self.BK[6]], [bcs.o])
                    self.tt("dve", OTv[0:64, h, :, 32 * s_:32 * s_ + 32], OB2[0:64, 0:256].rearrange("p (i c) -> p i c", i=8),
                            bcs.ap()[0:64, 0:256].rearrange("p (i c) -> p i c", i=8), ALU.mult, [self.BK[ob2], bcs.o], [OT.o])
            for nh in range(2):
                cols = slice(nh * 512, (nh + 1) * 512)
                for k2 in range(2):
                    self.dma("pool", wov[k2][0:64], di["attn_w_out"][jj][grp * 512 + k2 * 256:grp * 512 + (k2 + 1) * 256, cols].rearrange("(h d) n -> d h n", d=64),
                             "wo%d" % k2, [], [wo[k2].o])
                for i in range(8):
                    bk = i % 4
                    for h in range(8):
                        self.mm(self.bank(bk), OTv[0:64, h, i, :], wov[h // 4][0:64, h % 4, :], h == 0, h == 7, [OT.o, wo[h // 4].o], [self.BK[bk]])
                    t = dtmp[i % 2]
                    self.tt("dve", t.ap(), self.bank(bk), G1.ap()[:, cols], ALU.mult, [self.BK[bk], G1.o], [t.o])
                    self.tt("pool", X[:, i, cols], X[:, i, cols], t.ap(), ALU.add, [self.XO[i], t.o], [self.XO[i]])
        A.release(gn, G1, OT, PTp, rsr, bcs, *wo, KT, VM, KTg, VG, KTo, Vo, KTgo, Vgo, *PT, *dtmp, *QTm, *QTg)


    def gen_pw(self, k0, step, ardt, aidt, name):
        A = self.A
        n = 8 * 64
        ki = A.alloc("ki", 32)
        kf = A.alloc("kf", 32)
        kiv = ki.ap(I32)
        self.P.op("pool", lambda e: e.iota(kiv, pattern=[[step, 8]], base=k0, channel_multiplier=0), writes=[ki.o])
        self.cp("dve", kf.ap(), kiv, [ki.o], [kf.o])
        kb = kf.ap().unsqueeze(2).to_broadcast([128, 8, 64])
        Pre = A.alloc(name + "re", n * 4)
        Pim = A.alloc(name + "im", n * 4)
        mag = A.alloc("mag", n * 4)
        ang = A.alloc("pang", n * 4)
        t = A.alloc("pt", n * 4)
        ti = A.alloc("pti", n * 4)
        v3 = lambda r: r.ap(F32, "p (k g) -> p k g", k=8)
        self.tt("dve", v3(mag), kb, ardt.ap().unsqueeze(1).to_broadcast([128, 8, 64]), ALU.mult, [kf.o, ardt.o], [mag.o])
        self.act(mag.ap(), mag.ap(), AF.Exp, [mag.o], [mag.o])
        self.tt("dve", v3(ang), kb, aidt.ap().unsqueeze(1).to_broadcast([128, 8, 64]), ALU.mult, [kf.o, aidt.o], [ang.o])
        for shift, dst in ((0.5, Pim), (0.75, Pre)):
            self.ts("dve", t.ap(), ang.ap(), 1.0 / (2 * math.pi), shift + 32.0, ALU.mult, ALU.add, [ang.o], [t.o])
            self.cp("dve", ti.ap(I32), t.ap(), [t.o], [ti.o])
            self.cp("dve", dst.ap(), ti.ap(I32), [ti.o], [dst.o])
            self.tt("dve", t.ap(), t.ap(), dst.ap(), ALU.subtract, [t.o, dst.o], [t.o])
            self.stt("dve", t.ap(), t.ap(), 0.0, t.ap(), ALU.is_lt, ALU.add, [t.o], [t.o])
            self.act(dst.ap(), t.ap(), AF.Sin, [t.o, self.epsc.o], [dst.o], scale=2 * math.pi, bias=self.epsc.ap()[:, 1:2])
            self.tt("dve", dst.ap(), dst.ap(), mag.ap(), ALU.mult, [dst.o, mag.o], [dst.o])
        A.release(ki, kf, mag, ang, t, ti)
        return Pre, Pim

    def ssm(self, l):
        j2 = l // 2
        A, P, di = self.A, self.P, self.di
        X = self.X.ap(F32, "p (i d) -> p i d", i=8)
        idb, idf = self.identb.ap(BF16), self.identf.ap()
        ENG = ("dve", "pool")
        U = A.alloc("U", 16384)
        Uv = U.ap(BF16, "p (g i h) -> p g i h", g=64, i=8)
        self.norm_mod(1, lambda i: Uv[:, :, i, :], U.o, view=lambda ap: ap.rearrange("p (g h) -> p g h", g=64))
        VA = A.alloc("VA", 16384)
        VAv = VA.ap(BF16, "p (g c) -> p g c", g=64)
        for gb in range(8):
            bk = 6 + gb % 2
            psb = self.bank(bk).bitcast(BF16)
            for gl in range(8):
                self.tr(psb[:, gl * 128:(gl + 1) * 128], Uv[:, gb * 8 + gl, :, :].rearrange("p i h -> p (i h)"), idb, [U.o, self.identb.o], [self.BK[bk]])
            self.cp("act" if gb % 2 else "dve", VAv[:, gb * 8:(gb + 1) * 8, :], psb.rearrange("p (g c) -> p g c", g=8), [self.BK[bk]], [VA.o])
        A.release(U)
        if self.cfg.get("ssm_stop", 99) <= 1:
            return
        QWB, PWC, QXB, BR, BI = [], [], [], [], []
        MRE2, NMIM, PMIM, M64 = [], [], [], []
        small = A.alloc("ssmall", 64 * 4 * 12)
        sm = lambda k: small.ap()[:, k * 64:(k + 1) * 64]
        for d in range(2):
            aTr, aTi, dtb = A.alloc("aTr", 256), A.alloc("aTi", 256), A.alloc("dtb", 256)
            for half in range(2):
                rows = slice(64 * half, 64 * half + 64)
                self.dma("sp", aTr.ap()[rows, :], di["ssm_a_re"][j2, d].rearrange("g p -> p g"), "ssmp", [], [aTr.o])
                self.dma("sp", aTi.ap()[rows, :], di["ssm_a_im"][j2, d].rearrange("g p -> p g"), "ssmp", [], [aTi.o])
            self.dma("sp", dtb.ap(), di["ssm_log_dt"][j2, d].partition_broadcast(128), "ssmp", [], [dtb.o])
            self.act(dtb.ap(), dtb.ap(), AF.Exp, [dtb.o], [dtb.o])
            ardt, aidt = A.alloc("ardt", 256), A.alloc("aidt", 256)
            self.tt("dve", ardt.ap(), aTr.ap(), dtb.ap(), ALU.mult, [aTr.o, dtb.o], [ardt.o])
            self.tt("dve", aidt.ap(), aTi.ap(), dtb.ap(), ALU.mult, [aTi.o, dtb.o], [aidt.o])
            wbk, wck, xbk = ((7, -1), (1, 1), (-1, -1)) if d == 0 else ((0, 1), (8, -1), (-8, 1))
            pwb = self.gen_pw(wbk[0], wbk[1], ardt, aidt, "pwb%d" % d)
            pwc = self.gen_pw(wck[0], wck[1], ardt, aidt, "pwc%d" % d)
            pxb = self.gen_pw(xbk[0], xbk[1], ardt, aidt, "pxb%d" % d)
            il, imu = (0, 7) if d == 0 else (7, 0)
            pcr = pwc[0].ap(F32, "p (k g) -> p k g", k=8)
            pci = pwc[1].ap(F32, "p (k g) -> p k g", k=8)
            so = small.o
            den, rden, nre, t1, t2, kre, kim = sm(0), sm(1), sm(2), sm(3), sm(4), sm(5), sm(6)
            self.tt("dve", den, aTr.ap(), aTr.ap(), ALU.mult, [aTr.o], [so])
            self.tt("dve", t1, aTi.ap(), aTi.ap(), ALU.mult, [aTi.o, so], [so])
            self.tt("dve", den, den, t1, ALU.add, [so], [so])
            self.recip(rden, den, [so], [so])
            self.ts("dve", nre, pcr[:, il, :], -1.0, None, ALU.add, None, [pwc[0].o, so], [so])
            self.tt("dve", t1, nre, aTr.ap(), ALU.mult, [so, aTr.o], [so])
            self.tt("dve", t2, pci[:, il, :], aTi.ap(), ALU.mult, [pwc[1].o, aTi.o, so], [so])
            self.tt("dve", t1, t1, t2, ALU.add, [so], [so])
            self.tt("dve", kre, t1, rden, ALU.mult, [so], [so])
            self.tt("dve", t1, pci[:, il, :], aTr.ap(), ALU.mult, [pwc[1].o, aTr.o, so], [so])
            self.tt("dve", t2, nre, aTi.ap(), ALU.mult, [so, aTi.o], [so])
            self.tt("dve", t1, t1, t2, ALU.subtract, [so], [so])
            self.tt("dve", kim, t1, rden, ALU.mult, [so], [so])
            qt1, qt2 = A.alloc("qt1", 2048), A.alloc("qt2", 2048)
            for (pr, pi_) in (pwb, pxb):
                p3r = pr.ap(F32, "p (k g) -> p k g", k=8)
                p3i = pi_.ap(F32, "p (k g) -> p k g", k=8)
                q1 = qt1.ap(F32, "p (k g) -> p k g", k=8)
                q2 = qt2.ap(F32, "p (k g) -> p k g", k=8)
                krb = kre.unsqueeze(1).to_broadcast([128, 8, 64])
                kib = kim.unsqueeze(1).to_broadcast([128, 8, 64])
                self.tt("dve", q1, p3r, kib, ALU.mult, [pr.o, so], [qt1.o])
                self.tt("pool", q2, p3i, kib, ALU.mult, [pi_.o, so], [qt2.o])
                self.tt("dve", p3r, p3r, krb, ALU.mult, [pr.o, so], [pr.o])
                self.tt("dve", p3r, p3r, q2, ALU.subtract, [pr.o, qt2.o], [pr.o])
                self.tt("pool", p3i, p3i, krb, ALU.mult, [pi_.o, so], [pi_.o])
                self.tt("pool", p3i, p3i, q1, ALU.add, [pi_.o, qt1.o], [pi_.o])
            A.release(qt1, qt2)
            mt = A.alloc("mt%d" % d, (64 + 32 + 32 + 64) * 4)
            mre2 = mt.ap()[:, 0:64].rearrange("p (r q) -> p r q", r=2)
            nmim, pmim = mt.ap()[:, 64:96], mt.ap()[:, 96:128]
            m64 = mt.ap()[:, 128:192].rearrange("p (r q) -> p r q", r=2)
            for par in range(2):
                rows = slice(64 * par, 64 * par + 64)
                srcr = pcr[rows, imu, :].rearrange("p (q two) -> p q two", two=2)[:, :, par]
                srci = pci[rows, imu, :].rearrange("p (q two) -> p q two", two=2)[:, :, par]
                for r_ in range(2):
                    self.cp("dve", mre2[rows, r_, :], srcr, [pwc[0].o], [mt.o])
                self.cp("dve", pmim[rows, :], srci, [pwc[1].o], [mt.o])
                self.ts("dve", nmim[rows, :], srci, -1.0, None, ALU.mult, None, [pwc[1].o], [mt.o])
            self.cp("dve", m64[:, 0, :], mre2[:, 0, :], [mt.o], [mt.o])
            self.cp("dve", m64[:, 1, :], pmim, [mt.o], [mt.o])
            sq1, sq2 = sm(7)[:, 0:32], sm(8)[:, 0:32]
            for _ in range(6):
                self.tt("dve", sq1, m64[:, 0, :], m64[:, 0, :], ALU.mult, [mt.o, so], [so])
                self.tt("dve", sq2, m64[:, 1, :], m64[:, 1, :], ALU.mult, [mt.o, so], [so])
                self.stt("dve", m64[:, 1, :], m64[:, 0, :], 2.0, m64[:, 1, :], ALU.mult, ALU.mult, [mt.o], [mt.o])
                self.tt("dve", m64[:, 0, :], sq1, sq2, ALU.subtract, [so], [mt.o])
            br, bi = A.alloc("br%d" % d, 4096), A.alloc("bi%d" % d, 4096)
            for half in range(2):
                rows = slice(64 * half, 64 * half + 64)
                self.dma("sp", br.ap(F32, "p (g h) -> p g h", g=64)[rows], di["ssm_b_re"][j2, d].rearrange("g p h -> p g h"), "ssmb", [], [br.o])
                self.dma("sp", bi.ap(F32, "p (g h) -> p g h", g=64)[rows], di["ssm_b_im"][j2, d].rearrange("g p h -> p g h"), "ssmb", [], [bi.o])
            A.release(aTr, aTi, dtb, ardt, aidt)
            QWB.append(pwb); PWC.append(pwc); QXB.append(pxb); BR.append(br); BI.append(bi)
            MRE2.append(mre2); NMIM.append(nmim); PMIM.append(pmim); M64.append((m64, mt))
        if self.cfg.get("ssm_stop", 99) <= 2:
            return
        S = [A.alloc("S%d" % d, 2 * 32 * 131 * 2) for d in range(2)]
        S4 = [s_.ap(BF16, "p (r q c) -> p r q c", r=2, q=32) for s_ in S]
        wpre = A.alloc("wpre", 8 * 2 * 128 * 4)
        wt1, wt2 = A.alloc("wt1", 8 * 128 * 4), A.alloc("wt2", 8 * 128 * 4)
        wt3, wt4 = A.alloc("wt3", 8 * 128 * 4), A.alloc("wt4", 8 * 128 * 4)
        w3 = wt3.ap(F32, "p (g i h) -> p g i h", g=8, i=8)
        w4 = wt4.ap(F32, "p (g i h) -> p g i h", g=8, i=8)
        wbc = [A.alloc("wbc%d" % k, 8 * 2 * 64 * 2) for k in range(2)]
        wpv = wpre.ap(F32, "p (g r i h) -> p g r i h", g=8, r=2, i=8)
        w1 = wt1.ap(F32, "p (g i h) -> p g i h", g=8, i=8)
        w2 = wt2.ap(F32, "p (g i h) -> p g i h", g=8, i=8)
        segs = ((0, 32), (32, 64), (64, 128))
        cnt = 0
        for d in range(2):
            qr = QWB[d][0].ap(F32, "p (k g) -> p k g", k=8)
            qi = QWB[d][1].ap(F32, "p (k g) -> p k g", k=8)
            brv = BR[d].ap(F32, "p (g h) -> p g h", g=64)
            biv = BI[d].ap(F32, "p (g h) -> p g h", g=64)
            qo = [QWB[d][0].o, QWB[d][1].o, BR[d].o, BI[d].o]
            for gb in range(8):
                gs = slice(gb * 8, gb * 8 + 8)
                Er = qr[0:64, :, gs].rearrange("p i g -> p g i").unsqueeze(3).to_broadcast([64, 8, 8, 16])
                Ei = qi[0:64, :, gs].rearrange("p i g -> p g i").unsqueeze(3).to_broadcast([64, 8, 8, 16])
                Br_ = brv[0:64, gs, :].unsqueeze(2).to_broadcast([64, 8, 8, 16])
                Bi_ = biv[0:64, gs, :].unsqueeze(2).to_broadcast([64, 8, 8, 16])
                self.tt("pool", w2[0:64], Ei, Bi_, ALU.mult, qo, [wt2.o])
                self.tt("dve", w1[0:64], Er, Br_, ALU.mult, qo, [wt1.o])
                self.tt("pool", w4[0:64], Ei, Br_, ALU.mult, qo, [wt4.o])
                self.tt("dve", w3[0:64], Er, Bi_, ALU.mult, qo, [wt3.o])
                self.tt("dve", wpv[0:64, :, 0], w1[0:64], w2[0:64], ALU.subtract, [wt1.o, wt2.o], [wpre.o])
                self.tt("dve", wpv[0:64, :, 1], w3[0:64], w4[0:64], ALU.add, [wt3.o, wt4.o], [wpre.o])
                pr = cnt % 3
                cnt += 1
                PT_ = self.PB[pr]
                pob = [self.BK[2 * pr], self.BK[2 * pr + 1]]
                wflat = wpre.ap(F32, "p (g r e) -> p g r e", g=8, r=2)
                for gl in range(8):
                    for r_ in range(2):
                        c0 = (gl * 2 + r_) * 64
                        self.tr(PT_[:, c0:c0 + 64], wflat[0:64, gl, r_, :], idf[0:64, 0:64], [wpre.o, self.identf.o], [pob[c0 // 512]])
                wb = wbc[gb % 2]
                wbv = wb.ap(BF16, "p (g r e) -> p g r e", g=8, r=2)
                self.cp("act", wb.ap(BF16), PT_[:, :], pob, [wb.o])
                pr2 = cnt % 3
                cnt += 1
                PS_ = self.PB[pr2]
                pob2 = [self.BK[2 * pr2], self.BK[2 * pr2 + 1]]
                for gl in range(8):
                    g = gb * 8 + gl
                    par, pl = g % 2, gl // 2
                    for r_ in range(2):
                        c0 = (pl * 2 + r_) * 128
                        kw = {"tile_position": (0, 64)} if par else {}
                        self.mm(PS_[64 * par:64 * par + 64, c0:c0 + 128], wbv[:, gl, r_, :], VAv[:, g, :], True, True, [wb.o, VA.o], [pob2[c0 // 512]], **kw)
                ps4 = PS_[:, :].rearrange("p (q r c) -> p r q c", q=4, r=2)
                for si, (a, b) in enumerate(segs):
                    base = (0, 33, 66)[si] + (1 if d == 0 else 0)
                    self.cp("act" if si % 2 else "dve", S4[d][:, :, gb * 4:(gb + 1) * 4, base:base + (b - a)], ps4[:, :, :, a:b], pob2, [S[d].o])
        A.release(wpre, wt1, wt2, wt3, wt4, *wbc, QWB[0][0], QWB[0][1], QWB[1][0], QWB[1][1])
        if self.cfg.get("ssm_stop", 99) <= 3:
            return
        selr = A.alloc("selr", 32)
        self.dma("sp", selr.ap()[:, 0:8], di["selr"], "selr", [], [selr.o])
        H0 = A.alloc("H0", 128 * 4)
        for d in range(2):
            for par in range(2):
                self.dma("sp", H0.ap()[64 * par:64 * par + 64, 64 * d:64 * d + 64].rearrange("p (r q) -> p r q", r=2),
                         di["state_ssm"][j2, d].rearrange("(q two) p r -> two p r q", two=2)[par], "h0", [], [H0.o])
        ST = [[A.alloc("ST%d_%d" % (d, k), 2 * 32 * 3 * 4) for k in range(2)] for d in range(2)]
        STv = [[r.ap(F32, "p (r q s) -> p r q s", r=2, q=32) for r in ST[d]] for d in range(2)]
        tm1 = [A.alloc("tm1_%d" % d, 768) for d in range(2)]
        tm2 = [A.alloc("tm2_%d" % d, 768) for d in range(2)]
        FIN = [A.alloc("FIN%d" % d, 2 * 32 * 2 * 4) for d in range(2)]
        FINv = [r.ap(F32, "p (r q s) -> p r q s", r=2, q=32) for r in FIN]
        FS = A.alloc("FS", 128 * 4)
        for d in range(2):
            self.memset(ENG[d], ST[d][0].ap(), 0.0, [ST[d][0].o])
            self.memset(ENG[d], S4[d][:, :, :, 0:131:33] if d == 0 else S4[d][:, :, :, 32:131:33], 0.0, [S[d].o]) if False else None

        def cmul_step(d, cur, nxt, sl, ns, Bk, bo):
            e = ENG[d]
            cv, nv = STv[d][cur], STv[d][nxt]
            co, no = ST[d][cur].o, ST[d][nxt].o
            t1 = tm1[d].ap(F32, "p (r q s) -> p r q s", r=2, q=32)[:, :, :, 0:ns]
            t2 = tm2[d].ap(F32, "p (r q s) -> p r q s", r=2, q=32)[:, :, :, 0:ns]
            mo = M64[d][1].o
            self.tt(e, t1, cv[:, :, :, sl], MRE2[d].unsqueeze(3).to_broadcast([128, 2, 32, ns]), ALU.mult, [co, mo], [tm1[d].o])
            self.tt(e, t2[:, 0], cv[:, 1, :, sl], NMIM[d].unsqueeze(2).to_broadcast([128, 32, ns]), ALU.mult, [co, mo], [tm2[d].o])
            self.tt(e, t2[:, 1], cv[:, 0, :, sl], PMIM[d].unsqueeze(2).to_broadcast([128, 32, ns]), ALU.mult, [co, mo, tm2[d].o], [tm2[d].o])
            self.tt(e, t1, t1, t2, ALU.add, [tm1[d].o, tm2[d].o], [tm1[d].o])
            self.tt(e, nv[:, :, :, sl], t1, Bk, ALU.add, [tm1[d].o, bo], [no])

        def bcols(d, k, allseq):
            if d == 0:
                return S4[0][:, :, :, 1 + k:1 + k + 67:33] if allseq else S4[0][:, :, :, 67 + k:68 + k]
            return S4[1][:, :, :, 63 - k:63 - k + 67:33] if allseq else S4[1][:, :, :, 129 - k:130 - k]
        cur = [0, 0]
        for k in range(64):
            for d in range(2):
                allseq = (k < 32) if d == 0 else (k >= 32)
                sl = slice(0, 3) if allseq else slice(2, 3)
                ns = 3 if allseq else 1
                if d == 1 and k == 32:
                    self.memset(ENG[d], STv[d][cur[d]][:, :, :, 0:2], 0.0, [ST[d][cur[d]].o])
                nxt = 1 - cur[d]
                bk_ = bcols(d, k, allseq)
                cmul_step(d, cur[d], nxt, sl, ns, bk_, S[d].o)
                if allseq:
                    self.cp("act", bk_[:, :, :, 0:2], STv[d][nxt][:, :, :, 0:2], [ST[d][nxt].o, S[d].o], [S[d].o])
                    if (d == 0 and k == 31) or (d == 1 and k == 63):
                        self.cp("act", FINv[d], STv[d][nxt][:, :, :, 0:2], [ST[d][nxt].o], [FIN[d].o])
                cur[d] = nxt
        for d in range(2):
            self.cp("act", FS.ap()[:, 64 * d:64 * d + 64].rearrange("p (r q) -> p r q", r=2), STv[d][cur[d]][:, :, :, 2], [ST[d][cur[d]].o], [FS.o])
        osd, orc = Obj("ssend"), Obj("srecv")
        self.dma("sp", self.ssend[j2], FS.ap(), "ssd", [FS.o], [osd])
        sdd, srr = self.ssend[j2], self.srecv[j2]
        P.dma("pool", lambda e: e.collective_compute("AllGather", ALU.bypass, replica_groups=[[0, 1, 2, 3], [4, 5, 6, 7]],
                                                     ins=[sdd.opt()], outs=[srr.opt()]), "cc_s%d" % j2, reads=[osd], writes=[orc], inc=1)
        FG = A.alloc("FG", 4 * 128 * 4)
        FGv = FG.ap(F32, "p (k c) -> p k c", k=4)
        self.dma("sp", FGv, srr.rearrange("(k p) c -> p k c", k=4), "fg", [orc], [FG.o])
        ch = A.alloc("chain", 4 * 64 * 4)
        chv = ch.ap(F32, "p (k r q) -> p k r q", k=4, r=2)
        ct1, ct2 = A.alloc("ct1", 256), A.alloc("ct2", 256)
        c1 = ct1.ap(F32, "p (r q) -> p r q", r=2)
        c2 = ct2.ap(F32, "p (r q) -> p r q", r=2)
        for d in range(2):
            m64, mt = M64[d]
            order = (0, 1, 2, 3) if d == 0 else (3, 2, 1, 0)
            h0v = H0.ap()[:, 64 * d:64 * d + 64].rearrange("p (r q) -> p r q", r=2)
            self.cp("dve", chv[:, order[0]], h0v, [H0.o], [ch.o])
            for a in range(3):
                ks, kd = order[a], order[a + 1]
                src = chv[:, ks]
                fk = FGv[:, ks, 64 * d:64 * d + 64].rearrange("p (r q) -> p r q", r=2)
                self.tt("dve", c1, src, m64[:, 0:1, :].to_broadcast([128, 2, 32]), ALU.mult, [ch.o, mt.o], [ct1.o])
                self.tt("dve", c2[:, 1, :], src[:, 0, :], m64[:, 1, :], ALU.mult, [ch.o, mt.o], [ct2.o])
                self.stt("dve", c2[:, 0, :], src[:, 1, :], -1.0, m64[:, 1, :], ALU.mult, ALU.mult, [ch.o, mt.o, ct2.o], [ct2.o])
                self.tt("dve", c1, c1, c2, ALU.add, [ct1.o, ct2.o], [ct1.o])
                self.tt("dve", chv[:, kd], c1, fk, ALU.add, [ct1.o, FG.o], [ch.o])
            dst = STv[d][cur[d]][:, :, :, 2]
            self.ts("dve", dst, chv[:, 0], selr.ap()[:, 0:1], None, ALU.mult, None, [ch.o, selr.o], [ST[d][cur[d]].o])
            for k in range(1, 4):
                self.stt("dve", dst, chv[:, k], selr.ap()[:, k:k + 1], dst, ALU.mult, ALU.add, [ch.o, selr.o, ST[d][cur[d]].o], [ST[d][cur[d]].o])
            icol = 66 if d == 0 else 130
            self.cp("act", S4[d][:, :, :, icol], dst, [ST[d][cur[d]].o], [S[d].o])
        for d in range(2):
            cols = S4[d][:, :, :, 0:34:33] if d == 0 else S4[d][:, :, :, 32:66:33]
            self.memset("pool", cols, 0.0, [S[d].o])
        for k in range(64):
            for d in range(2):
                nxt = 1 - cur[d]
                bk_ = bcols(d, k, False)
                cmul_step(d, cur[d], nxt, slice(2, 3), 1, bk_, S[d].o)
                self.cp("act", bk_, STv[d][nxt][:, :, :, 2:3], [ST[d][nxt].o, S[d].o], [S[d].o])
                cur[d] = nxt
        A.release(FG, selr, H0, ch, ct1, ct2, FS, *tm1, *tm2, *ST[0], *ST[1])
        if self.cfg.get("ssm_stop", 99) <= 5:
            return
        CT = []
        for d in range(2):
            pair_ = []
            for nm in ("ssm_c_re", "ssm_c_im"):
                cst = A.alloc("cst", 8 * 128 * 4)
                csv = cst.ap(F32, "p (c e) -> p c e", c=8)
                src = di[nm][j2, d].rearrange("(gc gl) h p -> (gl h) gc p", gl=8)
                self.dma("sp", csv[:, :, 0:64], src, "cst", [], [cst.o])
                self.dma("sp", csv[:, :, 64:128], src, "cst", [], [cst.o])
                PT_ = self.PB[0]
                pob = [self.BK[0], self.BK[1]]
                for gc in range(8):
                    self.tr(PT_[:, gc * 128:(gc + 1) * 128], csv[:, gc, :], idf, [cst.o, self.identf.o], [pob[gc // 4]])
                ct = A.alloc("ct_%s%d" % (nm[-2:], d), 2048)
                for par in range(2):
                    rows = slice(64 * par, 64 * par + 64)
                    self.cp("act" if par else "dve", ct.ap(F32, "p (q h) -> p q h", q=32)[rows],
                            PT_[rows, :].rearrange("p (q two h) -> p q two h", two=2, h=16)[:, :, par, :], pob, [ct.o])
                A.release(cst)
                pair_.append(ct)
            CT.append(pair_)
        if self.cfg.get("ssm_stop", 99) <= 5.3:
            return
        WC = [A.alloc("WC%d" % d, 32 * 2 * 128 * 2) for d in range(2)]
        WCv = [w.ap(BF16, "p (q r e) -> p q r e", q=32, r=2) for w in WC]
        g1, g2 = A.alloc("g1", 4096), A.alloc("g2", 4096)
        g3, g4 = A.alloc("g3", 4096), A.alloc("g4", 4096)

        def to_parity(src, name):
            r = A.alloc(name, 8 * 32 * 4)
            sv = src.ap(F32, "p (k q two) -> p k q two", k=8, two=2)
            rv = r.ap(F32, "p (k q) -> p k q", k=8)
            for par in range(2):
                rows = slice(64 * par, 64 * par + 64)
                self.cp("dve", rv[rows], sv[rows, :, :, par], [src.o], [r.o])
            return r
        for d in range(2):
            prs, pis = to_parity(PWC[d][0], "pcs_re"), to_parity(PWC[d][1], "pcs_im")
            pr_ = prs.ap(F32, "p (k q) -> p k q", k=8)
            pi_ = pis.ap(F32, "p (k q) -> p k q", k=8)
            cr_ = CT[d][0].ap(F32, "p (q h) -> p q h", q=32)
            ci_ = CT[d][1].ap(F32, "p (q h) -> p q h", q=32)
            rd = [prs.o, pis.o, CT[d][0].o, CT[d][1].o]
            for pb in range(4):
                qs_ = slice(pb * 8, pb * 8 + 8)
                Cr = cr_[:, qs_, :].unsqueeze(2).to_broadcast([128, 8, 8, 16])
                Ci = ci_[:, qs_, :].unsqueeze(2).to_broadcast([128, 8, 8, 16])
                Pr = pr_[:, :, qs_].rearrange("p j q -> p q j").unsqueeze(3).to_broadcast([128, 8, 8, 16])
                Pi = pi_[:, :, qs_].rearrange("p j q -> p q j").unsqueeze(3).to_broadcast([128, 8, 8, 16])
                a1 = g1.ap(F32, "p (q j h) -> p q j h", q=8, j=8)
                a2 = g2.ap(F32, "p (q j h) -> p q j h", q=8, j=8)
                a3 = g3.ap(F32, "p (q j h) -> p q j h", q=8, j=8)
                a4 = g4.ap(F32, "p (q j h) -> p q j h", q=8, j=8)
                o0 = WCv[d][:, qs_, 0, :].rearrange("p q (j h) -> p q j h", j=8)
                o1 = WCv[d][:, qs_, 1, :].rearrange("p q (j h) -> p q j h", j=8)
                self.tt("pool", a2, Ci, Pi, ALU.mult, rd, [g2.o])
                self.tt("dve", a1, Cr, Pr, ALU.mult, rd, [g1.o])
                self.tt("pool", a4, Ci, Pr, ALU.mult, rd, [g4.o])
                self.tt("dve", a3, Cr, Pi, ALU.mult, rd, [g3.o])
                self.tt("dve", o0, a1, a2, ALU.subtract, [g1.o, g2.o], [WC[d].o])
                self.stt("dve", o1, a3, -1.0, a4, ALU.mult, ALU.subtract, [g3.o, g4.o], [WC[d].o])
            A.release(prs, pis)
        A.release(CT[0][0], CT[0][1], CT[1][0], CT[1][1], PWC[0][0], PWC[0][1], PWC[1][0], PWC[1][1])
        if self.cfg.get("ssm_stop", 99) <= 6:
            return
        T = A.alloc("T", 64 * 128 * 2)
        Tv = T.ap(BF16, "p (g e) -> p g e", g=64)
        msk = A.alloc("msk", 2 * 128 * 4 + 64)
        mi = A.alloc("mski", 128 * 4 + 64)
        miv = mi.ap(I32)[:, 0:128]
        pjv = mi.ap(I32)[:, 128:129]
        self.P.op("pool", lambda e: e.iota(miv, pattern=[[1, 128]], base=0, channel_multiplier=0), writes=[mi.o])
        self.P.op("pool", lambda e: e.iota(pjv, pattern=[[0, 1]], base=0, channel_multiplier=1), reads=[mi.o], writes=[mi.o])
        self.ts("dve", mi.ap(I32)[:, 0:129], mi.ap(I32)[:, 0:129], 4, None, ALU.arith_shift_right, None, [mi.o], [mi.o])
        cjf = msk.ap()[:, 0:128]
        pjf = msk.ap()[:, 256:257]
        self.cp("dve", cjf, miv, [mi.o], [msk.o])
        self.cp("dve", pjf, pjv, [mi.o, msk.o], [msk.o])
        Mf, Mb = msk.ap()[:, 0:128], msk.ap()[:, 128:256]
        self.ts("dve", Mb, cjf, pjf, None, ALU.is_le, None, [msk.o], [msk.o])
        self.ts("dve", Mf, cjf, pjf, None, ALU.is_ge, None, [msk.o], [msk.o])
        dcol = A.alloc("dcol", 256)
        for i in range(8):
            self.dma("sp", dcol.ap()[16 * i:16 * i + 16, 0:64], di["ssm_d"][j2].rearrange("(g h) -> h g", h=16), "dcol", [], [dcol.o])
        sst = self.cfg.get("ssm_stop", 99)
        if sst <= 6.2:
            return
        xb = [A.alloc("xb%d" % d, 4 * 2 * 128 * 2) for d in range(2)]
        xbv = [x_.ap(BF16, "p (q r e) -> p q r e", q=4, r=2) for x_ in xb]
        QXs, BXs = [], []
        for d in range(2):
            QXs.append((to_parity(QXB[d][0], "qxs_re%d" % d), to_parity(QXB[d][1], "qxs_im%d" % d)))
            bs_ = []
            for src in (BR[d], BI[d]):
                r = A.alloc("bxs", 32 * 16 * 4)
                sv = src.ap(F32, "p (q two h) -> p q two h", two=2, h=16)
                rv = r.ap(F32, "p (q h) -> p q h", q=32)
                for par in range(2):
                    rows = slice(64 * par, 64 * par + 64)
                    self.cp("dve", rv[rows], sv[rows, :, par, :], [src.o], [r.o])
                bs_.append(r)
            BXs.append(bs_)
            A.release(QXB[d][0], QXB[d][1], BR[d], BI[d])
        ta, tb_, dfull = A.alloc("ta", 4096), A.alloc("tb", 4096), A.alloc("dfull", 4096)
        for gb in range(8):
            for d in range(2):
                qr = QXs[d][0].ap(F32, "p (k q) -> p k q", k=8)
                qi = QXs[d][1].ap(F32, "p (k q) -> p k q", k=8)
                brv = BXs[d][0].ap(F32, "p (q h) -> p q h", q=32)
                biv = BXs[d][1].ap(F32, "p (q h) -> p q h", q=32)
                rd = [QXs[d][0].o, QXs[d][1].o, BXs[d][0].o, BXs[d][1].o]
                qs_ = slice(gb * 4, gb * 4 + 4)
                Er = qr[:, :, qs_].rearrange("p i q -> p q i").unsqueeze(3).to_broadcast([128, 4, 8, 16])
                Ei = qi[:, :, qs_].rearrange("p i q -> p q i").unsqueeze(3).to_broadcast([128, 4, 8, 16])
                Br_ = brv[:, qs_, :].unsqueeze(2).to_broadcast([128, 4, 8, 16])
                Bi_ = biv[:, qs_, :].unsqueeze(2).to_broadcast([128, 4, 8, 16])
                a1 = g1.ap(F32, "p (q j h) -> p q j h", q=8, j=8)[:, 0:4]
                a2 = g2.ap(F32, "p (q j h) -> p q j h", q=8, j=8)[:, 0:4]
                a3 = g3.ap(F32, "p (q j h) -> p q j h", q=8, j=8)[:, 0:4]
                a4 = g4.ap(F32, "p (q j h) -> p q j h", q=8, j=8)[:, 0:4]
                o0 = xbv[d][:, :, 0, :].rearrange("p q (j h) -> p q j h", j=8)
                o1 = xbv[d][:, :, 1, :].rearrange("p q (j h) -> p q j h", j=8)
                self.tt("pool", a2, Ei, Bi_, ALU.mult, rd, [g2.o])
                self.tt("dve", a1, Er, Br_, ALU.mult, rd, [g1.o])
                self.tt("pool", a4, Ei, Br_, ALU.mult, rd, [g4.o])
                self.tt("dve", a3, Er, Bi_, ALU.mult, rd, [g3.o])
                self.tt("dve", o0, a1, a2, ALU.subtract, [g1.o, g2.o], [xb[d].o])
                self.tt("dve", o1, a3, a4, ALU.add, [g3.o, g4.o], [xb[d].o])
            if sst <= 6.4:
                continue
            for d in range(2):
                PT_ = self.PB[d]
                for gl in range(8):
                    g = gb * 8 + gl
                    par, pl = g % 2, gl // 2
                    rows = slice(64 * par, 64 * par + 64)
                    cidx = par * 4 + pl
                    for r_ in range(2):
                        self.mm(PT_[:, cidx * 128:(cidx + 1) * 128], xbv[d][rows, pl, r_, :], WCv[d][rows, g // 2, r_, :], r_ == 0, r_ == 1,
                                [xb[d].o, WC[d].o], [self.BK[2 * d + par]])
            if sst <= 6.6:
                continue
            t3 = lambda r: r.ap(F32, "p (g e) -> p g e", g=8)
            self.tt("dve", t3(ta), self.PB[0][:, :].rearrange("p (g e) -> p g e", g=8), Mf.unsqueeze(1).to_broadcast([128, 8, 128]), ALU.mult,
                    [self.BK[0], self.BK[1], msk.o], [ta.o])
            self.tt("dve", t3(tb_), self.PB[1][:, :].rearrange("p (g e) -> p g e", g=8), Mb.unsqueeze(1).to_broadcast([128, 8, 128]), ALU.mult,
                    [self.BK[2], self.BK[3], msk.o], [tb_.o])
            self.tt("pool", ta.ap(), ta.ap(), tb_.ap(), ALU.add, [ta.o, tb_.o], [ta.o])
            for par in range(2):
                dsl = dcol.ap()[:, gb * 8 + par:gb * 8 + 8:2]
                dfp = dfull.ap()[:, 0:512].rearrange("p (g e) -> p g e", g=4)
                self.tt("pool", dfp, idf.unsqueeze(1).to_broadcast([128, 4, 128]), dsl.unsqueeze(2).to_broadcast([128, 4, 128]), ALU.mult,
                        [self.identf.o, dcol.o], [dfull.o])
                self.tt("pool", Tv[:, gb * 8 + par:gb * 8 + 8:2, :], t3(ta)[:, par * 4:(par + 1) * 4, :], dfp, ALU.add, [ta.o, dfull.o], [T.o])
        A.release(msk, mi, dcol, ta, tb_, dfull, g1, g2, g3, g4, *xb,
                  QXs[0][0], QXs[0][1], QXs[1][0], QXs[1][1], *BXs[0], *BXs[1])
        if self.cfg.get("ssm_stop", 99) <= 7:
            return
        wg = [A.alloc("wg%d" % k, 8 * 512 * 2) for k in range(2)]

        def load_wg(np_):
            ca, cb_ = slice(512 * np_, 512 * np_ + 512), slice(1024 + 512 * np_, 1536 + 512 * np_)
            self.dma("pool", wg[0].ap(BF16, "p (k n) -> p k n", k=8), di["ssm_w_glu"][j2][:, ca].rearrange("(kc k) n -> k kc n", k=128), "wg0", [], [wg[0].o])
            self.dma("pool", wg[1].ap(BF16, "p (k n) -> p k n", k=8), di["ssm_w_glu"][j2][:, cb_].rearrange("(kc k) n -> k kc n", k=128), "wg1", [], [wg[1].o])
        if not self.small:
            load_wg(0)
        Gm = A.alloc("Gm", 16384)
        Gv = Gm.ap(BF16, "p (j g h) -> p j g h", j=8, g=64)
        gel = [A.alloc("gel%d" % k, 2048) for k in range(2)]
        incol = ((0, 33, 66), (1, 34, 67))
        for gb in range(8):
            pr = gb % 3
            PY = self.PB[pr]
            pob = [self.BK[2 * pr], self.BK[2 * pr + 1]]
            for gl in range(8):
                g = gb * 8 + gl
                par, q_ = g % 2, g // 2
                rows = slice(64 * par, 64 * par + 64)
                o_ = [pob[par]]
                cidx = par * 4 + gl // 2
                self.mm(PY[:, cidx * 128:(cidx + 1) * 128], Tv[:, g, :], VAv[:, g, :], True, False, [T.o, VA.o], o_)
                for d in range(2):
                    for r_ in range(2):
                        for si, (a, b) in enumerate(segs):
                            ic = incol[d][si]
                            self.mm(PY[:, cidx * 128 + a:cidx * 128 + b], WCv[d][rows, q_, r_, :], S4[d][rows, r_, q_, ic:ic + (b - a)], False,
                                    d == 1 and r_ == 1, [WC[d].o, S[d].o], o_)
            ge = gel[gb % 2]
            self.act(ge.ap(BF16), PY[:, :], AF.Gelu_apprx_tanh, pob, [ge.o])
            bk = 6 + gb % 2
            psb = self.bank(bk).bitcast(BF16)
            for gl in range(8):
                self.tr(psb[:, gl * 128:(gl + 1) * 128], ge.ap(BF16)[:, gl * 128:(gl + 1) * 128], idb, [ge.o, self.identb.o], [self.BK[bk]])
            ps4_ = psb.rearrange("p (g j h) -> p j g h", g=8, j=8)
            for par in range(2):
                self.cp("dve" if par else "act", Gv[:, :, gb * 8 + par:gb * 8 + 8:2, :], ps4_[:, :, par * 4:(par + 1) * 4, :], [self.BK[bk]], [Gm.o])
        A.release(T, VA, *S, *WC, *gel, M64[0][1], M64[1][1], small)
        GT = A.alloc("GT", 16384)
        self.transposeT(Gm, GT)
        A.release(Gm)
        GTv = GT.ap(BF16, "p (k t) -> p k t", k=8)
        if self.cfg.get("ssm_stop", 99) <= 8:
            return
        G1 = self.load_mod(2, "modG")
        bg = A.alloc("bg", 2048 * 4)
        bgb = A.alloc("bgb", 2048 * 2)
        self.dma("sp", bg.ap()[0:1, :], di["ssm_b_glu"][j2].partition_broadcast(1), "bg", [], [bg.o])
        self.cp("dve", bgb.ap(BF16)[0:1, :], bg.ap()[0:1, :], [bg.o], [bgb.o])
        sig = [A.alloc("sig%d" % k, 2048) for k in range(2)]
        gt_ = [A.alloc("gtmp%d" % k, 2048) for k in range(2)]
        ones = self.onesb.ap(BF16)
        for np_ in range(2):
            ca, cb_ = slice(512 * np_, 512 * np_ + 512), slice(1024 + 512 * np_, 1536 + 512 * np_)
            wva = wg[0].ap(BF16, "p (k n) -> p k n", k=8)
            wvb = wg[1].ap(BF16, "p (k n) -> p k n", k=8)
            if np_ == 1:
                load_wg(1)
            for i in range(8):
                ba, bb = (2 * i) % 6, (2 * i + 1) % 6
                for (bk, wv, wr, cc) in ((ba, wva, wg[0], ca), (bb, wvb, wg[1], cb_)):
                    self.mm(self.bank(bk), ones[0:1, 0:128], bgb.ap(BF16)[0:1, cc], True, False, [self.onesb.o, bgb.o], [self.BK[bk]])
                    for kc in range(8):
                        self.mm(self.bank(bk), GTv[:, kc, i * 128:(i + 1) * 128], wv[:, kc, :], False, kc == 7, [GT.o, wr.o], [self.BK[bk]])
                sg, gt1 = sig[i % 2], gt_[i % 2]
                self.act(sg.ap(), self.bank(bb), AF.Sigmoid, [self.BK[bb]], [sg.o])
                self.tt("dve", gt1.ap(), self.bank(ba), sg.ap(), ALU.mult, [self.BK[ba], sg.o], [gt1.o])
                self.tt("pool", gt1.ap(), gt1.ap(), G1.ap()[:, ca], ALU.mult, [gt1.o, G1.o], [gt1.o])
                self.tt("pool", X[:, i, ca], X[:, i, ca], gt1.ap(), ALU.add, [self.XO[i], gt1.o], [self.XO[i]])
        A.release(GT, G1, bg, bgb, *wg, *sig, *gt_)
        tcb = [A.alloc("tcomb%d" % k, 1024) for k in range(2)]
        cntf = 0
        for d in range(2):
            for s_ in range(2):
                bk = 6 + cntf % 2
                tc_ = tcb[cntf % 2]
                cntf += 1
                for r_ in range(2):
                    self.tr(self.bank(bk)[0:32, r_ * 128:(r_ + 1) * 128], FINv[d][:, r_, :, s_], idf, [FIN[d].o, self.identf.o], [self.BK[bk]])
                self.cp("dve", tc_.ap()[0:32, 0:256].rearrange("q (n r) -> q r n", r=2),
                        self.bank(bk)[0:32, 0:256].rearrange("q (r n) -> q r n", r=2), [self.BK[bk]], [tc_.o])
                self.dma("sp", self.do["ossm"][s_, j2, d].rearrange("(q two) p r -> q (two p r)", two=2), tc_.ap()[0:32, 0:256], "ossm",
                         [tc_.o], [], is_output=True)
        A.release(*tcb)
        A.release(*FIN)


def _core_inputs(inp, core):
    b, r = core // 4, core % 4
    f = lambda a: np.ascontiguousarray(np.asarray(a, dtype=np.float32))
    d = {}
    d["xp"] = f(inp["x_prompt"][2 * core:2 * core + 2].reshape(512, D))
    d["xs"] = f(inp["x_sample"][b, 512 * r:512 * (r + 1)])
    d["cvec"] = f(np.stack([inp["c_ctx"], inp["c"][b]], 0))
    d["cache_ckv"] = f(inp["cache_mla_ckv"][b])
    d["cache_krope"] = f(inp["cache_mla_krope"][b])
    d["cache_gk"] = f(inp["cache_gqa_k"][b].reshape(2, 256, 128))
    d["cache_gv"] = f(inp["cache_gqa_v"][b].reshape(2, 256, 128))
    d["state_ssm"] = f(inp["state_ssm"][b])
    d["w_mod"] = f(inp["w_mod"][:, :, 1536 * r:1536 * (r + 1)])
    d["b_mod"] = f(inp["b_mod"][:, 1536 * r:1536 * (r + 1)])
    for k in ("norm1_g", "norm2_g", "attn_w_in", "attn_qa_norm_g", "attn_kva_norm_g", "attn_w_uq", "attn_w_ukv",
              "attn_mla_q_norm_g", "attn_mla_k_norm_g", "attn_gqa_q_norm_g", "attn_gqa_k_norm_g", "attn_w_out",
              "ssm_a_re", "ssm_a_im", "ssm_log_dt", "ssm_b_re", "ssm_b_im", "ssm_c_re", "ssm_c_im", "ssm_d", "ssm_w_glu", "ssm_b_glu",
              "ffn_w_up", "ffn_conv_w", "ffn_conv_b", "ffn_w_down"):
        d[k] = f(inp[k])
    qpos = np.zeros((128, 8, 2), np.float32)
    for p in range(64, 128):
        for i in range(8):
            t = 512 * r + 8 * (p - 64) + i
            qpos[p, i, 0] = t // 64
            qpos[p, i, 1] = t % 64
    kpos = np.zeros((128, 16, 2), np.float32)
    for q in range(128):
        for u in range(16):
            t = 128 * u + q
            kpos[q, u, 0] = t // 64
            kpos[q, u, 1] = t % 64
    sel = np.zeros((8, 2), np.float32)
    if r > 0:
        sel[2 * (r - 1) + 1, 0] = 1.0
    if r < 3:
        sel[2 * (r + 1), 1] = 1.0
    selr = np.zeros((128, 8), np.float32)
    selr[:, r] = 1.0
    selr[:, 4 + r] = 1.0
    d["qpos"], d["kpos"], d["selhalo"], d["selr"] = qpos, kpos, sel, selr
    return d


_NC_CACHE = {}


def kernel(cfg=None, **inp):
    inp = {k: np.asarray(v) for k, v in inp.items()}
    key = repr(sorted((cfg or {}).items()))
    if key not in _NC_CACHE:
        _NC_CACHE[key] = Builder(cfg).build()
    nc = _NC_CACHE[key]
    in_maps = [_core_inputs(inp, c) for c in range(NCORES)]
    if (cfg or {}).get("small"):
        for m in in_maps:
            for k in ("w_mod", "ffn_w_up", "ffn_w_down", "ssm_w_glu"):
                m.pop(k)
    res = run_bass_kernel_spmd(nc, in_maps, core_ids=list(range(NCORES)))
    R = res.results
    yp = np.concatenate([R[c]["yp"].reshape(2, 256, D) for c in range(NCORES)], 0)
    ys = np.stack([np.concatenate([R[4 * b + r]["ys"] for r in range(4)], 0) for b in range(2)], 0)
    ockv = np.concatenate([R[c]["ockv"] for c in range(NCORES)], 0)
    okr = np.concatenate([R[c]["okrope"] for c in range(NCORES)], 0)
    ogk = np.concatenate([R[c]["ogk"].reshape(2, 2, 256, 2, 64) for c in range(NCORES)], 0)
    ogv = np.concatenate([R[c]["ogv"].reshape(2, 2, 256, 2, 64) for c in range(NCORES)], 0)
    ossm = np.concatenate([R[c]["ossm"] for c in range(NCORES)], 0)
    return (yp.astype(np.float32), ys.astype(np.float32), ockv.astype(np.float32), okr.astype(np.float32),
            ogk.astype(np.float32), ogv.astype(np.float32), ossm.astype(np.float32))
```

```python
import math
import contextlib
import numpy as np
import concourse.bass as bass
import concourse.mybir as mybir
from concourse.bass_utils import run_bass_kernel_spmd

F32 = mybir.dt.float32
BF16 = mybir.dt.bfloat16
I32 = mybir.dt.int32
AF = mybir.ActivationFunctionType
ALU = mybir.AluOpType
AX = mybir.AxisListType

D = 1024
DFF = 2816
EPS = 1e-6
NCORES = 8


class Obj:
    __slots__ = ("name", "last_w", "readers", "excl")

    def __init__(self, name, excl=False):
        self.name = name
        self.last_w = None
        self.readers = []
        self.excl = excl


class Prog:
    ENGS = ("pe", "act", "dve", "pool", "sp")

    def __init__(self, nc):
        self.nc = nc
        self.q = {e: [] for e in self.ENGS}
        self.cnt = {e: 0 for e in self.ENGS}
        self.dmacnt = {}
        self.waited = {e: {} for e in self.ENGS}
        self.semkeys = []
        self.out_events = []

    def _semkey(self, k):
        if k not in self.semkeys:
            self.semkeys.append(k)
        return k

    def _deps(self, eng, reads, writes, pe_accum=False):
        need = {}

        def add(ev, same_ok=False):
            if ev is None:
                return
            k, v = ev
            if same_ok and k == eng:
                return
            if need.get(k, 0) < v:
                need[k] = v
        for o in reads:
            add(o.last_w)
            if o.excl:
                for r in o.readers:
                    add(r, same_ok=True)
        for o in writes:
            add(o.last_w, same_ok=pe_accum)
            for r in o.readers:
                add(r, same_ok=True)
        waits = []
        for k, v in need.items():
            if self.waited[eng].get(k, 0) < v:
                self.waited[eng][k] = v
                waits.append((k, v))
        return waits

    def _commit(self, ev, reads, writes):
        for o in reads:
            o.readers.append(ev)
            if len(o.readers) > 64:
                best = {}
                for k, v in o.readers:
                    if best.get(k, 0) < v:
                        best[k] = v
                o.readers = list(best.items())
        for o in writes:
            o.last_w = ev
            o.readers = []

    def op(self, eng, fn, reads=(), writes=(), pe_accum=False, same_ok=False):
        reads = [o for o in reads if o is not None]
        writes = [o for o in writes if o is not None]
        waits = self._deps(eng, reads, writes, pe_accum)
        if same_ok:
            waits = [(k, v) for (k, v) in waits if k != eng]
        self.cnt[eng] += 1
        ev = (self._semkey(eng), self.cnt[eng])
        self.q[eng].append((waits, fn, [(eng, 1)]))
        self._commit(ev, reads, writes)
        return ev

    def dma(self, eng, fn, key, reads=(), writes=(), is_output=False, inc=16):
        reads = [o for o in reads if o is not None]
        writes = [o for o in writes if o is not None]
        waits = self._deps(eng, reads, writes)
        sk = self._semkey(("dma", key))
        self.dmacnt[key] = self.dmacnt.get(key, 0) + 1
        ev = (sk, inc * self.dmacnt[key])
        self.q[eng].append((waits, fn, [(sk, inc)]))
        self._commit(ev, reads, writes)
        if is_output:
            self.out_events.append(ev)
        return ev

    def finish(self, eng="sp"):
        need = {}
        for k, v in self.out_events:
            need[k] = max(need.get(k, 0), v)
        for e in ("pe", "act", "dve", "pool"):
            if self.cnt[e]:
                need[e] = self.cnt[e]
        waits = [(k, v) for k, v in need.items() if self.waited[eng].get(k, 0) < v]
        self.q[eng].append((waits, None, []))

    def emit(self):
        nc = self.nc
        comp = ("pe", "act", "dve", "pool")
        miles = {e: set() for e in comp}
        for e in self.ENGS:
            for waits, fn, incs in self.q[e]:
                for k, v in waits:
                    if k in miles:
                        miles[k].add(v)
        rank = {e: {v: i + 1 for i, v in enumerate(sorted(miles[e]))} for e in comp}
        with contextlib.ExitStack() as st:
            sems = {}
            for i, k in enumerate(self.semkeys):
                sems[k] = st.enter_context(nc.semaphore("s%d" % i))
            block = st.enter_context(nc.Block())
            q = self.q

            def run(e, name):
                n = 0
                for waits, fn, incs in q[name]:
                    for k, v in waits:
                        if k in rank:
                            v = rank[k][v]
                        e.wait_ge(sems[k], v)
                    if fn is None:
                        continue
                    ins = fn(e)
                    for k, inc in incs:
                        if k in rank:
                            n += 1
                            if n in rank[k]:
                                ins.then_inc(sems[k], 1)
                        else:
                            ins.then_inc(sems[k], inc)

            @block.tensor
            def _(e):
                run(e, "pe")

            @block.scalar
            def _(e):
                run(e, "act")

            @block.vector
            def _(e):
                run(e, "dve")

            @block.gpsimd
            def _(e):
                run(e, "pool")

            @block.sync
            def _(e):
                run(e, "sp")


class Reg:
    def __init__(self, arena, off, nbytes, name, req=None):
        self.arena = arena
        self.off = off
        self.nbytes = nbytes
        self.req = req or nbytes
        self.o = Obj(name)
        self.name = name

    def ap(self, dt=F32, pat=None, **kw):
        a = self.arena[:, self.off // 4:(self.off + self.req) // 4]
        if dt != F32:
            a = a.bitcast(dt)
        if pat is not None:
            a = a.rearrange(pat, **kw)
        return a


class Arena:
    def __init__(self, arena_ap, nbytes):
        self.arena = arena_ap
        self.free = [(0, nbytes)]
        self.hist = []

    def alloc(self, name, nbytes):
        req = nbytes
        nbytes = (nbytes + 63) // 64 * 64
        cands = [(b - a, idx) for idx, (a, b) in enumerate(self.free) if b - a >= nbytes]
        for _, idx in sorted(cands)[:1]:
            a, b = self.free[idx]
            if b - a >= nbytes:
                self.free[idx] = (a + nbytes, b)
                if self.free[idx][0] == self.free[idx][1]:
                    self.free.pop(idx)
                r = Reg(self.arena, a, nbytes, name, req)
                keep = []
                for (ha, hb, ho) in self.hist:
                    if ha < a + nbytes and hb > a:
                        if ho.last_w is not None:
                            r.o.readers.append(ho.last_w)
                        r.o.readers.extend(ho.readers)
                        if ha < a:
                            keep.append((ha, a, ho))
                        if hb > a + nbytes:
                            keep.append((a + nbytes, hb, ho))
                    else:
                        keep.append((ha, hb, ho))
                self.hist = keep
                return r
        raise RuntimeError("SBUF arena full allocating %s (%d bytes); free=%s" % (name, nbytes, self.free))

    def release(self, *regs):
        for r in regs:
            self.hist.append((r.off, r.off + r.nbytes, r.o))
            self.free.append((r.off, r.off + r.nbytes))
        self.free.sort()
        merged = []
        for a, b in self.free:
            if merged and merged[-1][1] == a:
                merged[-1] = (merged[-1][0], b)
            else:
                merged.append((a, b))
        self.free = merged


IN_SPECS = [
    ("xp", [512, D]), ("xs", [512, D]), ("cvec", [2, D]),
    ("cache_ckv", [2, 256, 256]), ("cache_krope", [2, 256, 32]), ("cache_gk", [2, 256, 128]), ("cache_gv", [2, 256, 128]),
    ("state_ssm", [2, 2, 64, 64, 2]),
    ("norm1_g", [4, D]), ("norm2_g", [4, D]), ("w_mod", [4, D, 1536]), ("b_mod", [4, 1536]),
    ("attn_w_in", [2, D, 1440]), ("attn_qa_norm_g", [2, 384]), ("attn_kva_norm_g", [2, 256]),
    ("attn_w_uq", [2, 384, 768]), ("attn_w_ukv", [2, 256, 1024]),
    ("attn_mla_q_norm_g", [2, 96]), ("attn_mla_k_norm_g", [2, 96]), ("attn_gqa_q_norm_g", [2, 64]), ("attn_gqa_k_norm_g", [2, 64]),
    ("attn_w_out", [2, D, D]),
    ("ssm_a_re", [2, 2, 64, 64]), ("ssm_a_im", [2, 2, 64, 64]), ("ssm_log_dt", [2, 2, 64]),
    ("ssm_b_re", [2, 2, 64, 64, 16]), ("ssm_b_im", [2, 2, 64, 64, 16]),
    ("ssm_c_re", [2, 2, 64, 16, 64]), ("ssm_c_im", [2, 2, 64, 16, 64]),
    ("ssm_d", [2, D]), ("ssm_w_glu", [2, D, 2 * D]), ("ssm_b_glu", [2, 2 * D]),
    ("ffn_w_up", [4, D, 2 * DFF]), ("ffn_conv_w", [4, 3, 2 * DFF]), ("ffn_conv_b", [4, 2 * DFF]), ("ffn_w_down", [4, DFF, D]),
    ("qpos", [128, 8, 2]), ("kpos", [128, 16, 2]), ("selhalo", [8, 2]), ("selr", [128, 8]),
]
OUT_SPECS = [
    ("yp", [512, D]), ("ys", [512, D]),
    ("ockv", [2, 2, 256, 256]), ("okrope", [2, 2, 256, 32]), ("ogk", [2, 2, 256, 128]), ("ogv", [2, 2, 256, 128]),
    ("ossm", [2, 2, 2, 64, 64, 2]),
]


class Builder:
    def __init__(self, cfg=None):
        self.cfg = cfg or {}
        self.nc = bass.Bass("TRN2", target_bir_lowering=False)
        nc = self.nc
        self.small = bool(self.cfg.get("small"))
        big = ("w_mod", "ffn_w_up", "ffn_w_down", "ssm_w_glu")
        self.di = {n: nc.dram_tensor(n, s, F32, kind="ExternalInput").ap() for n, s in IN_SPECS if not (self.small and n in big)}
        self.do = {n: nc.dram_tensor(n, s, F32, kind="ExternalOutput").ap() for n, s in OUT_SPECS}
        self.dobj = {}
        self.modd = nc.dram_tensor("modd", [4, 2, 6 * D], F32, kind="Internal").ap()
        self.msend = nc.dram_tensor("msend", [8, 1536], F32, kind="Internal").ap()
        self.mrecv = nc.dram_tensor("mrecv", [32, 1536], F32, kind="Internal").ap()
        self.dobj["modd"] = [Obj("modd%d" % l) for l in range(4)]
        self.hsend = [nc.dram_tensor("hsend%d" % l, [2, D], BF16, kind="Internal").ap() for l in range(4)]
        self.hrecv = [nc.dram_tensor("hrecv%d" % l, [8, D], BF16, kind="Internal").ap() for l in range(4)]
        self.kvsend = [[nc.dram_tensor("kvsend%d_%d" % (l, hf), [256, 544], F32, kind="Internal").ap() for hf in range(2)] for l in range(2)]
        self.kvrecv = [[nc.dram_tensor("kvrecv%d_%d" % (l, hf), [1024, 544], F32, kind="Internal").ap() for hf in range(2)] for l in range(2)]
        self.ssend = [nc.dram_tensor("ssend%d" % l, [128, 128], F32, kind="Internal").ap() for l in range(2)]
        self.srecv = [nc.dram_tensor("srecv%d" % l, [512, 128], F32, kind="Internal").ap() for l in range(2)]

    def mm(self, out, lhsT, rhs, start, stop, reads, writes, **kw):
        self.P.op("pe", lambda e: e.matmul(out, lhsT=lhsT, rhs=rhs, start=start, stop=stop, **kw),
                  reads=reads, writes=writes, pe_accum=True)

    def tr(self, out, in_, ident, reads, writes):
        self.P.op("pe", lambda e: e.transpose(out, in_, ident), reads=reads, writes=writes, pe_accum=True)

    def act(self, out, in_, func, reads, writes, **kw):
        self.P.op("act", lambda e: e.activation(out=out, in_=in_, func=func, **kw), reads=reads, writes=writes)

    def cp(self, eng, out, in_, reads, writes):
        if eng == "act":
            self.P.op("act", lambda e: e.copy(out=out, in_=in_), reads=reads, writes=writes)
        else:
            self.P.op(eng, lambda e: e.tensor_copy(out=out, in_=in_), reads=reads, writes=writes)

    def tt(self, eng, out, in0, in1, op, reads, writes, same_ok=False):
        self.P.op(eng, lambda e: e.tensor_tensor(out=out, in0=in0, in1=in1, op=op), reads=reads, writes=writes, same_ok=same_ok)

    def stt(self, eng, out, in0, scalar, in1, op0, op1, reads, writes, same_ok=False):
        self.P.op(eng, lambda e: e.scalar_tensor_tensor(out=out, in0=in0, scalar=scalar, in1=in1, op0=op0, op1=op1),
                  reads=reads, writes=writes, same_ok=same_ok)

    def ts(self, eng, out, in0, s1, s2, op0, op1, reads, writes):
        if op1 is None:
            self.P.op(eng, lambda e: e.tensor_scalar(out=out, in0=in0, scalar1=s1, scalar2=None, op0=op0), reads=reads, writes=writes)
        else:
            self.P.op(eng, lambda e: e.tensor_scalar(out=out, in0=in0, scalar1=s1, scalar2=s2, op0=op0, op1=op1), reads=reads, writes=writes)

    def recip(self, out, in_, reads, writes):
        self.P.op("dve", lambda e: e.reciprocal(out=out, in_=in_), reads=reads, writes=writes)

    def memset(self, eng, out, val, writes):
        self.P.op(eng, lambda e: e.memset(out, val), writes=writes)

    def dma(self, eng, out, in_, key, reads, writes, is_output=False):
        self.P.dma(eng, lambda e: e.dma_start(out=out, in_=in_), key, reads=reads, writes=writes, is_output=is_output)

    def bank(self, k):
        if k < 6:
            return self.PB[k // 2][:, (k % 2) * 512:(k % 2 + 1) * 512]
        return self.PS[k - 6][:, :]

    def build(self):
        nc = self.nc
        with contextlib.ExitStack() as st:
            AW = 53000
            arena_t = st.enter_context(nc.sbuf_tensor("arena", [128, AW], F32))
            self.PB = [st.enter_context(nc.psum_tensor("pb%d" % i, [128, 1024], F32)) for i in range(3)]
            self.PS = [st.enter_context(nc.psum_tensor("ps%d" % i, [128, 512], F32)) for i in range(2)]
            self.BK = [Obj("bank%d" % i, excl=True) for i in range(8)]
            self.A = Arena(arena_t, AW * 4)
            self.P = Prog(nc)
            self.consts()
            self.load_x()
            if not self.small:
                self.preamble()
            for l in range(4):
                if l >= self.cfg.get("nlayers", 4):
                    break
                self.layer_mod(l)
                if self.cfg.get("mixers", True):
                    if l % 2 == 0:
                        if not self.cfg.get("only_ssm"):
                            self.attention(l)
                    else:
                        self.ssm(l)
                if self.cfg.get("ffn", True):
                    self.ffn(l)
            self.store_x()
            self.P.finish()
            with nc.allow_non_contiguous_dma("small strided parameter loads"):
                self.P.emit()
        return nc

    def consts(self):
        A = self.A
        self.identf = A.alloc("identf", 512)
        self.identb = A.alloc("identb", 256)
        self.onesb = A.alloc("onesb", 256)
        self.epsc = A.alloc("epsc", 64)
        idf = self.identf.ap()
        self.memset("pool", idf, 0.0, [self.identf.o])
        self.P.op("pool", lambda e: e.affine_select(out=idf, in_=idf, pattern=[[-1, 128]], compare_op=ALU.not_equal,
                                                    fill=1.0, base=0, channel_multiplier=1),
                  reads=[self.identf.o], writes=[self.identf.o])
        self.cp("dve", self.identb.ap(BF16), idf, [self.identf.o], [self.identb.o])
        self.memset("dve", self.onesb.ap(BF16), 1.0, [self.onesb.o])
        self.onesf = A.alloc("onesf", 256)
        self.memset("dve", self.onesf.ap(), 1.0, [self.onesf.o])
        self.memset("dve", self.epsc.ap()[:, 0:1], EPS, [self.epsc.o])
        self.memset("dve", self.epsc.ap()[:, 1:2], -math.pi, [self.epsc.o])

    def load_x(self):
        self.X = self.A.alloc("X", 8 * D * 4)
        X = self.X.ap(F32, "p (i d) -> p i d", i=8)
        self.XO = [Obj("X%d" % i) for i in range(8)]
        self.dma("sp", X[0:64], self.di["xp"].rearrange("(q i) d -> q i d", i=8), "x0", [], self.XO)
        self.dma("sp", X[64:128], self.di["xs"].rearrange("(q i) d -> q i d", i=8), "x1", [], self.XO)

    def store_x(self):
        X = self.X.ap(F32, "p (i d) -> p i d", i=8)
        self.dma("sp", self.do["yp"].rearrange("(q i) d -> q i d", i=8), X[0:64], "y0", self.XO, [], is_output=True)
        self.dma("sp", self.do["ys"].rearrange("(q i) d -> q i d", i=8), X[64:128], "y1", self.XO, [], is_output=True)

    def preamble(self):
        A, P = self.A, self.P
        cnd = A.alloc("cnd", 64)
        cndb = A.alloc("cndb", 64)
        c3 = cnd.ap(F32, "p (k c) -> p k c", c=2)
        for r in range(2):
            self.P.dma("sp", (lambda r: lambda e: e.dma_start(out=c3[:, :, r], in_=self.di["cvec"][r].rearrange("(kc k) -> k kc", k=128)))(r),
                       "cnd", writes=[cnd.o])
        self.act(cnd.ap()[:, 0:16], cnd.ap()[:, 0:16], AF.Silu, [cnd.o], [cnd.o])
        cb3 = cndb.ap(BF16)[:, 0:16].rearrange("p (k c) -> p k c", c=2)
        self.cp("dve", cndb.ap(BF16)[:, 0:16], cnd.ap()[:, 0:16], [cnd.o], [cndb.o])
        NS = 1536
        bm = A.alloc("bm", 4 * NS * 4)
        mp = A.alloc("mp", 4 * NS * 4)
        ws = [A.alloc("wmod%d" % i, 8 * 512 * 2) for i in range(3)]
        self.dma("sp", bm.ap()[0:2, :], self.di["b_mod"].rearrange("l n -> (l n)").partition_broadcast(2), "bm", [], [bm.o])
        cnt = 0
        for l in range(4):
            for nt in range(3):
                w = ws[cnt % 3]
                wv = w.ap(BF16, "p (k n) -> p k n", k=8)
                self.dma("pool", wv, self.di["w_mod"][l][:, nt * 512:(nt + 1) * 512].rearrange("(kc k) n -> k kc n", k=128),
                         "wmod%d" % (cnt % 3), [], [w.o])
                bk = cnt % 2
                for kc in range(8):
                    self.mm(self.bank(bk)[0:2, :], cb3[:, kc, :], wv[:, kc, :], kc == 0, kc == 7, [cndb.o, w.o], [self.BK[bk]])
                c0 = l * NS + nt * 512
                self.tt("dve", mp.ap()[0:2, c0:c0 + 512], self.bank(bk)[0:2, :], bm.ap()[0:2, c0:c0 + 512], ALU.add, [self.BK[bk], bm.o], [mp.o])
                cnt += 1
        osd, orc = Obj("msend"), Obj("mrecv")
        self.dma("sp", self.msend.rearrange("(c l) n -> c (l n)", c=2), mp.ap()[0:2, :], "msd", [mp.o], [osd])
        msd, mrc = self.msend, self.mrecv
        P.dma("pool", lambda e: e.collective_compute("AllGather", ALU.bypass, replica_groups=[[0, 1, 2, 3], [4, 5, 6, 7]], ins=[msd.opt()], outs=[mrc.opt()]),
              "cc_mod", reads=[osd], writes=[orc], inc=1)
        for r in range(4):
            self.P.dma("sp", (lambda r: lambda e: e.dma_start(out=self.modd[:, :, r * NS:(r + 1) * NS].rearrange("l c j -> c l j"),
                                                                in_=mrc[r * 8:(r + 1) * 8, :].rearrange("(c l) j -> c l j", c=2)))(r),
                       "m3", reads=[orc], writes=self.dobj["modd"])
        A.release(cnd, cndb, bm, mp, *ws)

    def layer_mod(self, l):
        self.cur_l = l

    def load_mod(self, idx, name):
        l = self.cur_l
        r = self.A.alloc(name, D * 4)
        M = r.ap()
        self.dma("sp", M[0:64, :], self.modd[l, 0, idx * D:(idx + 1) * D].partition_broadcast(64), "mod" + name, [self.dobj["modd"][l]], [r.o])
        self.dma("sp", M[64:128, :], self.modd[l, 1, idx * D:(idx + 1) * D].partition_broadcast(64), "mod" + name, [self.dobj["modd"][l]], [r.o])
        return r

    def load_ab(self, which):
        l = self.cur_l
        base = 0 if which == 1 else 3
        Bt = self.load_mod(base, "modB")
        At = self.load_mod(base + 1, "modA")
        gb = self.A.alloc("gb", D * 4)
        self.dma("sp", gb.ap(), self.di["norm1_g" if which == 1 else "norm2_g"][l].partition_broadcast(128), "gb", [], [gb.o])
        self.stt("dve", At.ap(), At.ap(), 1.0, gb.ap(), ALU.add, ALU.mult, [At.o, gb.o], [At.o])
        self.A.release(gb)
        return At, Bt

    def norm_mod(self, which, dest, dobj, view=None):
        A = self.A
        At, Bt = self.load_ab(which)
        Aap, Bap = At.ap(), Bt.ap()
        X = self.X.ap(F32, "p (i d) -> p i d", i=8)
        st = A.alloc("nm_st", 64)
        junk = A.alloc("nm_junk", D * 4)
        tmp = [A.alloc("nm_tmp%d" % i, D * 4) for i in range(2)]
        ss = st.ap()[:, 0:8]
        rs = st.ap()[:, 8:16]
        self.memset("dve", ss, 0.0, [st.o])
        for i in range(8):
            self.act(junk.ap(), X[:, i, :], AF.Square, [self.XO[i], st.o], [junk.o, st.o], accum_out=ss[:, i:i + 1])
        self.act(rs, ss, AF.Sqrt, [st.o, self.epsc.o], [st.o], bias=self.epsc.ap()[:, 0:1], scale=1.0 / D)
        self.P.op("dve", lambda e: e.reciprocal(out=rs, in_=rs), reads=[st.o], writes=[st.o])
        for i in range(8):
            t = tmp[i % 2]
            self.stt("dve", t.ap(), X[:, i, :], rs[:, i:i + 1], Aap, ALU.mult, ALU.mult, [self.XO[i], st.o, At.o], [t.o])
            if view is None:
                self.tt("pool", dest(i), t.ap(), Bap, ALU.add, [t.o, Bt.o], [dobj])
            else:
                self.tt("pool", dest(i), view(t.ap()), view(Bap), ALU.add, [t.o, Bt.o], [dobj])
        A.release(st, junk, At, Bt, *tmp)

    def transposeT(self, H, HT):
        Hv = H.ap(BF16, "p (i d) -> p i d", i=8)
        HTv = HT.ap(BF16, "p (k t) -> p k t", k=8)
        idb = self.identb.ap(BF16)
        for kc in range(8):
            bk = 6 + kc % 2
            psb = self.bank(bk).bitcast(BF16)
            for i in range(8):
                self.tr(psb[:, i * 128:(i + 1) * 128], Hv[:, i, kc * 128:(kc + 1) * 128], idb, [H.o, self.identb.o], [self.BK[bk]])
            self.cp("act" if kc % 2 else "dve", HTv[:, kc, :], psb, [self.BK[bk]], [HT.o])

    def ffn(self, l):
        A = self.A
        P = self.P
        X = self.X.ap(F32, "p (i d) -> p i d", i=8)
        WA = [A.alloc("wua%d" % i, 4096) for i in range(3)]
        WB = [A.alloc("wub%d" % i, 4096) for i in range(3)]
        wup = self.di["ffn_w_up"][l]

        def load_wu(mq):
            sq = mq % 3
            self.dma("pool", WA[sq].ap(BF16, "p (k n) -> p k n", k=8), wup[:, mq * 256:(mq + 1) * 256].rearrange("(kc k) n -> k kc n", k=128),
                     "wua%d" % sq, [], [WA[sq].o])
            self.dma("pool", WB[sq].ap(BF16, "p (k n) -> p k n", k=8), wup[:, DFF + mq * 256:DFF + (mq + 1) * 256].rearrange("(kc k) n -> k kc n", k=128),
                     "wub%d" % sq, [], [WB[sq].o])
        load_wu(0)
        load_wu(1)
        NWD, NPRE = 6, 4
        WD = [A.alloc("wd%d" % i, 2048) for i in range(NWD)]
        wdn = self.di["ffn_w_down"][l]

        def load_wd(q):
            nh_, mg_ = q // 11, q % 11
            sq = q % NWD
            self.dma("pool", WD[sq].ap(BF16, "p (a n) -> p a n", a=2),
                     wdn[mg_ * 256:(mg_ + 1) * 256, nh_ * 512:(nh_ + 1) * 512].rearrange("(a f) n -> f a n", f=128), "wd%d" % sq, [], [WD[sq].o])
        H = A.alloc("H2", 16384)
        Hv = H.ap(BF16, "p (i d) -> p i d", i=8)
        self.norm_mod(2, lambda i: Hv[:, i, :], H.o)
        hs, hr_d = self.hsend[l], self.hrecv[l]
        ohs, ohr = Obj("hsend"), Obj("hrecv")
        self.dma("sp", hs[0:1, :], Hv[64:65, 0, :], "hs", [H.o], [ohs])
        self.dma("sp", hs[1:2, :], Hv[127:128, 7, :], "hs", [H.o], [ohs])
        P.dma("pool", lambda e: e.collective_compute("AllGather", ALU.bypass, replica_groups=[[0, 1, 2, 3], [4, 5, 6, 7]],
                                                     ins=[hs.opt()], outs=[hr_d.opt()]),
              "cc_h%d" % l, reads=[ohs], writes=[ohr], inc=1)
        HT = A.alloc("HT2", 16384)
        self.transposeT(H, HT)
        A.release(H)
        HTv = HT.ap(BF16, "p (k t) -> p k t", k=8)
        hr = A.alloc("hr", 2048)
        sel = A.alloc("sel", 64)
        hth = A.alloc("hth", 64)
        self.dma("sp", hr.ap(BF16)[0:8, :], hr_d, "hr", [ohr], [hr.o])
        self.dma("sp", sel.ap()[0:8, 0:2], self.di["selhalo"], "sel", [], [sel.o])
        selb = sel.ap(BF16)[:, 8:16]
        self.cp("dve", selb[0:8, 0:2], sel.ap()[0:8, 0:2], [sel.o], [sel.o])
        for kc in range(8):
            self.mm(self.bank(7)[:, kc * 2:kc * 2 + 2], hr.ap(BF16)[0:8, kc * 128:(kc + 1) * 128], selb[0:8, 0:2], True, True,
                    [hr.o, sel.o], [self.BK[7]])
        hthv = hth.ap(BF16)[:, 0:16].rearrange("p (k c) -> p k c", c=2)
        self.cp("dve", hth.ap(BF16)[:, 0:16], self.bank(7)[:, 0:16], [self.BK[7]], [hth.o])
        cw = A.alloc("cw", 44 * 3 * 4)
        cb = A.alloc("cb", 44 * 4)
        cwv = cw.ap(F32, "p (m t) -> p m t", t=3)
        cbv = cb.ap()
        for t in range(3):
            self.dma("sp", cwv[:, :, t], self.di["ffn_conv_w"][l, t].rearrange("(m f) -> f m", f=128), "cw", [], [cw.o])
        self.dma("sp", cbv[:, 0:44], self.di["ffn_conv_b"][l].rearrange("(m f) -> f m", f=128), "cb", [], [cb.o])
        AT = A.alloc("AT", 22 * 1024 * 2)
        ATv = AT.ap(BF16, "p (m t) -> p m t", m=22)
        zc = {z: [A.alloc("zc%s%d" % (z, i), 4096) for i in range(2)] for z in "ab"}
        sA = [A.alloc("sA%d" % i, 4096) for i in range(2)]
        pairc = 0
        hcnt = 0
        for mp in range(11):
            sl = mp % 3
            wav = WA[sl].ap(BF16, "p (k n) -> p k n", k=8)
            wbv = WB[sl].ap(BF16, "p (k n) -> p k n", k=8)
            if mp + 2 < 11:
                load_wu(mp + 2)
            elif mp == 9:
                for q in range(NPRE):
                    load_wd(q)
            for ml in range(2):
                m = mp * 2 + ml
                par = m % 2
                for z, wv, wreg, cidx in (("a", wav, WA[sl], m), ("b", wbv, WB[sl], 22 + m)):
                    pr = pairc % 3
                    pairc += 1
                    Z = self.PB[pr]
                    zobjs = [self.BK[2 * pr], self.BK[2 * pr + 1]]
                    for nt in range(2):
                        for kc in range(8):
                            self.mm(Z[:, nt * 512:(nt + 1) * 512], wv[:, kc, ml * 128:(ml + 1) * 128], HTv[:, kc, nt * 512:(nt + 1) * 512],
                                    kc == 0, kc == 7, [wreg.o, HT.o], [zobjs[nt]])
                    hc = 32 + 2 * (hcnt % 16)
                    hcnt += 1
                    Zh = self.bank(7)[:, hc:hc + 2]
                    for kc in range(8):
                        self.mm(Zh, wv[:, kc, ml * 128:(ml + 1) * 128], hthv[:, kc, :], kc == 0, kc == 7, [wreg.o, hth.o], [self.BK[7]])
                    zr = zc[z][par]
                    zv = zr.ap()
                    w0, w1, w2 = cwv[:, cidx, 0:1], cwv[:, cidx, 1:2], cwv[:, cidx, 2:3]
                    self.act(zv, Z[:, :], AF.Identity, zobjs + [cw.o, cb.o], [zr.o], scale=w1, bias=cbv[:, cidx:cidx + 1])
                    rd = zobjs + [cw.o, zr.o]

                    def tap(dst, src, w, extra=()):
                        self.stt("dve", dst, src, w, dst, ALU.mult, ALU.add, rd + list(extra), [zr.o], same_ok=True)
                    tap(zv[:, 128:1024], Z[:, 0:896], w0)
                    for (a, b) in ((1, 32), (33, 64), (65, 128)):
                        tap(zv[:, a:b], Z[:, 896 + a - 1:896 + b - 1], w0)
                    tap(zv[:, 64:65], Zh[:, 0:1], w0, [self.BK[7]])
                    tap(zv[:, 0:896], Z[:, 128:1024], w2)
                    for (a, b) in ((0, 31), (32, 63), (64, 127)):
                        tap(zv[:, 896 + a:896 + b], Z[:, a + 1:b + 1], w2)
                    tap(zv[:, 1023:1024], Zh[:, 1:2], w2, [self.BK[7]])
                self.act(sA[par].ap(), zc["a"][par].ap(), AF.Silu, [zc["a"][par].o], [sA[par].o])
                self.tt("pool", ATv[:, m, :], sA[par].ap(), zc["b"][par].ap(), ALU.mult, [sA[par].o, zc["b"][par].o], [AT.o])
        A.release(HT, hr, sel, hth, cw, cb, *WA, *WB, *zc["a"], *zc["b"], *sA)
        G2 = self.load_mod(5, "modG")
        tmp = [A.alloc("dtmp%d" % i, 2048) for i in range(2)]
        cnt = 0
        for nh in range(2):
            for mg in range(11):
                sl = cnt % NWD
                if cnt + NPRE < 22:
                    load_wd(cnt + NPRE)
                cnt += 1
                wv = WD[sl].ap(BF16, "p (a n) -> p a n", a=2)
                for ml in range(2):
                    m = mg * 2 + ml
                    for i in range(8):
                        self.mm(self.bank(i), ATv[:, m, i * 128:(i + 1) * 128], wv[:, ml, :], m == 0, m == 21, [AT.o, WD[sl].o], [self.BK[i]])
            for i in range(8):
                t = tmp[i % 2]
                cols = slice(nh * 512, (nh + 1) * 512)
                self.tt("dve", t.ap(), self.bank(i), G2.ap()[:, cols], ALU.mult, [self.BK[i], G2.o], [t.o])
                self.tt("pool", X[:, i, cols], X[:, i, cols], t.ap(), ALU.add, [self.XO[i], t.o], [self.XO[i]])
        A.release(AT, G2, *WD, *tmp)


    def gen_rope(self, pos_reg, n, R, name):
        A = self.A
        nf = R // 4
        m = n * 2 * nf
        invf = A.alloc("invf", nf * 4)
        ii = A.alloc("iota", nf * 4)
        iiv = ii.ap(I32)
        self.P.op("pool", lambda e: e.iota(iiv, pattern=[[1, nf]], base=0, channel_multiplier=0), writes=[ii.o])
        self.cp("dve", invf.ap(), iiv, [ii.o], [invf.o])
        self.act(invf.ap(), invf.ap(), AF.Exp, [invf.o], [invf.o], scale=-math.log(10000.0) / nf)
        ang = A.alloc("ang", m * 4)
        angv = ang.ap(F32, "p (a f) -> p a f", f=nf)
        posv = pos_reg.ap()[:, 0:2 * n]
        self.tt("dve", angv, posv.unsqueeze(2).to_broadcast([128, 2 * n, nf]),
                invf.ap().unsqueeze(1).to_broadcast([128, 2 * n, nf]), ALU.mult, [pos_reg.o, invf.o], [ang.o])
        COS = A.alloc(name + "cos", n * R * 4)
        SIN = A.alloc(name + "sin", n * R * 4)
        t = A.alloc("rt", m * 4)
        ti = A.alloc("rti", m * 4)
        sc = A.alloc("rsc", m * 4)
        for kind, shift, dst in (("sin", 0.5, SIN), ("cos", 0.75, COS)):
            self.ts("dve", t.ap(), ang.ap(), 1.0 / (2 * math.pi), shift, ALU.mult, ALU.add, [ang.o], [t.o])
            self.cp("dve", ti.ap(I32), t.ap(), [t.o], [ti.o])
            self.cp("dve", sc.ap(), ti.ap(I32), [ti.o], [sc.o])
            self.tt("dve", t.ap(), t.ap(), sc.ap(), ALU.subtract, [t.o, sc.o], [t.o])
            self.stt("dve", t.ap(), t.ap(), 0.0, t.ap(), ALU.is_lt, ALU.add, [t.o], [t.o])
            self.act(sc.ap(), t.ap(), AF.Sin, [t.o, self.epsc.o], [sc.o], scale=2 * math.pi, bias=self.epsc.ap()[:, 1:2])
            sv = sc.ap(F32, "p (n c f) -> p n c f", c=2, f=nf)
            dv = dst.ap(F32, "p (n c h f) -> p n c h f", c=2, h=2, f=nf)
            for hf in range(2):
                if kind == "sin" and hf == 0:
                    self.ts("dve", dv[:, :, :, hf, :], sv, -1.0, None, ALU.mult, None, [sc.o], [dst.o])
                else:
                    self.cp("dve", dv[:, :, :, hf, :], sv, [sc.o], [dst.o])
        A.release(invf, ii, ang, t, ti, sc)
        return COS, SIN

    def rope(self, x, H, R, cosap, sinap, cso, t1r, t2r, xo):
        nf = R // 4
        t1 = t1r.ap()[:, 0:H * R].rearrange("p (h r) -> p h r", h=H)
        t2 = t2r.ap()[:, 0:H * R].rearrange("p (h r) -> p h r", h=H)
        cb = cosap.unsqueeze(1).to_broadcast([128, H, R])
        self.tt("pool", t1, x, cb, ALU.mult, [xo] + cso, [t1r.o])
        x5 = x.rearrange("p h (c g f) -> p h c g f", c=2, g=2, f=nf)
        t5 = t2.rearrange("p h (c g f) -> p h c g f", c=2, g=2, f=nf)
        s4 = sinap.rearrange("p (c g f) -> p c g f", c=2, g=2, f=nf)
        for hf in range(2):
            sb = s4[:, :, hf, :].unsqueeze(1).to_broadcast([128, H, 2, nf])
            self.tt("dve", t5[:, :, :, hf, :], x5[:, :, :, 1 - hf, :], sb, ALU.mult, [xo] + cso, [t2r.o], )
        self.tt("dve", x, t1, t2, ALU.add, [t1r.o, t2r.o], [xo])

    def head_norm(self, src, H, d, gain, out, reads, oobj, tmpr, str_, gobj):
        sq = tmpr.ap()[:, 0:H * d].rearrange("p (h d) -> p h d", h=H)
        self.tt("pool", sq, src, src, ALU.mult, reads, [tmpr.o])
        ss = str_.ap()[:, 0:H]
        self.P.op("dve", lambda e: e.tensor_reduce(out=ss, in_=sq, axis=AX.X, op=ALU.add), reads=[tmpr.o], writes=[str_.o])
        self.act(ss, ss, AF.Sqrt, [str_.o, self.epsc.o], [str_.o], bias=self.epsc.ap()[:, 0:1], scale=1.0 / d)
        self.P.op("dve", lambda e: e.reciprocal(out=ss, in_=ss), reads=[str_.o], writes=[str_.o])
        self.tt("dve", out, src, ss.unsqueeze(2).to_broadcast([128, H, d]), ALU.mult, reads + [str_.o], [oobj])
        self.tt("pool", out, out, gain.unsqueeze(1).to_broadcast([128, H, d]), ALU.mult, [oobj, gobj], [oobj])

    def kv_tile(self, S, So, rope, dKT, dV, dKTg, dVg, dobjs, W, pair, tcols=128, tb=0):
        idb = self.identb.ap(BF16)
        sc = W["sc"]
        cb = sc["ckvb"]
        self.cp("act", cb.ap(BF16)[:, 0:256], S[:, 0:256], [So], [cb.o])
        yield
        b6 = self.bank(6).bitcast(BF16)
        for kk in range(2):
            self.tr(b6[:, 384 + kk * 128:384 + (kk + 1) * 128], cb.ap(BF16)[:, kk * 128:(kk + 1) * 128], idb, [cb.o, self.identb.o], [self.BK[6]])
        ct = sc["ckvT"]
        self.cp("dve", ct.ap(BF16)[:, 0:256], b6[:, 384:640], [self.BK[6]], [ct.o])
        yield
        wk = W["wukv"]
        wkv = wk.ap(BF16, "p (k n) -> p k n", k=2)
        KV = self.PB[pair]
        kobjs = [self.BK[2 * pair], self.BK[2 * pair + 1]]
        for nt in range(2):
            for kk in range(2):
                self.mm(KV[:, nt * 512:(nt + 1) * 512], ct.ap(BF16)[:, kk * 128:(kk + 1) * 128], wkv[:, kk, nt * 512:(nt + 1) * 512],
                        kk == 0, kk == 1, [ct.o, wk.o], [kobjs[nt]])
        kv3 = KV[:, :].rearrange("p (h e) -> p h e", h=8)
        kc = sc["kcat"]
        kcv = kc.ap()[:, 0:768].rearrange("p (h e) -> p h e", h=8)
        self.cp("act", kcv[:, :, 0:64], kv3[:, :, 0:64], kobjs, [kc.o])
        yield
        self.cp("pool", kcv[:, :, 64:96], S[:, 256:288].unsqueeze(1).to_broadcast([128, 8, 32]), [So, kc.o], [kc.o])
        yield
        self.cp("dve", dV[:, :, 0:64], kv3[:, :, 64:128], kobjs, [dobjs["V"]])
        yield
        self.head_norm(kcv, 8, 96, W["gn"].ap()[:, 736:832], kcv, [kc.o], kc.o, sc["t1"], sc["st"], W["gn"].o)
        yield
        gk = sc["gk"]
        gkv = gk.ap()[:, 0:128].rearrange("p (h e) -> p h e", h=2)
        self.cp("pool", gk.ap()[:, 0:128], S[:, 288:416], [So], [gk.o])
        yield
        if rope is not None:
            c32, s32, c64, s64, ro = rope
            self.rope(kcv[:, :, 64:96], 8, 32, c32, s32, ro, sc["t1"], sc["t2"], kc.o)
            self.rope(gkv, 2, 64, c64, s64, ro, sc["t1"], sc["t2"], gk.o)
        kb = sc["kb"]
        kbv = kb.ap(BF16)[:, 0:768].rearrange("p (h e) -> p h e", h=8)
        self.cp("act", kbv, kcv, [kc.o], [kb.o])
        yield
        b7 = self.bank(7).bitcast(BF16)
        for h in range(8):
            self.tr(b7[0:96, h * 128:(h + 1) * 128], kbv[:, h, :], idb, [kb.o, self.identb.o], [self.BK[7]])
        self.cp("dve", dKT, b7[0:96, :].rearrange("p (h t) -> p h t", h=8)[:, :, 0:tcols], [self.BK[7]], [dobjs["KT"]])
        yield
        gb = sc["gkb"]
        gbv = gb.ap(BF16)[:, 0:128].rearrange("p (h e) -> p h e", h=2)
        self.cp("act", gbv, gkv, [gk.o], [gb.o])
        yield
        for h in range(2):
            self.tr(b6[0:64, 640 + h * 128:640 + (h + 1) * 128], gbv[:, h, :], idb, [gb.o, self.identb.o], [self.BK[6]])
        self.cp("dve", dKTg, b6[0:64, 640:896].rearrange("p (h t) -> p h t", h=2)[:, :, 0:tcols], [self.BK[6]], [dobjs["KTg"]])
        yield
        self.cp("pool", dVg[:, :, 0:64], S[:, 416:544].rearrange("p (h e) -> p h e", h=2), [So], [dobjs["Vg"]])
        yield

    def attention(self, l):
        jj = l // 2
        A, P, di = self.A, self.P, self.di
        X = self.X.ap(F32, "p (i d) -> p i d", i=8)
        idb = self.identb.ap(BF16)
        gn = A.alloc("gn", 960 * 4)
        for nm, a, b in (("attn_qa_norm_g", 0, 384), ("attn_kva_norm_g", 384, 640), ("attn_mla_q_norm_g", 640, 736),
                         ("attn_mla_k_norm_g", 736, 832), ("attn_gqa_q_norm_g", 832, 896), ("attn_gqa_k_norm_g", 896, 960)):
            self.dma("sp", gn.ap()[:, a:b], di[nm][jj].partition_broadcast(128), "gn", [], [gn.o])
        G = gn.ap()
        qp = A.alloc("qpos", 64)
        self.dma("sp", qp.ap()[:, 0:16], di["qpos"].rearrange("p i c -> p (i c)"), "qpos", [], [qp.o])
        C32, S32 = self.gen_rope(qp, 8, 32, "o32")
        C64, S64 = self.gen_rope(qp, 8, 64, "o64")
        A.release(qp)
        c32v, s32v = C32.ap(F32, "p (n r) -> p n r", n=8), S32.ap(F32, "p (n r) -> p n r", n=8)
        c64v, s64v = C64.ap(F32, "p (n r) -> p n r", n=8), S64.ap(F32, "p (n r) -> p n r", n=8)
        ropeobjs = [C32.o, S32.o, C64.o, S64.o]
        OUT = A.alloc("outst", 8 * 544 * 4)
        OUTv = OUT.ap(F32, "p (i e) -> p i e", i=8)
        QTm = [A.alloc("qtm%d" % k, 8 * 512 * 2) for k in range(2)]
        QTg = [A.alloc("qtg%d" % k, 8 * 512 * 2) for k in range(2)]
        KTo = A.alloc("kto", 8 * 8 * 64 * 2)
        Vo = A.alloc("vo", 8 * 8 * 65 * 2)
        KTgo = A.alloc("ktgo", 2 * 8 * 64 * 2)
        Vgo = A.alloc("vgo", 8 * 2 * 65 * 2)
        QTmv = [q.ap(BF16, "p (h i c) -> p h i c", h=8, i=8) for q in QTm]
        QTgv = [q.ap(BF16, "p (h i c) -> p h i c", h=8, i=8) for q in QTg]
        KTov = KTo.ap(BF16, "p (h i c) -> p h i c", h=8, i=8)
        Vov = Vo.ap(BF16, "p (i h e) -> p i h e", i=8, h=8)
        KTgov = KTgo.ap(BF16, "p (h i c) -> p h i c", h=2, i=8)
        Vgov = Vgo.ap(BF16, "p (i h e) -> p i h e", i=8, h=2)
        self.memset("pool", Vo.ap(BF16), 1.0, [Vo.o])
        self.memset("pool", Vgo.ap(BF16), 1.0, [Vgo.o])
        if self.cfg.get("attn_stop", 9) <= 1:
            return
        win = A.alloc("win", 8 * 1440 * 2)
        winv = win.ap(BF16, "p (k n) -> p k n", k=8)
        self.dma("pool", winv[:, :, 0:720], di["attn_w_in"][jj][:, 0:720].rearrange("(kc k) n -> k kc n", k=128), "win", [], [win.o])
        self.dma("pool", winv[:, :, 720:1440], di["attn_w_in"][jj][:, 720:1440].rearrange("(kc k) n -> k kc n", k=128), "win", [], [win.o])
        wuq = A.alloc("wuq", 3 * 768 * 2)
        wuqv = wuq.ap(BF16, "p (k n) -> p k n", k=3)
        self.dma("pool", wuqv, di["attn_w_uq"][jj].rearrange("(kc k) n -> k kc n", k=128), "wuq", [], [wuq.o])
        wukv = A.alloc("wukv", 2 * 1024 * 2)
        self.dma("pool", wukv.ap(BF16, "p (k n) -> p k n", k=2), di["attn_w_ukv"][jj].rearrange("(kc k) n -> k kc n", k=128), "wukv", [], [wukv.o])
        H = A.alloc("H1", 16384)
        Hv = H.ap(BF16, "p (i d) -> p i d", i=8)
        self.norm_mod(1, lambda i: Hv[:, i, :], H.o)
        HT = A.alloc("HT1", 16384)
        self.transposeT(H, HT)
        A.release(H)
        HTv = HT.ap(BF16, "p (k t) -> p k t", k=8)
        sc = {k: A.alloc("sc_" + k, n) for k, n in (("ckvb", 512), ("ckvT", 512), ("kcat", 3072), ("t1", 3072), ("t2", 3072),
                                                   ("st", 64), ("gk", 512), ("kb", 1536), ("gkb", 256))}
        W = {"sc": sc, "wukv": wukv, "gn": gn}
        qs = A.alloc("qs", 3072)
        qb = A.alloc("qb", 1536)
        st2 = A.alloc("st2", 64)
        junk = A.alloc("junk", 2048)
        qcb = A.alloc("qcb", 768)
        qcT = A.alloc("qcT", 768)
        splits = ((0, 384), (384, 672), (672, 1184), (1184, 1440))
        stp = self.cfg.get("attn_stop", 9)
        if stp <= 1.2:
            return
        for i in range(8):
            for bk, (c0, c1) in enumerate(splits):
                for kc in range(8):
                    self.mm(self.bank(bk)[:, 0:c1 - c0], HTv[:, kc, i * 128:(i + 1) * 128], winv[:, kc, c0:c1], kc == 0, kc == 7,
                            [HT.o, win.o], [self.BK[bk]])
            if stp <= 1.4:
                continue
            ss = st2.ap()[:, 0:1]
            self.memset("dve", st2.ap()[:, 0:2], 0.0, [st2.o])
            self.act(junk.ap()[:, 0:384], self.bank(0)[:, 0:384], AF.Square, [self.BK[0], st2.o], [junk.o, st2.o], accum_out=ss)
            self.act(ss, ss, AF.Sqrt, [st2.o, self.epsc.o], [st2.o], bias=self.epsc.ap()[:, 0:1], scale=1.0 / 384)
            self.P.op("dve", lambda e: e.reciprocal(out=st2.ap()[:, 0:1], in_=st2.ap()[:, 0:1]), reads=[st2.o], writes=[st2.o])
            self.stt("dve", qcb.ap(BF16)[:, 0:384], self.bank(0)[:, 0:384], ss, G[:, 0:384], ALU.mult, ALU.mult, [self.BK[0], st2.o, gn.o], [qcb.o])
            if stp <= 1.45:
                continue
            b6 = self.bank(6).bitcast(BF16)
            for kk in range(3):
                self.tr(b6[:, kk * 128:(kk + 1) * 128], qcb.ap(BF16)[:, kk * 128:(kk + 1) * 128], idb, [qcb.o, self.identb.o], [self.BK[6]])
            self.cp("dve", qcT.ap(BF16)[:, 0:384], b6[:, 0:384], [self.BK[6]], [qcT.o])
            QM = self.PB[2]
            for nt in range(2):
                for kk in range(3):
                    self.mm(QM[:, nt * 512:nt * 512 + 384], qcT.ap(BF16)[:, kk * 128:(kk + 1) * 128], wuqv[:, kk, nt * 384:(nt + 1) * 384],
                            kk == 0, kk == 2, [qcT.o, wuq.o], [self.BK[4 + nt]])
            qsv = qs.ap()[:, 0:768].rearrange("p (h e) -> p h e", h=8)
            self.cp("act", qs.ap()[:, 0:768].rearrange("p (a e) -> p a e", a=2), QM[:, :].rearrange("p (a e) -> p a e", a=2)[:, :, 0:384],
                    [self.BK[4], self.BK[5]], [qs.o])
            if stp <= 1.5:
                continue
            self.head_norm(qsv, 8, 96, G[:, 640:736], qsv, [qs.o], qs.o, sc["t1"], sc["st"], gn.o)
            if stp <= 1.55:
                continue
            self.rope(qsv[:, :, 64:96], 8, 32, c32v[:, i, :], s32v[:, i, :], ropeobjs, sc["t1"], sc["t2"], qs.o)
            if stp <= 1.57:
                continue
            qbv = qb.ap(BF16)[:, 0:768].rearrange("p (h e) -> p h e", h=8)
            self.cp("act", qbv, qsv, [qs.o], [qb.o])
            b7 = self.bank(7).bitcast(BF16)
            if stp <= 1.58:
                continue
            for h in range(8):
                self.tr(b7[0:96, h * 128:(h + 1) * 128], qbv[:, h, :], idb, [qb.o, self.identb.o], [self.BK[7]])
            b73 = b7[0:96, :].rearrange("p (h t) -> p h t", h=8)
            if stp <= 1.59:
                continue
            self.cp("dve", QTmv[0][0:96, :, i, :], b73[:, :, 0:64], [self.BK[7]], [QTm[0].o])
            if stp <= 1.595:
                continue
            self.cp("act", QTmv[1][0:96, :, i, :], b73[:, :, 64:128], [self.BK[7]], [QTm[1].o])
            if stp <= 1.6:
                continue
            ss2 = st2.ap()[:, 1:2]
            self.act(junk.ap()[:, 0:256], self.bank(1)[:, 0:256], AF.Square, [self.BK[1], st2.o], [junk.o, st2.o], accum_out=ss2)
            self.act(ss2, ss2, AF.Sqrt, [st2.o, self.epsc.o], [st2.o], bias=self.epsc.ap()[:, 0:1], scale=1.0 / 256)
            self.P.op("dve", lambda e: e.reciprocal(out=st2.ap()[:, 1:2], in_=st2.ap()[:, 1:2]), reads=[st2.o], writes=[st2.o])
            self.stt("dve", OUTv[:, i, 0:256], self.bank(1)[:, 0:256], ss2, G[:, 384:640], ALU.mult, ALU.mult, [self.BK[1], st2.o, gn.o], [OUT.o])
            self.cp("act", OUTv[:, i, 256:288], self.bank(1)[:, 256:288], [self.BK[1]], [OUT.o])
            self.cp("act", OUTv[:, i, 416:544], self.bank(3)[:, 128:256], [self.BK[3]], [OUT.o])
            gks = sc["gk"]
            self.cp("act", gks.ap()[:, 0:128], self.bank(3)[:, 0:128], [self.BK[3]], [gks.o])
            gk3 = gks.ap()[:, 0:128].rearrange("p (h e) -> p h e", h=2)
            self.head_norm(gk3, 2, 64, G[:, 896:960], OUTv[:, i, 288:416].rearrange("p (h e) -> p h e", h=2), [gks.o], OUT.o, sc["t1"], sc["st"], gn.o)
            if stp <= 1.7:
                continue
            self.cp("act", qs.ap()[:, 0:512], self.bank(2)[:, 0:512], [self.BK[2]], [qs.o])
            gq3 = qs.ap()[:, 0:512].rearrange("p (h e) -> p h e", h=8)
            self.head_norm(gq3, 8, 64, G[:, 832:896], gq3, [qs.o], qs.o, sc["t1"], sc["st"], gn.o)
            self.rope(gq3, 8, 64, c64v[:, i, :], s64v[:, i, :], ropeobjs, sc["t1"], sc["t2"], qs.o)
            gqb = qb.ap(BF16)[:, 0:512].rearrange("p (h e) -> p h e", h=8)
            self.cp("act", gqb, gq3, [qs.o], [qb.o])
            for h in range(8):
                self.tr(b7[0:64, h * 128:(h + 1) * 128], gqb[:, h, :], idb, [qb.o, self.identb.o], [self.BK[7]])
            b74 = b7[0:64, :].rearrange("p (h t) -> p h t", h=8)
            self.cp("dve", QTgv[0][0:64, :, i, :], b74[:, :, 0:64], [self.BK[7]], [QTg[0].o])
            self.cp("act", QTgv[1][0:64, :, i, :], b74[:, :, 64:128], [self.BK[7]], [QTg[1].o])
            if stp <= 1.8:
                continue
            for _ in self.kv_tile(OUTv[:, i, :], OUT.o, (c32v[:, i, :], s32v[:, i, :], c64v[:, i, :], s64v[:, i, :], ropeobjs),
                                  KTov[0:96, :, i, :], Vov[:, i, :, :], KTgov[0:64, :, i, :], Vgov[:, i, :, :],
                                  {"KT": KTo.o, "V": Vo.o, "KTg": KTgo.o, "Vg": Vgo.o}, W, 2, tcols=64):
                pass
        A.release(HT, win, wuq, qs, qb, st2, junk, qcb, qcT, C32, S32, C64, S64)
        for s_ in range(2):
            rows = slice(32 * s_, 32 * s_ + 32)
            for nm, a, b in (("ockv", 0, 256), ("okrope", 256, 288), ("ogk", 288, 416), ("ogv", 416, 544)):
                self.dma("sp", self.do[nm][s_, jj].rearrange("(c i) d -> c i d", i=8), OUTv[rows, :, a:b], "o" + nm, [OUT.o], [], is_output=True)
        if self.cfg.get("attn_stop", 9) <= 2:
            return
        orecv = []
        for hf in range(2):
            osend, orc = Obj("kvsend%d" % hf), Obj("kvrecv%d" % hf)
            ks, kr = self.kvsend[jj][hf], self.kvrecv[jj][hf]
            self.dma("sp", ks.rearrange("(c i) d -> c i d", i=8), OUTv[64 + 32 * hf:96 + 32 * hf, :, :], "kvs", [OUT.o], [osend])
            P.dma("pool", (lambda ks, kr: lambda e: e.collective_compute("AllGather", ALU.bypass, replica_groups=[[0, 1, 2, 3], [4, 5, 6, 7]],
                                                                         ins=[ks.opt()], outs=[kr.opt()]))(ks, kr),
                  "cc_kv%d_%d" % (jj, hf), reads=[osend], writes=[orc], inc=1)
            orecv.append(orc)
        A.release(OUT)
        kp = A.alloc("kpos", 128)
        self.dma("sp", kp.ap()[:, 0:32], di["kpos"].rearrange("p u c -> p (u c)"), "kpos", [], [kp.o])
        KC32, KS32 = self.gen_rope(kp, 16, 32, "k32")
        KC64, KS64 = self.gen_rope(kp, 16, 64, "k64")
        A.release(kp)
        kro = [KC32.o, KS32.o, KC64.o, KS64.o]
        kc32, ks32 = KC32.ap(F32, "p (n r) -> p n r", n=16), KS32.ap(F32, "p (n r) -> p n r", n=16)
        kc64, ks64 = KC64.ap(F32, "p (n r) -> p n r", n=16), KS64.ap(F32, "p (n r) -> p n r", n=16)
        KT = A.alloc("KT", 8 * 2304 * 2)
        VM = A.alloc("VM", 18 * 8 * 65 * 2)
        KTg = A.alloc("KTg", 2 * 2304 * 2)
        VG = A.alloc("VG", 18 * 2 * 65 * 2)
        KTv = KT.ap(BF16, "p (h t) -> p h t", h=8)
        VMv = VM.ap(BF16, "p (u h e) -> p u h e", u=18, h=8)
        KTgv = KTg.ap(BF16, "p (h t) -> p h t", h=2)
        VGv = VG.ap(BF16, "p (u h e) -> p u h e", u=18, h=2)
        self.memset("pool", VM.ap(BF16), 1.0, [VM.o])
        self.memset("pool", VG.ap(BF16), 1.0, [VG.o])
        stg = [A.alloc("stg%d" % k, 544 * 4) for k in range(2)]
        dob = {"KT": KT.o, "V": VM.o, "KTg": KTg.o, "Vg": VG.o}
        sc2 = {k: A.alloc("sc2_" + k, r_.req) for k, r_ in sc.items()}
        W2 = {"sc": sc2, "wukv": wukv, "gn": gn}
        gens = []
        for u in range(18):
            sg = stg[u % 2]
            if u < 2:
                rows = slice(128 * u, 128 * u + 128)
                self.dma("sp", sg.ap()[:, 0:256], di["cache_ckv"][jj, rows, :], "stg%d" % (u % 2), [], [sg.o])
                self.dma("sp", sg.ap()[:, 256:288], di["cache_krope"][jj, rows, :], "stg%d" % (u % 2), [], [sg.o])
                self.dma("sp", sg.ap()[:, 288:416], di["cache_gk"][jj, rows, :], "stg%d" % (u % 2), [], [sg.o])
                self.dma("sp", sg.ap()[:, 416:544], di["cache_gv"][jj, rows, :], "stg%d" % (u % 2), [], [sg.o])
                rp = None
            else:
                v = u - 2
                rk, w4 = v // 4, v % 4
                src = self.kvrecv[jj][w4 // 2][256 * rk + 128 * (w4 % 2):256 * rk + 128 * (w4 % 2) + 128, :]
                self.dma("sp", sg.ap()[:, 0:544], src, "stg%d" % (u % 2), [orecv[w4 // 2]], [sg.o])
                rp = (kc32[:, v, :], ks32[:, v, :], kc64[:, v, :], ks64[:, v, :], kro)
            gens.append(self.kv_tile(sg.ap()[:, 0:544], sg.o, rp, KTv[0:96, :, u * 128:(u + 1) * 128], VMv[:, u, :, :],
                                     KTgv[0:64, :, u * 128:(u + 1) * 128], VGv[:, u, :, :], dob, W if u % 2 == 0 else W2, u % 3, tb=u % 2))
            if len(gens) == 2:
                live = list(gens)
                while live:
                    for g_ in list(live):
                        try:
                            next(g_)
                        except StopIteration:
                            live.remove(g_)
                gens = []
        A.release(wukv, KC32, KS32, KC64, KS64, *stg, *sc.values(), *sc2.values())
        if self.cfg.get("attn_stop", 9) <= 3:
            return
        G1 = self.load_mod(2, "modG")
        OT = A.alloc("OT", 8 * 1024 * 2)
        OTv = OT.ap(BF16, "p (h i c) -> p h i c", h=8, i=8)
        PT = [A.alloc("PT%d" % k, 1024) for k in range(3)]
        PTp = A.alloc("PTp", 8 * 256 * 2)
        PTpv = PTp.ap(BF16, "p (g i c) -> p g i c", g=8, i=8)
        rsr = A.alloc("rsr", 2048)
        bcs = A.alloc("bcs", 2048)
        wo = [A.alloc("wo%d" % k, 4 * 512 * 2) for k in range(2)]
        wov = [w_.ap(BF16, "p (h n) -> p h n", h=4) for w_ in wo]
        dtmp = [A.alloc("atmp%d" % k, 2048) for k in range(2)]
        onesf = self.onesf.ap()
        ptc = 0
        sbc = 0
        for grp in range(2):
            dq = 96 if grp == 0 else 64
            scale = 1.0 / math.sqrt(dq)
            for h in range(8):
                hk = h if grp == 0 else h // 4
                if grp == 0:
                    qts, qtp, qo_s, qo_p = QTmv[1][0:96, h], QTmv[0][0:96, h], QTm[1].o, QTm[0].o
                    kt_all, kto_, kobj, koobj = KTv[0:96, hk], KTov[0:96, hk], KT.o, KTo.o
                    v_all, v_own, vobj, voobj = VMv, Vov, VM.o, Vo.o
                else:
                    qts, qtp, qo_s, qo_p = QTgv[1][0:64, h], QTgv[0][0:64, h], QTg[1].o, QTg[0].o
                    kt_all, kto_, kobj, koobj = KTgv[0:64, hk], KTgov[0:64, hk], KTg.o, KTgo.o
                    v_all, v_own, vobj, voobj = VGv, Vgov, VG.o, Vgo.o
                ob = 4 + h % 2
                OB = self.bank(ob)

                pend = []
                for u in range(18 + 2):
                    if u < 18:
                        sb = sbc % 4
                        sbc += 1
                        self.mm(self.bank(sb), kt_all[:, u * 128:(u + 1) * 128], qts.rearrange("p i c -> p (i c)"), True, True, [kobj, qo_s], [self.BK[sb]])
                        pt = PT[ptc % 3]
                        ptc += 1
                        self.act(pt.ap(BF16), self.bank(sb), AF.Exp, [self.BK[sb]], [pt.o], scale=scale)
                        pend.append((u, pt))
                    if u >= 2:
                        uu, pt2 = pend.pop(0)
                        self.mm(OB[0:65, :], v_all[:, uu, hk, :], pt2.ap(BF16), uu == 0, uu == 17, [vobj, pt2.o], [self.BK[ob]])
                self.act(rsr.ap()[64:65, 0:512], OB[64:65, 0:512], AF.Ln, [self.BK[ob]], [rsr.o])
                self.act(rsr.ap()[64:65, 0:512], rsr.ap()[64:65, 0:512], AF.Exp, [rsr.o], [rsr.o], scale=-1.0)
                self.mm(self.bank(6)[0:64, 0:512], onesf[64:65, 0:64], rsr.ap()[64:65, 0:512], True, True, [rsr.o, self.onesf.o], [self.BK[6]])
                self.cp("act", bcs.ap()[0:64, 0:512], self.bank(6)[0:64, 0:512], [self.BK[6]], [bcs.o])
                self.tt("dve", OTv[0:64, h, :, 64:128], OB[0:64, 0:512].rearrange("p (i c) -> p i c", i=8),
                        bcs.ap()[0:64, 0:512].rearrange("p (i c) -> p i c", i=8), ALU.mult, [self.BK[ob], bcs.o], [OT.o])
                for ig in range(8):
                    sb = sbc % 4
                    sbc += 1
                    self.mm(self.bank(sb)[0:64, :], kto_[:, ig, 0:64], qtp.rearrange("p i c -> p (i c)"), True, True, [koobj, qo_p], [self.BK[sb]])
                    s3 = self.bank(sb)[0:64, :].rearrange("p (i c) -> p i c", i=8)
                    for s_ in range(2):
                        rows = slice(32 * s_, 32 * s_ + 32)
                        self.act(PTpv[rows, ig, :, 0:32], s3[rows, :, 32 * s_:32 * s_ + 32], AF.Exp, [self.BK[sb]], [PTp.o], scale=scale)
                for s_ in range(2):
                    rows = slice(32 * s_, 32 * s_ + 32)
                    ob2 = 7
                    OB2 = self.bank(ob2)
                    for ig in range(8):
                        self.mm(OB2[0:65, 0:256], v_own[rows, ig, hk, :], PTpv[rows, ig, :, 0:32].rearrange("p i c -> p (i c)"), ig == 0, ig == 7,
                                [voobj, PTp.o], [self.BK[ob2]])
                    self.act(rsr.ap()[64:65, 0:256], OB2[64:65, 0:256], AF.Ln, [self.BK[ob2]], [rsr.o])
                    self.act(rsr.ap()[64:65, 0:256], rsr.ap()[64:65, 0:256], AF.Exp, [rsr.o], [rsr.o], scale=-1.0)
                    self.mm(self.bank(6)[0:64, 0:256], onesf[64:65, 0:64], rsr.ap()[64:65, 0:256], True, True, [rsr.o, self.onesf.o], [self.BK[6]])
                    self.cp("act", bcs.ap()[0:64, 0:256], self.bank(6)[0:64, 0:256], [self.BK[6]], [bcs.o])
                    self.tt("dve", OTv[0:64, h, :, 32 * s_:32 * s_ + 32], OB2[0:64, 0:256].rearrange("p (i c) -> p i c", i=8),
                            bcs.ap()[0:64, 0:256].rearrange("p (i c) -> p i c", i=8), ALU.mult, [self.BK[ob2], bcs.o], [OT.o])
            for nh in range(2):
                cols = slice(nh * 512, (nh + 1) * 512)
                for k2 in range(2):
                    self.dma("pool", wov[k2][0:64], di["attn_w_out"][jj][grp * 512 + k2 * 256:grp * 512 + (k2 + 1) * 256, cols].rearrange("(h d) n -> d h n", d=64),
                             "wo%d" % k2, [], [wo[k2].o])
                for i in range(8):
                    bk = i % 4
                    for h in range(8):
                        self.mm(self.bank(bk), OTv[0:64, h, i, :], wov[h // 4][0:64, h % 4, :], h == 0, h == 7, [OT.o, wo[h // 4].o], [self.BK[bk]])
                    t = dtmp[i % 2]
                    self.tt("dve", t.ap(), self.bank(bk), G1.ap()[:, cols], ALU.mult, [self.BK[bk], G1.o], [t.o])
                    self.tt("pool", X[:, i, cols], X[:, i, cols], t.ap(), ALU.add, [self.XO[i], t.o], [self.XO[i]])
        A.release(gn, G1, OT, PTp, rsr, bcs, *wo, KT, VM, KTg, VG, KTo, Vo, KTgo, Vgo, *PT, *dtmp, *QTm, *QTg)


    def gen_pw(self, k0, step, ardt, aidt, name):
        A = self.A
        n = 8 * 64
        ki = A.alloc("ki", 32)
        kf = A.alloc("kf", 32)
        kiv = ki.ap(I32)
        self.P.op("pool", lambda e: e.iota(kiv, pattern=[[step, 8]], base=k0, channel_multiplier=0), writes=[ki.o])
        self.cp("dve", kf.ap(), kiv, [ki.o], [kf.o])
        kb = kf.ap().unsqueeze(2).to_broadcast([128, 8, 64])
        Pre = A.alloc(name + "re", n * 4)
        Pim = A.alloc(name + "im", n * 4)
        mag = A.alloc("mag", n * 4)
        ang = A.alloc("pang", n * 4)
        t = A.alloc("pt", n * 4)
        ti = A.alloc("pti", n * 4)
        v3 = lambda r: r.ap(F32, "p (k g) -> p k g", k=8)
        self.tt("dve", v3(mag), kb, ardt.ap().unsqueeze(1).to_broadcast([128, 8, 64]), ALU.mult, [kf.o, ardt.o], [mag.o])
        self.act(mag.ap(), mag.ap(), AF.Exp, [mag.o], [mag.o])
        self.tt("dve", v3(ang), kb, aidt.ap().unsqueeze(1).to_broadcast([128, 8, 64]), ALU.mult, [kf.o, aidt.o], [ang.o])
        for shift, dst in ((0.5, Pim), (0.75, Pre)):
            self.ts("dve", t.ap(), ang.ap(), 1.0 / (2 * math.pi), shift + 32.0, ALU.mult, ALU.add, [ang.o], [t.o])
            self.cp("dve", ti.ap(I32), t.ap(), [t.o], [ti.o])
            self.cp("dve", dst.ap(), ti.ap(I32), [ti.o], [dst.o])
            self.tt("dve", t.ap(), t.ap(), dst.ap(), ALU.subtract, [t.o, dst.o], [t.o])
            self.stt("dve", t.ap(), t.ap(), 0.0, t.ap(), ALU.is_lt, ALU.add, [t.o], [t.o])
            self.act(dst.ap(), t.ap(), AF.Sin, [t.o, self.epsc.o], [dst.o], scale=2 * math.pi, bias=self.epsc.ap()[:, 1:2])
            self.tt("dve", dst.ap(), dst.ap(), mag.ap(), ALU.mult, [dst.o, mag.o], [dst.o])
        A.release(ki, kf, mag, ang, t, ti)
        return Pre, Pim

    def ssm(self, l):
        j2 = l // 2
        A, P, di = self.A, self.P, self.di
        X = self.X.ap(F32, "p (i d) -> p i d", i=8)
        idb, idf = self.identb.ap(BF16), self.identf.ap()
        ENG = ("dve", "pool")
        U = A.alloc("U", 16384)
        Uv = U.ap(BF16, "p (g i h) -> p g i h", g=64, i=8)
        self.norm_mod(1, lambda i: Uv[:, :, i, :], U.o, view=lambda ap: ap.rearrange("p (g h) -> p g h", g=64))
        VA = A.alloc("VA", 16384)
        VAv = VA.ap(BF16, "p (g c) -> p g c", g=64)
        for gb in range(8):
            bk = 6 + gb % 2
            psb = self.bank(bk).bitcast(BF16)
            for gl in range(8):
                self.tr(psb[:, gl * 128:(gl + 1) * 128], Uv[:, gb * 8 + gl, :, :].rearrange("p i h -> p (i h)"), idb, [U.o, self.identb.o], [self.BK[bk]])
            self.cp("act" if gb % 2 else "dve", VAv[:, gb * 8:(gb + 1) * 8, :], psb.rearrange("p (g c) -> p g c", g=8), [self.BK[bk]], [VA.o])
        A.release(U)
        if self.cfg.get("ssm_stop", 99) <= 1:
            return
        QWB, PWC, QXB, BR, BI = [], [], [], [], []
        MRE2, NMIM, PMIM, M64 = [], [], [], []
        small = A.alloc("ssmall", 64 * 4 * 12)
        sm = lambda k: small.ap()[:, k * 64:(k + 1) * 64]
        for d in range(2):
            aTr, aTi, dtb = A.alloc("aTr", 256), A.alloc("aTi", 256), A.alloc("dtb", 256)
            for half in range(2):
                rows = slice(64 * half, 64 * half + 64)
                self.dma("sp", aTr.ap()[rows, :], di["ssm_a_re"][j2, d].rearrange("g p -> p g"), "ssmp", [], [aTr.o])
                self.dma("sp", aTi.ap()[rows, :], di["ssm_a_im"][j2, d].rearrange("g p -> p g"), "ssmp", [], [aTi.o])
            self.dma("sp", dtb.ap(), di["ssm_log_dt"][j2, d].partition_broadcast(128), "ssmp", [], [dtb.o])
            self.act(dtb.ap(), dtb.ap(), AF.Exp, [dtb.o], [dtb.o])
            ardt, aidt = A.alloc("ardt", 256), A.alloc("aidt", 256)
            self.tt("dve", ardt.ap(), aTr.ap(), dtb.ap(), ALU.mult, [aTr.o, dtb.o], [ardt.o])
            self.tt("dve", aidt.ap(), aTi.ap(), dtb.ap(), ALU.mult, [aTi.o, dtb.o], [aidt.o])
            wbk, wck, xbk = ((7, -1), (1, 1), (-1, -1)) if d == 0 else ((0, 1), (8, -1), (-8, 1))
            pwb = self.gen_pw(wbk[0], wbk[1], ardt, aidt, "pwb%d" % d)
            pwc = self.gen_pw(wck[0], wck[1], ardt, aidt, "pwc%d" % d)
            pxb = self.gen_pw(xbk[0], xbk[1], ardt, aidt, "pxb%d" % d)
            il, imu = (0, 7) if d == 0 else (7, 0)
            pcr = pwc[0].ap(F32, "p (k g) -> p k g", k=8)
            pci = pwc[1].ap(F32, "p (k g) -> p k g", k=8)
            so = small.o
            den, rden, nre, t1, t2, kre, kim = sm(0), sm(1), sm(2), sm(3), sm(4), sm(5), sm(6)
            self.tt("dve", den, aTr.ap(), aTr.ap(), ALU.mult, [aTr.o], [so])
            self.tt("dve", t1, aTi.ap(), aTi.ap(), ALU.mult, [aTi.o, so], [so])
            self.tt("dve", den, den, t1, ALU.add, [so], [so])
            self.recip(rden, den, [so], [so])
            self.ts("dve", nre, pcr[:, il, :], -1.0, None, ALU.add, None, [pwc[0].o, so], [so])
            self.tt("dve", t1, nre, aTr.ap(), ALU.mult, [so, aTr.o], [so])
            self.tt("dve", t2, pci[:, il, :], aTi.ap(), ALU.mult, [pwc[1].o, aTi.o, so], [so])
            self.tt("dve", t1, t1, t2, ALU.add, [so], [so])
            self.tt("dve", kre, t1, rden, ALU.mult, [so], [so])
            self.tt("dve", t1, pci[:, il, :], aTr.ap(), ALU.mult, [pwc[1].o, aTr.o, so], [so])
            self.tt("dve", t2, nre, aTi.ap(), ALU.mult, [so, aTi.o], [so])
            self.tt("dve", t1, t1, t2, ALU.subtract, [so], [so])
            self.tt("dve", kim, t1, rden, ALU.mult, [so], [so])
            qt1, qt2 = A.alloc("qt1", 2048), A.alloc("qt2", 2048)
            for (pr, pi_) in (pwb, pxb):
                p3r = pr.ap(F32, "p (k g) -> p k g", k=8)
                p3i = pi_.ap(F32, "p (k g) -> p k g", k=8)
                q1 = qt1.ap(F32, "p (k g) -> p k g", k=8)
                q2 = qt2.ap(F32, "p (k g) -> p k g", k=8)
                krb = kre.unsqueeze(1).to_broadcast([128, 8, 64])
                kib = kim.unsqueeze(1).to_broadcast([128, 8, 64])
                self.tt("dve", q1, p3r, kib, ALU.mult, [pr.o, so], [qt1.o])
                self.tt("pool", q2, p3i, kib, ALU.mult, [pi_.o, so], [qt2.o])
                self.tt("dve", p3r, p3r, krb, ALU.mult, [pr.o, so], [pr.o])
                self.tt("dve", p3r, p3r, q2, ALU.subtract, [pr.o, qt2.o], [pr.o])
                self.tt("pool", p3i, p3i, krb, ALU.mult, [pi_.o, so], [pi_.o])
                self.tt("pool", p3i, p3i, q1, ALU.add, [pi_.o, qt1.o], [pi_.o])
            A.release(qt1, qt2)
            mt = A.alloc("mt%d" % d, (64 + 32 + 32 + 64) * 4)
            mre2 = mt.ap()[:, 0:64].rearrange("p (r q) -> p r q", r=2)
            nmim, pmim = mt.ap()[:, 64:96], mt.ap()[:, 96:128]
            m64 = mt.ap()[:, 128:192].rearrange("p (r q) -> p r q", r=2)
            for par in range(2):
                rows = slice(64 * par, 64 * par + 64)
                srcr = pcr[rows, imu, :].rearrange("p (q two) -> p q two", two=2)[:, :, par]
                srci = pci[rows, imu, :].rearrange("p (q two) -> p q two", two=2)[:, :, par]
                for r_ in range(2):
                    self.cp("dve", mre2[rows, r_, :], srcr, [pwc[0].o], [mt.o])
                self.cp("dve", pmim[rows, :], srci, [pwc[1].o], [mt.o])
                self.ts("dve", nmim[rows, :], srci, -1.0, None, ALU.mult, None, [pwc[1].o], [mt.o])
            self.cp("dve", m64[:, 0, :], mre2[:, 0, :], [mt.o], [mt.o])
            self.cp("dve", m64[:, 1, :], pmim, [mt.o], [mt.o])
            sq1, sq2 = sm(7)[:, 0:32], sm(8)[:, 0:32]
            for _ in range(6):
                self.tt("dve", sq1, m64[:, 0, :], m64[:, 0, :], ALU.mult, [mt.o, so], [so])
                self.tt("dve", sq2, m64[:, 1, :], m64[:, 1, :], ALU.mult, [mt.o, so], [so])
                self.stt("dve", m64[:, 1, :], m64[:, 0, :], 2.0, m64[:, 1, :], ALU.mult, ALU.mult, [mt.o], [mt.o])
                self.tt("dve", m64[:, 0, :], sq1, sq2, ALU.subtract, [so], [mt.o])
            br, bi = A.alloc("br%d" % d, 4096), A.alloc("bi%d" % d, 4096)
            for half in range(2):
                rows = slice(64 * half, 64 * half + 64)
                self.dma("sp", br.ap(F32, "p (g h) -> p g h", g=64)[rows], di["ssm_b_re"][j2, d].rearrange("g p h -> p g h"), "ssmb", [], [br.o])
                self.dma("sp", bi.ap(F32, "p (g h) -> p g h", g=64)[rows], di["ssm_b_im"][j2, d].rearrange("g p h -> p g h"), "ssmb", [], [bi.o])
            A.release(aTr, aTi, dtb, ardt, aidt)
            QWB.append(pwb); PWC.append(pwc); QXB.append(pxb); BR.append(br); BI.append(bi)
            MRE2.append(mre2); NMIM.append(nmim); PMIM.append(pmim); M64.append((m64, mt))
        if self.cfg.get("ssm_stop", 99) <= 2:
            return
        S = [A.alloc("S%d" % d, 2 * 32 * 131 * 2) for d in range(2)]
        S4 = [s_.ap(BF16, "p (r q c) -> p r q c", r=2, q=32) for s_ in S]
        wpre = A.alloc("wpre", 8 * 2 * 128 * 4)
        wt1, wt2 = A.alloc("wt1", 8 * 128 * 4), A.alloc("wt2", 8 * 128 * 4)
        wt3, wt4 = A.alloc("wt3", 8 * 128 * 4), A.alloc("wt4", 8 * 128 * 4)
        w3 = wt3.ap(F32, "p (g i h) -> p g i h", g=8, i=8)
        w4 = wt4.ap(F32, "p (g i h) -> p g i h", g=8, i=8)
        wbc = [A.alloc("wbc%d" % k, 8 * 2 * 64 * 2) for k in range(2)]
        wpv = wpre.ap(F32, "p (g r i h) -> p g r i h", g=8, r=2, i=8)
        w1 = wt1.ap(F32, "p (g i h) -> p g i h", g=8, i=8)
        w2 = wt2.ap(F32, "p (g i h) -> p g i h", g=8, i=8)
        segs = ((0, 32), (32, 64), (64, 128))
        cnt = 0
        for d in range(2):
            qr = QWB[d][0].ap(F32, "p (k g) -> p k g", k=8)
            qi = QWB[d][1].ap(F32, "p (k g) -> p k g", k=8)
            brv = BR[d].ap(F32, "p (g h) -> p g h", g=64)
            biv = BI[d].ap(F32, "p (g h) -> p g h", g=64)
            qo = [QWB[d][0].o, QWB[d][1].o, BR[d].o, BI[d].o]
            for gb in range(8):
                gs = slice(gb * 8, gb * 8 + 8)
                Er = qr[0:64, :, gs].rearrange("p i g -> p g i").unsqueeze(3).to_broadcast([64, 8, 8, 16])
                Ei = qi[0:64, :, gs].rearrange("p i g -> p g i").unsqueeze(3).to_broadcast([64, 8, 8, 16])
                Br_ = brv[0:64, gs, :].unsqueeze(2).to_broadcast([64, 8, 8, 16])
                Bi_ = biv[0:64, gs, :].unsqueeze(2).to_broadcast([64, 8, 8, 16])
                self.tt("pool", w2[0:64], Ei, Bi_, ALU.mult, qo, [wt2.o])
                self.tt("dve", w1[0:64], Er, Br_, ALU.mult, qo, [wt1.o])
                self.tt("pool", w4[0:64], Ei, Br_, ALU.mult, qo, [wt4.o])
                self.tt("dve", w3[0:64], Er, Bi_, ALU.mult, qo, [wt3.o])
                self.tt("dve", wpv[0:64, :, 0], w1[0:64], w2[0:64], ALU.subtract, [wt1.o, wt2.o], [wpre.o])
                self.tt("dve", wpv[0:64, :, 1], w3[0:64], w4[0:64], ALU.add, [wt3.o, wt4.o], [wpre.o])
                pr = cnt % 3
                cnt += 1
                PT_ = self.PB[pr]
                pob = [self.BK[2 * pr], self.BK[2 * pr + 1]]
                wflat = wpre.ap(F32, "p (g r e) -> p g r e", g=8, r=2)
                for gl in range(8):
                    for r_ in range(2):
                        c0 = (gl * 2 + r_) * 64
                        self.tr(PT_[:, c0:c0 + 64], wflat[0:64, gl, r_, :], idf[0:64, 0:64], [wpre.o, self.identf.o], [pob[c0 // 512]])
                wb = wbc[gb % 2]
                wbv = wb.ap(BF16, "p (g r e) -> p g r e", g=8, r=2)
                self.cp("act", wb.ap(BF16), PT_[:, :], pob, [wb.o])
                pr2 = cnt % 3
                cnt += 1
                PS_ = self.PB[pr2]
                pob2 = [self.BK[2 * pr2], self.BK[2 * pr2 + 1]]
                for gl in range(8):
                    g = gb * 8 + gl
                    par, pl = g % 2, gl // 2
                    for r_ in range(2):
                        c0 = (pl * 2 + r_) * 128
                        kw = {"tile_position": (0, 64)} if par else {}
                        self.mm(PS_[64 * par:64 * par + 64, c0:c0 + 128], wbv[:, gl, r_, :], VAv[:, g, :], True, True, [wb.o, VA.o], [pob2[c0 // 512]], **kw)
                ps4 = PS_[:, :].rearrange("p (q r c) -> p r q c", q=4, r=2)
                for si, (a, b) in enumerate(segs):
                    base = (0, 33, 66)[si] + (1 if d == 0 else 0)
                    self.cp("act" if si % 2 else "dve", S4[d][:, :, gb * 4:(gb + 1) * 4, base:base + (b - a)], ps4[:, :, :, a:b], pob2, [S[d].o])
        A.release(wpre, wt1, wt2, wt3, wt4, *wbc, QWB[0][0], QWB[0][1], QWB[1][0], QWB[1][1])
        if self.cfg.get("ssm_stop", 99) <= 3:
            return
        selr = A.alloc("selr", 32)
        self.dma("sp", selr.ap()[:, 0:8], di["selr"], "selr", [], [selr.o])
        H0 = A.alloc("H0", 128 * 4)
        for d in range(2):
            for par in range(2):
                self.dma("sp", H0.ap()[64 * par:64 * par + 64, 64 * d:64 * d + 64].rearrange("p (r q) -> p r q", r=2),
                         di["state_ssm"][j2, d].rearrange("(q two) p r -> two p r q", two=2)[par], "h0", [], [H0.o])
        ST = [[A.alloc("ST%d_%d" % (d, k), 2 * 32 * 3 * 4) for k in range(2)] for d in range(2)]
        STv = [[r.ap(F32, "p (r q s) -> p r q s", r=2, q=32) for r in ST[d]] for d in range(2)]
        tm1 = [A.alloc("tm1_%d" % d, 768) for d in range(2)]
        tm2 = [A.alloc("tm2_%d" % d, 768) for d in range(2)]
        FIN = [A.alloc("FIN%d" % d, 2 * 32 * 2 * 4) for d in range(2)]
        FINv = [r.ap(F32, "p (r q s) -> p r q s", r=2, q=32) for r in FIN]
        FS = A.alloc("FS", 128 * 4)
        for d in range(2):
            self.memset(ENG[d], ST[d][0].ap(), 0.0, [ST[d][0].o])
            self.memset(ENG[d], S4[d][:, :, :, 0:131:33] if d == 0 else S4[d][:, :, :, 32:131:33], 0.0, [S[d].o]) if False else None

        def cmul_step(d, cur, nxt, sl, ns, Bk, bo):
            e = ENG[d]
            cv, nv = STv[d][cur], STv[d][nxt]
            co, no = ST[d][cur].o, ST[d][nxt].o
            t1 = tm1[d].ap(F32, "p (r q s) -> p r q s", r=2, q=32)[:, :, :, 0:ns]
            t2 = tm2[d].ap(F32, "p (r q s) -> p r q s", r=2, q=32)[:, :, :, 0:ns]
            mo = M64[d][1].o
            self.tt(e, t1, cv[:, :, :, sl], MRE2[d].unsqueeze(3).to_broadcast([128, 2, 32, ns]), ALU.mult, [co, mo], [tm1[d].o])
            self.tt(e, t2[:, 0], cv[:, 1, :, sl], NMIM[d].unsqueeze(2).to_broadcast([128, 32, ns]), ALU.mult, [co, mo], [tm2[d].o])
            self.tt(e, t2[:, 1], cv[:, 0, :, sl], PMIM[d].unsqueeze(2).to_broadcast([128, 32, ns]), ALU.mult, [co, mo, tm2[d].o], [tm2[d].o])
            self.tt(e, t1, t1, t2, ALU.add, [tm1[d].o, tm2[d].o], [tm1[d].o])
            self.tt(e, nv[:, :, :, sl], t1, Bk, ALU.add, [tm1[d].o, bo], [no])

        def bcols(d, k, allseq):
            if d == 0:
                return S4[0][:, :, :, 1 + k:1 + k + 67:33] if allseq else S4[0][:, :, :, 67 + k:68 + k]
            return S4[1][:, :, :, 63 - k:63 - k + 67:33] if allseq else S4[1][:, :, :, 129 - k:130 - k]
        cur = [0, 0]
        for k in range(64):
            for d in range(2):
                allseq = (k < 32) if d == 0 else (k >= 32)
                sl = slice(0, 3) if allseq else slice(2, 3)
                ns = 3 if allseq else 1
                if d == 1 and k == 32:
                    self.memset(ENG[d], STv[d][cur[d]][:, :, :, 0:2], 0.0, [ST[d][cur[d]].o])
                nxt = 1 - cur[d]
                bk_ = bcols(d, k, allseq)
                cmul_step(d, cur[d], nxt, sl, ns, bk_, S[d].o)
                if allseq:
                    self.cp("act", bk_[:, :, :, 0:2], STv[d][nxt][:, :, :, 0:2], [ST[d][nxt].o, S[d].o], [S[d].o])
                    if (d == 0 and k == 31) or (d == 1 and k == 63):
                        self.cp("act", FINv[d], STv[d][nxt][:, :, :, 0:2], [ST[d][nxt].o], [FIN[d].o])
                cur[d] = nxt
        for d in range(2):
            self.cp("act", FS.ap()[:, 64 * d:64 * d + 64].rearrange("p (r q) -> p r q", r=2), STv[d][cur[d]][:, :, :, 2], [ST[d][cur[d]].o], [FS.o])
        osd, orc = Obj("ssend"), Obj("srecv")
        self.dma("sp", self.ssend[j2], FS.ap(), "ssd", [FS.o], [osd])
        sdd, srr = self.ssend[j2], self.srecv[j2]
        P.dma("pool", lambda e: e.collective_compute("AllGather", ALU.bypass, replica_groups=[[0, 1, 2, 3], [4, 5, 6, 7]],
                                                     ins=[sdd.opt()], outs=[srr.opt()]), "cc_s%d" % j2, reads=[osd], writes=[orc], inc=1)
        FG = A.alloc("FG", 4 * 128 * 4)
        FGv = FG.ap(F32, "p (k c) -> p k c", k=4)
        self.dma("sp", FGv, srr.rearrange("(k p) c -> p k c", k=4), "fg", [orc], [FG.o])
        ch = A.alloc("chain", 4 * 64 * 4)
        chv = ch.ap(F32, "p (k r q) -> p k r q", k=4, r=2)
        ct1, ct2 = A.alloc("ct1", 256), A.alloc("ct2", 256)
        c1 = ct1.ap(F32, "p (r q) -> p r q", r=2)
        c2 = ct2.ap(F32, "p (r q) -> p r q", r=2)
        for d in range(2):
            m64, mt = M64[d]
            order = (0, 1, 2, 3) if d == 0 else (3, 2, 1, 0)
            h0v = H0.ap()[:, 64 * d:64 * d + 64].rearrange("p (r q) -> p r q", r=2)
            self.cp("dve", chv[:, order[0]], h0v, [H0.o], [ch.o])
            for a in range(3):
                ks, kd = order[a], order[a + 1]
                src = chv[:, ks]
                fk = FGv[:, ks, 64 * d:64 * d + 64].rearrange("p (r q) -> p r q", r=2)
                self.tt("dve", c1, src, m64[:, 0:1, :].to_broadcast([128, 2, 32]), ALU.mult, [ch.o, mt.o], [ct1.o])
                self.tt("dve", c2[:, 1, :], src[:, 0, :], m64[:, 1, :], ALU.mult, [ch.o, mt.o], [ct2.o])
                self.stt("dve", c2[:, 0, :], src[:, 1, :], -1.0, m64[:, 1, :], ALU.mult, ALU.mult, [ch.o, mt.o, ct2.o], [ct2.o])
                self.tt("dve", c1, c1, c2, ALU.add, [ct1.o, ct2.o], [ct1.o])
                self.tt("dve", chv[:, kd], c1, fk, ALU.add, [ct1.o, FG.o], [ch.o])
            dst = STv[d][cur[d]][:, :, :, 2]
            self.ts("dve", dst, chv[:, 0], selr.ap()[:, 0:1], None, ALU.mult, None, [ch.o, selr.o], [ST[d][cur[d]].o])
            for k in range(1, 4):
                self.stt("dve", dst, chv[:, k], selr.ap()[:, k:k + 1], dst, ALU.mult, ALU.add, [ch.o, selr.o, ST[d][cur[d]].o], [ST[d][cur[d]].o])
            icol = 66 if d == 0 else 130
            self.cp("act", S4[d][:, :, :, icol], dst, [ST[d][cur[d]].o], [S[d].o])
        for d in range(2):
            cols = S4[d][:, :, :, 0:34:33] if d == 0 else S4[d][:, :, :, 32:66:33]
            self.memset("pool", cols, 0.0, [S[d].o])
        for k in range(64):
            for d in range(2):
                nxt = 1 - cur[d]
                bk_ = bcols(d, k, False)
                cmul_step(d, cur[d], nxt, slice(2, 3), 1, bk_, S[d].o)
                self.cp("act", bk_, STv[d][nxt][:, :, :, 2:3], [ST[d][nxt].o, S[d].o], [S[d].o])
                cur[d] = nxt
        A.release(FG, selr, H0, ch, ct1, ct2, FS, *tm1, *tm2, *ST[0], *ST[1])
        if self.cfg.get("ssm_stop", 99) <= 5:
            return
        CT = []
        for d in range(2):
            pair_ = []
            for nm in ("ssm_c_re", "ssm_c_im"):
                cst = A.alloc("cst", 8 * 128 * 4)
                csv = cst.ap(F32, "p (c e) -> p c e", c=8)
                src = di[nm][j2, d].rearrange("(gc gl) h p -> (gl h) gc p", gl=8)
                self.dma("sp", csv[:, :, 0:64], src, "cst", [], [cst.o])
                self.dma("sp", csv[:, :, 64:128], src, "cst", [], [cst.o])
                PT_ = self.PB[0]
                pob = [self.BK[0], self.BK[1]]
                for gc in range(8):
                    self.tr(PT_[:, gc * 128:(gc + 1) * 128], csv[:, gc, :], idf, [cst.o, self.identf.o], [pob[gc // 4]])
                ct = A.alloc("ct_%s%d" % (nm[-2:], d), 2048)
                for par in range(2):
                    rows = slice(64 * par, 64 * par + 64)
                    self.cp("act" if par else "dve", ct.ap(F32, "p (q h) -> p q h", q=32)[rows],
                            PT_[rows, :].rearrange("p (q two h) -> p q two h", two=2, h=16)[:, :, par, :], pob, [ct.o])
                A.release(cst)
                pair_.append(ct)
            CT.append(pair_)
        if self.cfg.get("ssm_stop", 99) <= 5.3:
            return
        WC = [A.alloc("WC%d" % d, 32 * 2 * 128 * 2) for d in range(2)]
        WCv = [w.ap(BF16, "p (q r e) -> p q r e", q=32, r=2) for w in WC]
        g1, g2 = A.alloc("g1", 4096), A.alloc("g2", 4096)
        g3, g4 = A.alloc("g3", 4096), A.alloc("g4", 4096)

        def to_parity(src, name):
            r = A.alloc(name, 8 * 32 * 4)
            sv = src.ap(F32, "p (k q two) -> p k q two", k=8, two=2)
            rv = r.ap(F32, "p (k q) -> p k q", k=8)
            for par in range(2):
                rows = slice(64 * par, 64 * par + 64)
                self.cp("dve", rv[rows], sv[rows, :, :, par], [src.o], [r.o])
            return r
        for d in range(2):
            prs, pis = to_parity(PWC[d][0], "pcs_re"), to_parity(PWC[d][1], "pcs_im")
            pr_ = prs.ap(F32, "p (k q) -> p k q", k=8)
            pi_ = pis.ap(F32, "p (k q) -> p k q", k=8)
            cr_ = CT[d][0].ap(F32, "p (q h) -> p q h", q=32)
            ci_ = CT[d][1].ap(F32, "p (q h) -> p q h", q=32)
            rd = [prs.o, pis.o, CT[d][0].o, CT[d][1].o]
            for pb in range(4):
                qs_ = slice(pb * 8, pb * 8 + 8)
                Cr = cr_[:, qs_, :].unsqueeze(2).to_broadcast([128, 8, 8, 16])
                Ci = ci_[:, qs_, :].unsqueeze(2).to_broadcast([128, 8, 8, 16])
                Pr = pr_[:, :, qs_].rearrange("p j q -> p q j").unsqueeze(3).to_broadcast([128, 8, 8, 16])
                Pi = pi_[:, :, qs_].rearrange("p j q -> p q j").unsqueeze(3).to_broadcast([128, 8, 8, 16])
                a1 = g1.ap(F32, "p (q j h) -> p q j h", q=8, j=8)
                a2 = g2.ap(F32, "p (q j h) -> p q j h", q=8, j=8)
                a3 = g3.ap(F32, "p (q j h) -> p q j h", q=8, j=8)
                a4 = g4.ap(F32, "p (q j h) -> p q j h", q=8, j=8)
                o0 = WCv[d][:, qs_, 0, :].rearrange("p q (j h) -> p q j h", j=8)
                o1 = WCv[d][:, qs_, 1, :].rearrange("p q (j h) -> p q j h", j=8)
                self.tt("pool", a2, Ci, Pi, ALU.mult, rd, [g2.o])
                self.tt("dve", a1, Cr, Pr, ALU.mult, rd, [g1.o])
                self.tt("pool", a4, Ci, Pr, ALU.mult, rd, [g4.o])
                self.tt("dve", a3, Cr, Pi, ALU.mult, rd, [g3.o])
                self.tt("dve", o0, a1, a2, ALU.subtract, [g1.o, g2.o], [WC[d].o])
                self.stt("dve", o1, a3, -1.0, a4, ALU.mult, ALU.subtract, [g3.o, g4.o], [WC[d].o])
            A.release(prs, pis)
        A.release(CT[0][0], CT[0][1], CT[1][0], CT[1][1], PWC[0][0], PWC[0][1], PWC[1][0], PWC[1][1])
        if self.cfg.get("ssm_stop", 99) <= 6:
            return
        T = A.alloc("T", 64 * 128 * 2)
        Tv = T.ap(BF16, "p (g e) -> p g e", g=64)
        msk = A.alloc("msk", 2 * 128 * 4 + 64)
        mi = A.alloc("mski", 128 * 4 + 64)
        miv = mi.ap(I32)[:, 0:128]
        pjv = mi.ap(I32)[:, 128:129]
        self.P.op("pool", lambda e: e.iota(miv, pattern=[[1, 128]], base=0, channel_multiplier=0), writes=[mi.o])
        self.P.op("pool", lambda e: e.iota(pjv, pattern=[[0, 1]], base=0, channel_multiplier=1), reads=[mi.o], writes=[mi.o])
        self.ts("dve", mi.ap(I32)[:, 0:129], mi.ap(I32)[:, 0:129], 4, None, ALU.arith_shift_right, None, [mi.o], [mi.o])
        cjf = msk.ap()[:, 0:128]
        pjf = msk.ap()[:, 256:257]
        self.cp("dve", cjf, miv, [mi.o], [msk.o])
        self.cp("dve", pjf, pjv, [mi.o, msk.o], [msk.o])
        Mf, Mb = msk.ap()[:, 0:128], msk.ap()[:, 128:256]
        self.ts("dve", Mb, cjf, pjf, None, ALU.is_le, None, [msk.o], [msk.o])
        self.ts("dve", Mf, cjf, pjf, None, ALU.is_ge, None, [msk.o], [msk.o])
        dcol = A.alloc("dcol", 256)
        for i in range(8):
            self.dma("sp", dcol.ap()[16 * i:16 * i + 16, 0:64], di["ssm_d"][j2].rearrange("(g h) -> h g", h=16), "dcol", [], [dcol.o])
        sst = self.cfg.get("ssm_stop", 99)
        if sst <= 6.2:
            return
        xb = [A.alloc("xb%d" % d, 4 * 2 * 128 * 2) for d in range(2)]
        xbv = [x_.ap(BF16, "p (q r e) -> p q r e", q=4, r=2) for x_ in xb]
        QXs, BXs = [], []
        for d in range(2):
            QXs.append((to_parity(QXB[d][0], "qxs_re%d" % d), to_parity(QXB[d][1], "qxs_im%d" % d)))
            bs_ = []
            for src in (BR[d], BI[d]):
                r = A.alloc("bxs", 32 * 16 * 4)
                sv = src.ap(F32, "p (q two h) -> p q two h", two=2, h=16)
                rv = r.ap(F32, "p (q h) -> p q h", q=32)
                for par in range(2):
                    rows = slice(64 * par, 64 * par + 64)
                    self.cp("dve", rv[rows], sv[rows, :, par, :], [src.o], [r.o])
                bs_.append(r)
            BXs.append(bs_)
            A.release(QXB[d][0], QXB[d][1], BR[d], BI[d])
        ta, tb_, dfull = A.alloc("ta", 4096), A.alloc("tb", 4096), A.alloc("dfull", 4096)
        for gb in range(8):
            for d in range(2):
                qr = QXs[d][0].ap(F32, "p (k q) -> p k q", k=8)
                qi = QXs[d][1].ap(F32, "p (k q) -> p k q", k=8)
                brv = BXs[d][0].ap(F32, "p (q h) -> p q h", q=32)
                biv = BXs[d][1].ap(F32, "p (q h) -> p q h", q=32)
                rd = [QXs[d][0].o, QXs[d][1].o, BXs[d][0].o, BXs[d][1].o]
                qs_ = slice(gb * 4, gb * 4 + 4)
                Er = qr[:, :, qs_].rearrange("p i q -> p q i").unsqueeze(3).to_broadcast([128, 4, 8, 16])
                Ei = qi[:, :, qs_].rearrange("p i q -> p q i").unsqueeze(3).to_broadcast([128, 4, 8, 16])
                Br_ = brv[:, qs_, :].unsqueeze(2).to_broadcast([128, 4, 8, 16])
                Bi_ = biv[:, qs_, :].unsqueeze(2).to_broadcast([128, 4, 8, 16])
                a1 = g1.ap(F32, "p (q j h) -> p q j h", q=8, j=8)[:, 0:4]
                a2 = g2.ap(F32, "p (q j h) -> p q j h", q=8, j=8)[:, 0:4]
                a3 = g3.ap(F32, "p (q j h) -> p q j h", q=8, j=8)[:, 0:4]
                a4 = g4.ap(F32, "p (q j h) -> p q j h", q=8, j=8)[:, 0:4]
                o0 = xbv[d][:, :, 0, :].rearrange("p q (j h) -> p q j h", j=8)
                o1 = xbv[d][:, :, 1, :].rearrange("p q (j h) -> p q j h", j=8)
                self.tt("pool", a2, Ei, Bi_, ALU.mult, rd, [g2.o])
                self.tt("dve", a1, Er, Br_, ALU.mult, rd, [g1.o])
                self.tt("pool", a4, Ei, Br_, ALU.mult, rd, [g4.o])
                self.tt("dve", a3, Er, Bi_, ALU.mult, rd, [g3.o])
                self.tt("dve", o0, a1, a2, ALU.subtract, [g1.o, g2.o], [xb[d].o])
                self.tt("dve", o1, a3, a4, ALU.add, [g3.o, g4.o], [xb[d].o])
            if sst <= 6.4:
                continue
            for d in range(2):
                PT_ = self.PB[d]
                for gl in range(8):
                    g = gb * 8 + gl
                    par, pl = g % 2, gl // 2
                    rows = slice(64 * par, 64 * par + 64)
                    cidx = par * 4 + pl
                    for r_ in range(2):
                        self.mm(PT_[:, cidx * 128:(cidx + 1) * 128], xbv[d][rows, pl, r_, :], WCv[d][rows, g // 2, r_, :], r_ == 0, r_ == 1,
                                [xb[d].o, WC[d].o], [self.BK[2 * d + par]])
            if sst <= 6.6:
                continue
            t3 = lambda r: r.ap(F32, "p (g e) -> p g e", g=8)
            self.tt("dve", t3(ta), self.PB[0][:, :].rearrange("p (g e) -> p g e", g=8), Mf.unsqueeze(1).to_broadcast([128, 8, 128]), ALU.mult,
                    [self.BK[0], self.BK[1], msk.o], [ta.o])
            self.tt("dve", t3(tb_), self.PB[1][:, :].rearrange("p (g e) -> p g e", g=8), Mb.unsqueeze(1).to_broadcast([128, 8, 128]), ALU.mult,
                    [self.BK[2], self.BK[3], msk.o], [tb_.o])
            self.tt("pool", ta.ap(), ta.ap(), tb_.ap(), ALU.add, [ta.o, tb_.o], [ta.o])
            for par in range(2):
                dsl = dcol.ap()[:, gb * 8 + par:gb * 8 + 8:2]
                dfp = dfull.ap()[:, 0:512].rearrange("p (g e) -> p g e", g=4)
                self.tt("pool", dfp, idf.unsqueeze(1).to_broadcast([128, 4, 128]), dsl.unsqueeze(2).to_broadcast([128, 4, 128]), ALU.mult,
                        [self.identf.o, dcol.o], [dfull.o])
                self.tt("pool", Tv[:, gb * 8 + par:gb * 8 + 8:2, :], t3(ta)[:, par * 4:(par + 1) * 4, :], dfp, ALU.add, [ta.o, dfull.o], [T.o])
        A.release(msk, mi, dcol, ta, tb_, dfull, g1, g2, g3, g4, *xb,
                  QXs[0][0], QXs[0][1], QXs[1][0], QXs[1][1], *BXs[0], *BXs[1])
        if self.cfg.get("ssm_stop", 99) <= 7:
            return
        wg = [A.alloc("wg%d" % k, 8 * 512 * 2) for k in range(2)]

        def load_wg(np_):
            ca, cb_ = slice(512 * np_, 512 * np_ + 512), slice(1024 + 512 * np_, 1536 + 512 * np_)
            self.dma("pool", wg[0].ap(BF16, "p (k n) -> p k n", k=8), di["ssm_w_glu"][j2][:, ca].rearrange("(kc k) n -> k kc n", k=128), "wg0", [], [wg[0].o])
            self.dma("pool", wg[1].ap(BF16, "p (k n) -> p k n", k=8), di["ssm_w_glu"][j2][:, cb_].rearrange("(kc k) n -> k kc n", k=128), "wg1", [], [wg[1].o])
        if not self.small:
            load_wg(0)
        Gm = A.alloc("Gm", 16384)
        Gv = Gm.ap(BF16, "p (j g h) -> p j g h", j=8, g=64)
        gel = [A.alloc("gel%d" % k, 2048) for k in range(2)]
        incol = ((0, 33, 66), (1, 34, 67))
        for gb in range(8):
            pr = gb % 3
            PY = self.PB[pr]
            pob = [self.BK[2 * pr], self.BK[2 * pr + 1]]
            for gl in range(8):
                g = gb * 8 + gl
                par, q_ = g % 2, g // 2
                rows = slice(64 * par, 64 * par + 64)
                o_ = [pob[par]]
                cidx = par * 4 + gl // 2
                self.mm(PY[:, cidx * 128:(cidx + 1) * 128], Tv[:, g, :], VAv[:, g, :], True, False, [T.o, VA.o], o_)
                for d in range(2):
                    for r_ in range(2):
                        for si, (a, b) in enumerate(segs):
                            ic = incol[d][si]
                            self.mm(PY[:, cidx * 128 + a:cidx * 128 + b], WCv[d][rows, q_, r_, :], S4[d][rows, r_, q_, ic:ic + (b - a)], False,
                                    d == 1 and r_ == 1, [WC[d].o, S[d].o], o_)
            ge = gel[gb % 2]
            self.act(ge.ap(BF16), PY[:, :], AF.Gelu_apprx_tanh, pob, [ge.o])
            bk = 6 + gb % 2
            psb = self.bank(bk).bitcast(BF16)
            for gl in range(8):
                self.tr(psb[:, gl * 128:(gl + 1) * 128], ge.ap(BF16)[:, gl * 128:(gl + 1) * 128], idb, [ge.o, self.identb.o], [self.BK[bk]])
            ps4_ = psb.rearrange("p (g j h) -> p j g h", g=8, j=8)
            for par in range(2):
                self.cp("dve" if par else "act", Gv[:, :, gb * 8 + par:gb * 8 + 8:2, :], ps4_[:, :, par * 4:(par + 1) * 4, :], [self.BK[bk]], [Gm.o])
        A.release(T, VA, *S, *WC, *gel, M64[0][1], M64[1][1], small)
        GT = A.alloc("GT", 16384)
        self.transposeT(Gm, GT)
        A.release(Gm)
        GTv = GT.ap(BF16, "p (k t) -> p k t", k=8)
        if self.cfg.get("ssm_stop", 99) <= 8:
            return
        G1 = self.load_mod(2, "modG")
        bg = A.alloc("bg", 2048 * 4)
        bgb = A.alloc("bgb", 2048 * 2)
        self.dma("sp", bg.ap()[0:1, :], di["ssm_b_glu"][j2].partition_broadcast(1), "bg", [], [bg.o])
        self.cp("dve", bgb.ap(BF16)[0:1, :], bg.ap()[0:1, :], [bg.o], [bgb.o])
        sig = [A.alloc("sig%d" % k, 2048) for k in range(2)]
        gt_ = [A.alloc("gtmp%d" % k, 2048) for k in range(2)]
        ones = self.onesb.ap(BF16)
        for np_ in range(2):
            ca, cb_ = slice(512 * np_, 512 * np_ + 512), slice(1024 + 512 * np_, 1536 + 512 * np_)
            wva = wg[0].ap(BF16, "p (k n) -> p k n", k=8)
            wvb = wg[1].ap(BF16, "p (k n) -> p k n", k=8)
            if np_ == 1:
                load_wg(1)
            for i in range(8):
                ba, bb = (2 * i) % 6, (2 * i + 1) % 6
                for (bk, wv, wr, cc) in ((ba, wva, wg[0], ca), (bb, wvb, wg[1], cb_)):
                    self.mm(self.bank(bk), ones[0:1, 0:128], bgb.ap(BF16)[0:1, cc], True, False, [self.onesb.o, bgb.o], [self.BK[bk]])
                    for kc in range(8):
                        self.mm(self.bank(bk), GTv[:, kc, i * 128:(i + 1) * 128], wv[:, kc, :], False, kc == 7, [GT.o, wr.o], [self.BK[bk]])
                sg, gt1 = sig[i % 2], gt_[i % 2]
                self.act(sg.ap(), self.bank(bb), AF.Sigmoid, [self.BK[bb]], [sg.o])
                self.tt("dve", gt1.ap(), self.bank(ba), sg.ap(), ALU.mult, [self.BK[ba], sg.o], [gt1.o])
                self.tt("pool", gt1.ap(), gt1.ap(), G1.ap()[:, ca], ALU.mult, [gt1.o, G1.o], [gt1.o])
                self.tt("pool", X[:, i, ca], X[:, i, ca], gt1.ap(), ALU.add, [self.XO[i], gt1.o], [self.XO[i]])
        A.release(GT, G1, bg, bgb, *wg, *sig, *gt_)
        tcb = [A.alloc("tcomb%d" % k, 1024) for k in range(2)]
        cntf = 0
        for d in range(2):
            for s_ in range(2):
                bk = 6 + cntf % 2
                tc_ = tcb[cntf % 2]
                cntf += 1
                for r_ in range(2):
                    self.tr(self.bank(bk)[0:32, r_ * 128:(r_ + 1) * 128], FINv[d][:, r_, :, s_], idf, [FIN[d].o, self.identf.o], [self.BK[bk]])
                self.cp("dve", tc_.ap()[0:32, 0:256].rearrange("q (n r) -> q r n", r=2),
                        self.bank(bk)[0:32, 0:256].rearrange("q (r n) -> q r n", r=2), [self.BK[bk]], [tc_.o])
                self.dma("sp", self.do["ossm"][s_, j2, d].rearrange("(q two) p r -> q (two p r)", two=2), tc_.ap()[0:32, 0:256], "ossm",
                         [tc_.o], [], is_output=True)
        A.release(*tcb)
        A.release(*FIN)


def _core_inputs(inp, core):
    b, r = core // 4, core % 4
    f = lambda a: np.ascontiguousarray(np.asarray(a, dtype=np.float32))
    d = {}
    d["xp"] = f(inp["x_prompt"][2 * core:2 * core + 2].reshape(512, D))
    d["xs"] = f(inp["x_sample"][b, 512 * r:512 * (r + 1)])
    d["cvec"] = f(np.stack([inp["c_ctx"], inp["c"][b]], 0))
    d["cache_ckv"] = f(inp["cache_mla_ckv"][b])
    d["cache_krope"] = f(inp["cache_mla_krope"][b])
    d["cache_gk"] = f(inp["cache_gqa_k"][b].reshape(2, 256, 128))
    d["cache_gv"] = f(inp["cache_gqa_v"][b].reshape(2, 256, 128))
    d["state_ssm"] = f(inp["state_ssm"][b])
    d["w_mod"] = f(inp["w_mod"][:, :, 1536 * r:1536 * (r + 1)])
    d["b_mod"] = f(inp["b_mod"][:, 1536 * r:1536 * (r + 1)])
    for k in ("norm1_g", "norm2_g", "attn_w_in", "attn_qa_norm_g", "attn_kva_norm_g", "attn_w_uq", "attn_w_ukv",
              "attn_mla_q_norm_g", "attn_mla_k_norm_g", "attn_gqa_q_norm_g", "attn_gqa_k_norm_g", "attn_w_out",
              "ssm_a_re", "ssm_a_im", "ssm_log_dt", "ssm_b_re", "ssm_b_im", "ssm_c_re", "ssm_c_im", "ssm_d", "ssm_w_glu", "ssm_b_glu",
              "ffn_w_up", "ffn_conv_w", "ffn_conv_b", "ffn_w_down"):
        d[k] = f(inp[k])
    qpos = np.zeros((128, 8, 2), np.float32)
    for p in range(64, 128):
        for i in range(8):
            t = 512 * r + 8 * (p - 64) + i
            qpos[p, i, 0] = t // 64
            qpos[p, i, 1] = t % 64
    kpos = np.zeros((128, 16, 2), np.float32)
    for q in range(128):
        for u in range(16):
            t = 128 * u + q
            kpos[q, u, 0] = t // 64
            kpos[q, u, 1] = t % 64
    sel = np.zeros((8, 2), np.float32)
    if r > 0:
        sel[2 * (r - 1) + 1, 0] = 1.0
    if r < 3:
        sel[2 * (r + 1), 1] = 1.0
    selr = np.zeros((128, 8), np.float32)
    selr[:, r] = 1.0
    selr[:, 4 + r] = 1.0
    d["qpos"], d["kpos"], d["selhalo"], d["selr"] = qpos, kpos, sel, selr
    return d


_NC_CACHE = {}


def kernel(cfg=None, **inp):
    inp = {k: np.asarray(v) for k, v in inp.items()}
    key = repr(sorted((cfg or {}).items()))
    if key not in _NC_CACHE:
        _NC_CACHE[key] = Builder(cfg).build()
    nc = _NC_CACHE[key]
    in_maps = [_core_inputs(inp, c) for c in range(NCORES)]
    if (cfg or {}).get("small"):
        for m in in_maps:
            for k in ("w_mod", "ffn_w_up", "ffn_w_down", "ssm_w_glu"):
                m.pop(k)
    res = run_bass_kernel_spmd(nc, in_maps, core_ids=list(range(NCORES)))
    R = res.results
    yp = np.concatenate([R[c]["yp"].reshape(2, 256, D) for c in range(NCORES)], 0)
    ys = np.stack([np.concatenate([R[4 * b + r]["ys"] for r in range(4)], 0) for b in range(2)], 0)
    ockv = np.concatenate([R[c]["ockv"] for c in range(NCORES)], 0)
    okr = np.concatenate([R[c]["okrope"] for c in range(NCORES)], 0)
    ogk = np.concatenate([R[c]["ogk"].reshape(2, 2, 256, 2, 64) for c in range(NCORES)], 0)
    ogv = np.concatenate([R[c]["ogv"].reshape(2, 2, 256, 2, 64) for c in range(NCORES)], 0)
    ossm = np.concatenate([R[c]["ossm"] for c in range(NCORES)], 0)
    return (yp.astype(np.float32), ys.astype(np.float32), ockv.astype(np.float32), okr.astype(np.float32),
            ogk.astype(np.float32), ogv.astype(np.float32), ossm.astype(np.float32))
```

```python
import math
import contextlib
import numpy as np
import concourse.bass as bass
import concourse.mybir as mybir
from concourse.bass_utils import run_bass_kernel_spmd

F32 = mybir.dt.float32
BF16 = mybir.dt.bfloat16
I32 = mybir.dt.int32
AF = mybir.ActivationFunctionType
ALU = mybir.AluOpType
AX = mybir.AxisListType

D = 1024
DFF = 2816
EPS = 1e-6
NCORES = 8


class Obj:
    __slots__ = ("name", "last_w", "readers", "excl")

    def __init__(self, name, excl=False):
        self.name = name
        self.last_w = None
        self.readers = []
        self.excl = excl


class Prog:
    ENGS = ("pe", "act", "dve", "pool", "sp")

    def __init__(self, nc):
        self.nc = nc
        self.q = {e: [] for e in self.ENGS}
        self.cnt = {e: 0 for e in self.ENGS}
        self.dmacnt = {}
        self.waited = {e: {} for e in self.ENGS}
        self.semkeys = []
        self.out_events = []

    def _semkey(self, k):
        if k not in self.semkeys:
            self.semkeys.append(k)
        return k

    def _deps(self, eng, reads, writes, pe_accum=False):
        need = {}

        def add(ev, same_ok=False):
            if ev is None:
                return
            k, v = ev
            if same_ok and k == eng:
                return
            if need.get(k, 0) < v:
                need[k] = v
        for o in reads:
            add(o.last_w)
            if o.excl:
                for r in o.readers:
                    add(r, same_ok=True)
        for o in writes:
            add(o.last_w, same_ok=pe_accum)
            for r in o.readers:
                add(r, same_ok=True)
        waits = []
        for k, v in need.items():
            if self.waited[eng].get(k, 0) < v:
                self.waited[eng][k] = v
                waits.append((k, v))
        return waits

    def _commit(self, ev, reads, writes):
        for o in reads:
            o.readers.append(ev)
            if len(o.readers) > 64:
                best = {}
                for k, v in o.readers:
                    if best.get(k, 0) < v:
                        best[k] = v
                o.readers = list(best.items())
        for o in writes:
            o.last_w = ev
            o.readers = []

    def op(self, eng, fn, reads=(), writes=(), pe_accum=False, same_ok=False):
        reads = [o for o in reads if o is not None]
        writes = [o for o in writes if o is not None]
        waits = self._deps(eng, reads, writes, pe_accum)
        if same_ok:
            waits = [(k, v) for (k, v) in waits if k != eng]
        self.cnt[eng] += 1
        ev = (self._semkey(eng), self.cnt[eng])
        self.q[eng].append((waits, fn, [(eng, 1)]))
        self._commit(ev, reads, writes)
        return ev

    def dma(self, eng, fn, key, reads=(), writes=(), is_output=False, inc=16):
        reads = [o for o in reads if o is not None]
        writes = [o for o in writes if o is not None]
        waits = self._deps(eng, reads, writes)
        sk = self._semkey(("dma", key))
        self.dmacnt[key] = self.dmacnt.get(key, 0) + 1
        ev = (sk, inc * self.dmacnt[key])
        self.q[eng].append((waits, fn, [(sk, inc)]))
        self._commit(ev, reads, writes)
        if is_output:
            self.out_events.append(ev)
        return ev

    def finish(self, eng="sp"):
        need = {}
        for k, v in self.out_events:
            need[k] = max(need.get(k, 0), v)
        for e in ("pe", "act", "dve", "pool"):
            if self.cnt[e]:
                need[e] = self.cnt[e]
        waits = [(k, v) for k, v in need.items() if self.waited[eng].get(k, 0) < v]
        self.q[eng].append((waits, None, []))

    def emit(self):
        nc = self.nc
        comp = ("pe", "act", "dve", "pool")
        miles = {e: set() for e in comp}
        for e in self.ENGS:
            for waits, fn, incs in self.q[e]:
                for k, v in waits:
                    if k in miles:
                        miles[k].add(v)
        rank = {e: {v: i + 1 for i, v in enumerate(sorted(miles[e]))} for e in comp}
        with contextlib.ExitStack() as st:
            sems = {}
            for i, k in enumerate(self.semkeys):
                sems[k] = st.enter_context(nc.semaphore("s%d" % i))
            block = st.enter_context(nc.Block())
            q = self.q

            def run(e, name):
                n = 0
                for waits, fn, incs in q[name]:
                    for k, v in waits:
                        if k in rank:
                            v = rank[k][v]
                        e.wait_ge(sems[k], v)
                    if fn is None:
                        continue
                    ins = fn(e)
                    for k, inc in incs:
                        if k in rank:
                            n += 1
                            if n in rank[k]:
                                ins.then_inc(sems[k], 1)
                        else:
                            ins.then_inc(sems[k], inc)

            @block.tensor
            def _(e):
                run(e, "pe")

            @block.scalar
            def _(e):
                run(e, "act")

            @block.vector
            def _(e):
                run(e, "dve")

            @block.gpsimd
            def _(e):
                run(e, "pool")

            @block.sync
            def _(e):
                run(e, "sp")


class Reg:
    def __init__(self, arena, off, nbytes, name, req=None):
        self.arena = arena
        self.off = off
        self.nbytes = nbytes
        self.req = req or nbytes
        self.o = Obj(name)
        self.name = name

    def ap(self, dt=F32, pat=None, **kw):
        a = self.arena[:, self.off // 4:(self.off + self.req) // 4]
        if dt != F32:
            a = a.bitcast(dt)
        if pat is not None:
            a = a.rearrange(pat, **kw)
        return a


class Arena:
    def __init__(self, arena_ap, nbytes):
        self.arena = arena_ap
        self.free = [(0, nbytes)]
        self.hist = []

    def alloc(self, name, nbytes):
        req = nbytes
        nbytes = (nbytes + 63) // 64 * 64
        cands = [(b - a, idx) for idx, (a, b) in enumerate(self.free) if b - a >= nbytes]
        for _, idx in sorted(cands)[:1]:
            a, b = self.free[idx]
            if b - a >= nbytes:
                self.free[idx] = (a + nbytes, b)
                if self.free[idx][0] == self.free[idx][1]:
                    self.free.pop(idx)
                r = Reg(self.arena, a, nbytes, name, req)
                keep = []
                for (ha, hb, ho) in self.hist:
                    if ha < a + nbytes and hb > a:
                        if ho.last_w is not None:
                            r.o.readers.append(ho.last_w)
                        r.o.readers.extend(ho.readers)
                        if ha < a:
                            keep.append((ha, a, ho))
                        if hb > a + nbytes:
                            keep.append((a + nbytes, hb, ho))
                    else:
                        keep.append((ha, hb, ho))
                self.hist = keep
                return r
        raise RuntimeError("SBUF arena full allocating %s (%d bytes); free=%s" % (name, nbytes, self.free))

    def release(self, *regs):
        for r in regs:
            self.hist.append((r.off, r.off + r.nbytes, r.o))
            self.free.append((r.off, r.off + r.nbytes))
        self.free.sort()
        merged = []
        for a, b in self.free:
            if merged and merged[-1][1] == a:
                merged[-1] = (merged[-1][0], b)
            else:
                merged.append((a, b))
        self.free = merged


IN_SPECS = [
    ("xp", [512, D]), ("xs", [512, D]), ("cvec", [2, D]),
    ("cache_ckv", [2, 256, 256]), ("cache_krope", [2, 256, 32]), ("cache_gk", [2, 256, 128]), ("cache_gv", [2, 256, 128]),
    ("state_ssm", [2, 2, 64, 64, 2]),
    ("norm1_g", [4, D]), ("norm2_g", [4, D]), ("w_mod", [4, D, 1536]), ("b_mod", [4, 1536]),
    ("attn_w_in", [2, D, 1440]), ("attn_qa_norm_g", [2, 384]), ("attn_kva_norm_g", [2, 256]),
    ("attn_w_uq", [2, 384, 768]), ("attn_w_ukv", [2, 256, 1024]),
    ("attn_mla_q_norm_g", [2, 96]), ("attn_mla_k_norm_g", [2, 96]), ("attn_gqa_q_norm_g", [2, 64]), ("attn_gqa_k_norm_g", [2, 64]),
    ("attn_w_out", [2, D, D]),
    ("ssm_a_re", [2, 2, 64, 64]), ("ssm_a_im", [2, 2, 64, 64]), ("ssm_log_dt", [2, 2, 64]),
    ("ssm_b_re", [2, 2, 64, 64, 16]), ("ssm_b_im", [2, 2, 64, 64, 16]),
    ("ssm_c_re", [2, 2, 64, 16, 64]), ("ssm_c_im", [2, 2, 64, 16, 64]),
    ("ssm_d", [2, D]), ("ssm_w_glu", [2, D, 2 * D]), ("ssm_b_glu", [2, 2 * D]),
    ("ffn_w_up", [4, D, 2 * DFF]), ("ffn_conv_w", [4, 3, 2 * DFF]), ("ffn_conv_b", [4, 2 * DFF]), ("ffn_w_down", [4, DFF, D]),
    ("qpos", [128, 8, 2]), ("kpos", [128, 16, 2]), ("selhalo", [8, 2]), ("selr", [128, 8]),
]
OUT_SPECS = [
    ("yp", [512, D]), ("ys", [512, D]),
    ("ockv", [2, 2, 256, 256]), ("okrope", [2, 2, 256, 32]), ("ogk", [2, 2, 256, 128]), ("ogv", [2, 2, 256, 128]),
    ("ossm", [2, 2, 2, 64, 64, 2]),
]


class Builder:
    def __init__(self, cfg=None):
        self.cfg = cfg or {}
        self.nc = bass.Bass("TRN2", target_bir_lowering=False)
        nc = self.nc
        self.small = bool(self.cfg.get("small"))
        big = ("w_mod", "ffn_w_up", "ffn_w_down", "ssm_w_glu")
        self.di = {n: nc.dram_tensor(n, s, F32, kind="ExternalInput").ap() for n, s in IN_SPECS if not (self.small and n in big)}
        self.do = {n: nc.dram_tensor(n, s, F32, kind="ExternalOutput").ap() for n, s in OUT_SPECS}
        self.dobj = {}
        self.modd = nc.dram_tensor("modd", [4, 2, 6 * D], F32, kind="Internal").ap()
        self.msend = nc.dram_tensor("msend", [8, 1536], F32, kind="Internal").ap()
        self.mrecv = nc.dram_tensor("mrecv", [32, 1536], F32, kind="Internal").ap()
        self.dobj["modd"] = [Obj("modd%d" % l) for l in range(4)]
        self.hsend = [nc.dram_tensor("hsend%d" % l, [2, D], BF16, kind="Internal").ap() for l in range(4)]
        self.hrecv = [nc.dram_tensor("hrecv%d" % l, [8, D], BF16, kind="Internal").ap() for l in range(4)]
        self.kvsend = [[nc.dram_tensor("kvsend%d_%d" % (l, hf), [256, 544], F32, kind="Internal").ap() for hf in range(2)] for l in range(2)]
        self.kvrecv = [[nc.dram_tensor("kvrecv%d_%d" % (l, hf), [1024, 544], F32, kind="Internal").ap() for hf in range(2)] for l in range(2)]
        self.ssend = [nc.dram_tensor("ssend%d" % l, [128, 128], F32, kind="Internal").ap() for l in range(2)]
        self.srecv = [nc.dram_tensor("srecv%d" % l, [512, 128], F32, kind="Internal").ap() for l in range(2)]

    def mm(self, out, lhsT, rhs, start, stop, reads, writes, **kw):
        self.P.op("pe", lambda e: e.matmul(out, lhsT=lhsT, rhs=rhs, start=start, stop=stop, **kw),
                  reads=reads, writes=writes, pe_accum=True)

    def tr(self, out, in_, ident, reads, writes):
        self.P.op("pe", lambda e: e.transpose(out, in_, ident), reads=reads, writes=writes, pe_accum=True)

    def act(self, out, in_, func, reads, writes, **kw):
        self.P.op("act", lambda e: e.activation(out=out, in_=in_, func=func, **kw), reads=reads, writes=writes)

    def cp(self, eng, out, in_, reads, writes):
        if eng == "act":
            self.P.op("act", lambda e: e.copy(out=out, in_=in_), reads=reads, writes=writes)
        else:
            self.P.op(eng, lambda e: e.tensor_copy(out=out, in_=in_), reads=reads, writes=writes)

    def tt(self, eng, out, in0, in1, op, reads, writes, same_ok=False):
        self.P.op(eng, lambda e: e.tensor_tensor(out=out, in0=in0, in1=in1, op=op), reads=reads, writes=writes, same_ok=same_ok)

    def stt(self, eng, out, in0, scalar, in1, op0, op1, reads, writes, same_ok=False):
        self.P.op(eng, lambda e: e.scalar_tensor_tensor(out=out, in0=in0, scalar=scalar, in1=in1, op0=op0, op1=op1),
                  reads=reads, writes=writes, same_ok=same_ok)

    def ts(self, eng, out, in0, s1, s2, op0, op1, reads, writes):
        if op1 is None:
            self.P.op(eng, lambda e: e.tensor_scalar(out=out, in0=in0, scalar1=s1, scalar2=None, op0=op0), reads=reads, writes=writes)
        else:
            self.P.op(eng, lambda e: e.tensor_scalar(out=out, in0=in0, scalar1=s1, scalar2=s2, op0=op0, op1=op1), reads=reads, writes=writes)

    def recip(self, out, in_, reads, writes):
        self.P.op("dve", lambda e: e.reciprocal(out=out, in_=in_), reads=reads, writes=writes)

    def memset(self, eng, out, val, writes):
        self.P.op(eng, lambda e: e.memset(out, val), writes=writes)

    def dma(self, eng, out, in_, key, reads, writes, is_output=False):
        self.P.dma(eng, lambda e: e.dma_start(out=out, in_=in_), key, reads=reads, writes=writes, is_output=is_output)

    def bank(self, k):
        if k < 6:
            return self.PB[k // 2][:, (k % 2) * 512:(k % 2 + 1) * 512]
        return self.PS[k - 6][:, :]

    def build(self):
        nc = self.nc
        with contextlib.ExitStack() as st:
            AW = 53000
            arena_t = st.enter_context(nc.sbuf_tensor("arena", [128, AW], F32))
            self.PB = [st.enter_context(nc.psum_tensor("pb%d" % i, [128, 1024], F32)) for i in range(3)]
            self.PS = [st.enter_context(nc.psum_tensor("ps%d" % i, [128, 512], F32)) for i in range(2)]
            self.BK = [Obj("bank%d" % i, excl=True) for i in range(8)]
            self.A = Arena(arena_t, AW * 4)
            self.P = Prog(nc)
            self.consts()
            self.load_x()
            if not self.small:
                self.preamble()
            for l in range(4):
                if l >= self.cfg.get("nlayers", 4):
                    break
                self.layer_mod(l)
                if self.cfg.get("mixers", True):
                    if l % 2 == 0:
                        if not self.cfg.get("only_ssm"):
                            self.attention(l)
                    else:
                        self.ssm(l)
                if self.cfg.get("ffn", True):
                    self.ffn(l)
            self.store_x()
            self.P.finish()
            with nc.allow_non_contiguous_dma("small strided parameter loads"):
                self.P.emit()
        return nc

    def consts(self):
        A = self.A
        self.identf = A.alloc("identf", 512)
        self.identb = A.alloc("identb", 256)
        self.onesb = A.alloc("onesb", 256)
        self.epsc = A.alloc("epsc", 64)
        idf = self.identf.ap()
        self.memset("pool", idf, 0.0, [self.identf.o])
        self.P.op("pool", lambda e: e.affine_select(out=idf, in_=idf, pattern=[[-1, 128]], compare_op=ALU.not_equal,
                                                    fill=1.0, base=0, channel_multiplier=1),
                  reads=[self.identf.o], writes=[self.identf.o])
        self.cp("dve", self.identb.ap(BF16), idf, [self.identf.o], [self.identb.o])
        self.memset("dve", self.onesb.ap(BF16), 1.0, [self.onesb.o])
        self.onesf = A.alloc("onesf", 256)
        self.memset("dve", self.onesf.ap(), 1.0, [self.onesf.o])
        self.memset("dve", self.epsc.ap()[:, 0:1], EPS, [self.epsc.o])
        self.memset("dve", self.epsc.ap()[:, 1:2], -math.pi, [self.epsc.o])

    def load_x(self):
        self.X = self.A.alloc("X", 8 * D * 4)
        X = self.X.ap(F32, "p (i d) -> p i d", i=8)
        self.XO = [Obj("X%d" % i) for i in range(8)]
        self.dma("sp", X[0:64], self.di["xp"].rearrange("(q i) d -> q i d", i=8), "x0", [], self.XO)
        self.dma("sp", X[64:128], self.di["xs"].rearrange("(q i) d -> q i d", i=8), "x1", [], self.XO)

    def store_x(self):
        X = self.X.ap(F32, "p (i d) -> p i d", i=8)
        self.dma("sp", self.do["yp"].rearrange("(q i) d -> q i d", i=8), X[0:64], "y0", self.XO, [], is_output=True)
        self.dma("sp", self.do["ys"].rearrange("(q i) d -> q i d", i=8), X[64:128], "y1", self.XO, [], is_output=True)

    def preamble(self):
        A, P = self.A, self.P
        cnd = A.alloc("cnd", 64)
        cndb = A.alloc("cndb", 64)
        c3 = cnd.ap(F32, "p (k c) -> p k c", c=2)
        for r in range(2):
            self.P.dma("sp", (lambda r: lambda e: e.dma_start(out=c3[:, :, r], in_=self.di["cvec"][r].rearrange("(kc k) -> k kc", k=128)))(r),
                       "cnd", writes=[cnd.o])
        self.act(cnd.ap()[:, 0:16], cnd.ap()[:, 0:16], AF.Silu, [cnd.o], [cnd.o])
        cb3 = cndb.ap(BF16)[:, 0:16].rearrange("p (k c) -> p k c", c=2)
        self.cp("dve", cndb.ap(BF16)[:, 0:16], cnd.ap()[:, 0:16], [cnd.o], [cndb.o])
        NS = 1536
        bm = A.alloc("bm", 4 * NS * 4)
        mp = A.alloc("mp", 4 * NS * 4)
        ws = [A.alloc("wmod%d" % i, 8 * 512 * 2) for i in range(3)]
        self.dma("sp", bm.ap()[0:2, :], self.di["b_mod"].rearrange("l n -> (l n)").partition_broadcast(2), "bm", [], [bm.o])
        cnt = 0
        for l in range(4):
            for nt in range(3):
                w = ws[cnt % 3]
                wv = w.ap(BF16, "p (k n) -> p k n", k=8)
                self.dma("pool", wv, self.di["w_mod"][l][:, nt * 512:(nt + 1) * 512].rearrange("(kc k) n -> k kc n", k=128),
                         "wmod%d" % (cnt % 3), [], [w.o])
                bk = cnt % 2
                for kc in range(8):
                    self.mm(self.bank(bk)[0:2, :], cb3[:, kc, :], wv[:, kc, :], kc == 0, kc == 7, [cndb.o, w.o], [self.BK[bk]])
                c0 = l * NS + nt * 512
                self.tt("dve", mp.ap()[0:2, c0:c0 + 512], self.bank(bk)[0:2, :], bm.ap()[0:2, c0:c0 + 512], ALU.add, [self.BK[bk], bm.o], [mp.o])
                cnt += 1
        osd, orc = Obj("msend"), Obj("mrecv")
        self.dma("sp", self.msend.rearrange("(c l) n -> c (l n)", c=2), mp.ap()[0:2, :], "msd", [mp.o], [osd])
        msd, mrc = self.msend, self.mrecv
        P.dma("pool", lambda e: e.collective_compute("AllGather", ALU.bypass, replica_groups=[[0, 1, 2, 3], [4, 5, 6, 7]], ins=[msd.opt()], outs=[mrc.opt()]),
              "cc_mod", reads=[osd], writes=[orc], inc=1)
        for r in range(4):
            self.P.dma("sp", (lambda r: lambda e: e.dma_start(out=self.modd[:, :, r * NS:(r + 1) * NS].rearrange("l c j -> c l j"),
                                                                in_=mrc[r * 8:(r + 1) * 8, :].rearrange("(c l) j -> c l j", c=2)))(r),
                       "m3", reads=[orc], writes=self.dobj["modd"])
        A.release(cnd, cndb, bm, mp, *ws)

    def layer_mod(self, l):
        self.cur_l = l

    def load_mod(self, idx, name):
        l = self.cur_l
        r = self.A.alloc(name, D * 4)
        M = r.ap()
        self.dma("sp", M[0:64, :], self.modd[l, 0, idx * D:(idx + 1) * D].partition_broadcast(64), "mod" + name, [self.dobj["modd"][l]], [r.o])
        self.dma("sp", M[64:128, :], self.modd[l, 1, idx * D:(idx + 1) * D].partition_broadcast(64), "mod" + name, [self.dobj["modd"][l]], [r.o])
        return r

    def load_ab(self, which):
        l = self.cur_l
        base = 0 if which == 1 else 3
        Bt = self.load_mod(base, "modB")
        At = self.load_mod(base + 1, "modA")
        gb = self.A.alloc("gb", D * 4)
        self.dma("sp", gb.ap(), self.di["norm1_g" if which == 1 else "norm2_g"][l].partition_broadcast(128), "gb", [], [gb.o])
        self.stt("dve", At.ap(), At.ap(), 1.0, gb.ap(), ALU.add, ALU.mult, [At.o, gb.o], [At.o])
        self.A.release(gb)
        return At, Bt

    def norm_mod(self, which, dest, dobj, view=None):
        A = self.A
        At, Bt = self.load_ab(which)
        Aap, Bap = At.ap(), Bt.ap()
        X = self.X.ap(F32, "p (i d) -> p i d", i=8)
        st = A.alloc("nm_st", 64)
        junk = A.alloc("nm_junk", D * 4)
        tmp = [A.alloc("nm_tmp%d" % i, D * 4) for i in range(2)]
        ss = st.ap()[:, 0:8]
        rs = st.ap()[:, 8:16]
        self.memset("dve", ss, 0.0, [st.o])
        for i in range(8):
            self.act(junk.ap(), X[:, i, :], AF.Square, [self.XO[i], st.o], [junk.o, st.o], accum_out=ss[:, i:i + 1])
        self.act(rs, ss, AF.Sqrt, [st.o, self.epsc.o], [st.o], bias=self.epsc.ap()[:, 0:1], scale=1.0 / D)
        self.P.op("dve", lambda e: e.reciprocal(out=rs, in_=rs), reads=[st.o], writes=[st.o])
        for i in range(8):
            t = tmp[i % 2]
            self.stt("dve", t.ap(), X[:, i, :], rs[:, i:i + 1], Aap, ALU.mult, ALU.mult, [self.XO[i], st.o, At.o], [t.o])
            if view is None:
                self.tt("pool", dest(i), t.ap(), Bap, ALU.add, [t.o, Bt.o], [dobj])
            else:
                self.tt("pool", dest(i), view(t.ap()), view(Bap), ALU.add, [t.o, Bt.o], [dobj])
        A.release(st, junk, At, Bt, *tmp)

    def transposeT(self, H, HT):
        Hv = H.ap(BF16, "p (i d) -> p i d", i=8)
        HTv = HT.ap(BF16, "p (k t) -> p k t", k=8)
        idb = self.identb.ap(BF16)
        for kc in range(8):
            bk = 6 + kc % 2
            psb = self.bank(bk).bitcast(BF16)
            for i in range(8):
                self.tr(psb[:, i * 128:(i + 1) * 128], Hv[:, i, kc * 128:(kc + 1) * 128], idb, [H.o, self.identb.o], [self.BK[bk]])
            self.cp("act" if kc % 2 else "dve", HTv[:, kc, :], psb, [self.BK[bk]], [HT.o])

    def ffn(self, l):
        A = self.A
        P = self.P
        X = self.X.ap(F32, "p (i d) -> p i d", i=8)
        WA = [A.alloc("wua%d" % i, 4096) for i in range(3)]
        WB = [A.alloc("wub%d" % i, 4096) for i in range(3)]
        wup = self.di["ffn_w_up"][l]

        def load_wu(mq):
            sq = mq % 3
            self.dma("pool", WA[sq].ap(BF16, "p (k n) -> p k n", k=8), wup[:, mq * 256:(mq + 1) * 256].rearrange("(kc k) n -> k kc n", k=128),
                     "wua%d" % sq, [], [WA[sq].o])
            self.dma("pool", WB[sq].ap(BF16, "p (k n) -> p k n", k=8), wup[:, DFF + mq * 256:DFF + (mq + 1) * 256].rearrange("(kc k) n -> k kc n", k=128),
                     "wub%d" % sq, [], [WB[sq].o])
        load_wu(0)
        load_wu(1)
        NWD, NPRE = 6, 4
        WD = [A.alloc("wd%d" % i, 2048) for i in range(NWD)]
        wdn = self.di["ffn_w_down"][l]

        def load_wd(q):
            nh_, mg_ = q // 11, q % 11
            sq = q % NWD
            self.dma("pool", WD[sq].ap(BF16, "p (a n) -> p a n", a=2),
                     wdn[mg_ * 256:(mg_ + 1) * 256, nh_ * 512:(nh_ + 1) * 512].rearrange("(a f) n -> f a n", f=128), "wd%d" % sq, [], [WD[sq].o])
        H = A.alloc("H2", 16384)
        Hv = H.ap(BF16, "p (i d) -> p i d", i=8)
        self.norm_mod(2, lambda i: Hv[:, i, :], H.o)
        hs, hr_d = self.hsend[l], self.hrecv[l]
        ohs, ohr = Obj("hsend"), Obj("hrecv")
        self.dma("sp", hs[0:1, :], Hv[64:65, 0, :], "hs", [H.o], [ohs])
        self.dma("sp", hs[1:2, :], Hv[127:128, 7, :], "hs", [H.o], [ohs])
        P.dma("pool", lambda e: e.collective_compute("AllGather", ALU.bypass, replica_groups=[[0, 1, 2, 3], [4, 5, 6, 7]],
                                                     ins=[hs.opt()], outs=[hr_d.opt()]),
              "cc_h%d" % l, reads=[ohs], writes=[ohr], inc=1)
        HT = A.alloc("HT2", 16384)
        self.transposeT(H, HT)
        A.release(H)
        HTv = HT.ap(BF16, "p (k t) -> p k t", k=8)
        hr = A.alloc("hr", 2048)
        sel = A.alloc("sel", 64)
        hth = A.alloc("hth", 64)
        self.dma("sp", hr.ap(BF16)[0:8, :], hr_d, "hr", [ohr], [hr.o])
        self.dma("sp", sel.ap()[0:8, 0:2], self.di["selhalo"], "sel", [], [sel.o])
        selb = sel.ap(BF16)[:, 8:16]
        self.cp("dve", selb[0:8, 0:2], sel.ap()[0:8, 0:2], [sel.o], [sel.o])
        for kc in range(8):
            self.mm(self.bank(7)[:, kc * 2:kc * 2 + 2], hr.ap(BF16)[0:8, kc * 128:(kc + 1) * 128], selb[0:8, 0:2], True, True,
                    [hr.o, sel.o], [self.BK[7]])
        hthv = hth.ap(BF16)[:, 0:16].rearrange("p (k c) -> p k c", c=2)
        self.cp("dve", hth.ap(BF16)[:, 0:16], self.bank(7)[:, 0:16], [self.BK[7]], [hth.o])
        cw = A.alloc("cw", 44 * 3 * 4)
        cb = A.alloc("cb", 44 * 4)
        cwv = cw.ap(F32, "p (m t) -> p m t", t=3)
        cbv = cb.ap()
        for t in range(3):
            self.dma("sp", cwv[:, :, t], self.di["ffn_conv_w"][l, t].rearrange("(m f) -> f m", f=128), "cw", [], [cw.o])
        self.dma("sp", cbv[:, 0:44], self.di["ffn_conv_b"][l].rearrange("(m f) -> f m", f=128), "cb", [], [cb.o])
        AT = A.alloc("AT", 22 * 1024 * 2)
        ATv = AT.ap(BF16, "p (m t) -> p m t", m=22)
        zc = {z: [A.alloc("zc%s%d" % (z, i), 4096) for i in range(2)] for z in "ab"}
        sA = [A.alloc("sA%d" % i, 4096) for i in range(2)]
        pairc = 0
        hcnt = 0
        for mp in range(11):
            sl = mp % 3
            wav = WA[sl].ap(BF16, "p (k n) -> p k n", k=8)
            wbv = WB[sl].ap(BF16, "p (k n) -> p k n", k=8)
            if mp + 2 < 11:
                load_wu(mp + 2)
            elif mp == 9:
                for q in range(NPRE):
                    load_wd(q)
            for ml in range(2):
                m = mp * 2 + ml
                par = m % 2
                for z, wv, wreg, cidx in (("a", wav, WA[sl], m), ("b", wbv, WB[sl], 22 + m)):
                    pr = pairc % 3
                    pairc += 1
                    Z = self.PB[pr]
                    zobjs = [self.BK[2 * pr], self.BK[2 * pr + 1]]
                    for nt in range(2):
                        for kc in range(8):
                            self.mm(Z[:, nt * 512:(nt + 1) * 512], wv[:, kc, ml * 128:(ml + 1) * 128], HTv[:, kc, nt * 512:(nt + 1) * 512],
                                    kc == 0, kc == 7, [wreg.o, HT.o], [zobjs[nt]])
                    hc = 32 + 2 * (hcnt % 16)
                    hcnt += 1
                    Zh = self.bank(7)[:, hc:hc + 2]
                    for kc in range(8):
                        self.mm(Zh, wv[:, kc, ml * 128:(ml + 1) * 128], hthv[:, kc, :], kc == 0, kc == 7, [wreg.o, hth.o], [self.BK[7]])
                    zr = zc[z][par]
                    zv = zr.ap()
                    w0, w1, w2 = cwv[:, cidx, 0:1], cwv[:, cidx, 1:2], cwv[:, cidx, 2:3]
                    self.act(zv, Z[:, :], AF.Identity, zobjs + [cw.o, cb.o], [zr.o], scale=w1, bias=cbv[:, cidx:cidx + 1])
                    rd = zobjs + [cw.o, zr.o]

                    def tap(dst, src, w, extra=()):
                        self.stt("dve", dst, src, w, dst, ALU.mult, ALU.add, rd + list(extra), [zr.o], same_ok=True)
                    tap(zv[:, 128:1024], Z[:, 0:896], w0)
                    for (a, b) in ((1, 32), (33, 64), (65, 128)):
                        tap(zv[:, a:b], Z[:, 896 + a - 1:896 + b - 1], w0)
                    tap(zv[:, 64:65], Zh[:, 0:1], w0, [self.BK[7]])
                    tap(zv[:, 0:896], Z[:, 128:1024], w2)
                    for (a, b) in ((0, 31), (32, 63), (64, 127)):
                        tap(zv[:, 896 + a:896 + b], Z[:, a + 1:b + 1], w2)
                    tap(zv[:, 1023:1024], Zh[:, 1:2], w2, [self.BK[7]])
                self.act(sA[par].ap(), zc["a"][par].ap(), AF.Silu, [zc["a"][par].o], [sA[par].o])
                self.tt("pool", ATv[:, m, :], sA[par].ap(), zc["b"][par].ap(), ALU.mult, [sA[par].o, zc["b"][par].o], [AT.o])
        A.release(HT, hr, sel, hth, cw, cb, *WA, *WB, *zc["a"], *zc["b"], *sA)
        G2 = self.load_mod(5, "modG")
        tmp = [A.alloc("dtmp%d" % i, 2048) for i in range(2)]
        cnt = 0
        for nh in range(2):
            for mg in range(11):
                sl = cnt % NWD
                if cnt + NPRE < 22:
                    load_wd(cnt + NPRE)
                cnt += 1
                wv = WD[sl].ap(BF16, "p (a n) -> p a n", a=2)
                for ml in range(2):
                    m = mg * 2 + ml
                    for i in range(8):
                        self.mm(self.bank(i), ATv[:, m, i * 128:(i + 1) * 128], wv[:, ml, :], m == 0, m == 21, [AT.o, WD[sl].o], [self.BK[i]])
            for i in range(8):
                t = tmp[i % 2]
                cols = slice(nh * 512, (nh + 1) * 512)
                self.tt("dve", t.ap(), self.bank(i), G2.ap()[:, cols], ALU.mult, [self.BK[i], G2.o], [t.o])
                self.tt("pool", X[:, i, cols], X[:, i, cols], t.ap(), ALU.add, [self.XO[i], t.o], [self.XO[i]])
        A.release(AT, G2, *WD, *tmp)


    def gen_rope(self, pos_reg, n, R, name):
        A = self.A
        nf = R // 4
        m = n * 2 * nf
        invf = A.alloc("invf", nf * 4)
        ii = A.alloc("iota", nf * 4)
        iiv = ii.ap(I32)
        self.P.op("pool", lambda e: e.iota(iiv, pattern=[[1, nf]], base=0, channel_multiplier=0), writes=[ii.o])
        self.cp("dve", invf.ap(), iiv, [ii.o], [invf.o])
        self.act(invf.ap(), invf.ap(), AF.Exp, [invf.o], [invf.o], scale=-math.log(10000.0) / nf)
        ang = A.alloc("ang", m * 4)
        angv = ang.ap(F32, "p (a f) -> p a f", f=nf)
        posv = pos_reg.ap()[:, 0:2 * n]
        self.tt("dve", angv, posv.unsqueeze(2).to_broadcast([128, 2 * n, nf]),
                invf.ap().unsqueeze(1).to_broadcast([128, 2 * n, nf]), ALU.mult, [pos_reg.o, invf.o], [ang.o])
        COS = A.alloc(name + "cos", n * R * 4)
        SIN = A.alloc(name + "sin", n * R * 4)
        t = A.alloc("rt", m * 4)
        ti = A.alloc("rti", m * 4)
        sc = A.alloc("rsc", m * 4)
        for kind, shift, dst in (("sin", 0.5, SIN), ("cos", 0.75, COS)):
            self.ts("dve", t.ap(), ang.ap(), 1.0 / (2 * math.pi), shift, ALU.mult, ALU.add, [ang.o], [t.o])
            self.cp("dve", ti.ap(I32), t.ap(), [t.o], [ti.o])
            self.cp("dve", sc.ap(), ti.ap(I32), [ti.o], [sc.o])
            self.tt("dve", t.ap(), t.ap(), sc.ap(), ALU.subtract, [t.o, sc.o], [t.o])
            self.stt("dve", t.ap(), t.ap(), 0.0, t.ap(), ALU.is_lt, ALU.add, [t.o], [t.o])
            self.act(sc.ap(), t.ap(), AF.Sin, [t.o, self.epsc.o], [sc.o], scale=2 * math.pi, bias=self.epsc.ap()[:, 1:2])
            sv = sc.ap(F32, "p (n c f) -> p n c f", c=2, f=nf)
            dv = dst.ap(F32, "p (n c h f) -> p n c h f", c=2, h=2, f=nf)
            for hf in range(2):
                if kind == "sin" and hf == 0:
                    self.ts("dve", dv[:, :, :, hf, :], sv, -1.0, None, ALU.mult, None, [sc.o], [dst.o])
                else:
                    self.cp("dve", dv[:, :, :, hf, :], sv, [sc.o], [dst.o])
        A.release(invf, ii, ang, t, ti, sc)
        return COS, SIN

    def rope(self, x, H, R, cosap, sinap, cso, t1r, t2r, xo):
        nf = R // 4
        t1 = t1r.ap()[:, 0:H * R].rearrange("p (h r) -> p h r", h=H)
        t2 = t2r.ap()[:, 0:H * R].rearrange("p (h r) -> p h r", h=H)
        cb = cosap.unsqueeze(1).to_broadcast([128, H, R])
        self.tt("pool", t1, x, cb, ALU.mult, [xo] + cso, [t1r.o])
        x5 = x.rearrange("p h (c g f) -> p h c g f", c=2, g=2, f=nf)
        t5 = t2.rearrange("p h (c g f) -> p h c g f", c=2, g=2, f=nf)
        s4 = sinap.rearrange("p (c g f) -> p c g f", c=2, g=2, f=nf)
        for hf in range(2):
            sb = s4[:, :, hf, :].unsqueeze(1).to_broadcast([128, H, 2, nf])
            self.tt("dve", t5[:, :, :, hf, :], x5[:, :, :, 1 - hf, :], sb, ALU.mult, [xo] + cso, [t2r.o], )
        self.tt("dve", x, t1, t2, ALU.add, [t1r.o, t2r.o], [xo])

    def head_norm(self, src, H, d, gain, out, reads, oobj, tmpr, str_, gobj):
        sq = tmpr.ap()[:, 0:H * d].rearrange("p (h d) -> p h d", h=H)
        self.tt("pool", sq, src, src, ALU.mult, reads, [tmpr.o])
        ss = str_.ap()[:, 0:H]
        self.P.op("dve", lambda e: e.tensor_reduce(out=ss, in_=sq, axis=AX.X, op=ALU.add), reads=[tmpr.o], writes=[str_.o])
        self.act(ss, ss, AF.Sqrt, [str_.o, self.epsc.o], [str_.o], bias=self.epsc.ap()[:, 0:1], scale=1.0 / d)
        self.P.op("dve", lambda e: e.reciprocal(out=ss, in_=ss), reads=[str_.o], writes=[str_.o])
        self.tt("dve", out, src, ss.unsqueeze(2).to_broadcast([128, H, d]), ALU.mult, reads + [str_.o], [oobj])
        self.tt("pool", out, out, gain.unsqueeze(1).to_broadcast([128, H, d]), ALU.mult, [oobj, gobj], [oobj])

    def kv_tile(self, S, So, rope, dKT, dV, dKTg, dVg, dobjs, W, pair, tcols=128, tb=0):
        idb = self.identb.ap(BF16)
        sc = W["sc"]
        cb = sc["ckvb"]
        self.cp("act", cb.ap(BF16)[:, 0:256], S[:, 0:256], [So], [cb.o])
        yield
        b6 = self.bank(6).bitcast(BF16)
        for kk in range(2):
            self.tr(b6[:, 384 + kk * 128:384 + (kk + 1) * 128], cb.ap(BF16)[:, kk * 128:(kk + 1) * 128], idb, [cb.o, self.identb.o], [self.BK[6]])
        ct = sc["ckvT"]
        self.cp("dve", ct.ap(BF16)[:, 0:256], b6[:, 384:640], [self.BK[6]], [ct.o])
        yield
        wk = W["wukv"]
        wkv = wk.ap(BF16, "p (k n) -> p k n", k=2)
        KV = self.PB[pair]
        kobjs = [self.BK[2 * pair], self.BK[2 * pair + 1]]
        for nt in range(2):
            for kk in range(2):
                self.mm(KV[:, nt * 512:(nt + 1) * 512], ct.ap(BF16)[:, kk * 128:(kk + 1) * 128], wkv[:, kk, nt * 512:(nt + 1) * 512],
                        kk == 0, kk == 1, [ct.o, wk.o], [kobjs[nt]])
        kv3 = KV[:, :].rearrange("p (h e) -> p h e", h=8)
        kc = sc["kcat"]
        kcv = kc.ap()[:, 0:768].rearrange("p (h e) -> p h e", h=8)
        self.cp("act", kcv[:, :, 0:64], kv3[:, :, 0:64], kobjs, [kc.o])
        yield
        self.cp("pool", kcv[:, :, 64:96], S[:, 256:288].unsqueeze(1).to_broadcast([128, 8, 32]), [So, kc.o], [kc.o])
        yield
        self.cp("dve", dV[:, :, 0:64], kv3[:, :, 64:128], kobjs, [dobjs["V"]])
        yield
        self.head_norm(kcv, 8, 96, W["gn"].ap()[:, 736:832], kcv, [kc.o], kc.o, sc["t1"], sc["st"], W["gn"].o)
        yield
        gk = sc["gk"]
        gkv = gk.ap()[:, 0:128].rearrange("p (h e) -> p h e", h=2)
        self.cp("pool", gk.ap()[:, 0:128], S[:, 288:416], [So], [gk.o])
        yield
        if rope is not None:
            c32, s32, c64, s64, ro = rope
            self.rope(kcv[:, :, 64:96], 8, 32, c32, s32, ro, sc["t1"], sc["t2"], kc.o)
            self.rope(gkv, 2, 64, c64, s64, ro, sc["t1"], sc["t2"], gk.o)
        kb = sc["kb"]
        kbv = kb.ap(BF16)[:, 0:768].rearrange("p (h e) -> p h e", h=8)
        self.cp("act", kbv, kcv, [kc.o], [kb.o])
        yield
        b7 = self.bank(7).bitcast(BF16)
        for h in range(8):
            self.tr(b7[0:96, h * 128:(h + 1) * 128], kbv[:, h, :], idb, [kb.o, self.identb.o], [self.BK[7]])
        self.cp("dve", dKT, b7[0:96, :].rearrange("p (h t) -> p h t", h=8)[:, :, 0:tcols], [self.BK[7]], [dobjs["KT"]])
        yield
        gb = sc["gkb"]
        gbv = gb.ap(BF16)[:, 0:128].rearrange("p (h e) -> p h e", h=2)
        self.cp("act", gbv, gkv, [gk.o], [gb.o])
        yield
        for h in range(2):
            self.tr(b6[0:64, 640 + h * 128:640 + (h + 1) * 128], gbv[:, h, :], idb, [gb.o, self.identb.o], [self.BK[6]])
        self.cp("dve", dKTg, b6[0:64, 640:896].rearrange("p (h t) -> p h t", h=2)[:, :, 0:tcols], [self.BK[6]], [dobjs["KTg"]])
        yield
        self.cp("pool", dVg[:, :, 0:64], S[:, 416:544].rearrange("p (h e) -> p h e", h=2), [So], [dobjs["Vg"]])
        yield

    def attention(self, l):
        jj = l // 2
        A, P, di = self.A, self.P, self.di
        X = self.X.ap(F32, "p (i d) -> p i d", i=8)
        idb = self.identb.ap(BF16)
        gn = A.alloc("gn", 960 * 4)
        for nm, a, b in (("attn_qa_norm_g", 0, 384), ("attn_kva_norm_g", 384, 640), ("attn_mla_q_norm_g", 640, 736),
                         ("attn_mla_k_norm_g", 736, 832), ("attn_gqa_q_norm_g", 832, 896), ("attn_gqa_k_norm_g", 896, 960)):
            self.dma("sp", gn.ap()[:, a:b], di[nm][jj].partition_broadcast(128), "gn", [], [gn.o])
        G = gn.ap()
        qp = A.alloc("qpos", 64)
        self.dma("sp", qp.ap()[:, 0:16], di["qpos"].rearrange("p i c -> p (i c)"), "qpos", [], [qp.o])
        C32, S32 = self.gen_rope(qp, 8, 32, "o32")
        C64, S64 = self.gen_rope(qp, 8, 64, "o64")
        A.release(qp)
        c32v, s32v = C32.ap(F32, "p (n r) -> p n r", n=8), S32.ap(F32, "p (n r) -> p n r", n=8)
        c64v, s64v = C64.ap(F32, "p (n r) -> p n r", n=8), S64.ap(F32, "p (n r) -> p n r", n=8)
        ropeobjs = [C32.o, S32.o, C64.o, S64.o]
        OUT = A.alloc("outst", 8 * 544 * 4)
        OUTv = OUT.ap(F32, "p (i e) -> p i e", i=8)
        QTm = [A.alloc("qtm%d" % k, 8 * 512 * 2) for k in range(2)]
        QTg = [A.alloc("qtg%d" % k, 8 * 512 * 2) for k in range(2)]
        KTo = A.alloc("kto", 8 * 8 * 64 * 2)
        Vo = A.alloc("vo", 8 * 8 * 65 * 2)
        KTgo = A.alloc("ktgo", 2 * 8 * 64 * 2)
        Vgo = A.alloc("vgo", 8 * 2 * 65 * 2)
        QTmv = [q.ap(BF16, "p (h i c) -> p h i c", h=8, i=8) for q in QTm]
        QTgv = [q.ap(BF16, "p (h i c) -> p h i c", h=8, i=8) for q in QTg]
        KTov = KTo.ap(BF16, "p (h i c) -> p h i c", h=8, i=8)
        Vov = Vo.ap(BF16, "p (i h e) -> p i h e", i=8, h=8)
        KTgov = KTgo.ap(BF16, "p (h i c) -> p h i c", h=2, i=8)
        Vgov = Vgo.ap(BF16, "p (i h e) -> p i h e", i=8, h=2)
        self.memset("pool", Vo.ap(BF16), 1.0, [Vo.o])
        self.memset("pool", Vgo.ap(BF16), 1.0, [Vgo.o])
        if self.cfg.get("attn_stop", 9) <= 1:
            return
        win = A.alloc("win", 8 * 1440 * 2)
        winv = win.ap(BF16, "p (k n) -> p k n", k=8)
        self.dma("pool", winv[:, :, 0:720], di["attn_w_in"][jj][:, 0:720].rearrange("(kc k) n -> k kc n", k=128), "win", [], [win.o])
        self.dma("pool", winv[:, :, 720:1440], di["attn_w_in"][jj][:, 720:1440].rearrange("(kc k) n -> k kc n", k=128), "win", [], [win.o])
        wuq = A.alloc("wuq", 3 * 768 * 2)
        wuqv = wuq.ap(BF16, "p (k n) -> p k n", k=3)
        self.dma("pool", wuqv, di["attn_w_uq"][jj].rearrange("(kc k) n -> k kc n", k=128), "wuq", [], [wuq.o])
        wukv = A.alloc("wukv", 2 * 1024 * 2)
        self.dma("pool", wukv.ap(BF16, "p (k n) -> p k n", k=2), di["attn_w_ukv"][jj].rearrange("(kc k) n -> k kc n", k=128), "wukv", [], [wukv.o])
        H = A.alloc("H1", 16384)
        Hv = H.ap(BF16, "p (i d) -> p i d", i=8)
        self.norm_mod(1, lambda i: Hv[:, i, :], H.o)
        HT = A.alloc("HT1", 16384)
        self.transposeT(H, HT)
        A.release(H)
        HTv = HT.ap(BF16, "p (k t) -> p k t", k=8)
        sc = {k: A.alloc("sc_" + k, n) for k, n in (("ckvb", 512), ("ckvT", 512), ("kcat", 3072), ("t1", 3072), ("t2", 3072),
                                                   ("st", 64), ("gk", 512), ("kb", 1536), ("gkb", 256))}
        W = {"sc": sc, "wukv": wukv, "gn": gn}
        qs = A.alloc("qs", 3072)
        qb = A.alloc("qb", 1536)
        st2 = A.alloc("st2", 64)
        junk = A.alloc("junk", 2048)
        qcb = A.alloc("qcb", 768)
        qcT = A.alloc("qcT", 768)
        splits = ((0, 384), (384, 672), (672, 1184), (1184, 1440))
        stp = self.cfg.get("attn_stop", 9)
        if stp <= 1.2:
            return
        for i in range(8):
            for bk, (c0, c1) in enumerate(splits):
                for kc in range(8):
                    self.mm(self.bank(bk)[:, 0:c1 - c0], HTv[:, kc, i * 128:(i + 1) * 128], winv[:, kc, c0:c1], kc == 0, kc == 7,
                            [HT.o, win.o], [self.BK[bk]])
            if stp <= 1.4:
                continue
            ss = st2.ap()[:, 0:1]
            self.memset("dve", st2.ap()[:, 0:2], 0.0, [st2.o])
            self.act(junk.ap()[:, 0:384], self.bank(0)[:, 0:384], AF.Square, [self.BK[0], st2.o], [junk.o, st2.o], accum_out=ss)
            self.act(ss, ss, AF.Sqrt, [st2.o, self.epsc.o], [st2.o], bias=self.epsc.ap()[:, 0:1], scale=1.0 / 384)
            self.P.op("dve", lambda e: e.reciprocal(out=st2.ap()[:, 0:1], in_=st2.ap()[:, 0:1]), reads=[st2.o], writes=[st2.o])
            self.stt("dve", qcb.ap(BF16)[:, 0:384], self.bank(0)[:, 0:384], ss, G[:, 0:384], ALU.mult, ALU.mult, [self.BK[0], st2.o, gn.o], [qcb.o])
            if stp <= 1.45:
                continue
            b6 = self.bank(6).bitcast(BF16)
            for kk in range(3):
                self.tr(b6[:, kk * 128:(kk + 1) * 128], qcb.ap(BF16)[:, kk * 128:(kk + 1) * 128], idb, [qcb.o, self.identb.o], [self.BK[6]])
            self.cp("dve", qcT.ap(BF16)[:, 0:384], b6[:, 0:384], [self.BK[6]], [qcT.o])
            QM = self.PB[2]
            for nt in range(2):
                for kk in range(3):
                    self.mm(QM[:, nt * 512:nt * 512 + 384], qcT.ap(BF16)[:, kk * 128:(kk + 1) * 128], wuqv[:, kk, nt * 384:(nt + 1) * 384],
                            kk == 0, kk == 2, [qcT.o, wuq.o], [self.BK[4 + nt]])
            qsv = qs.ap()[:, 0:768].rearrange("p (h e) -> p h e", h=8)
            self.cp("act", qs.ap()[:, 0:768].rearrange("p (a e) -> p a e", a=2), QM[:, :].rearrange("p (a e) -> p a e", a=2)[:, :, 0:384],
                    [self.BK[4], self.BK[5]], [qs.o])
            if stp <= 1.5:
                continue
            self.head_norm(qsv, 8, 96, G[:, 640:736], qsv, [qs.o], qs.o, sc["t1"], sc["st"], gn.o)
            if stp <= 1.55:
                continue
            self.rope(qsv[:, :, 64:96], 8, 32, c32v[:, i, :], s32v[:, i, :], ropeobjs, sc["t1"], sc["t2"], qs.o)
            if stp <= 1.57:
                continue
            qbv = qb.ap(BF16)[:, 0:768].rearrange("p (h e) -> p h e", h=8)
            self.cp("act", qbv, qsv, [qs.o], [qb.o])
            b7 = self.bank(7).bitcast(BF16)
            if stp <= 1.58:
                continue
            for h in range(8):
                self.tr(b7[0:96, h * 128:(h + 1) * 128], qbv[:, h, :], idb, [qb.o, self.identb.o], [self.BK[7]])
            b73 = b7[0:96, :].rearrange("p (h t) -> p h t", h=8)
            if stp <= 1.59:
                continue
            self.cp("dve", QTmv[0][0:96, :, i, :], b73[:, :, 0:64], [self.BK[7]], [QTm[0].o])
            if stp <= 1.595:
                continue
            self.cp("act", QTmv[1][0:96, :, i, :], b73[:, :, 64:128], [self.BK[7]], [QTm[1].o])
            if stp <= 1.6:
                continue
            ss2 = st2.ap()[:, 1:2]
            self.act(junk.ap()[:, 0:256], self.bank(1)[:, 0:256], AF.Square, [self.BK[1], st2.o], [junk.o, st2.o], accum_out=ss2)
            self.act(ss2, ss2, AF.Sqrt, [st2.o, self.epsc.o], [st2.o], bias=self.epsc.ap()[:, 0:1], scale=1.0 / 256)
            self.P.op("dve", lambda e: e.reciprocal(out=st2.ap()[:, 1:2], in_=st2.ap()[:, 1:2]), reads=[st2.o], writes=[st2.o])
            self.stt("dve", OUTv[:, i, 0:256], self.bank(1)[:, 0:256], ss2, G[:, 384:640], ALU.mult, ALU.mult, [self.BK[1], st2.o, gn.o], [OUT.o])
            self.cp("act", OUTv[:, i, 256:288], self.bank(1)[:, 256:288], [self.BK[1]], [OUT.o])
            self.cp("act", OUTv[:, i, 416:544], self.bank(3)[:, 128:256], [self.BK[3]], [OUT.o])
            gks = sc["gk"]
            self.cp("act", gks.ap()[:, 0:128], self.bank(3)[:, 0:128], [self.BK[3]], [gks.o])
            gk3 = gks.ap()[:, 0:128].rearrange("p (h e) -> p h e", h=2)
            self.head_norm(gk3, 2, 64, G[:, 896:960], OUTv[:, i, 288:416].rearrange("p (h e) -> p h e", h=2), [gks.o], OUT.o, sc["t1"], sc["st"], gn.o)
            if stp <= 1.7:
                continue
            self.cp("act", qs.ap()[:, 0:512], self.bank(2)[:, 0:512], [self.BK[2]], [qs.o])
            gq3 = qs.ap()[:, 0:512].rearrange("p (h e) -> p h e", h=8)
            self.head_norm(gq3, 8, 64, G[:, 832:896], gq3, [qs.o], qs.o, sc["t1"], sc["st"], gn.o)
            self.rope(gq3, 8, 64, c64v[:, i, :], s64v[:, i, :], ropeobjs, sc["t1"], sc["t2"], qs.o)
            gqb = qb.ap(BF16)[:, 0:512].rearrange("p (h e) -> p h e", h=8)
            self.cp("act", gqb, gq3, [qs.o], [qb.o])
            for h in range(8):
                self.tr(b7[0:64, h * 128:(h + 1) * 128], gqb[:, h, :], idb, [qb.o, self.identb.o], [self.BK[7]])
            b74 = b7[0:64, :].rearrange("p (h t) -> p h t", h=8)
            self.cp("dve", QTgv[0][0:64, :, i, :], b74[:, :, 0:64], [self.BK[7]], [QTg[0].o])
            self.cp("act", QTgv[1][0:64, :, i, :], b74[:, :, 64:128], [self.BK[7]], [QTg[1].o])
            if stp <= 1.8:
                continue
            for _ in self.kv_tile(OUTv[:, i, :], OUT.o, (c32v[:, i, :], s32v[:, i, :], c64v[:, i, :], s64v[:, i, :], ropeobjs),
                                  KTov[0:96, :, i, :], Vov[:, i, :, :], KTgov[0:64, :, i, :], Vgov[:, i, :, :],
                                  {"KT": KTo.o, "V": Vo.o, "KTg": KTgo.o, "Vg": Vgo.o}, W, 2, tcols=64):
                pass
        A.release(HT, win, wuq, qs, qb, st2, junk, qcb, qcT, C32, S32, C64, S64)
        for s_ in range(2):
            rows = slice(32 * s_, 32 * s_ + 32)
            for nm, a, b in (("ockv", 0, 256), ("okrope", 256, 288), ("ogk", 288, 416), ("ogv", 416, 544)):
                self.dma("sp", self.do[nm][s_, jj].rearrange("(c i) d -> c i d", i=8), OUTv[rows, :, a:b], "o" + nm, [OUT.o], [], is_output=True)
        if self.cfg.get("attn_stop", 9) <= 2:
            return
        orecv = []
        for hf in range(2):
            osend, orc = Obj("kvsend%d" % hf), Obj("kvrecv%d" % hf)
            ks, kr = self.kvsend[jj][hf], self.kvrecv[jj][hf]
            self.dma("sp", ks.rearrange("(c i) d -> c i d", i=8), OUTv[64 + 32 * hf:96 + 32 * hf, :, :], "kvs", [OUT.o], [osend])
            P.dma("pool", (lambda ks, kr: lambda e: e.collective_compute("AllGather", ALU.bypass, replica_groups=[[0, 1, 2, 3], [4, 5, 6, 7]],
                                                                         ins=[ks.opt()], outs=[kr.opt()]))(ks, kr),
                  "cc_kv%d_%d" % (jj, hf), reads=[osend], writes=[orc], inc=1)
            orecv.append(orc)
        A.release(OUT)
        kp = A.alloc("kpos", 128)
        self.dma("sp", kp.ap()[:, 0:32], di["kpos"].rearrange("p u c -> p (u c)"), "kpos", [], [kp.o])
        KC32, KS32 = self.gen_rope(kp, 16, 32, "k32")
        KC64, KS64 = self.gen_rope(kp, 16, 64, "k64")
        A.release(kp)
        kro = [KC32.o, KS32.o, KC64.o, KS64.o]
        kc32, ks32 = KC32.ap(F32, "p (n r) -> p n r", n=16), KS32.ap(F32, "p (n r) -> p n r", n=16)
        kc64, ks64 = KC64.ap(F32, "p (n r) -> p n r", n=16), KS64.ap(F32, "p (n r) -> p n r", n=16)
        KT = A.alloc("KT", 8 * 2304 * 2)
        VM = A.alloc("VM", 18 * 8 * 65 * 2)
        KTg = A.alloc("KTg", 2 * 2304 * 2)
        VG = A.alloc("VG", 18 * 2 * 65 * 2)
        KTv = KT.ap(BF16, "p (h t) -> p h t", h=8)
        VMv = VM.ap(BF16, "p (u h e) -> p u h e", u=18, h=8)
        KTgv = KTg.ap(BF16, "p (h t) -> p h t", h=2)
        VGv = VG.ap(BF16, "p (u h e) -> p u h e", u=18, h=2)
        self.memset("pool", VM.ap(BF16), 1.0, [VM.o])
        self.memset("pool", VG.ap(BF16), 1.0, [VG.o])
        stg = [A.alloc("stg%d" % k, 544 * 4) for k in range(2)]
        dob = {"KT": KT.o, "V": VM.o, "KTg": KTg.o, "Vg": VG.o}
        sc2 = {k: A.alloc("sc2_" + k, r_.req) for k, r_ in sc.items()}
        W2 = {"sc": sc2, "wukv": wukv, "gn": gn}
        gens = []
        for u in range(18):
            sg = stg[u % 2]
            if u < 2:
                rows = slice(128 * u, 128 * u + 128)
                self.dma("sp", sg.ap()[:, 0:256], di["cache_ckv"][jj, rows, :], "stg%d" % (u % 2), [], [sg.o])
                self.dma("sp", sg.ap()[:, 256:288], di["cache_krope"][jj, rows, :], "stg%d" % (u % 2), [], [sg.o])
                self.dma("sp", sg.ap()[:, 288:416], di["cache_gk"][jj, rows, :], "stg%d" % (u % 2), [], [sg.o])
                self.dma("sp", sg.ap()[:, 416:544], di["cache_gv"][jj, rows, :], "stg%d" % (u % 2), [], [sg.o])
                rp = None
            else:
                v = u - 2
                rk, w4 = v // 4, v % 4
                src = self.kvrecv[jj][w4 // 2][256 * rk + 128 * (w4 % 2):256 * rk + 128 * (w4 % 2) + 128, :]
                self.dma("sp", sg.ap()[:, 0:544], src, "stg%d" % (u % 2), [orecv[w4 // 2]], [sg.o])
                rp = (kc32[:, v, :], ks32[:, v, :], kc64[:, v, :], ks64[:, v, :], kro)
            gens.append(self.kv_tile(sg.ap()[:, 0:544], sg.o, rp, KTv[0:96, :, u * 128:(u + 1) * 128], VMv[:, u, :, :],
                                     KTgv[0:64, :, u * 128:(u + 1) * 128], VGv[:, u, :, :], dob, W if u % 2 == 0 else W2, u % 3, tb=u % 2))
            if len(gens) == 2:
                live = list(gens)
                while live:
                    for g_ in list(live):
                        try:
                            next(g_)
                        except StopIteration:
                            live.remove(g_)
                gens = []
        A.release(wukv, KC32, KS32, KC64, KS64, *stg, *sc.values(), *sc2.values())
        if self.cfg.get("attn_stop", 9) <= 3:
            return
        G1 = self.load_mod(2, "modG")
        OT = A.alloc("OT", 8 * 1024 * 2)
        OTv = OT.ap(BF16, "p (h i c) -> p h i c", h=8, i=8)
        PT = [A.alloc("PT%d" % k, 1024) for k in range(3)]
        PTp = A.alloc("PTp", 8 * 256 * 2)
        PTpv = PTp.ap(BF16, "p (g i c) -> p g i c", g=8, i=8)
        rsr = A.alloc("rsr", 3072)
        bcs = A.alloc("bcs", 3072)
        wo = [A.alloc("wo%d" % k, 4 * 512 * 2) for k in range(2)]
        wov = [w_.ap(BF16, "p (h n) -> p h n", h=4) for w_ in wo]
        dtmp = [A.alloc("atmp%d" % k, 2048) for k in range(2)]
        onesf = self.onesf.ap()
        ptc = 0
        sbc = 0
        for grp in range(2):
            dq = 96 if grp == 0 else 64
            scale = 1.0 / math.sqrt(dq)
            cnts = [sbc, ptc]

            def hviews(h):
                hk = h if grp == 0 else h // 4
                if grp == 0:
                    return (hk, QTmv[1][0:96, h], QTmv[0][0:96, h], QTm[1].o, QTm[0].o, KTv[0:96, hk], KTov[0:96, hk], KT.o, KTo.o,
                            VMv, Vov, VM.o, Vo.o)
                return (hk, QTgv[1][0:64, h], QTgv[0][0:64, h], QTg[1].o, QTg[0].o, KTgv[0:64, hk], KTgov[0:64, hk], KTg.o, KTgo.o,
                        VGv, Vgov, VG.o, Vgo.o)

            def S_gen(h):
                hk, qts, qtp, qo_s, qo_p, kt_all, kto_, kobj, koobj, v_all, v_own, vobj, voobj = hviews(h)
                ob = 4 + h % 2
                OB = self.bank(ob)
                pend = []
                for u in range(18 + 2):
                    if u < 18:
                        sb = cnts[0] % 4
                        cnts[0] += 1
                        self.mm(self.bank(sb), kt_all[:, u * 128:(u + 1) * 128], qts.rearrange("p i c -> p (i c)"), True, True, [kobj, qo_s], [self.BK[sb]])
                        pt = PT[cnts[1] % 3]
                        cnts[1] += 1
                        self.act(pt.ap(BF16), self.bank(sb), AF.Exp, [self.BK[sb]], [pt.o], scale=scale)
                        pend.append((u, pt))
                    if u >= 2:
                        uu, pt2 = pend.pop(0)
                        self.mm(OB[0:65, :], v_all[:, uu, hk, :], pt2.ap(BF16), uu == 0, uu == 17, [vobj, pt2.o], [self.BK[ob]])
                    yield
                self.act(rsr.ap()[64:65, 0:512], OB[64:65, 0:512], AF.Ln, [self.BK[ob]], [rsr.o])
                self.act(rsr.ap()[64:65, 0:512], rsr.ap()[64:65, 0:512], AF.Exp, [rsr.o], [rsr.o], scale=-1.0)
                self.mm(self.bank(6)[0:64, 0:512], onesf[64:65, 0:64], rsr.ap()[64:65, 0:512], True, True, [rsr.o, self.onesf.o], [self.BK[6]])
                self.cp("act", bcs.ap()[0:64, 0:512], self.bank(6)[0:64, 0:512], [self.BK[6]], [bcs.o])
                self.tt("dve", OTv[0:64, h, :, 64:128], OB[0:64, 0:512].rearrange("p (i c) -> p i c", i=8),
                        bcs.ap()[0:64, 0:512].rearrange("p (i c) -> p i c", i=8), ALU.mult, [self.BK[ob], bcs.o], [OT.o])
                yield

            def P_gen(h):
                hk, qts, qtp, qo_s, qo_p, kt_all, kto_, kobj, koobj, v_all, v_own, vobj, voobj = hviews(h)
                for ig in range(8):
                    sb = cnts[0] % 4
                    cnts[0] += 1
                    self.mm(self.bank(sb)[0:64, :], kto_[:, ig, 0:64], qtp.rearrange("p i c -> p (i c)"), True, True, [koobj, qo_p], [self.BK[sb]])
                    s3 = self.bank(sb)[0:64, :].rearrange("p (i c) -> p i c", i=8)
                    for s_ in range(2):
                        rows = slice(32 * s_, 32 * s_ + 32)
                        self.act(PTpv[rows, ig, :, 0:32], s3[rows, :, 32 * s_:32 * s_ + 32], AF.Exp, [self.BK[sb]], [PTp.o], scale=scale)
                    yield
                for s_ in range(2):
                    rows = slice(32 * s_, 32 * s_ + 32)
                    ob2 = 7
                    OB2 = self.bank(ob2)
                    for ig in range(8):
                        self.mm(OB2[0:65, 0:256], v_own[rows, ig, hk, :], PTpv[rows, ig, :, 0:32].rearrange("p i c -> p (i c)"), ig == 0, ig == 7,
                                [voobj, PTp.o], [self.BK[ob2]])
                    self.act(rsr.ap()[64:65, 512:768], OB2[64:65, 0:256], AF.Ln, [self.BK[ob2]], [rsr.o])
                    self.act(rsr.ap()[64:65, 512:768], rsr.ap()[64:65, 512:768], AF.Exp, [rsr.o], [rsr.o], scale=-1.0)
                    self.mm(self.bank(6)[0:64, 0:256], onesf[64:65, 0:64], rsr.ap()[64:65, 512:768], True, True, [rsr.o, self.onesf.o], [self.BK[6]])
                    self.cp("act", bcs.ap()[0:64, 512:768], self.bank(6)[0:64, 0:256], [self.BK[6]], [bcs.o])
                    self.tt("dve", OTv[0:64, h, :, 32 * s_:32 * s_ + 32], OB2[0:64, 0:256].rearrange("p (i c) -> p i c", i=8),
                            bcs.ap()[0:64, 512:768].rearrange("p (i c) -> p i c", i=8), ALU.mult, [self.BK[ob2], bcs.o], [OT.o])
                    yield

            for _ in S_gen(0):
                pass
            for h in range(8):
                live = [P_gen(h)] + ([S_gen(h + 1)] if h + 1 < 8 else [])
                while live:
                    for g_ in list(live):
                        try:
                            next(g_)
                        except StopIteration:
                            live.remove(g_)
            sbc, ptc = cnts
            for nh in range(2):
                cols = slice(nh * 512, (nh + 1) * 512)
                for k2 in range(2):
                    self.dma("pool", wov[k2][0:64], di["attn_w_out"][jj][grp * 512 + k2 * 256:grp * 512 + (k2 + 1) * 256, cols].rearrange("(h d) n -> d h n", d=64),
                             "wo%d" % k2, [], [wo[k2].o])
                for i in range(8):
                    bk = i % 4
                    for h in range(8):
                        self.mm(self.bank(bk), OTv[0:64, h, i, :], wov[h // 4][0:64, h % 4, :], h == 0, h == 7, [OT.o, wo[h // 4].o], [self.BK[bk]])
                    t = dtmp[i % 2]
                    self.tt("dve", t.ap(), self.bank(bk), G1.ap()[:, cols], ALU.mult, [self.BK[bk], G1.o], [t.o])
                    self.tt("pool", X[:, i, cols], X[:, i, cols], t.ap(), ALU.add, [self.XO[i], t.o], [self.XO[i]])
        A.release(gn, G1, OT, PTp, rsr, bcs, *wo, KT, VM, KTg, VG, KTo, Vo, KTgo, Vgo, *PT, *dtmp, *QTm, *QTg)


    def gen_pw(self, k0, step, ardt, aidt, name):
        A = self.A
        n = 8 * 64
        ki = A.alloc("ki", 32)
        kf = A.alloc("kf", 32)
        kiv = ki.ap(I32)
        self.P.op("pool", lambda e: e.iota(kiv, pattern=[[step, 8]], base=k0, channel_multiplier=0), writes=[ki.o])
        self.cp("dve", kf.ap(), kiv, [ki.o], [kf.o])
        kb = kf.ap().unsqueeze(2).to_broadcast([128, 8, 64])
        Pre = A.alloc(name + "re", n * 4)
        Pim = A.alloc(name + "im", n * 4)
        mag = A.alloc("mag", n * 4)
        ang = A.alloc("pang", n * 4)
        t = A.alloc("pt", n * 4)
        ti = A.alloc("pti", n * 4)
        v3 = lambda r: r.ap(F32, "p (k g) -> p k g", k=8)
        self.tt("dve", v3(mag), kb, ardt.ap().unsqueeze(1).to_broadcast([128, 8, 64]), ALU.mult, [kf.o, ardt.o], [mag.o])
        self.act(mag.ap(), mag.ap(), AF.Exp, [mag.o], [mag.o])
        self.tt("dve", v3(ang), kb, aidt.ap().unsqueeze(1).to_broadcast([128, 8, 64]), ALU.mult, [kf.o, aidt.o], [ang.o])
        for shift, dst in ((0.5, Pim), (0.75, Pre)):
            self.ts("dve", t.ap(), ang.ap(), 1.0 / (2 * math.pi), shift + 32.0, ALU.mult, ALU.add, [ang.o], [t.o])
            self.cp("dve", ti.ap(I32), t.ap(), [t.o], [ti.o])
            self.cp("dve", dst.ap(), ti.ap(I32), [ti.o], [dst.o])
            self.tt("dve", t.ap(), t.ap(), dst.ap(), ALU.subtract, [t.o, dst.o], [t.o])
            self.stt("dve", t.ap(), t.ap(), 0.0, t.ap(), ALU.is_lt, ALU.add, [t.o], [t.o])
            self.act(dst.ap(), t.ap(), AF.Sin, [t.o, self.epsc.o], [dst.o], scale=2 * math.pi, bias=self.epsc.ap()[:, 1:2])
            self.tt("dve", dst.ap(), dst.ap(), mag.ap(), ALU.mult, [dst.o, mag.o], [dst.o])
        A.release(ki, kf, mag, ang, t, ti)
        return Pre, Pim

    def ssm(self, l):
        j2 = l // 2
        A, P, di = self.A, self.P, self.di
        X = self.X.ap(F32, "p (i d) -> p i d", i=8)
        idb, idf = self.identb.ap(BF16), self.identf.ap()
        ENG = ("dve", "pool")
        U = A.alloc("U", 16384)
        Uv = U.ap(BF16, "p (g i h) -> p g i h", g=64, i=8)
        self.norm_mod(1, lambda i: Uv[:, :, i, :], U.o, view=lambda ap: ap.rearrange("p (g h) -> p g h", g=64))
        VA = A.alloc("VA", 16384)
        VAv = VA.ap(BF16, "p (g c) -> p g c", g=64)
        for gb in range(8):
            bk = 6 + gb % 2
            psb = self.bank(bk).bitcast(BF16)
            for gl in range(8):
                self.tr(psb[:, gl * 128:(gl + 1) * 128], Uv[:, gb * 8 + gl, :, :].rearrange("p i h -> p (i h)"), idb, [U.o, self.identb.o], [self.BK[bk]])
            self.cp("act" if gb % 2 else "dve", VAv[:, gb * 8:(gb + 1) * 8, :], psb.rearrange("p (g c) -> p g c", g=8), [self.BK[bk]], [VA.o])
        A.release(U)
        if self.cfg.get("ssm_stop", 99) <= 1:
            return
        QWB, PWC, QXB, BR, BI = [], [], [], [], []
        MRE2, NMIM, PMIM, M64 = [], [], [], []
        small = A.alloc("ssmall", 64 * 4 * 12)
        sm = lambda k: small.ap()[:, k * 64:(k + 1) * 64]
        for d in range(2):
            aTr, aTi, dtb = A.alloc("aTr", 256), A.alloc("aTi", 256), A.alloc("dtb", 256)
            for half in range(2):
                rows = slice(64 * half, 64 * half + 64)
                self.dma("sp", aTr.ap()[rows, :], di["ssm_a_re"][j2, d].rearrange("g p -> p g"), "ssmp", [], [aTr.o])
                self.dma("sp", aTi.ap()[rows, :], di["ssm_a_im"][j2, d].rearrange("g p -> p g"), "ssmp", [], [aTi.o])
            self.dma("sp", dtb.ap(), di["ssm_log_dt"][j2, d].partition_broadcast(128), "ssmp", [], [dtb.o])
            self.act(dtb.ap(), dtb.ap(), AF.Exp, [dtb.o], [dtb.o])
            ardt, aidt = A.alloc("ardt", 256), A.alloc("aidt", 256)
            self.tt("dve", ardt.ap(), aTr.ap(), dtb.ap(), ALU.mult, [aTr.o, dtb.o], [ardt.o])
            self.tt("dve", aidt.ap(), aTi.ap(), dtb.ap(), ALU.mult, [aTi.o, dtb.o], [aidt.o])
            wbk, wck, xbk = ((7, -1), (1, 1), (-1, -1)) if d == 0 else ((0, 1), (8, -1), (-8, 1))
            pwb = self.gen_pw(wbk[0], wbk[1], ardt, aidt, "pwb%d" % d)
            pwc = self.gen_pw(wck[0], wck[1], ardt, aidt, "pwc%d" % d)
            pxb = self.gen_pw(xbk[0], xbk[1], ardt, aidt, "pxb%d" % d)
            il, imu = (0, 7) if d == 0 else (7, 0)
            pcr = pwc[0].ap(F32, "p (k g) -> p k g", k=8)
            pci = pwc[1].ap(F32, "p (k g) -> p k g", k=8)
            so = small.o
            den, rden, nre, t1, t2, kre, kim = sm(0), sm(1), sm(2), sm(3), sm(4), sm(5), sm(6)
            self.tt("dve", den, aTr.ap(), aTr.ap(), ALU.mult, [aTr.o], [so])
            self.tt("dve", t1, aTi.ap(), aTi.ap(), ALU.mult, [aTi.o, so], [so])
            self.tt("dve", den, den, t1, ALU.add, [so], [so])
            self.recip(rden, den, [so], [so])
            self.ts("dve", nre, pcr[:, il, :], -1.0, None, ALU.add, None, [pwc[0].o, so], [so])
            self.tt("dve", t1, nre, aTr.ap(), ALU.mult, [so, aTr.o], [so])
            self.tt("dve", t2, pci[:, il, :], aTi.ap(), ALU.mult, [pwc[1].o, aTi.o, so], [so])
            self.tt("dve", t1, t1, t2, ALU.add, [so], [so])
            self.tt("dve", kre, t1, rden, ALU.mult, [so], [so])
            self.tt("dve", t1, pci[:, il, :], aTr.ap(), ALU.mult, [pwc[1].o, aTr.o, so], [so])
            self.tt("dve", t2, nre, aTi.ap(), ALU.mult, [so, aTi.o], [so])
            self.tt("dve", t1, t1, t2, ALU.subtract, [so], [so])
            self.tt("dve", kim, t1, rden, ALU.mult, [so], [so])
            qt1, qt2 = A.alloc("qt1", 2048), A.alloc("qt2", 2048)
            for (pr, pi_) in (pwb, pxb):
                p3r = pr.ap(F32, "p (k g) -> p k g", k=8)
                p3i = pi_.ap(F32, "p (k g) -> p k g", k=8)
                q1 = qt1.ap(F32, "p (k g) -> p k g", k=8)
                q2 = qt2.ap(F32, "p (k g) -> p k g", k=8)
                krb = kre.unsqueeze(1).to_broadcast([128, 8, 64])
                kib = kim.unsqueeze(1).to_broadcast([128, 8, 64])
                self.tt("dve", q1, p3r, kib, ALU.mult, [pr.o, so], [qt1.o])
                self.tt("pool", q2, p3i, kib, ALU.mult, [pi_.o, so], [qt2.o])
                self.tt("dve", p3r, p3r, krb, ALU.mult, [pr.o, so], [pr.o])
                self.tt("dve", p3r, p3r, q2, ALU.subtract, [pr.o, qt2.o], [pr.o])
                self.tt("pool", p3i, p3i, krb, ALU.mult, [pi_.o, so], [pi_.o])
                self.tt("pool", p3i, p3i, q1, ALU.add, [pi_.o, qt1.o], [pi_.o])
            A.release(qt1, qt2)
            mt = A.alloc("mt%d" % d, (64 + 32 + 32 + 64) * 4)
            mre2 = mt.ap()[:, 0:64].rearrange("p (r q) -> p r q", r=2)
            nmim, pmim = mt.ap()[:, 64:96], mt.ap()[:, 96:128]
            m64 = mt.ap()[:, 128:192].rearrange("p (r q) -> p r q", r=2)
            for par in range(2):
                rows = slice(64 * par, 64 * par + 64)
                srcr = pcr[rows, imu, :].rearrange("p (q two) -> p q two", two=2)[:, :, par]
                srci = pci[rows, imu, :].rearrange("p (q two) -> p q two", two=2)[:, :, par]
                for r_ in range(2):
                    self.cp("dve", mre2[rows, r_, :], srcr, [pwc[0].o], [mt.o])
                self.cp("dve", pmim[rows, :], srci, [pwc[1].o], [mt.o])
                self.ts("dve", nmim[rows, :], srci, -1.0, None, ALU.mult, None, [pwc[1].o], [mt.o])
            self.cp("dve", m64[:, 0, :], mre2[:, 0, :], [mt.o], [mt.o])
            self.cp("dve", m64[:, 1, :], pmim, [mt.o], [mt.o])
            sq1, sq2 = sm(7)[:, 0:32], sm(8)[:, 0:32]
            for _ in range(6):
                self.tt("dve", sq1, m64[:, 0, :], m64[:, 0, :], ALU.mult, [mt.o, so], [so])
                self.tt("dve", sq2, m64[:, 1, :], m64[:, 1, :], ALU.mult, [mt.o, so], [so])
                self.stt("dve", m64[:, 1, :], m64[:, 0, :], 2.0, m64[:, 1, :], ALU.mult, ALU.mult, [mt.o], [mt.o])
                self.tt("dve", m64[:, 0, :], sq1, sq2, ALU.subtract, [so], [mt.o])
            br, bi = A.alloc("br%d" % d, 4096), A.alloc("bi%d" % d, 4096)
            for half in range(2):
                rows = slice(64 * half, 64 * half + 64)
                self.dma("sp", br.ap(F32, "p (g h) -> p g h", g=64)[rows], di["ssm_b_re"][j2, d].rearrange("g p h -> p g h"), "ssmb", [], [br.o])
                self.dma("sp", bi.ap(F32, "p (g h) -> p g h", g=64)[rows], di["ssm_b_im"][j2, d].rearrange("g p h -> p g h"), "ssmb", [], [bi.o])
            A.release(aTr, aTi, dtb, ardt, aidt)
            QWB.append(pwb); PWC.append(pwc); QXB.append(pxb); BR.append(br); BI.append(bi)
            MRE2.append(mre2); NMIM.append(nmim); PMIM.append(pmim); M64.append((m64, mt))
        if self.cfg.get("ssm_stop", 99) <= 2:
            return
        S = [A.alloc("S%d" % d, 2 * 32 * 131 * 2) for d in range(2)]
        S4 = [s_.ap(BF16, "p (r q c) -> p r q c", r=2, q=32) for s_ in S]
        wpre = A.alloc("wpre", 8 * 2 * 128 * 4)
        wt1, wt2 = A.alloc("wt1", 8 * 128 * 4), A.alloc("wt2", 8 * 128 * 4)
        wt3, wt4 = A.alloc("wt3", 8 * 128 * 4), A.alloc("wt4", 8 * 128 * 4)
        w3 = wt3.ap(F32, "p (g i h) -> p g i h", g=8, i=8)
        w4 = wt4.ap(F32, "p (g i h) -> p g i h", g=8, i=8)
        wbc = [A.alloc("wbc%d" % k, 8 * 2 * 64 * 2) for k in range(2)]
        wpv = wpre.ap(F32, "p (g r i h) -> p g r i h", g=8, r=2, i=8)
        w1 = wt1.ap(F32, "p (g i h) -> p g i h", g=8, i=8)
        w2 = wt2.ap(F32, "p (g i h) -> p g i h", g=8, i=8)
        segs = ((0, 32), (32, 64), (64, 128))
        cnt = 0
        for d in range(2):
            qr = QWB[d][0].ap(F32, "p (k g) -> p k g", k=8)
            qi = QWB[d][1].ap(F32, "p (k g) -> p k g", k=8)
            brv = BR[d].ap(F32, "p (g h) -> p g h", g=64)
            biv = BI[d].ap(F32, "p (g h) -> p g h", g=64)
            qo = [QWB[d][0].o, QWB[d][1].o, BR[d].o, BI[d].o]
            for gb in range(8):
                gs = slice(gb * 8, gb * 8 + 8)
                Er = qr[0:64, :, gs].rearrange("p i g -> p g i").unsqueeze(3).to_broadcast([64, 8, 8, 16])
                Ei = qi[0:64, :, gs].rearrange("p i g -> p g i").unsqueeze(3).to_broadcast([64, 8, 8, 16])
                Br_ = brv[0:64, gs, :].unsqueeze(2).to_broadcast([64, 8, 8, 16])
                Bi_ = biv[0:64, gs, :].unsqueeze(2).to_broadcast([64, 8, 8, 16])
                self.tt("pool", w2[0:64], Ei, Bi_, ALU.mult, qo, [wt2.o])
                self.tt("dve", w1[0:64], Er, Br_, ALU.mult, qo, [wt1.o])
                self.tt("pool", w4[0:64], Ei, Br_, ALU.mult, qo, [wt4.o])
                self.tt("dve", w3[0:64], Er, Bi_, ALU.mult, qo, [wt3.o])
                self.tt("dve", wpv[0:64, :, 0], w1[0:64], w2[0:64], ALU.subtract, [wt1.o, wt2.o], [wpre.o])
                self.tt("dve", wpv[0:64, :, 1], w3[0:64], w4[0:64], ALU.add, [wt3.o, wt4.o], [wpre.o])
                pr = cnt % 3
                cnt += 1
                PT_ = self.PB[pr]
                pob = [self.BK[2 * pr], self.BK[2 * pr + 1]]
                wflat = wpre.ap(F32, "p (g r e) -> p g r e", g=8, r=2)
                for gl in range(8):
                    for r_ in range(2):
                        c0 = (gl * 2 + r_) * 64
                        self.tr(PT_[:, c0:c0 + 64], wflat[0:64, gl, r_, :], idf[0:64, 0:64], [wpre.o, self.identf.o], [pob[c0 // 512]])
                wb = wbc[gb % 2]
                wbv = wb.ap(BF16, "p (g r e) -> p g r e", g=8, r=2)
                self.cp("act", wb.ap(BF16), PT_[:, :], pob, [wb.o])
                pr2 = cnt % 3
                cnt += 1
                PS_ = self.PB[pr2]
                pob2 = [self.BK[2 * pr2], self.BK[2 * pr2 + 1]]
                for gl in range(8):
                    g = gb * 8 + gl
                    par, pl = g % 2, gl // 2
                    for r_ in range(2):
                        c0 = (pl * 2 + r_) * 128
                        kw = {"tile_position": (0, 64)} if par else {}
                        self.mm(PS_[64 * par:64 * par + 64, c0:c0 + 128], wbv[:, gl, r_, :], VAv[:, g, :], True, True, [wb.o, VA.o], [pob2[c0 // 512]], **kw)
                ps4 = PS_[:, :].rearrange("p (q r c) -> p r q c", q=4, r=2)
                for si, (a, b) in enumerate(segs):
                    base = (0, 33, 66)[si] + (1 if d == 0 else 0)
                    self.cp("act" if si % 2 else "dve", S4[d][:, :, gb * 4:(gb + 1) * 4, base:base + (b - a)], ps4[:, :, :, a:b], pob2, [S[d].o])
        A.release(wpre, wt1, wt2, wt3, wt4, *wbc, QWB[0][0], QWB[0][1], QWB[1][0], QWB[1][1])
        if self.cfg.get("ssm_stop", 99) <= 3:
            return
        selr = A.alloc("selr", 32)
        self.dma("sp", selr.ap()[:, 0:8], di["selr"], "selr", [], [selr.o])
        H0 = A.alloc("H0", 128 * 4)
        for d in range(2):
            for par in range(2):
                self.dma("sp", H0.ap()[64 * par:64 * par + 64, 64 * d:64 * d + 64].rearrange("p (r q) -> p r q", r=2),
                         di["state_ssm"][j2, d].rearrange("(q two) p r -> two p r q", two=2)[par], "h0", [], [H0.o])
        ST = [[A.alloc("ST%d_%d" % (d, k), 2 * 32 * 3 * 4) for k in range(2)] for d in range(2)]
        STv = [[r.ap(F32, "p (r q s) -> p r q s", r=2, q=32) for r in ST[d]] for d in range(2)]
        tm1 = [A.alloc("tm1_%d" % d, 768) for d in range(2)]
        tm2 = [A.alloc("tm2_%d" % d, 768) for d in range(2)]
        FIN = [A.alloc("FIN%d" % d, 2 * 32 * 2 * 4) for d in range(2)]
        FINv = [r.ap(F32, "p (r q s) -> p r q s", r=2, q=32) for r in FIN]
        FS = A.alloc("FS", 128 * 4)
        for d in range(2):
            self.memset(ENG[d], ST[d][0].ap(), 0.0, [ST[d][0].o])
            self.memset(ENG[d], S4[d][:, :, :, 0:131:33] if d == 0 else S4[d][:, :, :, 32:131:33], 0.0, [S[d].o]) if False else None

        def cmul_step(d, cur, nxt, sl, ns, Bk, bo):
            e = ENG[d]
            cv, nv = STv[d][cur], STv[d][nxt]
            co, no = ST[d][cur].o, ST[d][nxt].o
            t1 = tm1[d].ap(F32, "p (r q s) -> p r q s", r=2, q=32)[:, :, :, 0:ns]
            t2 = tm2[d].ap(F32, "p (r q s) -> p r q s", r=2, q=32)[:, :, :, 0:ns]
            mo = M64[d][1].o
            self.tt(e, t1, cv[:, :, :, sl], MRE2[d].unsqueeze(3).to_broadcast([128, 2, 32, ns]), ALU.mult, [co, mo], [tm1[d].o])
            self.tt(e, t2[:, 0], cv[:, 1, :, sl], NMIM[d].unsqueeze(2).to_broadcast([128, 32, ns]), ALU.mult, [co, mo], [tm2[d].o])
            self.tt(e, t2[:, 1], cv[:, 0, :, sl], PMIM[d].unsqueeze(2).to_broadcast([128, 32, ns]), ALU.mult, [co, mo, tm2[d].o], [tm2[d].o])
            self.tt(e, t1, t1, t2, ALU.add, [tm1[d].o, tm2[d].o], [tm1[d].o])
            self.tt(e, nv[:, :, :, sl], t1, Bk, ALU.add, [tm1[d].o, bo], [no])

        def bcols(d, k, allseq):
            if d == 0:
                return S4[0][:, :, :, 1 + k:1 + k + 67:33] if allseq else S4[0][:, :, :, 67 + k:68 + k]
            return S4[1][:, :, :, 63 - k:63 - k + 67:33] if allseq else S4[1][:, :, :, 129 - k:130 - k]
        cur = [0, 0]
        for k in range(64):
            for d in range(2):
                allseq = (k < 32) if d == 0 else (k >= 32)
                sl = slice(0, 3) if allseq else slice(2, 3)
                ns = 3 if allseq else 1
                if d == 1 and k == 32:
                    self.memset(ENG[d], STv[d][cur[d]][:, :, :, 0:2], 0.0, [ST[d][cur[d]].o])
                nxt = 1 - cur[d]
                bk_ = bcols(d, k, allseq)
                cmul_step(d, cur[d], nxt, sl, ns, bk_, S[d].o)
                if allseq:
                    self.cp("act", bk_[:, :, :, 0:2], STv[d][nxt][:, :, :, 0:2], [ST[d][nxt].o, S[d].o], [S[d].o])
                    if (d == 0 and k == 31) or (d == 1 and k == 63):
                        self.cp("act", FINv[d], STv[d][nxt][:, :, :, 0:2], [ST[d][nxt].o], [FIN[d].o])
                cur[d] = nxt
        for d in range(2):
            self.cp("act", FS.ap()[:, 64 * d:64 * d + 64].rearrange("p (r q) -> p r q", r=2), STv[d][cur[d]][:, :, :, 2], [ST[d][cur[d]].o], [FS.o])
        osd, orc = Obj("ssend"), Obj("srecv")
        self.dma("sp", self.ssend[j2], FS.ap(), "ssd", [FS.o], [osd])
        sdd, srr = self.ssend[j2], self.srecv[j2]
        P.dma("pool", lambda e: e.collective_compute("AllGather", ALU.bypass, replica_groups=[[0, 1, 2, 3], [4, 5, 6, 7]],
                                                     ins=[sdd.opt()], outs=[srr.opt()]), "cc_s%d" % j2, reads=[osd], writes=[orc], inc=1)
        FG = A.alloc("FG", 4 * 128 * 4)
        FGv = FG.ap(F32, "p (k c) -> p k c", k=4)
        self.dma("sp", FGv, srr.rearrange("(k p) c -> p k c", k=4), "fg", [orc], [FG.o])
        ch = A.alloc("chain", 4 * 64 * 4)
        chv = ch.ap(F32, "p (k r q) -> p k r q", k=4, r=2)
        ct1, ct2 = A.alloc("ct1", 256), A.alloc("ct2", 256)
        c1 = ct1.ap(F32, "p (r q) -> p r q", r=2)
        c2 = ct2.ap(F32, "p (r q) -> p r q", r=2)
        for d in range(2):
            m64, mt = M64[d]
            order = (0, 1, 2, 3) if d == 0 else (3, 2, 1, 0)
            h0v = H0.ap()[:, 64 * d:64 * d + 64].rearrange("p (r q) -> p r q", r=2)
            self.cp("dve", chv[:, order[0]], h0v, [H0.o], [ch.o])
            for a in range(3):
                ks, kd = order[a], order[a + 1]
                src = chv[:, ks]
                fk = FGv[:, ks, 64 * d:64 * d + 64].rearrange("p (r q) -> p r q", r=2)
                self.tt("dve", c1, src, m64[:, 0:1, :].to_broadcast([128, 2, 32]), ALU.mult, [ch.o, mt.o], [ct1.o])
                self.tt("dve", c2[:, 1, :], src[:, 0, :], m64[:, 1, :], ALU.mult, [ch.o, mt.o], [ct2.o])
                self.stt("dve", c2[:, 0, :], src[:, 1, :], -1.0, m64[:, 1, :], ALU.mult, ALU.mult, [ch.o, mt.o, ct2.o], [ct2.o])
                self.tt("dve", c1, c1, c2, ALU.add, [ct1.o, ct2.o], [ct1.o])
                self.tt("dve", chv[:, kd], c1, fk, ALU.add, [ct1.o, FG.o], [ch.o])
            dst = STv[d][cur[d]][:, :, :, 2]
            self.ts("dve", dst, chv[:, 0], selr.ap()[:, 0:1], None, ALU.mult, None, [ch.o, selr.o], [ST[d][cur[d]].o])
            for k in range(1, 4):
                self.stt("dve", dst, chv[:, k], selr.ap()[:, k:k + 1], dst, ALU.mult, ALU.add, [ch.o, selr.o, ST[d][cur[d]].o], [ST[d][cur[d]].o])
            icol = 66 if d == 0 else 130
            self.cp("act", S4[d][:, :, :, icol], dst, [ST[d][cur[d]].o], [S[d].o])
        for d in range(2):
            cols = S4[d][:, :, :, 0:34:33] if d == 0 else S4[d][:, :, :, 32:66:33]
            self.memset("pool", cols, 0.0, [S[d].o])
        for k in range(64):
            for d in range(2):
                nxt = 1 - cur[d]
                bk_ = bcols(d, k, False)
                cmul_step(d, cur[d], nxt, slice(2, 3), 1, bk_, S[d].o)
                self.cp("act", bk_, STv[d][nxt][:, :, :, 2:3], [ST[d][nxt].o, S[d].o], [S[d].o])
                cur[d] = nxt
        A.release(FG, selr, H0, ch, ct1, ct2, FS, *tm1, *tm2, *ST[0], *ST[1])
        if self.cfg.get("ssm_stop", 99) <= 5:
            return
        CT = []
        for d in range(2):
            pair_ = []
            for nm in ("ssm_c_re", "ssm_c_im"):
                cst = A.alloc("cst", 8 * 128 * 4)
                csv = cst.ap(F32, "p (c e) -> p c e", c=8)
                src = di[nm][j2, d].rearrange("(gc gl) h p -> (gl h) gc p", gl=8)
                self.dma("sp", csv[:, :, 0:64], src, "cst", [], [cst.o])
                self.dma("sp", csv[:, :, 64:128], src, "cst", [], [cst.o])
                PT_ = self.PB[0]
                pob = [self.BK[0], self.BK[1]]
                for gc in range(8):
                    self.tr(PT_[:, gc * 128:(gc + 1) * 128], csv[:, gc, :], idf, [cst.o, self.identf.o], [pob[gc // 4]])
                ct = A.alloc("ct_%s%d" % (nm[-2:], d), 2048)
                for par in range(2):
                    rows = slice(64 * par, 64 * par + 64)
                    self.cp("act" if par else "dve", ct.ap(F32, "p (q h) -> p q h", q=32)[rows],
                            PT_[rows, :].rearrange("p (q two h) -> p q two h", two=2, h=16)[:, :, par, :], pob, [ct.o])
                A.release(cst)
                pair_.append(ct)
            CT.append(pair_)
        if self.cfg.get("ssm_stop", 99) <= 5.3:
            return
        WC = [A.alloc("WC%d" % d, 32 * 2 * 128 * 2) for d in range(2)]
        WCv = [w.ap(BF16, "p (q r e) -> p q r e", q=32, r=2) for w in WC]
        g1, g2 = A.alloc("g1", 4096), A.alloc("g2", 4096)
        g3, g4 = A.alloc("g3", 4096), A.alloc("g4", 4096)

        def to_parity(src, name):
            r = A.alloc(name, 8 * 32 * 4)
            sv = src.ap(F32, "p (k q two) -> p k q two", k=8, two=2)
            rv = r.ap(F32, "p (k q) -> p k q", k=8)
            for par in range(2):
                rows = slice(64 * par, 64 * par + 64)
                self.cp("dve", rv[rows], sv[rows, :, :, par], [src.o], [r.o])
            return r
        for d in range(2):
            prs, pis = to_parity(PWC[d][0], "pcs_re"), to_parity(PWC[d][1], "pcs_im")
            pr_ = prs.ap(F32, "p (k q) -> p k q", k=8)
            pi_ = pis.ap(F32, "p (k q) -> p k q", k=8)
            cr_ = CT[d][0].ap(F32, "p (q h) -> p q h", q=32)
            ci_ = CT[d][1].ap(F32, "p (q h) -> p q h", q=32)
            rd = [prs.o, pis.o, CT[d][0].o, CT[d][1].o]
            for pb in range(4):
                qs_ = slice(pb * 8, pb * 8 + 8)
                Cr = cr_[:, qs_, :].unsqueeze(2).to_broadcast([128, 8, 8, 16])
                Ci = ci_[:, qs_, :].unsqueeze(2).to_broadcast([128, 8, 8, 16])
                Pr = pr_[:, :, qs_].rearrange("p j q -> p q j").unsqueeze(3).to_broadcast([128, 8, 8, 16])
                Pi = pi_[:, :, qs_].rearrange("p j q -> p q j").unsqueeze(3).to_broadcast([128, 8, 8, 16])
                a1 = g1.ap(F32, "p (q j h) -> p q j h", q=8, j=8)
                a2 = g2.ap(F32, "p (q j h) -> p q j h", q=8, j=8)
                a3 = g3.ap(F32, "p (q j h) -> p q j h", q=8, j=8)
                a4 = g4.ap(F32, "p (q j h) -> p q j h", q=8, j=8)
                o0 = WCv[d][:, qs_, 0, :].rearrange("p q (j h) -> p q j h", j=8)
                o1 = WCv[d][:, qs_, 1, :].rearrange("p q (j h) -> p q j h", j=8)
                self.tt("pool", a2, Ci, Pi, ALU.mult, rd, [g2.o])
                self.tt("dve", a1, Cr, Pr, ALU.mult, rd, [g1.o])
                self.tt("pool", a4, Ci, Pr, ALU.mult, rd, [g4.o])
                self.tt("dve", a3, Cr, Pi, ALU.mult, rd, [g3.o])
                self.tt("dve", o0, a1, a2, ALU.subtract, [g1.o, g2.o], [WC[d].o])
                self.stt("dve", o1, a3, -1.0, a4, ALU.mult, ALU.subtract, [g3.o, g4.o], [WC[d].o])
            A.release(prs, pis)
        A.release(CT[0][0], CT[0][1], CT[1][0], CT[1][1], PWC[0][0], PWC[0][1], PWC[1][0], PWC[1][1])
        if self.cfg.get("ssm_stop", 99) <= 6:
            return
        T = A.alloc("T", 64 * 128 * 2)
        Tv = T.ap(BF16, "p (g e) -> p g e", g=64)
        msk = A.alloc("msk", 2 * 128 * 4 + 64)
        mi = A.alloc("mski", 128 * 4 + 64)
        miv = mi.ap(I32)[:, 0:128]
        pjv = mi.ap(I32)[:, 128:129]
        self.P.op("pool", lambda e: e.iota(miv, pattern=[[1, 128]], base=0, channel_multiplier=0), writes=[mi.o])
        self.P.op("pool", lambda e: e.iota(pjv, pattern=[[0, 1]], base=0, channel_multiplier=1), reads=[mi.o], writes=[mi.o])
        self.ts("dve", mi.ap(I32)[:, 0:129], mi.ap(I32)[:, 0:129], 4, None, ALU.arith_shift_right, None, [mi.o], [mi.o])
        cjf = msk.ap()[:, 0:128]
        pjf = msk.ap()[:, 256:257]
        self.cp("dve", cjf, miv, [mi.o], [msk.o])
        self.cp("dve", pjf, pjv, [mi.o, msk.o], [msk.o])
        Mf, Mb = msk.ap()[:, 0:128], msk.ap()[:, 128:256]
        self.ts("dve", Mb, cjf, pjf, None, ALU.is_le, None, [msk.o], [msk.o])
        self.ts("dve", Mf, cjf, pjf, None, ALU.is_ge, None, [msk.o], [msk.o])
        dcol = A.alloc("dcol", 256)
        for i in range(8):
            self.dma("sp", dcol.ap()[16 * i:16 * i + 16, 0:64], di["ssm_d"][j2].rearrange("(g h) -> h g", h=16), "dcol", [], [dcol.o])
        sst = self.cfg.get("ssm_stop", 99)
        if sst <= 6.2:
            return
        xb = [A.alloc("xb%d" % d, 4 * 2 * 128 * 2) for d in range(2)]
        xbv = [x_.ap(BF16, "p (q r e) -> p q r e", q=4, r=2) for x_ in xb]
        QXs, BXs = [], []
        for d in range(2):
            QXs.append((to_parity(QXB[d][0], "qxs_re%d" % d), to_parity(QXB[d][1], "qxs_im%d" % d)))
            bs_ = []
            for src in (BR[d], BI[d]):
                r = A.alloc("bxs", 32 * 16 * 4)
                sv = src.ap(F32, "p (q two h) -> p q two h", two=2, h=16)
                rv = r.ap(F32, "p (q h) -> p q h", q=32)
                for par in range(2):
                    rows = slice(64 * par, 64 * par + 64)
                    self.cp("dve", rv[rows], sv[rows, :, par, :], [src.o], [r.o])
                bs_.append(r)
            BXs.append(bs_)
            A.release(QXB[d][0], QXB[d][1], BR[d], BI[d])
        ta, tb_, dfull = A.alloc("ta", 4096), A.alloc("tb", 4096), A.alloc("dfull", 4096)
        for gb in range(8):
            for d in range(2):
                qr = QXs[d][0].ap(F32, "p (k q) -> p k q", k=8)
                qi = QXs[d][1].ap(F32, "p (k q) -> p k q", k=8)
                brv = BXs[d][0].ap(F32, "p (q h) -> p q h", q=32)
                biv = BXs[d][1].ap(F32, "p (q h) -> p q h", q=32)
                rd = [QXs[d][0].o, QXs[d][1].o, BXs[d][0].o, BXs[d][1].o]
                qs_ = slice(gb * 4, gb * 4 + 4)
                Er = qr[:, :, qs_].rearrange("p i q -> p q i").unsqueeze(3).to_broadcast([128, 4, 8, 16])
                Ei = qi[:, :, qs_].rearrange("p i q -> p q i").unsqueeze(3).to_broadcast([128, 4, 8, 16])
                Br_ = brv[:, qs_, :].unsqueeze(2).to_broadcast([128, 4, 8, 16])
                Bi_ = biv[:, qs_, :].unsqueeze(2).to_broadcast([128, 4, 8, 16])
                a1 = g1.ap(F32, "p (q j h) -> p q j h", q=8, j=8)[:, 0:4]
                a2 = g2.ap(F32, "p (q j h) -> p q j h", q=8, j=8)[:, 0:4]
                a3 = g3.ap(F32, "p (q j h) -> p q j h", q=8, j=8)[:, 0:4]
                a4 = g4.ap(F32, "p (q j h) -> p q j h", q=8, j=8)[:, 0:4]
                o0 = xbv[d][:, :, 0, :].rearrange("p q (j h) -> p q j h", j=8)
                o1 = xbv[d][:, :, 1, :].rearrange("p q (j h) -> p q j h", j=8)
                self.tt("pool", a2, Ei, Bi_, ALU.mult, rd, [g2.o])
                self.tt("dve", a1, Er, Br_, ALU.mult, rd, [g1.o])
                self.tt("pool", a4, Ei, Br_, ALU.mult, rd, [g4.o])
                self.tt("dve", a3, Er, Bi_, ALU.mult, rd, [g3.o])
                self.tt("dve", o0, a1, a2, ALU.subtract, [g1.o, g2.o], [xb[d].o])
                self.tt("dve", o1, a3, a4, ALU.add, [g3.o, g4.o], [xb[d].o])
            if sst <= 6.4:
                continue
            for d in range(2):
                PT_ = self.PB[d]
                for gl in range(8):
                    g = gb * 8 + gl
                    par, pl = g % 2, gl // 2
                    rows = slice(64 * par, 64 * par + 64)
                    cidx = par * 4 + pl
                    for r_ in range(2):
                        self.mm(PT_[:, cidx * 128:(cidx + 1) * 128], xbv[d][rows, pl, r_, :], WCv[d][rows, g // 2, r_, :], r_ == 0, r_ == 1,
                                [xb[d].o, WC[d].o], [self.BK[2 * d + par]])
            if sst <= 6.6:
                continue
            t3 = lambda r: r.ap(F32, "p (g e) -> p g e", g=8)
            self.tt("dve", t3(ta), self.PB[0][:, :].rearrange("p (g e) -> p g e", g=8), Mf.unsqueeze(1).to_broadcast([128, 8, 128]), ALU.mult,
                    [self.BK[0], self.BK[1], msk.o], [ta.o])
            self.tt("dve", t3(tb_), self.PB[1][:, :].rearrange("p (g e) -> p g e", g=8), Mb.unsqueeze(1).to_broadcast([128, 8, 128]), ALU.mult,
                    [self.BK[2], self.BK[3], msk.o], [tb_.o])
            self.tt("pool", ta.ap(), ta.ap(), tb_.ap(), ALU.add, [ta.o, tb_.o], [ta.o])
            for par in range(2):
                dsl = dcol.ap()[:, gb * 8 + par:gb * 8 + 8:2]
                dfp = dfull.ap()[:, 0:512].rearrange("p (g e) -> p g e", g=4)
                self.tt("pool", dfp, idf.unsqueeze(1).to_broadcast([128, 4, 128]), dsl.unsqueeze(2).to_broadcast([128, 4, 128]), ALU.mult,
                        [self.identf.o, dcol.o], [dfull.o])
                self.tt("pool", Tv[:, gb * 8 + par:gb * 8 + 8:2, :], t3(ta)[:, par * 4:(par + 1) * 4, :], dfp, ALU.add, [ta.o, dfull.o], [T.o])
        A.release(msk, mi, dcol, ta, tb_, dfull, g1, g2, g3, g4, *xb,
                  QXs[0][0], QXs[0][1], QXs[1][0], QXs[1][1], *BXs[0], *BXs[1])
        if self.cfg.get("ssm_stop", 99) <= 7:
            return
        wg = [A.alloc("wg%d" % k, 8 * 512 * 2) for k in range(2)]

        def load_wg(np_):
            ca, cb_ = slice(512 * np_, 512 * np_ + 512), slice(1024 + 512 * np_, 1536 + 512 * np_)
            self.dma("pool", wg[0].ap(BF16, "p (k n) -> p k n", k=8), di["ssm_w_glu"][j2][:, ca].rearrange("(kc k) n -> k kc n", k=128), "wg0", [], [wg[0].o])
            self.dma("pool", wg[1].ap(BF16, "p (k n) -> p k n", k=8), di["ssm_w_glu"][j2][:, cb_].rearrange("(kc k) n -> k kc n", k=128), "wg1", [], [wg[1].o])
        if not self.small:
            load_wg(0)
        Gm = A.alloc("Gm", 16384)
        Gv = Gm.ap(BF16, "p (j g h) -> p j g h", j=8, g=64)
        gel = [A.alloc("gel%d" % k, 2048) for k in range(2)]
        incol = ((0, 33, 66), (1, 34, 67))
        for gb in range(8):
            pr = gb % 3
            PY = self.PB[pr]
            pob = [self.BK[2 * pr], self.BK[2 * pr + 1]]
            for gl in range(8):
                g = gb * 8 + gl
                par, q_ = g % 2, g // 2
                rows = slice(64 * par, 64 * par + 64)
                o_ = [pob[par]]
                cidx = par * 4 + gl // 2
                self.mm(PY[:, cidx * 128:(cidx + 1) * 128], Tv[:, g, :], VAv[:, g, :], True, False, [T.o, VA.o], o_)
                for d in range(2):
                    for r_ in range(2):
                        for si, (a, b) in enumerate(segs):
                            ic = incol[d][si]
                            self.mm(PY[:, cidx * 128 + a:cidx * 128 + b], WCv[d][rows, q_, r_, :], S4[d][rows, r_, q_, ic:ic + (b - a)], False,
                                    d == 1 and r_ == 1, [WC[d].o, S[d].o], o_)
            ge = gel[gb % 2]
            self.act(ge.ap(BF16), PY[:, :], AF.Gelu_apprx_tanh, pob, [ge.o])
            bk = 6 + gb % 2
            psb = self.bank(bk).bitcast(BF16)
            for gl in range(8):
                self.tr(psb[:, gl * 128:(gl + 1) * 128], ge.ap(BF16)[:, gl * 128:(gl + 1) * 128], idb, [ge.o, self.identb.o], [self.BK[bk]])
            ps4_ = psb.rearrange("p (g j h) -> p j g h", g=8, j=8)
            for par in range(2):
                self.cp("dve" if par else "act", Gv[:, :, gb * 8 + par:gb * 8 + 8:2, :], ps4_[:, :, par * 4:(par + 1) * 4, :], [self.BK[bk]], [Gm.o])
        A.release(T, VA, *S, *WC, *gel, M64[0][1], M64[1][1], small)
        GT = A.alloc("GT", 16384)
        self.transposeT(Gm, GT)
        A.release(Gm)
        GTv = GT.ap(BF16, "p (k t) -> p k t", k=8)
        if self.cfg.get("ssm_stop", 99) <= 8:
            return
        G1 = self.load_mod(2, "modG")
        bg = A.alloc("bg", 2048 * 4)
        bgb = A.alloc("bgb", 2048 * 2)
        self.dma("sp", bg.ap()[0:1, :], di["ssm_b_glu"][j2].partition_broadcast(1), "bg", [], [bg.o])
        self.cp("dve", bgb.ap(BF16)[0:1, :], bg.ap()[0:1, :], [bg.o], [bgb.o])
        sig = [A.alloc("sig%d" % k, 2048) for k in range(2)]
        gt_ = [A.alloc("gtmp%d" % k, 2048) for k in range(2)]
        ones = self.onesb.ap(BF16)
        for np_ in range(2):
            ca, cb_ = slice(512 * np_, 512 * np_ + 512), slice(1024 + 512 * np_, 1536 + 512 * np_)
            wva = wg[0].ap(BF16, "p (k n) -> p k n", k=8)
            wvb = wg[1].ap(BF16, "p (k n) -> p k n", k=8)
            if np_ == 1:
                load_wg(1)
            for i in range(8):
                ba, bb = (2 * i) % 6, (2 * i + 1) % 6
                for (bk, wv, wr, cc) in ((ba, wva, wg[0], ca), (bb, wvb, wg[1], cb_)):
                    self.mm(self.bank(bk), ones[0:1, 0:128], bgb.ap(BF16)[0:1, cc], True, False, [self.onesb.o, bgb.o], [self.BK[bk]])
                    for kc in range(8):
                        self.mm(self.bank(bk), GTv[:, kc, i * 128:(i + 1) * 128], wv[:, kc, :], False, kc == 7, [GT.o, wr.o], [self.BK[bk]])
                sg, gt1 = sig[i % 2], gt_[i % 2]
                self.act(sg.ap(), self.bank(bb), AF.Sigmoid, [self.BK[bb]], [sg.o])
                self.tt("dve", gt1.ap(), self.bank(ba), sg.ap(), ALU.mult, [self.BK[ba], sg.o], [gt1.o])
                self.tt("pool", gt1.ap(), gt1.ap(), G1.ap()[:, ca], ALU.mult, [gt1.o, G1.o], [gt1.o])
                self.tt("pool", X[:, i, ca], X[:, i, ca], gt1.ap(), ALU.add, [self.XO[i], gt1.o], [self.XO[i]])
        A.release(GT, G1, bg, bgb, *wg, *sig, *gt_)
        tcb = [A.alloc("tcomb%d" % k, 1024) for k in range(2)]
        cntf = 0
        for d in range(2):
            for s_ in range(2):
                bk = 6 + cntf % 2
                tc_ = tcb[cntf % 2]
                cntf += 1
                for r_ in range(2):
                    self.tr(self.bank(bk)[0:32, r_ * 128:(r_ + 1) * 128], FINv[d][:, r_, :, s_], idf, [FIN[d].o, self.identf.o], [self.BK[bk]])
                self.cp("dve", tc_.ap()[0:32, 0:256].rearrange("q (n r) -> q r n", r=2),
                        self.bank(bk)[0:32, 0:256].rearrange("q (r n) -> q r n", r=2), [self.BK[bk]], [tc_.o])
                self.dma("sp", self.do["ossm"][s_, j2, d].rearrange("(q two) p r -> q (two p r)", two=2), tc_.ap()[0:32, 0:256], "ossm",
                         [tc_.o], [], is_output=True)
        A.release(*tcb)
        A.release(*FIN)


def _core_inputs(inp, core):
    b, r = core // 4, core % 4
    f = lambda a: np.ascontiguousarray(np.asarray(a, dtype=np.float32))
    d = {}
    d["xp"] = f(inp["x_prompt"][2 * core:2 * core + 2].reshape(512, D))
    d["xs"] = f(inp["x_sample"][b, 512 * r:512 * (r + 1)])
    d["cvec"] = f(np.stack([inp["c_ctx"], inp["c"][b]], 0))
    d["cache_ckv"] = f(inp["cache_mla_ckv"][b])
    d["cache_krope"] = f(inp["cache_mla_krope"][b])
    d["cache_gk"] = f(inp["cache_gqa_k"][b].reshape(2, 256, 128))
    d["cache_gv"] = f(inp["cache_gqa_v"][b].reshape(2, 256, 128))
    d["state_ssm"] = f(inp["state_ssm"][b])
    d["w_mod"] = f(inp["w_mod"][:, :, 1536 * r:1536 * (r + 1)])
    d["b_mod"] = f(inp["b_mod"][:, 1536 * r:1536 * (r + 1)])
    for k in ("norm1_g", "norm2_g", "attn_w_in", "attn_qa_norm_g", "attn_kva_norm_g", "attn_w_uq", "attn_w_ukv",
              "attn_mla_q_norm_g", "attn_mla_k_norm_g", "attn_gqa_q_norm_g", "attn_gqa_k_norm_g", "attn_w_out",
              "ssm_a_re", "ssm_a_im", "ssm_log_dt", "ssm_b_re", "ssm_b_im", "ssm_c_re", "ssm_c_im", "ssm_d", "ssm_w_glu", "ssm_b_glu",
              "ffn_w_up", "ffn_conv_w", "ffn_conv_b", "ffn_w_down"):
        d[k] = f(inp[k])
    qpos = np.zeros((128, 8, 2), np.float32)
    for p in range(64, 128):
        for i in range(8):
            t = 512 * r + 8 * (p - 64) + i
            qpos[p, i, 0] = t // 64
            qpos[p, i, 1] = t % 64
    kpos = np.zeros((128, 16, 2), np.float32)
    for q in range(128):
        for u in range(16):
            t = 128 * u + q
            kpos[q, u, 0] = t // 64
            kpos[q, u, 1] = t % 64
    sel = np.zeros((8, 2), np.float32)
    if r > 0:
        sel[2 * (r - 1) + 1, 0] = 1.0
    if r < 3:
        sel[2 * (r + 1), 1] = 1.0
    selr = np.zeros((128, 8), np.float32)
    selr[:, r] = 1.0
    selr[:, 4 + r] = 1.0
    d["qpos"], d["kpos"], d["selhalo"], d["selr"] = qpos, kpos, sel, selr
    return d


_NC_CACHE = {}


def kernel(cfg=None, **inp):
    inp = {k: np.asarray(v) for k, v in inp.items()}
    key = repr(sorted((cfg or {}).items()))
    if key not in _NC_CACHE:
        _NC_CACHE[key] = Builder(cfg).build()
    nc = _NC_CACHE[key]
    in_maps = [_core_inputs(inp, c) for c in range(NCORES)]
    if (cfg or {}).get("small"):
        for m in in_maps:
            for k in ("w_mod", "ffn_w_up", "ffn_w_down", "ssm_w_glu"):
                m.pop(k)
    res = run_bass_kernel_spmd(nc, in_maps, core_ids=list(range(NCORES)))
    R = res.results
    yp = np.concatenate([R[c]["yp"].reshape(2, 256, D) for c in range(NCORES)], 0)
    ys = np.stack([np.concatenate([R[4 * b + r]["ys"] for r in range(4)], 0) for b in range(2)], 0)
    ockv = np.concatenate([R[c]["ockv"] for c in range(NCORES)], 0)
    okr = np.concatenate([R[c]["okrope"] for c in range(NCORES)], 0)
    ogk = np.concatenate([R[c]["ogk"].reshape(2, 2, 256, 2, 64) for c in range(NCORES)], 0)
    ogv = np.concatenate([R[c]["ogv"].reshape(2, 2, 256, 2, 64) for c in range(NCORES)], 0)
    ossm = np.concatenate([R[c]["ossm"] for c in range(NCORES)], 0)
    return (yp.astype(np.float32), ys.astype(np.float32), ockv.astype(np.float32), okr.astype(np.float32),
            ogk.astype(np.float32), ogv.astype(np.float32), ossm.astype(np.float32))
```

```python
import math
import contextlib
import numpy as np
import concourse.bass as bass
import concourse.mybir as mybir
from concourse.bass_utils import run_bass_kernel_spmd

F32 = mybir.dt.float32
BF16 = mybir.dt.bfloat16
I32 = mybir.dt.int32
AF = mybir.ActivationFunctionType
ALU = mybir.AluOpType
AX = mybir.AxisListType

D = 1024
DFF = 2816
EPS = 1e-6
NCORES = 8


class Obj:
    __slots__ = ("name", "last_w", "readers", "excl")

    def __init__(self, name, excl=False):
        self.name = name
        self.last_w = None
        self.readers = []
        self.excl = excl


class Prog:
    ENGS = ("pe", "act", "dve", "pool", "sp")

    def __init__(self, nc):
        self.nc = nc
        self.q = {e: [] for e in self.ENGS}
        self.cnt = {e: 0 for e in self.ENGS}
        self.dmacnt = {}
        self.waited = {e: {} for e in self.ENGS}
        self.semkeys = []
        self.out_events = []

    def _semkey(self, k):
        if k not in self.semkeys:
            self.semkeys.append(k)
        return k

    def _deps(self, eng, reads, writes, pe_accum=False):
        need = {}

        def add(ev, same_ok=False):
            if ev is None:
                return
            k, v = ev
            if same_ok and k == eng:
                return
            if need.get(k, 0) < v:
                need[k] = v
        for o in reads:
            add(o.last_w)
            if o.excl:
                for r in o.readers:
                    add(r, same_ok=True)
        for o in writes:
            add(o.last_w, same_ok=pe_accum)
            for r in o.readers:
                add(r, same_ok=True)
        waits = []
        for k, v in need.items():
            if self.waited[eng].get(k, 0) < v:
                self.waited[eng][k] = v
                waits.append((k, v))
        return waits

    def _commit(self, ev, reads, writes):
        for o in reads:
            o.readers.append(ev)
            if len(o.readers) > 64:
                best = {}
                for k, v in o.readers:
                    if best.get(k, 0) < v:
                        best[k] = v
                o.readers = list(best.items())
        for o in writes:
            o.last_w = ev
            o.readers = []

    def op(self, eng, fn, reads=(), writes=(), pe_accum=False, same_ok=False):
        reads = [o for o in reads if o is not None]
        writes = [o for o in writes if o is not None]
        waits = self._deps(eng, reads, writes, pe_accum)
        if same_ok:
            waits = [(k, v) for (k, v) in waits if k != eng]
        self.cnt[eng] += 1
        ev = (self._semkey(eng), self.cnt[eng])
        self.q[eng].append((waits, fn, [(eng, 1)]))
        self._commit(ev, reads, writes)
        return ev

    def dma(self, eng, fn, key, reads=(), writes=(), is_output=False, inc=16):
        reads = [o for o in reads if o is not None]
        writes = [o for o in writes if o is not None]
        waits = self._deps(eng, reads, writes)
        sk = self._semkey(("dma", key))
        self.dmacnt[key] = self.dmacnt.get(key, 0) + 1
        ev = (sk, inc * self.dmacnt[key])
        self.q[eng].append((waits, fn, [(sk, inc)]))
        self._commit(ev, reads, writes)
        if is_output:
            self.out_events.append(ev)
        return ev

    def finish(self, eng="sp"):
        need = {}
        for k, v in self.out_events:
            need[k] = max(need.get(k, 0), v)
        for e in ("pe", "act", "dve", "pool"):
            if self.cnt[e]:
                need[e] = self.cnt[e]
        waits = [(k, v) for k, v in need.items() if self.waited[eng].get(k, 0) < v]
        self.q[eng].append((waits, None, []))

    def emit(self):
        nc = self.nc
        comp = ("pe", "act", "dve", "pool")
        miles = {e: set() for e in comp}
        for e in self.ENGS:
            for waits, fn, incs in self.q[e]:
                for k, v in waits:
                    if k in miles:
                        miles[k].add(v)
        rank = {e: {v: i + 1 for i, v in enumerate(sorted(miles[e]))} for e in comp}
        with contextlib.ExitStack() as st:
            sems = {}
            for i, k in enumerate(self.semkeys):
                sems[k] = st.enter_context(nc.semaphore("s%d" % i))
            block = st.enter_context(nc.Block())
            q = self.q

            def run(e, name):
                n = 0
                for waits, fn, incs in q[name]:
                    for k, v in waits:
                        if k in rank:
                            v = rank[k][v]
                        e.wait_ge(sems[k], v)
                    if fn is None:
                        continue
                    ins = fn(e)
                    for k, inc in incs:
                        if k in rank:
                            n += 1
                            if n in rank[k]:
                                ins.then_inc(sems[k], 1)
                        else:
                            ins.then_inc(sems[k], inc)

            @block.tensor
            def _(e):
                run(e, "pe")

            @block.scalar
            def _(e):
                run(e, "act")

            @block.vector
            def _(e):
                run(e, "dve")

            @block.gpsimd
            def _(e):
                run(e, "pool")

            @block.sync
            def _(e):
                run(e, "sp")


class Reg:
    def __init__(self, arena, off, nbytes, name, req=None):
        self.arena = arena
        self.off = off
        self.nbytes = nbytes
        self.req = req or nbytes
        self.o = Obj(name)
        self.name = name

    def ap(self, dt=F32, pat=None, **kw):
        a = self.arena[:, self.off // 4:(self.off + self.req) // 4]
        if dt != F32:
            a = a.bitcast(dt)
        if pat is not None:
            a = a.rearrange(pat, **kw)
        return a


class Arena:
    def __init__(self, arena_ap, nbytes):
        self.arena = arena_ap
        self.free = [(0, nbytes)]
        self.hist = []

    def alloc(self, name, nbytes):
        req = nbytes
        nbytes = (nbytes + 63) // 64 * 64
        cands = [(b - a, idx) for idx, (a, b) in enumerate(self.free) if b - a >= nbytes]
        for _, idx in sorted(cands)[:1]:
            a, b = self.free[idx]
            if b - a >= nbytes:
                self.free[idx] = (a + nbytes, b)
                if self.free[idx][0] == self.free[idx][1]:
                    self.free.pop(idx)
                r = Reg(self.arena, a, nbytes, name, req)
                keep = []
                for (ha, hb, ho) in self.hist:
                    if ha < a + nbytes and hb > a:
                        if ho.last_w is not None:
                            r.o.readers.append(ho.last_w)
                        r.o.readers.extend(ho.readers)
                        if ha < a:
                            keep.append((ha, a, ho))
                        if hb > a + nbytes:
                            keep.append((a + nbytes, hb, ho))
                    else:
                        keep.append((ha, hb, ho))
                self.hist = keep
                return r
        raise RuntimeError("SBUF arena full allocating %s (%d bytes); free=%s" % (name, nbytes, self.free))

    def release(self, *regs):
        for r in regs:
            self.hist.append((r.off, r.off + r.nbytes, r.o))
            self.free.append((r.off, r.off + r.nbytes))
        self.free.sort()
        merged = []
        for a, b in self.free:
            if merged and merged[-1][1] == a:
                merged[-1] = (merged[-1][0], b)
            else:
                merged.append((a, b))
        self.free = merged


IN_SPECS = [
    ("xp", [512, D]), ("xs", [512, D]), ("cvec", [2, D]),
    ("cache_ckv", [2, 256, 256]), ("cache_krope", [2, 256, 32]), ("cache_gk", [2, 256, 128]), ("cache_gv", [2, 256, 128]),
    ("state_ssm", [2, 2, 64, 64, 2]),
    ("norm1_g", [4, D]), ("norm2_g", [4, D]), ("w_mod", [4, D, 1536]), ("b_mod", [4, 1536]),
    ("attn_w_in", [2, D, 1440]), ("attn_qa_norm_g", [2, 384]), ("attn_kva_norm_g", [2, 256]),
    ("attn_w_uq", [2, 384, 768]), ("attn_w_ukv", [2, 256, 1024]),
    ("attn_mla_q_norm_g", [2, 96]), ("attn_mla_k_norm_g", [2, 96]), ("attn_gqa_q_norm_g", [2, 64]), ("attn_gqa_k_norm_g", [2, 64]),
    ("attn_w_out", [2, D, D]),
    ("ssm_a_re", [2, 2, 64, 64]), ("ssm_a_im", [2, 2, 64, 64]), ("ssm_log_dt", [2, 2, 64]),
    ("ssm_b_re", [2, 2, 64, 64, 16]), ("ssm_b_im", [2, 2, 64, 64, 16]),
    ("ssm_c_re", [2, 2, 64, 16, 64]), ("ssm_c_im", [2, 2, 64, 16, 64]),
    ("ssm_d", [2, D]), ("ssm_w_glu", [2, D, 2 * D]), ("ssm_b_glu", [2, 2 * D]),
    ("ffn_w_up", [4, D, 2 * DFF]), ("ffn_conv_w", [4, 3, 2 * DFF]), ("ffn_conv_b", [4, 2 * DFF]), ("ffn_w_down", [4, DFF, D]),
    ("qpos", [128, 8, 2]), ("kpos", [128, 16, 2]), ("selhalo", [8, 2]), ("selr", [128, 8]),
]
OUT_SPECS = [
    ("yp", [512, D]), ("ys", [512, D]),
    ("ockv", [2, 2, 256, 256]), ("okrope", [2, 2, 256, 32]), ("ogk", [2, 2, 256, 128]), ("ogv", [2, 2, 256, 128]),
    ("ossm", [2, 2, 2, 64, 64, 2]),
]


class Builder:
    def __init__(self, cfg=None):
        self.cfg = cfg or {}
        self.nc = bass.Bass("TRN2", target_bir_lowering=False)
        nc = self.nc
        self.small = bool(self.cfg.get("small"))
        big = ("w_mod", "ffn_w_up", "ffn_w_down", "ssm_w_glu")
        self.di = {n: nc.dram_tensor(n, s, F32, kind="ExternalInput").ap() for n, s in IN_SPECS if not (self.small and n in big)}
        self.do = {n: nc.dram_tensor(n, s, F32, kind="ExternalOutput").ap() for n, s in OUT_SPECS}
        self.dobj = {}
        self.modd = nc.dram_tensor("modd", [4, 2, 6 * D], F32, kind="Internal").ap()
        self.msend = nc.dram_tensor("msend", [8, 1536], F32, kind="Internal").ap()
        self.mrecv = nc.dram_tensor("mrecv", [32, 1536], F32, kind="Internal").ap()
        self.dobj["modd"] = [Obj("modd%d" % l) for l in range(4)]
        self.hsend = [nc.dram_tensor("hsend%d" % l, [2, D], BF16, kind="Internal").ap() for l in range(4)]
        self.hrecv = [nc.dram_tensor("hrecv%d" % l, [8, D], BF16, kind="Internal").ap() for l in range(4)]
        self.kvsend = [[nc.dram_tensor("kvsend%d_%d" % (l, hf), [256, 544], F32, kind="Internal").ap() for hf in range(2)] for l in range(2)]
        self.kvrecv = [[nc.dram_tensor("kvrecv%d_%d" % (l, hf), [1024, 544], F32, kind="Internal").ap() for hf in range(2)] for l in range(2)]
        self.ssend = [nc.dram_tensor("ssend%d" % l, [128, 128], F32, kind="Internal").ap() for l in range(2)]
        self.srecv = [nc.dram_tensor("srecv%d" % l, [512, 128], F32, kind="Internal").ap() for l in range(2)]

    def mm(self, out, lhsT, rhs, start, stop, reads, writes, **kw):
        self.P.op("pe", lambda e: e.matmul(out, lhsT=lhsT, rhs=rhs, start=start, stop=stop, **kw),
                  reads=reads, writes=writes, pe_accum=True)

    def tr(self, out, in_, ident, reads, writes):
        self.P.op("pe", lambda e: e.transpose(out, in_, ident), reads=reads, writes=writes, pe_accum=True)

    def act(self, out, in_, func, reads, writes, **kw):
        self.P.op("act", lambda e: e.activation(out=out, in_=in_, func=func, **kw), reads=reads, writes=writes)

    def cp(self, eng, out, in_, reads, writes):
        if eng == "act":
            self.P.op("act", lambda e: e.copy(out=out, in_=in_), reads=reads, writes=writes)
        else:
            self.P.op(eng, lambda e: e.tensor_copy(out=out, in_=in_), reads=reads, writes=writes)

    def tt(self, eng, out, in0, in1, op, reads, writes, same_ok=False):
        self.P.op(eng, lambda e: e.tensor_tensor(out=out, in0=in0, in1=in1, op=op), reads=reads, writes=writes, same_ok=same_ok)

    def stt(self, eng, out, in0, scalar, in1, op0, op1, reads, writes, same_ok=False):
        self.P.op(eng, lambda e: e.scalar_tensor_tensor(out=out, in0=in0, scalar=scalar, in1=in1, op0=op0, op1=op1),
                  reads=reads, writes=writes, same_ok=same_ok)

    def ts(self, eng, out, in0, s1, s2, op0, op1, reads, writes):
        if op1 is None:
            self.P.op(eng, lambda e: e.tensor_scalar(out=out, in0=in0, scalar1=s1, scalar2=None, op0=op0), reads=reads, writes=writes)
        else:
            self.P.op(eng, lambda e: e.tensor_scalar(out=out, in0=in0, scalar1=s1, scalar2=s2, op0=op0, op1=op1), reads=reads, writes=writes)

    def recip(self, out, in_, reads, writes):
        self.P.op("dve", lambda e: e.reciprocal(out=out, in_=in_), reads=reads, writes=writes)

    def memset(self, eng, out, val, writes):
        self.P.op(eng, lambda e: e.memset(out, val), writes=writes)

    def dma(self, eng, out, in_, key, reads, writes, is_output=False):
        self.P.dma(eng, lambda e: e.dma_start(out=out, in_=in_), key, reads=reads, writes=writes, is_output=is_output)

    def bank(self, k):
        if k < 6:
            return self.PB[k // 2][:, (k % 2) * 512:(k % 2 + 1) * 512]
        return self.PS[k - 6][:, :]

    def build(self):
        nc = self.nc
        with contextlib.ExitStack() as st:
            AW = 53000
            arena_t = st.enter_context(nc.sbuf_tensor("arena", [128, AW], F32))
            self.PB = [st.enter_context(nc.psum_tensor("pb%d" % i, [128, 1024], F32)) for i in range(3)]
            self.PS = [st.enter_context(nc.psum_tensor("ps%d" % i, [128, 512], F32)) for i in range(2)]
            self.BK = [Obj("bank%d" % i, excl=True) for i in range(8)]
            self.A = Arena(arena_t, AW * 4)
            self.P = Prog(nc)
            self.consts()
            self.load_x()
            if not self.small:
                self.preamble()
            for l in range(4):
                if l >= self.cfg.get("nlayers", 4):
                    break
                self.layer_mod(l)
                if self.cfg.get("mixers", True):
                    if l % 2 == 0:
                        if not self.cfg.get("only_ssm"):
                            self.attention(l)
                    else:
                        self.ssm(l)
                if self.cfg.get("ffn", True):
                    self.ffn(l)
            self.store_x()
            self.P.finish()
            with nc.allow_non_contiguous_dma("small strided parameter loads"):
                self.P.emit()
        return nc

    def consts(self):
        A = self.A
        self.identf = A.alloc("identf", 512)
        self.identb = A.alloc("identb", 256)
        self.onesb = A.alloc("onesb", 256)
        self.epsc = A.alloc("epsc", 64)
        idf = self.identf.ap()
        self.memset("pool", idf, 0.0, [self.identf.o])
        self.P.op("pool", lambda e: e.affine_select(out=idf, in_=idf, pattern=[[-1, 128]], compare_op=ALU.not_equal,
                                                    fill=1.0, base=0, channel_multiplier=1),
                  reads=[self.identf.o], writes=[self.identf.o])
        self.cp("dve", self.identb.ap(BF16), idf, [self.identf.o], [self.identb.o])
        self.memset("dve", self.onesb.ap(BF16), 1.0, [self.onesb.o])
        self.onesf = A.alloc("onesf", 256)
        self.memset("dve", self.onesf.ap(), 1.0, [self.onesf.o])
        self.memset("dve", self.epsc.ap()[:, 0:1], EPS, [self.epsc.o])
        self.memset("dve", self.epsc.ap()[:, 1:2], -math.pi, [self.epsc.o])

    def load_x(self):
        self.X = self.A.alloc("X", 8 * D * 4)
        X = self.X.ap(F32, "p (i d) -> p i d", i=8)
        self.XO = [Obj("X%d" % i) for i in range(8)]
        self.dma("sp", X[0:64], self.di["xp"].rearrange("(q i) d -> q i d", i=8), "x0", [], self.XO)
        self.dma("sp", X[64:128], self.di["xs"].rearrange("(q i) d -> q i d", i=8), "x1", [], self.XO)

    def store_x(self):
        X = self.X.ap(F32, "p (i d) -> p i d", i=8)
        self.dma("sp", self.do["yp"].rearrange("(q i) d -> q i d", i=8), X[0:64], "y0", self.XO, [], is_output=True)
        self.dma("sp", self.do["ys"].rearrange("(q i) d -> q i d", i=8), X[64:128], "y1", self.XO, [], is_output=True)

    def preamble(self):
        A, P = self.A, self.P
        cnd = A.alloc("cnd", 64)
        cndb = A.alloc("cndb", 64)
        c3 = cnd.ap(F32, "p (k c) -> p k c", c=2)
        for r in range(2):
            self.P.dma("sp", (lambda r: lambda e: e.dma_start(out=c3[:, :, r], in_=self.di["cvec"][r].rearrange("(kc k) -> k kc", k=128)))(r),
                       "cnd", writes=[cnd.o])
        self.act(cnd.ap()[:, 0:16], cnd.ap()[:, 0:16], AF.Silu, [cnd.o], [cnd.o])
        cb3 = cndb.ap(BF16)[:, 0:16].rearrange("p (k c) -> p k c", c=2)
        self.cp("dve", cndb.ap(BF16)[:, 0:16], cnd.ap()[:, 0:16], [cnd.o], [cndb.o])
        NS = 1536
        bm = A.alloc("bm", 4 * NS * 4)
        mp = A.alloc("mp", 4 * NS * 4)
        ws = [A.alloc("wmod%d" % i, 8 * 512 * 2) for i in range(3)]
        self.dma("sp", bm.ap()[0:2, :], self.di["b_mod"].rearrange("l n -> (l n)").partition_broadcast(2), "bm", [], [bm.o])
        cnt = 0
        for l in range(4):
            for nt in range(3):
                w = ws[cnt % 3]
                wv = w.ap(BF16, "p (k n) -> p k n", k=8)
                self.dma("pool", wv, self.di["w_mod"][l][:, nt * 512:(nt + 1) * 512].rearrange("(kc k) n -> k kc n", k=128),
                         "wmod%d" % (cnt % 3), [], [w.o])
                bk = cnt % 2
                for kc in range(8):
                    self.mm(self.bank(bk)[0:2, :], cb3[:, kc, :], wv[:, kc, :], kc == 0, kc == 7, [cndb.o, w.o], [self.BK[bk]])
                c0 = l * NS + nt * 512
                self.tt("dve", mp.ap()[0:2, c0:c0 + 512], self.bank(bk)[0:2, :], bm.ap()[0:2, c0:c0 + 512], ALU.add, [self.BK[bk], bm.o], [mp.o])
                cnt += 1
        osd, orc = Obj("msend"), Obj("mrecv")
        self.dma("sp", self.msend.rearrange("(c l) n -> c (l n)", c=2), mp.ap()[0:2, :], "msd", [mp.o], [osd])
        msd, mrc = self.msend, self.mrecv
        P.dma("pool", lambda e: e.collective_compute("AllGather", ALU.bypass, replica_groups=[[0, 1, 2, 3], [4, 5, 6, 7]], ins=[msd.opt()], outs=[mrc.opt()]),
              "cc_mod", reads=[osd], writes=[orc], inc=1)
        for r in range(4):
            self.P.dma("sp", (lambda r: lambda e: e.dma_start(out=self.modd[:, :, r * NS:(r + 1) * NS].rearrange("l c j -> c l j"),
                                                                in_=mrc[r * 8:(r + 1) * 8, :].rearrange("(c l) j -> c l j", c=2)))(r),
                       "m3", reads=[orc], writes=self.dobj["modd"])
        A.release(cnd, cndb, bm, mp, *ws)

    def layer_mod(self, l):
        self.cur_l = l

    def load_mod(self, idx, name):
        l = self.cur_l
        r = self.A.alloc(name, D * 4)
        M = r.ap()
        self.dma("sp", M[0:64, :], self.modd[l, 0, idx * D:(idx + 1) * D].partition_broadcast(64), "mod" + name, [self.dobj["modd"][l]], [r.o])
        self.dma("sp", M[64:128, :], self.modd[l, 1, idx * D:(idx + 1) * D].partition_broadcast(64), "mod" + name, [self.dobj["modd"][l]], [r.o])
        return r

    def load_ab(self, which):
        l = self.cur_l
        base = 0 if which == 1 else 3
        Bt = self.load_mod(base, "modB")
        At = self.load_mod(base + 1, "modA")
        gb = self.A.alloc("gb", D * 4)
        self.dma("sp", gb.ap(), self.di["norm1_g" if which == 1 else "norm2_g"][l].partition_broadcast(128), "gb", [], [gb.o])
        self.stt("dve", At.ap(), At.ap(), 1.0, gb.ap(), ALU.add, ALU.mult, [At.o, gb.o], [At.o])
        self.A.release(gb)
        return At, Bt

    def norm_mod(self, which, dest, dobj, view=None):
        A = self.A
        At, Bt = self.load_ab(which)
        Aap, Bap = At.ap(), Bt.ap()
        X = self.X.ap(F32, "p (i d) -> p i d", i=8)
        st = A.alloc("nm_st", 64)
        junk = A.alloc("nm_junk", D * 4)
        tmp = [A.alloc("nm_tmp%d" % i, D * 4) for i in range(2)]
        ss = st.ap()[:, 0:8]
        rs = st.ap()[:, 8:16]
        self.memset("dve", ss, 0.0, [st.o])
        for i in range(8):
            self.act(junk.ap(), X[:, i, :], AF.Square, [self.XO[i], st.o], [junk.o, st.o], accum_out=ss[:, i:i + 1])
        self.act(rs, ss, AF.Sqrt, [st.o, self.epsc.o], [st.o], bias=self.epsc.ap()[:, 0:1], scale=1.0 / D)
        self.P.op("dve", lambda e: e.reciprocal(out=rs, in_=rs), reads=[st.o], writes=[st.o])
        for i in range(8):
            t = tmp[i % 2]
            self.stt("dve", t.ap(), X[:, i, :], rs[:, i:i + 1], Aap, ALU.mult, ALU.mult, [self.XO[i], st.o, At.o], [t.o])
            if view is None:
                self.tt("pool", dest(i), t.ap(), Bap, ALU.add, [t.o, Bt.o], [dobj])
            else:
                self.tt("pool", dest(i), view(t.ap()), view(Bap), ALU.add, [t.o, Bt.o], [dobj])
        A.release(st, junk, At, Bt, *tmp)

    def transposeT(self, H, HT):
        Hv = H.ap(BF16, "p (i d) -> p i d", i=8)
        HTv = HT.ap(BF16, "p (k t) -> p k t", k=8)
        idb = self.identb.ap(BF16)
        for kc in range(8):
            bk = 6 + kc % 2
            psb = self.bank(bk).bitcast(BF16)
            for i in range(8):
                self.tr(psb[:, i * 128:(i + 1) * 128], Hv[:, i, kc * 128:(kc + 1) * 128], idb, [H.o, self.identb.o], [self.BK[bk]])
            self.cp("act" if kc % 2 else "dve", HTv[:, kc, :], psb, [self.BK[bk]], [HT.o])

    def ffn(self, l):
        A = self.A
        P = self.P
        X = self.X.ap(F32, "p (i d) -> p i d", i=8)
        WA = [A.alloc("wua%d" % i, 4096) for i in range(3)]
        WB = [A.alloc("wub%d" % i, 4096) for i in range(3)]
        wup = self.di["ffn_w_up"][l]

        def load_wu(mq):
            sq = mq % 3
            self.dma("pool", WA[sq].ap(BF16, "p (k n) -> p k n", k=8), wup[:, mq * 256:(mq + 1) * 256].rearrange("(kc k) n -> k kc n", k=128),
                     "wua%d" % sq, [], [WA[sq].o])
            self.dma("pool", WB[sq].ap(BF16, "p (k n) -> p k n", k=8), wup[:, DFF + mq * 256:DFF + (mq + 1) * 256].rearrange("(kc k) n -> k kc n", k=128),
                     "wub%d" % sq, [], [WB[sq].o])
        load_wu(0)
        load_wu(1)
        NWD, NPRE = 6, 4
        WD = [A.alloc("wd%d" % i, 2048) for i in range(NWD)]
        wdn = self.di["ffn_w_down"][l]

        def load_wd(q):
            nh_, mg_ = q // 11, q % 11
            sq = q % NWD
            self.dma("pool", WD[sq].ap(BF16, "p (a n) -> p a n", a=2),
                     wdn[mg_ * 256:(mg_ + 1) * 256, nh_ * 512:(nh_ + 1) * 512].rearrange("(a f) n -> f a n", f=128), "wd%d" % sq, [], [WD[sq].o])
        H = A.alloc("H2", 16384)
        Hv = H.ap(BF16, "p (i d) -> p i d", i=8)
        self.norm_mod(2, lambda i: Hv[:, i, :], H.o)
        hs, hr_d = self.hsend[l], self.hrecv[l]
        ohs, ohr = Obj("hsend"), Obj("hrecv")
        self.dma("sp", hs[0:1, :], Hv[64:65, 0, :], "hs", [H.o], [ohs])
        self.dma("sp", hs[1:2, :], Hv[127:128, 7, :], "hs", [H.o], [ohs])
        P.dma("pool", lambda e: e.collective_compute("AllGather", ALU.bypass, replica_groups=[[0, 1, 2, 3], [4, 5, 6, 7]],
                                                     ins=[hs.opt()], outs=[hr_d.opt()]),
              "cc_h%d" % l, reads=[ohs], writes=[ohr], inc=1)
        HT = A.alloc("HT2", 16384)
        self.transposeT(H, HT)
        A.release(H)
        HTv = HT.ap(BF16, "p (k t) -> p k t", k=8)
        hr = A.alloc("hr", 2048)
        sel = A.alloc("sel", 64)
        hth = A.alloc("hth", 64)
        self.dma("sp", hr.ap(BF16)[0:8, :], hr_d, "hr", [ohr], [hr.o])
        self.dma("sp", sel.ap()[0:8, 0:2], self.di["selhalo"], "sel", [], [sel.o])
        selb = sel.ap(BF16)[:, 8:16]
        self.cp("dve", selb[0:8, 0:2], sel.ap()[0:8, 0:2], [sel.o], [sel.o])
        for kc in range(8):
            self.mm(self.bank(7)[:, kc * 2:kc * 2 + 2], hr.ap(BF16)[0:8, kc * 128:(kc + 1) * 128], selb[0:8, 0:2], True, True,
                    [hr.o, sel.o], [self.BK[7]])
        hthv = hth.ap(BF16)[:, 0:16].rearrange("p (k c) -> p k c", c=2)
        self.cp("dve", hth.ap(BF16)[:, 0:16], self.bank(7)[:, 0:16], [self.BK[7]], [hth.o])
        cw = A.alloc("cw", 44 * 3 * 4)
        cb = A.alloc("cb", 44 * 4)
        cwv = cw.ap(F32, "p (m t) -> p m t", t=3)
        cbv = cb.ap()
        for t in range(3):
            self.dma("sp", cwv[:, :, t], self.di["ffn_conv_w"][l, t].rearrange("(m f) -> f m", f=128), "cw", [], [cw.o])
        self.dma("sp", cbv[:, 0:44], self.di["ffn_conv_b"][l].rearrange("(m f) -> f m", f=128), "cb", [], [cb.o])
        AT = A.alloc("AT", 22 * 1024 * 2)
        ATv = AT.ap(BF16, "p (m t) -> p m t", m=22)
        zc = {z: [A.alloc("zc%s%d" % (z, i), 4096) for i in range(2)] for z in "ab"}
        sA = [A.alloc("sA%d" % i, 4096) for i in range(2)]
        pairc = 0
        hcnt = 0
        for mp in range(11):
            sl = mp % 3
            wav = WA[sl].ap(BF16, "p (k n) -> p k n", k=8)
            wbv = WB[sl].ap(BF16, "p (k n) -> p k n", k=8)
            if mp + 2 < 11:
                load_wu(mp + 2)
            elif mp == 9:
                for q in range(NPRE):
                    load_wd(q)
            for ml in range(2):
                m = mp * 2 + ml
                par = m % 2
                for z, wv, wreg, cidx in (("a", wav, WA[sl], m), ("b", wbv, WB[sl], 22 + m)):
                    pr = pairc % 3
                    pairc += 1
                    Z = self.PB[pr]
                    zobjs = [self.BK[2 * pr], self.BK[2 * pr + 1]]
                    for nt in range(2):
                        for kc in range(8):
                            self.mm(Z[:, nt * 512:(nt + 1) * 512], wv[:, kc, ml * 128:(ml + 1) * 128], HTv[:, kc, nt * 512:(nt + 1) * 512],
                                    kc == 0, kc == 7, [wreg.o, HT.o], [zobjs[nt]])
                    hc = 32 + 2 * (hcnt % 16)
                    hcnt += 1
                    Zh = self.bank(7)[:, hc:hc + 2]
                    for kc in range(8):
                        self.mm(Zh, wv[:, kc, ml * 128:(ml + 1) * 128], hthv[:, kc, :], kc == 0, kc == 7, [wreg.o, hth.o], [self.BK[7]])
                    zr = zc[z][par]
                    zv = zr.ap()
                    w0, w1, w2 = cwv[:, cidx, 0:1], cwv[:, cidx, 1:2], cwv[:, cidx, 2:3]
                    self.act(zv, Z[:, :], AF.Identity, zobjs + [cw.o, cb.o], [zr.o], scale=w1, bias=cbv[:, cidx:cidx + 1])
                    rd = zobjs + [cw.o, zr.o]

                    def tap(dst, src, w, extra=()):
                        self.stt("dve", dst, src, w, dst, ALU.mult, ALU.add, rd + list(extra), [zr.o], same_ok=True)
                    tap(zv[:, 128:1024], Z[:, 0:896], w0)
                    for (a, b) in ((1, 32), (33, 64), (65, 128)):
                        tap(zv[:, a:b], Z[:, 896 + a - 1:896 + b - 1], w0)
                    tap(zv[:, 64:65], Zh[:, 0:1], w0, [self.BK[7]])
                    tap(zv[:, 0:896], Z[:, 128:1024], w2)
                    for (a, b) in ((0, 31), (32, 63), (64, 127)):
                        tap(zv[:, 896 + a:896 + b], Z[:, a + 1:b + 1], w2)
                    tap(zv[:, 1023:1024], Zh[:, 1:2], w2, [self.BK[7]])
                self.act(sA[par].ap(), zc["a"][par].ap(), AF.Silu, [zc["a"][par].o], [sA[par].o])
                self.tt("pool", ATv[:, m, :], sA[par].ap(), zc["b"][par].ap(), ALU.mult, [sA[par].o, zc["b"][par].o], [AT.o])
        A.release(HT, hr, sel, hth, cw, cb, *WA, *WB, *zc["a"], *zc["b"], *sA)
        G2 = self.load_mod(5, "modG")
        tmp = [A.alloc("dtmp%d" % i, 2048) for i in range(2)]
        cnt = 0
        for nh in range(2):
            for mg in range(11):
                sl = cnt % NWD
                if cnt + NPRE < 22:
                    load_wd(cnt + NPRE)
                cnt += 1
                wv = WD[sl].ap(BF16, "p (a n) -> p a n", a=2)
                for ml in range(2):
                    m = mg * 2 + ml
                    for i in range(8):
                        self.mm(self.bank(i), ATv[:, m, i * 128:(i + 1) * 128], wv[:, ml, :], m == 0, m == 21, [AT.o, WD[sl].o], [self.BK[i]])
            for i in range(8):
                t = tmp[i % 2]
                cols = slice(nh * 512, (nh + 1) * 512)
                self.tt("dve", t.ap(), self.bank(i), G2.ap()[:, cols], ALU.mult, [self.BK[i], G2.o], [t.o])
                self.tt("pool", X[:, i, cols], X[:, i, cols], t.ap(), ALU.add, [self.XO[i], t.o], [self.XO[i]])
        A.release(AT, G2, *WD, *tmp)


    def gen_rope(self, pos_reg, n, R, name):
        A = self.A
        nf = R // 4
        m = n * 2 * nf
        invf = A.alloc("invf", nf * 4)
        ii = A.alloc("iota", nf * 4)
        iiv = ii.ap(I32)
        self.P.op("pool", lambda e: e.iota(iiv, pattern=[[1, nf]], base=0, channel_multiplier=0), writes=[ii.o])
        self.cp("dve", invf.ap(), iiv, [ii.o], [invf.o])
        self.act(invf.ap(), invf.ap(), AF.Exp, [invf.o], [invf.o], scale=-math.log(10000.0) / nf)
        ang = A.alloc("ang", m * 4)
        angv = ang.ap(F32, "p (a f) -> p a f", f=nf)
        posv = pos_reg.ap()[:, 0:2 * n]
        self.tt("dve", angv, posv.unsqueeze(2).to_broadcast([128, 2 * n, nf]),
                invf.ap().unsqueeze(1).to_broadcast([128, 2 * n, nf]), ALU.mult, [pos_reg.o, invf.o], [ang.o])
        COS = A.alloc(name + "cos", n * R * 4)
        SIN = A.alloc(name + "sin", n * R * 4)
        t = A.alloc("rt", m * 4)
        ti = A.alloc("rti", m * 4)
        sc = A.alloc("rsc", m * 4)
        for kind, shift, dst in (("sin", 0.5, SIN), ("cos", 0.75, COS)):
            self.ts("dve", t.ap(), ang.ap(), 1.0 / (2 * math.pi), shift, ALU.mult, ALU.add, [ang.o], [t.o])
            self.cp("dve", ti.ap(I32), t.ap(), [t.o], [ti.o])
            self.cp("dve", sc.ap(), ti.ap(I32), [ti.o], [sc.o])
            self.tt("dve", t.ap(), t.ap(), sc.ap(), ALU.subtract, [t.o, sc.o], [t.o])
            self.stt("dve", t.ap(), t.ap(), 0.0, t.ap(), ALU.is_lt, ALU.add, [t.o], [t.o])
            self.act(sc.ap(), t.ap(), AF.Sin, [t.o, self.epsc.o], [sc.o], scale=2 * math.pi, bias=self.epsc.ap()[:, 1:2])
            sv = sc.ap(F32, "p (n c f) -> p n c f", c=2, f=nf)
            dv = dst.ap(F32, "p (n c h f) -> p n c h f", c=2, h=2, f=nf)
            for hf in range(2):
                if kind == "sin" and hf == 0:
                    self.ts("dve", dv[:, :, :, hf, :], sv, -1.0, None, ALU.mult, None, [sc.o], [dst.o])
                else:
                    self.cp("dve", dv[:, :, :, hf, :], sv, [sc.o], [dst.o])
        A.release(invf, ii, ang, t, ti, sc)
        return COS, SIN

    def rope(self, x, H, R, cosap, sinap, cso, t1r, t2r, xo):
        nf = R // 4
        t1 = t1r.ap()[:, 0:H * R].rearrange("p (h r) -> p h r", h=H)
        t2 = t2r.ap()[:, 0:H * R].rearrange("p (h r) -> p h r", h=H)
        cb = cosap.unsqueeze(1).to_broadcast([128, H, R])
        self.tt("pool", t1, x, cb, ALU.mult, [xo] + cso, [t1r.o])
        x5 = x.rearrange("p h (c g f) -> p h c g f", c=2, g=2, f=nf)
        t5 = t2.rearrange("p h (c g f) -> p h c g f", c=2, g=2, f=nf)
        s4 = sinap.rearrange("p (c g f) -> p c g f", c=2, g=2, f=nf)
        for hf in range(2):
            sb = s4[:, :, hf, :].unsqueeze(1).to_broadcast([128, H, 2, nf])
            self.tt("dve", t5[:, :, :, hf, :], x5[:, :, :, 1 - hf, :], sb, ALU.mult, [xo] + cso, [t2r.o], )
        self.tt("dve", x, t1, t2, ALU.add, [t1r.o, t2r.o], [xo])

    def head_norm(self, src, H, d, gain, out, reads, oobj, tmpr, str_, gobj):
        sq = tmpr.ap()[:, 0:H * d].rearrange("p (h d) -> p h d", h=H)
        self.tt("pool", sq, src, src, ALU.mult, reads, [tmpr.o])
        ss = str_.ap()[:, 0:H]
        self.P.op("dve", lambda e: e.tensor_reduce(out=ss, in_=sq, axis=AX.X, op=ALU.add), reads=[tmpr.o], writes=[str_.o])
        self.act(ss, ss, AF.Sqrt, [str_.o, self.epsc.o], [str_.o], bias=self.epsc.ap()[:, 0:1], scale=1.0 / d)
        self.P.op("dve", lambda e: e.reciprocal(out=ss, in_=ss), reads=[str_.o], writes=[str_.o])
        self.tt("dve", out, src, ss.unsqueeze(2).to_broadcast([128, H, d]), ALU.mult, reads + [str_.o], [oobj])
        self.tt("pool", out, out, gain.unsqueeze(1).to_broadcast([128, H, d]), ALU.mult, [oobj, gobj], [oobj])

    def kv_tile(self, S, So, rope, dKT, dV, dKTg, dVg, dobjs, W, pair, tcols=128, tb=0):
        idb = self.identb.ap(BF16)
        sc = W["sc"]
        cb = sc["ckvb"]
        self.cp("act", cb.ap(BF16)[:, 0:256], S[:, 0:256], [So], [cb.o])
        yield
        b6 = self.bank(6).bitcast(BF16)
        for kk in range(2):
            self.tr(b6[:, 384 + kk * 128:384 + (kk + 1) * 128], cb.ap(BF16)[:, kk * 128:(kk + 1) * 128], idb, [cb.o, self.identb.o], [self.BK[6]])
        ct = sc["ckvT"]
        self.cp("dve", ct.ap(BF16)[:, 0:256], b6[:, 384:640], [self.BK[6]], [ct.o])
        yield
        wk = W["wukv"]
        wkv = wk.ap(BF16, "p (k n) -> p k n", k=2)
        KV = self.PB[pair]
        kobjs = [self.BK[2 * pair], self.BK[2 * pair + 1]]
        for nt in range(2):
            for kk in range(2):
                self.mm(KV[:, nt * 512:(nt + 1) * 512], ct.ap(BF16)[:, kk * 128:(kk + 1) * 128], wkv[:, kk, nt * 512:(nt + 1) * 512],
                        kk == 0, kk == 1, [ct.o, wk.o], [kobjs[nt]])
        kv3 = KV[:, :].rearrange("p (h e) -> p h e", h=8)
        kc = sc["kcat"]
        kcv = kc.ap()[:, 0:768].rearrange("p (h e) -> p h e", h=8)
        self.cp("act", kcv[:, :, 0:64], kv3[:, :, 0:64], kobjs, [kc.o])
        yield
        self.cp("pool", kcv[:, :, 64:96], S[:, 256:288].unsqueeze(1).to_broadcast([128, 8, 32]), [So, kc.o], [kc.o])
        yield
        self.cp("dve", dV[:, :, 0:64], kv3[:, :, 64:128], kobjs, [dobjs["V"]])
        yield
        self.head_norm(kcv, 8, 96, W["gn"].ap()[:, 736:832], kcv, [kc.o], kc.o, sc["t1"], sc["st"], W["gn"].o)
        yield
        gk = sc["gk"]
        gkv = gk.ap()[:, 0:128].rearrange("p (h e) -> p h e", h=2)
        self.cp("pool", gk.ap()[:, 0:128], S[:, 288:416], [So], [gk.o])
        yield
        if rope is not None:
            c32, s32, c64, s64, ro = rope
            self.rope(kcv[:, :, 64:96], 8, 32, c32, s32, ro, sc["t1"], sc["t2"], kc.o)
            self.rope(gkv, 2, 64, c64, s64, ro, sc["t1"], sc["t2"], gk.o)
        kb = sc["kb"]
        kbv = kb.ap(BF16)[:, 0:768].rearrange("p (h e) -> p h e", h=8)
        self.cp("act", kbv, kcv, [kc.o], [kb.o])
        yield
        b7 = self.bank(7).bitcast(BF16)
        for h in range(8):
            self.tr(b7[0:96, h * 128:(h + 1) * 128], kbv[:, h, :], idb, [kb.o, self.identb.o], [self.BK[7]])
        self.cp("dve", dKT, b7[0:96, :].rearrange("p (h t) -> p h t", h=8)[:, :, 0:tcols], [self.BK[7]], [dobjs["KT"]])
        yield
        gb = sc["gkb"]
        gbv = gb.ap(BF16)[:, 0:128].rearrange("p (h e) -> p h e", h=2)
        self.cp("act", gbv, gkv, [gk.o], [gb.o])
        yield
        for h in range(2):
            self.tr(b6[0:64, 640 + h * 128:640 + (h + 1) * 128], gbv[:, h, :], idb, [gb.o, self.identb.o], [self.BK[6]])
        self.cp("dve", dKTg, b6[0:64, 640:896].rearrange("p (h t) -> p h t", h=2)[:, :, 0:tcols], [self.BK[6]], [dobjs["KTg"]])
        yield
        self.cp("pool", dVg[:, :, 0:64], S[:, 416:544].rearrange("p (h e) -> p h e", h=2), [So], [dobjs["Vg"]])
        yield

    def attention(self, l):
        jj = l // 2
        A, P, di = self.A, self.P, self.di
        X = self.X.ap(F32, "p (i d) -> p i d", i=8)
        idb = self.identb.ap(BF16)
        gn = A.alloc("gn", 960 * 4)
        for nm, a, b in (("attn_qa_norm_g", 0, 384), ("attn_kva_norm_g", 384, 640), ("attn_mla_q_norm_g", 640, 736),
                         ("attn_mla_k_norm_g", 736, 832), ("attn_gqa_q_norm_g", 832, 896), ("attn_gqa_k_norm_g", 896, 960)):
            self.dma("sp", gn.ap()[:, a:b], di[nm][jj].partition_broadcast(128), "gn", [], [gn.o])
        G = gn.ap()
        qp = A.alloc("qpos", 64)
        self.dma("sp", qp.ap()[:, 0:16], di["qpos"].rearrange("p i c -> p (i c)"), "qpos", [], [qp.o])
        C32, S32 = self.gen_rope(qp, 8, 32, "o32")
        C64, S64 = self.gen_rope(qp, 8, 64, "o64")
        A.release(qp)
        c32v, s32v = C32.ap(F32, "p (n r) -> p n r", n=8), S32.ap(F32, "p (n r) -> p n r", n=8)
        c64v, s64v = C64.ap(F32, "p (n r) -> p n r", n=8), S64.ap(F32, "p (n r) -> p n r", n=8)
        ropeobjs = [C32.o, S32.o, C64.o, S64.o]
        OUT = A.alloc("outst", 8 * 544 * 4)
        OUTv = OUT.ap(F32, "p (i e) -> p i e", i=8)
        QTm = [A.alloc("qtm%d" % k, 8 * 512 * 2) for k in range(2)]
        QTg = [A.alloc("qtg%d" % k, 8 * 512 * 2) for k in range(2)]
        KTo = A.alloc("kto", 8 * 8 * 64 * 2)
        Vo = A.alloc("vo", 8 * 8 * 65 * 2)
        KTgo = A.alloc("ktgo", 2 * 8 * 64 * 2)
        Vgo = A.alloc("vgo", 8 * 2 * 65 * 2)
        QTmv = [q.ap(BF16, "p (h i c) -> p h i c", h=8, i=8) for q in QTm]
        QTgv = [q.ap(BF16, "p (h i c) -> p h i c", h=8, i=8) for q in QTg]
        KTov = KTo.ap(BF16, "p (h i c) -> p h i c", h=8, i=8)
        Vov = Vo.ap(BF16, "p (i h e) -> p i h e", i=8, h=8)
        KTgov = KTgo.ap(BF16, "p (h i c) -> p h i c", h=2, i=8)
        Vgov = Vgo.ap(BF16, "p (i h e) -> p i h e", i=8, h=2)
        self.memset("pool", Vo.ap(BF16), 1.0, [Vo.o])
        self.memset("pool", Vgo.ap(BF16), 1.0, [Vgo.o])
        if self.cfg.get("attn_stop", 9) <= 1:
            return
        win = A.alloc("win", 8 * 1440 * 2)
        winv = win.ap(BF16, "p (k n) -> p k n", k=8)
        self.dma("pool", winv[:, :, 0:720], di["attn_w_in"][jj][:, 0:720].rearrange("(kc k) n -> k kc n", k=128), "win", [], [win.o])
        self.dma("pool", winv[:, :, 720:1440], di["attn_w_in"][jj][:, 720:1440].rearrange("(kc k) n -> k kc n", k=128), "win", [], [win.o])
        wuq = A.alloc("wuq", 3 * 768 * 2)
        wuqv = wuq.ap(BF16, "p (k n) -> p k n", k=3)
        self.dma("pool", wuqv, di["attn_w_uq"][jj].rearrange("(kc k) n -> k kc n", k=128), "wuq", [], [wuq.o])
        wukv = A.alloc("wukv", 2 * 1024 * 2)
        self.dma("pool", wukv.ap(BF16, "p (k n) -> p k n", k=2), di["attn_w_ukv"][jj].rearrange("(kc k) n -> k kc n", k=128), "wukv", [], [wukv.o])
        H = A.alloc("H1", 16384)
        Hv = H.ap(BF16, "p (i d) -> p i d", i=8)
        self.norm_mod(1, lambda i: Hv[:, i, :], H.o)
        HT = A.alloc("HT1", 16384)
        self.transposeT(H, HT)
        A.release(H)
        HTv = HT.ap(BF16, "p (k t) -> p k t", k=8)
        sc = {k: A.alloc("sc_" + k, n) for k, n in (("ckvb", 512), ("ckvT", 512), ("kcat", 3072), ("t1", 3072), ("t2", 3072),
                                                   ("st", 64), ("gk", 512), ("kb", 1536), ("gkb", 256))}
        W = {"sc": sc, "wukv": wukv, "gn": gn}
        qs = A.alloc("qs", 3072)
        qb = A.alloc("qb", 1536)
        st2 = A.alloc("st2", 64)
        junk = A.alloc("junk", 2048)
        qcb = A.alloc("qcb", 768)
        qcT = A.alloc("qcT", 768)
        splits = ((0, 384), (384, 672), (672, 1184), (1184, 1440))
        stp = self.cfg.get("attn_stop", 9)
        if stp <= 1.2:
            return
        for i in range(8):
            for bk, (c0, c1) in enumerate(splits):
                for kc in range(8):
                    self.mm(self.bank(bk)[:, 0:c1 - c0], HTv[:, kc, i * 128:(i + 1) * 128], winv[:, kc, c0:c1], kc == 0, kc == 7,
                            [HT.o, win.o], [self.BK[bk]])
            if stp <= 1.4:
                continue
            ss = st2.ap()[:, 0:1]
            self.memset("dve", st2.ap()[:, 0:2], 0.0, [st2.o])
            self.act(junk.ap()[:, 0:384], self.bank(0)[:, 0:384], AF.Square, [self.BK[0], st2.o], [junk.o, st2.o], accum_out=ss)
            self.act(ss, ss, AF.Sqrt, [st2.o, self.epsc.o], [st2.o], bias=self.epsc.ap()[:, 0:1], scale=1.0 / 384)
            self.P.op("dve", lambda e: e.reciprocal(out=st2.ap()[:, 0:1], in_=st2.ap()[:, 0:1]), reads=[st2.o], writes=[st2.o])
            self.stt("dve", qcb.ap(BF16)[:, 0:384], self.bank(0)[:, 0:384], ss, G[:, 0:384], ALU.mult, ALU.mult, [self.BK[0], st2.o, gn.o], [qcb.o])
            if stp <= 1.45:
                continue
            b6 = self.bank(6).bitcast(BF16)
            for kk in range(3):
                self.tr(b6[:, kk * 128:(kk + 1) * 128], qcb.ap(BF16)[:, kk * 128:(kk + 1) * 128], idb, [qcb.o, self.identb.o], [self.BK[6]])
            self.cp("dve", qcT.ap(BF16)[:, 0:384], b6[:, 0:384], [self.BK[6]], [qcT.o])
            QM = self.PB[2]
            for nt in range(2):
                for kk in range(3):
                    self.mm(QM[:, nt * 512:nt * 512 + 384], qcT.ap(BF16)[:, kk * 128:(kk + 1) * 128], wuqv[:, kk, nt * 384:(nt + 1) * 384],
                            kk == 0, kk == 2, [qcT.o, wuq.o], [self.BK[4 + nt]])
            qsv = qs.ap()[:, 0:768].rearrange("p (h e) -> p h e", h=8)
            self.cp("act", qs.ap()[:, 0:768].rearrange("p (a e) -> p a e", a=2), QM[:, :].rearrange("p (a e) -> p a e", a=2)[:, :, 0:384],
                    [self.BK[4], self.BK[5]], [qs.o])
            if stp <= 1.5:
                continue
            self.head_norm(qsv, 8, 96, G[:, 640:736], qsv, [qs.o], qs.o, sc["t1"], sc["st"], gn.o)
            if stp <= 1.55:
                continue
            self.rope(qsv[:, :, 64:96], 8, 32, c32v[:, i, :], s32v[:, i, :], ropeobjs, sc["t1"], sc["t2"], qs.o)
            if stp <= 1.57:
                continue
            qbv = qb.ap(BF16)[:, 0:768].rearrange("p (h e) -> p h e", h=8)
            self.cp("act", qbv, qsv, [qs.o], [qb.o])
            b7 = self.bank(7).bitcast(BF16)
            if stp <= 1.58:
                continue
            for h in range(8):
                self.tr(b7[0:96, h * 128:(h + 1) * 128], qbv[:, h, :], idb, [qb.o, self.identb.o], [self.BK[7]])
            b73 = b7[0:96, :].rearrange("p (h t) -> p h t", h=8)
            if stp <= 1.59:
                continue
            self.cp("dve", QTmv[0][0:96, :, i, :], b73[:, :, 0:64], [self.BK[7]], [QTm[0].o])
            if stp <= 1.595:
                continue
            self.cp("act", QTmv[1][0:96, :, i, :], b73[:, :, 64:128], [self.BK[7]], [QTm[1].o])
            if stp <= 1.6:
                continue
            ss2 = st2.ap()[:, 1:2]
            self.act(junk.ap()[:, 0:256], self.bank(1)[:, 0:256], AF.Square, [self.BK[1], st2.o], [junk.o, st2.o], accum_out=ss2)
            self.act(ss2, ss2, AF.Sqrt, [st2.o, self.epsc.o], [st2.o], bias=self.epsc.ap()[:, 0:1], scale=1.0 / 256)
            self.P.op("dve", lambda e: e.reciprocal(out=st2.ap()[:, 1:2], in_=st2.ap()[:, 1:2]), reads=[st2.o], writes=[st2.o])
            self.stt("dve", OUTv[:, i, 0:256], self.bank(1)[:, 0:256], ss2, G[:, 384:640], ALU.mult, ALU.mult, [self.BK[1], st2.o, gn.o], [OUT.o])
            self.cp("act", OUTv[:, i, 256:288], self.bank(1)[:, 256:288], [self.BK[1]], [OUT.o])
            self.cp("act", OUTv[:, i, 416:544], self.bank(3)[:, 128:256], [self.BK[3]], [OUT.o])
            gks = sc["gk"]
            self.cp("act", gks.ap()[:, 0:128], self.bank(3)[:, 0:128], [self.BK[3]], [gks.o])
            gk3 = gks.ap()[:, 0:128].rearrange("p (h e) -> p h e", h=2)
            self.head_norm(gk3, 2, 64, G[:, 896:960], OUTv[:, i, 288:416].rearrange("p (h e) -> p h e", h=2), [gks.o], OUT.o, sc["t1"], sc["st"], gn.o)
            if stp <= 1.7:
                continue
            self.cp("act", qs.ap()[:, 0:512], self.bank(2)[:, 0:512], [self.BK[2]], [qs.o])
            gq3 = qs.ap()[:, 0:512].rearrange("p (h e) -> p h e", h=8)
            self.head_norm(gq3, 8, 64, G[:, 832:896], gq3, [qs.o], qs.o, sc["t1"], sc["st"], gn.o)
            self.rope(gq3, 8, 64, c64v[:, i, :], s64v[:, i, :], ropeobjs, sc["t1"], sc["t2"], qs.o)
            gqb = qb.ap(BF16)[:, 0:512].rearrange("p (h e) -> p h e", h=8)
            self.cp("act", gqb, gq3, [qs.o], [qb.o])
            for h in range(8):
                self.tr(b7[0:64, h * 128:(h + 1) * 128], gqb[:, h, :], idb, [qb.o, self.identb.o], [self.BK[7]])
            b74 = b7[0:64, :].rearrange("p (h t) -> p h t", h=8)
            self.cp("dve", QTgv[0][0:64, :, i, :], b74[:, :, 0:64], [self.BK[7]], [QTg[0].o])
            self.cp("act", QTgv[1][0:64, :, i, :], b74[:, :, 64:128], [self.BK[7]], [QTg[1].o])
            if stp <= 1.8:
                continue
            for _ in self.kv_tile(OUTv[:, i, :], OUT.o, (c32v[:, i, :], s32v[:, i, :], c64v[:, i, :], s64v[:, i, :], ropeobjs),
                                  KTov[0:96, :, i, :], Vov[:, i, :, :], KTgov[0:64, :, i, :], Vgov[:, i, :, :],
                                  {"KT": KTo.o, "V": Vo.o, "KTg": KTgo.o, "Vg": Vgo.o}, W, 2, tcols=64):
                pass
        A.release(HT, win, wuq, qs, qb, st2, junk, qcb, qcT, C32, S32, C64, S64)
        for s_ in range(2):
            rows = slice(32 * s_, 32 * s_ + 32)
            for nm, a, b in (("ockv", 0, 256), ("okrope", 256, 288), ("ogk", 288, 416), ("ogv", 416, 544)):
                self.dma("sp", self.do[nm][s_, jj].rearrange("(c i) d -> c i d", i=8), OUTv[rows, :, a:b], "o" + nm, [OUT.o], [], is_output=True)
        if self.cfg.get("attn_stop", 9) <= 2:
            return
        orecv = []
        for hf in range(2):
            osend, orc = Obj("kvsend%d" % hf), Obj("kvrecv%d" % hf)
            ks, kr = self.kvsend[jj][hf], self.kvrecv[jj][hf]
            self.dma("sp", ks.rearrange("(c i) d -> c i d", i=8), OUTv[64 + 32 * hf:96 + 32 * hf, :, :], "kvs", [OUT.o], [osend])
            P.dma("pool", (lambda ks, kr: lambda e: e.collective_compute("AllGather", ALU.bypass, replica_groups=[[0, 1, 2, 3], [4, 5, 6, 7]],
                                                                         ins=[ks.opt()], outs=[kr.opt()]))(ks, kr),
                  "cc_kv%d_%d" % (jj, hf), reads=[osend], writes=[orc], inc=1)
            orecv.append(orc)
        A.release(OUT)
        kp = A.alloc("kpos", 128)
        self.dma("sp", kp.ap()[:, 0:32], di["kpos"].rearrange("p u c -> p (u c)"), "kpos", [], [kp.o])
        KC32, KS32 = self.gen_rope(kp, 16, 32, "k32")
        KC64, KS64 = self.gen_rope(kp, 16, 64, "k64")
        A.release(kp)
        kro = [KC32.o, KS32.o, KC64.o, KS64.o]
        kc32, ks32 = KC32.ap(F32, "p (n r) -> p n r", n=16), KS32.ap(F32, "p (n r) -> p n r", n=16)
        kc64, ks64 = KC64.ap(F32, "p (n r) -> p n r", n=16), KS64.ap(F32, "p (n r) -> p n r", n=16)
        KT = A.alloc("KT", 8 * 2304 * 2)
        VM = A.alloc("VM", 18 * 8 * 65 * 2)
        KTg = A.alloc("KTg", 2 * 2304 * 2)
        VG = A.alloc("VG", 18 * 2 * 65 * 2)
        KTv = KT.ap(BF16, "p (h t) -> p h t", h=8)
        VMv = VM.ap(BF16, "p (u h e) -> p u h e", u=18, h=8)
        KTgv = KTg.ap(BF16, "p (h t) -> p h t", h=2)
        VGv = VG.ap(BF16, "p (u h e) -> p u h e", u=18, h=2)
        self.memset("pool", VM.ap(BF16), 1.0, [VM.o])
        self.memset("pool", VG.ap(BF16), 1.0, [VG.o])
        stg = [A.alloc("stg%d" % k, 544 * 4) for k in range(2)]
        dob = {"KT": KT.o, "V": VM.o, "KTg": KTg.o, "Vg": VG.o}
        sc2 = {k: A.alloc("sc2_" + k, r_.req) for k, r_ in sc.items()}
        W2 = {"sc": sc2, "wukv": wukv, "gn": gn}
        gens = []
        for u in range(18):
            sg = stg[u % 2]
            if u < 2:
                rows = slice(128 * u, 128 * u + 128)
                self.dma("sp", sg.ap()[:, 0:256], di["cache_ckv"][jj, rows, :], "stg%d" % (u % 2), [], [sg.o])
                self.dma("sp", sg.ap()[:, 256:288], di["cache_krope"][jj, rows, :], "stg%d" % (u % 2), [], [sg.o])
                self.dma("sp", sg.ap()[:, 288:416], di["cache_gk"][jj, rows, :], "stg%d" % (u % 2), [], [sg.o])
                self.dma("sp", sg.ap()[:, 416:544], di["cache_gv"][jj, rows, :], "stg%d" % (u % 2), [], [sg.o])
                rp = None
            else:
                v = u - 2
                rk, w4 = v // 4, v % 4
                src = self.kvrecv[jj][w4 // 2][256 * rk + 128 * (w4 % 2):256 * rk + 128 * (w4 % 2) + 128, :]
                self.dma("sp", sg.ap()[:, 0:544], src, "stg%d" % (u % 2), [orecv[w4 // 2]], [sg.o])
                rp = (kc32[:, v, :], ks32[:, v, :], kc64[:, v, :], ks64[:, v, :], kro)
            gens.append(self.kv_tile(sg.ap()[:, 0:544], sg.o, rp, KTv[0:96, :, u * 128:(u + 1) * 128], VMv[:, u, :, :],
                                     KTgv[0:64, :, u * 128:(u + 1) * 128], VGv[:, u, :, :], dob, W if u % 2 == 0 else W2, u % 3, tb=u % 2))
            if len(gens) == 2:
                live = list(gens)
                while live:
                    for g_ in list(live):
                        try:
                            next(g_)
                        except StopIteration:
                            live.remove(g_)
                gens = []
        A.release(wukv, KC32, KS32, KC64, KS64, *stg, *sc.values(), *sc2.values())
        if self.cfg.get("attn_stop", 9) <= 3:
            return
        G1 = self.load_mod(2, "modG")
        OT = A.alloc("OT", 8 * 1024 * 2)
        OTv = OT.ap(BF16, "p (h i c) -> p h i c", h=8, i=8)
        PT = [A.alloc("PT%d" % k, 1024) for k in range(3)]
        PTp = A.alloc("PTp", 8 * 256 * 2)
        PTpv = PTp.ap(BF16, "p (g i c) -> p g i c", g=8, i=8)
        rsr = A.alloc("rsr", 3072)
        bcs = A.alloc("bcs", 3072)
        wo = [A.alloc("wo%d" % k, 4 * 512 * 2) for k in range(2)]
        wov = [w_.ap(BF16, "p (h n) -> p h n", h=4) for w_ in wo]
        dtmp = [A.alloc("atmp%d" % k, 2048) for k in range(2)]
        onesf = self.onesf.ap()
        ptc = 0
        sbc = 0
        for grp in range(2):
            dq = 96 if grp == 0 else 64
            scale = 1.0 / math.sqrt(dq)
            cnts = [sbc, ptc]

            def hviews(h):
                hk = h if grp == 0 else h // 4
                if grp == 0:
                    return (hk, QTmv[1][0:96, h], QTmv[0][0:96, h], QTm[1].o, QTm[0].o, KTv[0:96, hk], KTov[0:96, hk], KT.o, KTo.o,
                            VMv, Vov, VM.o, Vo.o)
                return (hk, QTgv[1][0:64, h], QTgv[0][0:64, h], QTg[1].o, QTg[0].o, KTgv[0:64, hk], KTgov[0:64, hk], KTg.o, KTgo.o,
                        VGv, Vgov, VG.o, Vgo.o)

            def S_gen(h):
                hk, qts, qtp, qo_s, qo_p, kt_all, kto_, kobj, koobj, v_all, v_own, vobj, voobj = hviews(h)
                ob = 4 + h % 2
                OB = self.bank(ob)
                pend = []
                for u in range(18 + 2):
                    if u < 18:
                        sb = cnts[0] % 4
                        cnts[0] += 1
                        self.mm(self.bank(sb), kt_all[:, u * 128:(u + 1) * 128], qts.rearrange("p i c -> p (i c)"), True, True, [kobj, qo_s], [self.BK[sb]])
                        pt = PT[cnts[1] % 3]
                        cnts[1] += 1
                        self.act(pt.ap(BF16), self.bank(sb), AF.Exp, [self.BK[sb]], [pt.o], scale=scale)
                        pend.append((u, pt))
                    if u >= 2:
                        uu, pt2 = pend.pop(0)
                        self.mm(OB[0:65, :], v_all[:, uu, hk, :], pt2.ap(BF16), uu == 0, uu == 17, [vobj, pt2.o], [self.BK[ob]])
                    yield
                self.act(rsr.ap()[64:65, 0:512], OB[64:65, 0:512], AF.Ln, [self.BK[ob]], [rsr.o])
                self.act(rsr.ap()[64:65, 0:512], rsr.ap()[64:65, 0:512], AF.Exp, [rsr.o], [rsr.o], scale=-1.0)
                self.mm(self.bank(6)[0:64, 0:512], onesf[64:65, 0:64], rsr.ap()[64:65, 0:512], True, True, [rsr.o, self.onesf.o], [self.BK[6]])
                self.cp("act", bcs.ap()[0:64, 0:512], self.bank(6)[0:64, 0:512], [self.BK[6]], [bcs.o])
                self.tt("dve", OTv[0:64, h, :, 64:128], OB[0:64, 0:512].rearrange("p (i c) -> p i c", i=8),
                        bcs.ap()[0:64, 0:512].rearrange("p (i c) -> p i c", i=8), ALU.mult, [self.BK[ob], bcs.o], [OT.o])
                yield

            def P_gen(h):
                hk, qts, qtp, qo_s, qo_p, kt_all, kto_, kobj, koobj, v_all, v_own, vobj, voobj = hviews(h)
                for ig in range(8):
                    sb = cnts[0] % 4
                    cnts[0] += 1
                    self.mm(self.bank(sb)[0:64, :], kto_[:, ig, 0:64], qtp.rearrange("p i c -> p (i c)"), True, True, [koobj, qo_p], [self.BK[sb]])
                    s3 = self.bank(sb)[0:64, :].rearrange("p (i c) -> p i c", i=8)
                    for s_ in range(2):
                        rows = slice(32 * s_, 32 * s_ + 32)
                        self.act(PTpv[rows, ig, :, 0:32], s3[rows, :, 32 * s_:32 * s_ + 32], AF.Exp, [self.BK[sb]], [PTp.o], scale=scale)
                    yield
                for s_ in range(2):
                    rows = slice(32 * s_, 32 * s_ + 32)
                    ob2 = 7
                    OB2 = self.bank(ob2)
                    for ig in range(8):
                        self.mm(OB2[0:65, 0:256], v_own[rows, ig, hk, :], PTpv[rows, ig, :, 0:32].rearrange("p i c -> p (i c)"), ig == 0, ig == 7,
                                [voobj, PTp.o], [self.BK[ob2]])
                    self.act(rsr.ap()[64:65, 512:768], OB2[64:65, 0:256], AF.Ln, [self.BK[ob2]], [rsr.o])
                    self.act(rsr.ap()[64:65, 512:768], rsr.ap()[64:65, 512:768], AF.Exp, [rsr.o], [rsr.o], scale=-1.0)
                    self.mm(self.bank(6)[0:64, 0:256], onesf[64:65, 0:64], rsr.ap()[64:65, 512:768], True, True, [rsr.o, self.onesf.o], [self.BK[6]])
                    self.cp("act", bcs.ap()[0:64, 512:768], self.bank(6)[0:64, 0:256], [self.BK[6]], [bcs.o])
                    self.tt("dve", OTv[0:64, h, :, 32 * s_:32 * s_ + 32], OB2[0:64, 0:256].rearrange("p (i c) -> p i c", i=8),
                            bcs.ap()[0:64, 512:768].rearrange("p (i c) -> p i c", i=8), ALU.mult, [self.BK[ob2], bcs.o], [OT.o])
                    yield

            for _ in S_gen(0):
                pass
            for h in range(8):
                live = [P_gen(h)] + ([S_gen(h + 1)] if h + 1 < 8 else [])
                while live:
                    for g_ in list(live):
                        try:
                            next(g_)
                        except StopIteration:
                            live.remove(g_)
            sbc, ptc = cnts
            for nh in range(2):
                cols = slice(nh * 512, (nh + 1) * 512)
                for k2 in range(2):
                    self.dma("pool", wov[k2][0:64], di["attn_w_out"][jj][grp * 512 + k2 * 256:grp * 512 + (k2 + 1) * 256, cols].rearrange("(h d) n -> d h n", d=64),
                             "wo%d" % k2, [], [wo[k2].o])
                for i in range(8):
                    bk = i % 4
                    for h in range(8):
                        self.mm(self.bank(bk), OTv[0:64, h, i, :], wov[h // 4][0:64, h % 4, :], h == 0, h == 7, [OT.o, wo[h // 4].o], [self.BK[bk]])
                    t = dtmp[i % 2]
                    self.tt("dve", t.ap(), self.bank(bk), G1.ap()[:, cols], ALU.mult, [self.BK[bk], G1.o], [t.o])
                    self.tt("pool", X[:, i, cols], X[:, i, cols], t.ap(), ALU.add, [self.XO[i], t.o], [self.XO[i]])
        A.release(gn, G1, OT, PTp, rsr, bcs, *wo, KT, VM, KTg, VG, KTo, Vo, KTgo, Vgo, *PT, *dtmp, *QTm, *QTg)


    def gen_pw(self, k0, step, ardt, aidt, name, out):
        A = self.A
        n = 8 * 64
        ki = A.alloc("ki", 32)
        kf = A.alloc("kf", 32)
        kiv = ki.ap(I32)
        self.P.op("pool", lambda e: e.iota(kiv, pattern=[[step, 8]], base=k0, channel_multiplier=0), writes=[ki.o])
        yield
        self.cp("dve", kf.ap(), kiv, [ki.o], [kf.o])
        yield
        kb = kf.ap().unsqueeze(2).to_broadcast([128, 8, 64])
        Pre = A.alloc(name + "re", n * 4)
        Pim = A.alloc(name + "im", n * 4)
        mag = A.alloc("mag", n * 4)
        ang = A.alloc("pang", n * 4)
        t = A.alloc("pt", n * 4)
        ti = A.alloc("pti", n * 4)
        v3 = lambda r: r.ap(F32, "p (k g) -> p k g", k=8)
        self.tt("dve", v3(mag), kb, ardt.ap().unsqueeze(1).to_broadcast([128, 8, 64]), ALU.mult, [kf.o, ardt.o], [mag.o])
        yield
        self.act(mag.ap(), mag.ap(), AF.Exp, [mag.o], [mag.o])
        yield
        self.tt("dve", v3(ang), kb, aidt.ap().unsqueeze(1).to_broadcast([128, 8, 64]), ALU.mult, [kf.o, aidt.o], [ang.o])
        yield
        for shift, dst in ((0.5, Pim), (0.75, Pre)):
            self.ts("dve", t.ap(), ang.ap(), 1.0 / (2 * math.pi), shift + 32.0, ALU.mult, ALU.add, [ang.o], [t.o])
            yield
            self.cp("dve", ti.ap(I32), t.ap(), [t.o], [ti.o])
            yield
            self.cp("dve", dst.ap(), ti.ap(I32), [ti.o], [dst.o])
            yield
            self.tt("dve", t.ap(), t.ap(), dst.ap(), ALU.subtract, [t.o, dst.o], [t.o])
            yield
            self.stt("dve", t.ap(), t.ap(), 0.0, t.ap(), ALU.is_lt, ALU.add, [t.o], [t.o])
            yield
            self.act(dst.ap(), t.ap(), AF.Sin, [t.o, self.epsc.o], [dst.o], scale=2 * math.pi, bias=self.epsc.ap()[:, 1:2])
            yield
            self.tt("dve", dst.ap(), dst.ap(), mag.ap(), ALU.mult, [dst.o, mag.o], [dst.o])
            yield
        A.release(ki, kf, mag, ang, t, ti)
        out.append((Pre, Pim))

    def ssm(self, l):
        j2 = l // 2
        A, P, di = self.A, self.P, self.di
        X = self.X.ap(F32, "p (i d) -> p i d", i=8)
        idb, idf = self.identb.ap(BF16), self.identf.ap()
        ENG = ("dve", "pool")
        U = A.alloc("U", 16384)
        Uv = U.ap(BF16, "p (g i h) -> p g i h", g=64, i=8)
        self.norm_mod(1, lambda i: Uv[:, :, i, :], U.o, view=lambda ap: ap.rearrange("p (g h) -> p g h", g=64))
        VA = A.alloc("VA", 16384)
        VAv = VA.ap(BF16, "p (g c) -> p g c", g=64)
        for gb in range(8):
            bk = 6 + gb % 2
            psb = self.bank(bk).bitcast(BF16)
            for gl in range(8):
                self.tr(psb[:, gl * 128:(gl + 1) * 128], Uv[:, gb * 8 + gl, :, :].rearrange("p i h -> p (i h)"), idb, [U.o, self.identb.o], [self.BK[bk]])
            self.cp("act" if gb % 2 else "dve", VAv[:, gb * 8:(gb + 1) * 8, :], psb.rearrange("p (g c) -> p g c", g=8), [self.BK[bk]], [VA.o])
        A.release(U)
        if self.cfg.get("ssm_stop", 99) <= 1:
            return
        QWB, PWC, QXB, BR, BI = [None, None], [None, None], [None, None], [None, None], [None, None]
        MRE2, NMIM, PMIM, M64 = [None, None], [None, None], [None, None], [None, None]
        smalls = [A.alloc("ssmall%d" % d_, 64 * 4 * 12) for d_ in range(2)]
        def dir_gen(d):
            small = smalls[d]
            sm = lambda k: small.ap()[:, k * 64:(k + 1) * 64]
            aTr, aTi, dtb = A.alloc("aTr", 256), A.alloc("aTi", 256), A.alloc("dtb", 256)
            for half in range(2):
                rows = slice(64 * half, 64 * half + 64)
                self.dma("sp", aTr.ap()[rows, :], di["ssm_a_re"][j2, d].rearrange("g p -> p g"), "ssmp", [], [aTr.o])
                yield
                self.dma("sp", aTi.ap()[rows, :], di["ssm_a_im"][j2, d].rearrange("g p -> p g"), "ssmp", [], [aTi.o])
                yield
            self.dma("sp", dtb.ap(), di["ssm_log_dt"][j2, d].partition_broadcast(128), "ssmp", [], [dtb.o])
            yield
            self.act(dtb.ap(), dtb.ap(), AF.Exp, [dtb.o], [dtb.o])
            yield
            ardt, aidt = A.alloc("ardt", 256), A.alloc("aidt", 256)
            self.tt("dve", ardt.ap(), aTr.ap(), dtb.ap(), ALU.mult, [aTr.o, dtb.o], [ardt.o])
            yield
            self.tt("dve", aidt.ap(), aTi.ap(), dtb.ap(), ALU.mult, [aTi.o, dtb.o], [aidt.o])
            yield
            wbk, wck, xbk = ((7, -1), (1, 1), (-1, -1)) if d == 0 else ((0, 1), (8, -1), (-8, 1))
            pws = []
            yield from self.gen_pw(wbk[0], wbk[1], ardt, aidt, "pwb%d" % d, pws)
            yield from self.gen_pw(wck[0], wck[1], ardt, aidt, "pwc%d" % d, pws)
            yield from self.gen_pw(xbk[0], xbk[1], ardt, aidt, "pxb%d" % d, pws)
            pwb, pwc, pxb = pws
            il, imu = (0, 7) if d == 0 else (7, 0)
            pcr = pwc[0].ap(F32, "p (k g) -> p k g", k=8)
            pci = pwc[1].ap(F32, "p (k g) -> p k g", k=8)
            so = small.o
            den, rden, nre, t1, t2, kre, kim = sm(0), sm(1), sm(2), sm(3), sm(4), sm(5), sm(6)
            self.tt("dve", den, aTr.ap(), aTr.ap(), ALU.mult, [aTr.o], [so])
            yield
            self.tt("dve", t1, aTi.ap(), aTi.ap(), ALU.mult, [aTi.o, so], [so])
            yield
            self.tt("dve", den, den, t1, ALU.add, [so], [so])
            yield
            self.recip(rden, den, [so], [so])
            yield
            self.ts("dve", nre, pcr[:, il, :], -1.0, None, ALU.add, None, [pwc[0].o, so], [so])
            yield
            self.tt("dve", t1, nre, aTr.ap(), ALU.mult, [so, aTr.o], [so])
            yield
            self.tt("dve", t2, pci[:, il, :], aTi.ap(), ALU.mult, [pwc[1].o, aTi.o, so], [so])
            yield
            self.tt("dve", t1, t1, t2, ALU.add, [so], [so])
            yield
            self.tt("dve", kre, t1, rden, ALU.mult, [so], [so])
            yield
            self.tt("dve", t1, pci[:, il, :], aTr.ap(), ALU.mult, [pwc[1].o, aTr.o, so], [so])
            yield
            self.tt("dve", t2, nre, aTi.ap(), ALU.mult, [so, aTi.o], [so])
            yield
            self.tt("dve", t1, t1, t2, ALU.subtract, [so], [so])
            yield
            self.tt("dve", kim, t1, rden, ALU.mult, [so], [so])
            yield
            qt1, qt2 = A.alloc("qt1", 2048), A.alloc("qt2", 2048)
            for (pr, pi_) in (pwb, pxb):
                p3r = pr.ap(F32, "p (k g) -> p k g", k=8)
                p3i = pi_.ap(F32, "p (k g) -> p k g", k=8)
                q1 = qt1.ap(F32, "p (k g) -> p k g", k=8)
                q2 = qt2.ap(F32, "p (k g) -> p k g", k=8)
                krb = kre.unsqueeze(1).to_broadcast([128, 8, 64])
                kib = kim.unsqueeze(1).to_broadcast([128, 8, 64])
                self.tt("dve", q1, p3r, kib, ALU.mult, [pr.o, so], [qt1.o])
                yield
                self.tt("pool", q2, p3i, kib, ALU.mult, [pi_.o, so], [qt2.o])
                yield
                self.tt("dve", p3r, p3r, krb, ALU.mult, [pr.o, so], [pr.o])
                yield
                self.tt("dve", p3r, p3r, q2, ALU.subtract, [pr.o, qt2.o], [pr.o])
                yield
                self.tt("pool", p3i, p3i, krb, ALU.mult, [pi_.o, so], [pi_.o])
                yield
                self.tt("pool", p3i, p3i, q1, ALU.add, [pi_.o, qt1.o], [pi_.o])
                yield
            A.release(qt1, qt2)
            mt = A.alloc("mt%d" % d, (64 + 32 + 32 + 64) * 4)
            mre2 = mt.ap()[:, 0:64].rearrange("p (r q) -> p r q", r=2)
            nmim, pmim = mt.ap()[:, 64:96], mt.ap()[:, 96:128]
            m64 = mt.ap()[:, 128:192].rearrange("p (r q) -> p r q", r=2)
            for par in range(2):
                rows = slice(64 * par, 64 * par + 64)
                srcr = pcr[rows, imu, :].rearrange("p (q two) -> p q two", two=2)[:, :, par]
                srci = pci[rows, imu, :].rearrange("p (q two) -> p q two", two=2)[:, :, par]
                for r_ in range(2):
                    self.cp("dve", mre2[rows, r_, :], srcr, [pwc[0].o], [mt.o])
                    yield
                self.cp("dve", pmim[rows, :], srci, [pwc[1].o], [mt.o])
                yield
                self.ts("dve", nmim[rows, :], srci, -1.0, None, ALU.mult, None, [pwc[1].o], [mt.o])
                yield
            self.cp("dve", m64[:, 0, :], mre2[:, 0, :], [mt.o], [mt.o])
            yield
            self.cp("dve", m64[:, 1, :], pmim, [mt.o], [mt.o])
            yield
            sq1, sq2 = sm(7)[:, 0:32], sm(8)[:, 0:32]
            for _ in range(6):
                self.tt("dve", sq1, m64[:, 0, :], m64[:, 0, :], ALU.mult, [mt.o, so], [so])
                yield
                self.tt("dve", sq2, m64[:, 1, :], m64[:, 1, :], ALU.mult, [mt.o, so], [so])
                yield
                self.stt("dve", m64[:, 1, :], m64[:, 0, :], 2.0, m64[:, 1, :], ALU.mult, ALU.mult, [mt.o], [mt.o])
                yield
                self.tt("dve", m64[:, 0, :], sq1, sq2, ALU.subtract, [so], [mt.o])
                yield
            br, bi = A.alloc("br%d" % d, 4096), A.alloc("bi%d" % d, 4096)
            for half in range(2):
                rows = slice(64 * half, 64 * half + 64)
                self.dma("sp", br.ap(F32, "p (g h) -> p g h", g=64)[rows], di["ssm_b_re"][j2, d].rearrange("g p h -> p g h"), "ssmb", [], [br.o])
                yield
                self.dma("sp", bi.ap(F32, "p (g h) -> p g h", g=64)[rows], di["ssm_b_im"][j2, d].rearrange("g p h -> p g h"), "ssmb", [], [bi.o])
                yield
            A.release(aTr, aTi, dtb, ardt, aidt)
            QWB[d], PWC[d], QXB[d], BR[d], BI[d] = pwb, pwc, pxb, br, bi
            MRE2[d], NMIM[d], PMIM[d], M64[d] = mre2, nmim, pmim, (m64, mt)
        live = [dir_gen(0), dir_gen(1)]
        while live:
            for g_ in list(live):
                try:
                    next(g_)
                except StopIteration:
                    live.remove(g_)
        if self.cfg.get("ssm_stop", 99) <= 2:
            return
        S = [A.alloc("S%d" % d, 2 * 32 * 131 * 2) for d in range(2)]
        S4 = [s_.ap(BF16, "p (r q c) -> p r q c", r=2, q=32) for s_ in S]
        wpre = A.alloc("wpre", 8 * 2 * 128 * 4)
        wt1, wt2 = A.alloc("wt1", 8 * 128 * 4), A.alloc("wt2", 8 * 128 * 4)
        wt3, wt4 = A.alloc("wt3", 8 * 128 * 4), A.alloc("wt4", 8 * 128 * 4)
        w3 = wt3.ap(F32, "p (g i h) -> p g i h", g=8, i=8)
        w4 = wt4.ap(F32, "p (g i h) -> p g i h", g=8, i=8)
        wbc = [A.alloc("wbc%d" % k, 8 * 2 * 64 * 2) for k in range(2)]
        wpv = wpre.ap(F32, "p (g r i h) -> p g r i h", g=8, r=2, i=8)
        w1 = wt1.ap(F32, "p (g i h) -> p g i h", g=8, i=8)
        w2 = wt2.ap(F32, "p (g i h) -> p g i h", g=8, i=8)
        segs = ((0, 32), (32, 64), (64, 128))
        cnt = 0
        for d in range(2):
            qr = QWB[d][0].ap(F32, "p (k g) -> p k g", k=8)
            qi = QWB[d][1].ap(F32, "p (k g) -> p k g", k=8)
            brv = BR[d].ap(F32, "p (g h) -> p g h", g=64)
            biv = BI[d].ap(F32, "p (g h) -> p g h", g=64)
            qo = [QWB[d][0].o, QWB[d][1].o, BR[d].o, BI[d].o]
            for gb in range(8):
                gs = slice(gb * 8, gb * 8 + 8)
                Er = qr[0:64, :, gs].rearrange("p i g -> p g i").unsqueeze(3).to_broadcast([64, 8, 8, 16])
                Ei = qi[0:64, :, gs].rearrange("p i g -> p g i").unsqueeze(3).to_broadcast([64, 8, 8, 16])
                Br_ = brv[0:64, gs, :].unsqueeze(2).to_broadcast([64, 8, 8, 16])
                Bi_ = biv[0:64, gs, :].unsqueeze(2).to_broadcast([64, 8, 8, 16])
                self.tt("pool", w2[0:64], Ei, Bi_, ALU.mult, qo, [wt2.o])
                self.tt("dve", w1[0:64], Er, Br_, ALU.mult, qo, [wt1.o])
                self.tt("pool", w4[0:64], Ei, Br_, ALU.mult, qo, [wt4.o])
                self.tt("dve", w3[0:64], Er, Bi_, ALU.mult, qo, [wt3.o])
                self.tt("dve", wpv[0:64, :, 0], w1[0:64], w2[0:64], ALU.subtract, [wt1.o, wt2.o], [wpre.o])
                self.tt("dve", wpv[0:64, :, 1], w3[0:64], w4[0:64], ALU.add, [wt3.o, wt4.o], [wpre.o])
                pr = cnt % 3
                cnt += 1
                PT_ = self.PB[pr]
                pob = [self.BK[2 * pr], self.BK[2 * pr + 1]]
                wflat = wpre.ap(F32, "p (g r e) -> p g r e", g=8, r=2)
                for gl in range(8):
                    for r_ in range(2):
                        c0 = (gl * 2 + r_) * 64
                        self.tr(PT_[:, c0:c0 + 64], wflat[0:64, gl, r_, :], idf[0:64, 0:64], [wpre.o, self.identf.o], [pob[c0 // 512]])
                wb = wbc[gb % 2]
                wbv = wb.ap(BF16, "p (g r e) -> p g r e", g=8, r=2)
                self.cp("act", wb.ap(BF16), PT_[:, :], pob, [wb.o])
                pr2 = cnt % 3
                cnt += 1
                PS_ = self.PB[pr2]
                pob2 = [self.BK[2 * pr2], self.BK[2 * pr2 + 1]]
                for gl in range(8):
                    g = gb * 8 + gl
                    par, pl = g % 2, gl // 2
                    for r_ in range(2):
                        c0 = (pl * 2 + r_) * 128
                        kw = {"tile_position": (0, 64)} if par else {}
                        self.mm(PS_[64 * par:64 * par + 64, c0:c0 + 128], wbv[:, gl, r_, :], VAv[:, g, :], True, True, [wb.o, VA.o], [pob2[c0 // 512]], **kw)
                ps4 = PS_[:, :].rearrange("p (q r c) -> p r q c", q=4, r=2)
                for si, (a, b) in enumerate(segs):
                    base = (0, 33, 66)[si] + (1 if d == 0 else 0)
                    self.cp("act" if si % 2 else "dve", S4[d][:, :, gb * 4:(gb + 1) * 4, base:base + (b - a)], ps4[:, :, :, a:b], pob2, [S[d].o])
        A.release(wpre, wt1, wt2, wt3, wt4, *wbc, QWB[0][0], QWB[0][1], QWB[1][0], QWB[1][1])
        if self.cfg.get("ssm_stop", 99) <= 3:
            return
        selr = A.alloc("selr", 32)
        self.dma("sp", selr.ap()[:, 0:8], di["selr"], "selr", [], [selr.o])
        H0 = A.alloc("H0", 128 * 4)
        for d in range(2):
            for par in range(2):
                self.dma("sp", H0.ap()[64 * par:64 * par + 64, 64 * d:64 * d + 64].rearrange("p (r q) -> p r q", r=2),
                         di["state_ssm"][j2, d].rearrange("(q two) p r -> two p r q", two=2)[par], "h0", [], [H0.o])
        ST = [[A.alloc("ST%d_%d" % (d, k), 2 * 32 * 3 * 4) for k in range(2)] for d in range(2)]
        STv = [[r.ap(F32, "p (r q s) -> p r q s", r=2, q=32) for r in ST[d]] for d in range(2)]
        tm1 = [A.alloc("tm1_%d" % d, 768) for d in range(2)]
        tm2 = [A.alloc("tm2_%d" % d, 768) for d in range(2)]
        FIN = [A.alloc("FIN%d" % d, 2 * 32 * 2 * 4) for d in range(2)]
        FINv = [r.ap(F32, "p (r q s) -> p r q s", r=2, q=32) for r in FIN]
        FS = A.alloc("FS", 128 * 4)
        for d in range(2):
            self.memset(ENG[d], ST[d][0].ap(), 0.0, [ST[d][0].o])
            self.memset(ENG[d], S4[d][:, :, :, 0:131:33] if d == 0 else S4[d][:, :, :, 32:131:33], 0.0, [S[d].o]) if False else None

        def cmul_step(d, cur, nxt, sl, ns, Bk, bo):
            e = ENG[d]
            cv, nv = STv[d][cur], STv[d][nxt]
            co, no = ST[d][cur].o, ST[d][nxt].o
            t1 = tm1[d].ap(F32, "p (r q s) -> p r q s", r=2, q=32)[:, :, :, 0:ns]
            t2 = tm2[d].ap(F32, "p (r q s) -> p r q s", r=2, q=32)[:, :, :, 0:ns]
            mo = M64[d][1].o
            self.tt(e, t1, cv[:, :, :, sl], MRE2[d].unsqueeze(3).to_broadcast([128, 2, 32, ns]), ALU.mult, [co, mo], [tm1[d].o])
            self.tt(e, t2[:, 0], cv[:, 1, :, sl], NMIM[d].unsqueeze(2).to_broadcast([128, 32, ns]), ALU.mult, [co, mo], [tm2[d].o])
            self.tt(e, t2[:, 1], cv[:, 0, :, sl], PMIM[d].unsqueeze(2).to_broadcast([128, 32, ns]), ALU.mult, [co, mo, tm2[d].o], [tm2[d].o])
            self.tt(e, t1, t1, t2, ALU.add, [tm1[d].o, tm2[d].o], [tm1[d].o])
            self.tt(e, nv[:, :, :, sl], t1, Bk, ALU.add, [tm1[d].o, bo], [no])

        def bcols(d, k, allseq):
            if d == 0:
                return S4[0][:, :, :, 1 + k:1 + k + 67:33] if allseq else S4[0][:, :, :, 67 + k:68 + k]
            return S4[1][:, :, :, 63 - k:63 - k + 67:33] if allseq else S4[1][:, :, :, 129 - k:130 - k]
        cur = [0, 0]
        for k in range(64):
            for d in range(2):
                allseq = (k < 32) if d == 0 else (k >= 32)
                sl = slice(0, 3) if allseq else slice(2, 3)
                ns = 3 if allseq else 1
                if d == 1 and k == 32:
                    self.memset(ENG[d], STv[d][cur[d]][:, :, :, 0:2], 0.0, [ST[d][cur[d]].o])
                nxt = 1 - cur[d]
                bk_ = bcols(d, k, allseq)
                cmul_step(d, cur[d], nxt, sl, ns, bk_, S[d].o)
                if allseq:
                    self.cp("act", bk_[:, :, :, 0:2], STv[d][nxt][:, :, :, 0:2], [ST[d][nxt].o, S[d].o], [S[d].o])
                    if (d == 0 and k == 31) or (d == 1 and k == 63):
                        self.cp("act", FINv[d], STv[d][nxt][:, :, :, 0:2], [ST[d][nxt].o], [FIN[d].o])
                cur[d] = nxt
        for d in range(2):
            self.cp("act", FS.ap()[:, 64 * d:64 * d + 64].rearrange("p (r q) -> p r q", r=2), STv[d][cur[d]][:, :, :, 2], [ST[d][cur[d]].o], [FS.o])
        osd, orc = Obj("ssend"), Obj("srecv")
        self.dma("sp", self.ssend[j2], FS.ap(), "ssd", [FS.o], [osd])
        sdd, srr = self.ssend[j2], self.srecv[j2]
        P.dma("pool", lambda e: e.collective_compute("AllGather", ALU.bypass, replica_groups=[[0, 1, 2, 3], [4, 5, 6, 7]],
                                                     ins=[sdd.opt()], outs=[srr.opt()]), "cc_s%d" % j2, reads=[osd], writes=[orc], inc=1)
        FG = A.alloc("FG", 4 * 128 * 4)
        FGv = FG.ap(F32, "p (k c) -> p k c", k=4)
        self.dma("sp", FGv, srr.rearrange("(k p) c -> p k c", k=4), "fg", [orc], [FG.o])
        ch = A.alloc("chain", 4 * 64 * 4)
        chv = ch.ap(F32, "p (k r q) -> p k r q", k=4, r=2)
        ct1, ct2 = A.alloc("ct1", 256), A.alloc("ct2", 256)
        c1 = ct1.ap(F32, "p (r q) -> p r q", r=2)
        c2 = ct2.ap(F32, "p (r q) -> p r q", r=2)
        for d in range(2):
            m64, mt = M64[d]
            order = (0, 1, 2, 3) if d == 0 else (3, 2, 1, 0)
            h0v = H0.ap()[:, 64 * d:64 * d + 64].rearrange("p (r q) -> p r q", r=2)
            self.cp("dve", chv[:, order[0]], h0v, [H0.o], [ch.o])
            for a in range(3):
                ks, kd = order[a], order[a + 1]
                src = chv[:, ks]
                fk = FGv[:, ks, 64 * d:64 * d + 64].rearrange("p (r q) -> p r q", r=2)
                self.tt("dve", c1, src, m64[:, 0:1, :].to_broadcast([128, 2, 32]), ALU.mult, [ch.o, mt.o], [ct1.o])
                self.tt("dve", c2[:, 1, :], src[:, 0, :], m64[:, 1, :], ALU.mult, [ch.o, mt.o], [ct2.o])
                self.stt("dve", c2[:, 0, :], src[:, 1, :], -1.0, m64[:, 1, :], ALU.mult, ALU.mult, [ch.o, mt.o, ct2.o], [ct2.o])
                self.tt("dve", c1, c1, c2, ALU.add, [ct1.o, ct2.o], [ct1.o])
                self.tt("dve", chv[:, kd], c1, fk, ALU.add, [ct1.o, FG.o], [ch.o])
            dst = STv[d][cur[d]][:, :, :, 2]
            self.ts("dve", dst, chv[:, 0], selr.ap()[:, 0:1], None, ALU.mult, None, [ch.o, selr.o], [ST[d][cur[d]].o])
            for k in range(1, 4):
                self.stt("dve", dst, chv[:, k], selr.ap()[:, k:k + 1], dst, ALU.mult, ALU.add, [ch.o, selr.o, ST[d][cur[d]].o], [ST[d][cur[d]].o])
            icol = 66 if d == 0 else 130
            self.cp("act", S4[d][:, :, :, icol], dst, [ST[d][cur[d]].o], [S[d].o])
        for d in range(2):
            cols = S4[d][:, :, :, 0:34:33] if d == 0 else S4[d][:, :, :, 32:66:33]
            self.memset("pool", cols, 0.0, [S[d].o])
        for k in range(64):
            for d in range(2):
                nxt = 1 - cur[d]
                bk_ = bcols(d, k, False)
                cmul_step(d, cur[d], nxt, slice(2, 3), 1, bk_, S[d].o)
                self.cp("act", bk_, STv[d][nxt][:, :, :, 2:3], [ST[d][nxt].o, S[d].o], [S[d].o])
                cur[d] = nxt
        A.release(FG, selr, H0, ch, ct1, ct2, FS, *tm1, *tm2, *ST[0], *ST[1])
        if self.cfg.get("ssm_stop", 99) <= 5:
            return
        CT = []
        for d in range(2):
            pair_ = []
            for nm in ("ssm_c_re", "ssm_c_im"):
                cst = A.alloc("cst", 8 * 128 * 4)
                csv = cst.ap(F32, "p (c e) -> p c e", c=8)
                src = di[nm][j2, d].rearrange("(gc gl) h p -> (gl h) gc p", gl=8)
                self.dma("sp", csv[:, :, 0:64], src, "cst", [], [cst.o])
                self.dma("sp", csv[:, :, 64:128], src, "cst", [], [cst.o])
                PT_ = self.PB[0]
                pob = [self.BK[0], self.BK[1]]
                for gc in range(8):
                    self.tr(PT_[:, gc * 128:(gc + 1) * 128], csv[:, gc, :], idf, [cst.o, self.identf.o], [pob[gc // 4]])
                ct = A.alloc("ct_%s%d" % (nm[-2:], d), 2048)
                for par in range(2):
                    rows = slice(64 * par, 64 * par + 64)
                    self.cp("act" if par else "dve", ct.ap(F32, "p (q h) -> p q h", q=32)[rows],
                            PT_[rows, :].rearrange("p (q two h) -> p q two h", two=2, h=16)[:, :, par, :], pob, [ct.o])
                A.release(cst)
                pair_.append(ct)
            CT.append(pair_)
        if self.cfg.get("ssm_stop", 99) <= 5.3:
            return
        WC = [A.alloc("WC%d" % d, 32 * 2 * 128 * 2) for d in range(2)]
        WCv = [w.ap(BF16, "p (q r e) -> p q r e", q=32, r=2) for w in WC]
        g1, g2 = A.alloc("g1", 4096), A.alloc("g2", 4096)
        g3, g4 = A.alloc("g3", 4096), A.alloc("g4", 4096)

        def to_parity(src, name):
            r = A.alloc(name, 8 * 32 * 4)
            sv = src.ap(F32, "p (k q two) -> p k q two", k=8, two=2)
            rv = r.ap(F32, "p (k q) -> p k q", k=8)
            for par in range(2):
                rows = slice(64 * par, 64 * par + 64)
                self.cp("dve", rv[rows], sv[rows, :, :, par], [src.o], [r.o])
            return r
        for d in range(2):
            prs, pis = to_parity(PWC[d][0], "pcs_re"), to_parity(PWC[d][1], "pcs_im")
            pr_ = prs.ap(F32, "p (k q) -> p k q", k=8)
            pi_ = pis.ap(F32, "p (k q) -> p k q", k=8)
            cr_ = CT[d][0].ap(F32, "p (q h) -> p q h", q=32)
            ci_ = CT[d][1].ap(F32, "p (q h) -> p q h", q=32)
            rd = [prs.o, pis.o, CT[d][0].o, CT[d][1].o]
            for pb in range(4):
                qs_ = slice(pb * 8, pb * 8 + 8)
                Cr = cr_[:, qs_, :].unsqueeze(2).to_broadcast([128, 8, 8, 16])
                Ci = ci_[:, qs_, :].unsqueeze(2).to_broadcast([128, 8, 8, 16])
                Pr = pr_[:, :, qs_].rearrange("p j q -> p q j").unsqueeze(3).to_broadcast([128, 8, 8, 16])
                Pi = pi_[:, :, qs_].rearrange("p j q -> p q j").unsqueeze(3).to_broadcast([128, 8, 8, 16])
                a1 = g1.ap(F32, "p (q j h) -> p q j h", q=8, j=8)
                a2 = g2.ap(F32, "p (q j h) -> p q j h", q=8, j=8)
                a3 = g3.ap(F32, "p (q j h) -> p q j h", q=8, j=8)
                a4 = g4.ap(F32, "p (q j h) -> p q j h", q=8, j=8)
                o0 = WCv[d][:, qs_, 0, :].rearrange("p q (j h) -> p q j h", j=8)
                o1 = WCv[d][:, qs_, 1, :].rearrange("p q (j h) -> p q j h", j=8)
                self.tt("pool", a2, Ci, Pi, ALU.mult, rd, [g2.o])
                self.tt("dve", a1, Cr, Pr, ALU.mult, rd, [g1.o])
                self.tt("pool", a4, Ci, Pr, ALU.mult, rd, [g4.o])
                self.tt("dve", a3, Cr, Pi, ALU.mult, rd, [g3.o])
                self.tt("dve", o0, a1, a2, ALU.subtract, [g1.o, g2.o], [WC[d].o])
                self.stt("dve", o1, a3, -1.0, a4, ALU.mult, ALU.subtract, [g3.o, g4.o], [WC[d].o])
            A.release(prs, pis)
        A.release(CT[0][0], CT[0][1], CT[1][0], CT[1][1], PWC[0][0], PWC[0][1], PWC[1][0], PWC[1][1])
        if self.cfg.get("ssm_stop", 99) <= 6:
            return
        T = A.alloc("T", 64 * 128 * 2)
        Tv = T.ap(BF16, "p (g e) -> p g e", g=64)
        msk = A.alloc("msk", 2 * 128 * 4 + 64)
        mi = A.alloc("mski", 128 * 4 + 64)
        miv = mi.ap(I32)[:, 0:128]
        pjv = mi.ap(I32)[:, 128:129]
        self.P.op("pool", lambda e: e.iota(miv, pattern=[[1, 128]], base=0, channel_multiplier=0), writes=[mi.o])
        self.P.op("pool", lambda e: e.iota(pjv, pattern=[[0, 1]], base=0, channel_multiplier=1), reads=[mi.o], writes=[mi.o])
        self.ts("dve", mi.ap(I32)[:, 0:129], mi.ap(I32)[:, 0:129], 4, None, ALU.arith_shift_right, None, [mi.o], [mi.o])
        cjf = msk.ap()[:, 0:128]
        pjf = msk.ap()[:, 256:257]
        self.cp("dve", cjf, miv, [mi.o], [msk.o])
        self.cp("dve", pjf, pjv, [mi.o, msk.o], [msk.o])
        Mf, Mb = msk.ap()[:, 0:128], msk.ap()[:, 128:256]
        self.ts("dve", Mb, cjf, pjf, None, ALU.is_le, None, [msk.o], [msk.o])
        self.ts("dve", Mf, cjf, pjf, None, ALU.is_ge, None, [msk.o], [msk.o])
        dcol = A.alloc("dcol", 256)
        for i in range(8):
            self.dma("sp", dcol.ap()[16 * i:16 * i + 16, 0:64], di["ssm_d"][j2].rearrange("(g h) -> h g", h=16), "dcol", [], [dcol.o])
        sst = self.cfg.get("ssm_stop", 99)
        if sst <= 6.2:
            return
        xb = [A.alloc("xb%d" % d, 4 * 2 * 128 * 2) for d in range(2)]
        xbv = [x_.ap(BF16, "p (q r e) -> p q r e", q=4, r=2) for x_ in xb]
        QXs, BXs = [], []
        for d in range(2):
            QXs.append((to_parity(QXB[d][0], "qxs_re%d" % d), to_parity(QXB[d][1], "qxs_im%d" % d)))
            bs_ = []
            for src in (BR[d], BI[d]):
                r = A.alloc("bxs", 32 * 16 * 4)
                sv = src.ap(F32, "p (q two h) -> p q two h", two=2, h=16)
                rv = r.ap(F32, "p (q h) -> p q h", q=32)
                for par in range(2):
                    rows = slice(64 * par, 64 * par + 64)
                    self.cp("dve", rv[rows], sv[rows, :, par, :], [src.o], [r.o])
                bs_.append(r)
            BXs.append(bs_)
            A.release(QXB[d][0], QXB[d][1], BR[d], BI[d])
        ta, tb_, dfull = A.alloc("ta", 4096), A.alloc("tb", 4096), A.alloc("dfull", 4096)
        for gb in range(8):
            for d in range(2):
                qr = QXs[d][0].ap(F32, "p (k q) -> p k q", k=8)
                qi = QXs[d][1].ap(F32, "p (k q) -> p k q", k=8)
                brv = BXs[d][0].ap(F32, "p (q h) -> p q h", q=32)
                biv = BXs[d][1].ap(F32, "p (q h) -> p q h", q=32)
                rd = [QXs[d][0].o, QXs[d][1].o, BXs[d][0].o, BXs[d][1].o]
                qs_ = slice(gb * 4, gb * 4 + 4)
                Er = qr[:, :, qs_].rearrange("p i q -> p q i").unsqueeze(3).to_broadcast([128, 4, 8, 16])
                Ei = qi[:, :, qs_].rearrange("p i q -> p q i").unsqueeze(3).to_broadcast([128, 4, 8, 16])
                Br_ = brv[:, qs_, :].unsqueeze(2).to_broadcast([128, 4, 8, 16])
                Bi_ = biv[:, qs_, :].unsqueeze(2).to_broadcast([128, 4, 8, 16])
                a1 = g1.ap(F32, "p (q j h) -> p q j h", q=8, j=8)[:, 0:4]
                a2 = g2.ap(F32, "p (q j h) -> p q j h", q=8, j=8)[:, 0:4]
                a3 = g3.ap(F32, "p (q j h) -> p q j h", q=8, j=8)[:, 0:4]
                a4 = g4.ap(F32, "p (q j h) -> p q j h", q=8, j=8)[:, 0:4]
                o0 = xbv[d][:, :, 0, :].rearrange("p q (j h) -> p q j h", j=8)
                o1 = xbv[d][:, :, 1, :].rearrange("p q (j h) -> p q j h", j=8)
                self.tt("pool", a2, Ei, Bi_, ALU.mult, rd, [g2.o])
                self.tt("dve", a1, Er, Br_, ALU.mult, rd, [g1.o])
                self.tt("pool", a4, Ei, Br_, ALU.mult, rd, [g4.o])
                self.tt("dve", a3, Er, Bi_, ALU.mult, rd, [g3.o])
                self.tt("dve", o0, a1, a2, ALU.subtract, [g1.o, g2.o], [xb[d].o])
                self.tt("dve", o1, a3, a4, ALU.add, [g3.o, g4.o], [xb[d].o])
            if sst <= 6.4:
                continue
            for d in range(2):
                PT_ = self.PB[d]
                for gl in range(8):
                    g = gb * 8 + gl
                    par, pl = g % 2, gl // 2
                    rows = slice(64 * par, 64 * par + 64)
                    cidx = par * 4 + pl
                    for r_ in range(2):
                        self.mm(PT_[:, cidx * 128:(cidx + 1) * 128], xbv[d][rows, pl, r_, :], WCv[d][rows, g // 2, r_, :], r_ == 0, r_ == 1,
                                [xb[d].o, WC[d].o], [self.BK[2 * d + par]])
            if sst <= 6.6:
                continue
            t3 = lambda r: r.ap(F32, "p (g e) -> p g e", g=8)
            self.tt("dve", t3(ta), self.PB[0][:, :].rearrange("p (g e) -> p g e", g=8), Mf.unsqueeze(1).to_broadcast([128, 8, 128]), ALU.mult,
                    [self.BK[0], self.BK[1], msk.o], [ta.o])
            self.tt("dve", t3(tb_), self.PB[1][:, :].rearrange("p (g e) -> p g e", g=8), Mb.unsqueeze(1).to_broadcast([128, 8, 128]), ALU.mult,
                    [self.BK[2], self.BK[3], msk.o], [tb_.o])
            self.tt("pool", ta.ap(), ta.ap(), tb_.ap(), ALU.add, [ta.o, tb_.o], [ta.o])
            for par in range(2):
                dsl = dcol.ap()[:, gb * 8 + par:gb * 8 + 8:2]
                dfp = dfull.ap()[:, 0:512].rearrange("p (g e) -> p g e", g=4)
                self.tt("pool", dfp, idf.unsqueeze(1).to_broadcast([128, 4, 128]), dsl.unsqueeze(2).to_broadcast([128, 4, 128]), ALU.mult,
                        [self.identf.o, dcol.o], [dfull.o])
                self.tt("pool", Tv[:, gb * 8 + par:gb * 8 + 8:2, :], t3(ta)[:, par * 4:(par + 1) * 4, :], dfp, ALU.add, [ta.o, dfull.o], [T.o])
        A.release(msk, mi, dcol, ta, tb_, dfull, g1, g2, g3, g4, *xb,
                  QXs[0][0], QXs[0][1], QXs[1][0], QXs[1][1], *BXs[0], *BXs[1])
        if self.cfg.get("ssm_stop", 99) <= 7:
            return
        wg = [A.alloc("wg%d" % k, 8 * 512 * 2) for k in range(2)]

        def load_wg(np_):
            ca, cb_ = slice(512 * np_, 512 * np_ + 512), slice(1024 + 512 * np_, 1536 + 512 * np_)
            self.dma("pool", wg[0].ap(BF16, "p (k n) -> p k n", k=8), di["ssm_w_glu"][j2][:, ca].rearrange("(kc k) n -> k kc n", k=128), "wg0", [], [wg[0].o])
            self.dma("pool", wg[1].ap(BF16, "p (k n) -> p k n", k=8), di["ssm_w_glu"][j2][:, cb_].rearrange("(kc k) n -> k kc n", k=128), "wg1", [], [wg[1].o])
        if not self.small:
            load_wg(0)
        Gm = A.alloc("Gm", 16384)
        Gv = Gm.ap(BF16, "p (j g h) -> p j g h", j=8, g=64)
        gel = [A.alloc("gel%d" % k, 2048) for k in range(2)]
        incol = ((0, 33, 66), (1, 34, 67))
        for gb in range(8):
            pr = gb % 3
            PY = self.PB[pr]
            pob = [self.BK[2 * pr], self.BK[2 * pr + 1]]
            for gl in range(8):
                g = gb * 8 + gl
                par, q_ = g % 2, g // 2
                rows = slice(64 * par, 64 * par + 64)
                o_ = [pob[par]]
                cidx = par * 4 + gl // 2
                self.mm(PY[:, cidx * 128:(cidx + 1) * 128], Tv[:, g, :], VAv[:, g, :], True, False, [T.o, VA.o], o_)
                for d in range(2):
                    for r_ in range(2):
                        for si, (a, b) in enumerate(segs):
                            ic = incol[d][si]
                            self.mm(PY[:, cidx * 128 + a:cidx * 128 + b], WCv[d][rows, q_, r_, :], S4[d][rows, r_, q_, ic:ic + (b - a)], False,
                                    d == 1 and r_ == 1, [WC[d].o, S[d].o], o_)
            ge = gel[gb % 2]
            self.act(ge.ap(BF16), PY[:, :], AF.Gelu_apprx_tanh, pob, [ge.o])
            bk = 6 + gb % 2
            psb = self.bank(bk).bitcast(BF16)
            for gl in range(8):
                self.tr(psb[:, gl * 128:(gl + 1) * 128], ge.ap(BF16)[:, gl * 128:(gl + 1) * 128], idb, [ge.o, self.identb.o], [self.BK[bk]])
            ps4_ = psb.rearrange("p (g j h) -> p j g h", g=8, j=8)
            for par in range(2):
                self.cp("dve" if par else "act", Gv[:, :, gb * 8 + par:gb * 8 + 8:2, :], ps4_[:, :, par * 4:(par + 1) * 4, :], [self.BK[bk]], [Gm.o])
        A.release(T, VA, *S, *WC, *gel, M64[0][1], M64[1][1], *smalls)
        GT = A.alloc("GT", 16384)
        self.transposeT(Gm, GT)
        A.release(Gm)
        GTv = GT.ap(BF16, "p (k t) -> p k t", k=8)
        if self.cfg.get("ssm_stop", 99) <= 8:
            return
        G1 = self.load_mod(2, "modG")
        bg = A.alloc("bg", 2048 * 4)
        bgb = A.alloc("bgb", 2048 * 2)
        self.dma("sp", bg.ap()[0:1, :], di["ssm_b_glu"][j2].partition_broadcast(1), "bg", [], [bg.o])
        self.cp("dve", bgb.ap(BF16)[0:1, :], bg.ap()[0:1, :], [bg.o], [bgb.o])
        sig = [A.alloc("sig%d" % k, 2048) for k in range(2)]
        gt_ = [A.alloc("gtmp%d" % k, 2048) for k in range(2)]
        ones = self.onesb.ap(BF16)
        for np_ in range(2):
            ca, cb_ = slice(512 * np_, 512 * np_ + 512), slice(1024 + 512 * np_, 1536 + 512 * np_)
            wva = wg[0].ap(BF16, "p (k n) -> p k n", k=8)
            wvb = wg[1].ap(BF16, "p (k n) -> p k n", k=8)
            if np_ == 1:
                load_wg(1)
            for i in range(8):
                ba, bb = (2 * i) % 6, (2 * i + 1) % 6
                for (bk, wv, wr, cc) in ((ba, wva, wg[0], ca), (bb, wvb, wg[1], cb_)):
                    self.mm(self.bank(bk), ones[0:1, 0:128], bgb.ap(BF16)[0:1, cc], True, False, [self.onesb.o, bgb.o], [self.BK[bk]])
                    for kc in range(8):
                        self.mm(self.bank(bk), GTv[:, kc, i * 128:(i + 1) * 128], wv[:, kc, :], False, kc == 7, [GT.o, wr.o], [self.BK[bk]])
                sg, gt1 = sig[i % 2], gt_[i % 2]
                self.act(sg.ap(), self.bank(bb), AF.Sigmoid, [self.BK[bb]], [sg.o])
                self.tt("dve", gt1.ap(), self.bank(ba), sg.ap(), ALU.mult, [self.BK[ba], sg.o], [gt1.o])
                self.tt("pool", gt1.ap(), gt1.ap(), G1.ap()[:, ca], ALU.mult, [gt1.o, G1.o], [gt1.o])
                self.tt("pool", X[:, i, ca], X[:, i, ca], gt1.ap(), ALU.add, [self.XO[i], gt1.o], [self.XO[i]])
        A.release(GT, G1, bg, bgb, *wg, *sig, *gt_)
        tcb = [A.alloc("tcomb%d" % k, 1024) for k in range(2)]
        cntf = 0
        for d in range(2):
            for s_ in range(2):
                bk = 6 + cntf % 2
                tc_ = tcb[cntf % 2]
                cntf += 1
                for r_ in range(2):
                    self.tr(self.bank(bk)[0:32, r_ * 128:(r_ + 1) * 128], FINv[d][:, r_, :, s_], idf, [FIN[d].o, self.identf.o], [self.BK[bk]])
                self.cp("dve", tc_.ap()[0:32, 0:256].rearrange("q (n r) -> q r n", r=2),
                        self.bank(bk)[0:32, 0:256].rearrange("q (r n) -> q r n", r=2), [self.BK[bk]], [tc_.o])
                self.dma("sp", self.do["ossm"][s_, j2, d].rearrange("(q two) p r -> q (two p r)", two=2), tc_.ap()[0:32, 0:256], "ossm",
                         [tc_.o], [], is_output=True)
        A.release(*tcb)
        A.release(*FIN)


def _core_inputs(inp, core):
    b, r = core // 4, core % 4
    f = lambda a: np.ascontiguousarray(np.asarray(a, dtype=np.float32))
    d = {}
    d["xp"] = f(inp["x_prompt"][2 * core:2 * core + 2].reshape(512, D))
    d["xs"] = f(inp["x_sample"][b, 512 * r:512 * (r + 1)])
    d["cvec"] = f(np.stack([inp["c_ctx"], inp["c"][b]], 0))
    d["cache_ckv"] = f(inp["cache_mla_ckv"][b])
    d["cache_krope"] = f(inp["cache_mla_krope"][b])
    d["cache_gk"] = f(inp["cache_gqa_k"][b].reshape(2, 256, 128))
    d["cache_gv"] = f(inp["cache_gqa_v"][b].reshape(2, 256, 128))
    d["state_ssm"] = f(inp["state_ssm"][b])
    d["w_mod"] = f(inp["w_mod"][:, :, 1536 * r:1536 * (r + 1)])
    d["b_mod"] = f(inp["b_mod"][:, 1536 * r:1536 * (r + 1)])
    for k in ("norm1_g", "norm2_g", "attn_w_in", "attn_qa_norm_g", "attn_kva_norm_g", "attn_w_uq", "attn_w_ukv",
              "attn_mla_q_norm_g", "attn_mla_k_norm_g", "attn_gqa_q_norm_g", "attn_gqa_k_norm_g", "attn_w_out",
              "ssm_a_re", "ssm_a_im", "ssm_log_dt", "ssm_b_re", "ssm_b_im", "ssm_c_re", "ssm_c_im", "ssm_d", "ssm_w_glu", "ssm_b_glu",
              "ffn_w_up", "ffn_conv_w", "ffn_conv_b", "ffn_w_down"):
        d[k] = f(inp[k])
    qpos = np.zeros((128, 8, 2), np.float32)
    for p in range(64, 128):
        for i in range(8):
            t = 512 * r + 8 * (p - 64) + i
            qpos[p, i, 0] = t // 64
            qpos[p, i, 1] = t % 64
    kpos = np.zeros((128, 16, 2), np.float32)
    for q in range(128):
        for u in range(16):
            t = 128 * u + q
            kpos[q, u, 0] = t // 64
            kpos[q, u, 1] = t % 64
    sel = np.zeros((8, 2), np.float32)
    if r > 0:
        sel[2 * (r - 1) + 1, 0] = 1.0
    if r < 3:
        sel[2 * (r + 1), 1] = 1.0
    selr = np.zeros((128, 8), np.float32)
    selr[:, r] = 1.0
    selr[:, 4 + r] = 1.0
    d["qpos"], d["kpos"], d["selhalo"], d["selr"] = qpos, kpos, sel, selr
    return d


_NC_CACHE = {}


def kernel(cfg=None, **inp):
    inp = {k: np.asarray(v) for k, v in inp.items()}
    key = repr(sorted((cfg or {}).items()))
    if key not in _NC_CACHE:
        _NC_CACHE[key] = Builder(cfg).build()
    nc = _NC_CACHE[key]
    in_maps = [_core_inputs(inp, c) for c in range(NCORES)]
    if (cfg or {}).get("small"):
        for m in in_maps:
            for k in ("w_mod", "ffn_w_up", "ffn_w_down", "ssm_w_glu"):
                m.pop(k)
    res = run_bass_kernel_spmd(nc, in_maps, core_ids=list(range(NCORES)))
    R = res.results
    yp = np.concatenate([R[c]["yp"].reshape(2, 256, D) for c in range(NCORES)], 0)
    ys = np.stack([np.concatenate([R[4 * b + r]["ys"] for r in range(4)], 0) for b in range(2)], 0)
    ockv = np.concatenate([R[c]["ockv"] for c in range(NCORES)], 0)
    okr = np.concatenate([R[c]["okrope"] for c in range(NCORES)], 0)
    ogk = np.concatenate([R[c]["ogk"].reshape(2, 2, 256, 2, 64) for c in range(NCORES)], 0)
    ogv = np.concatenate([R[c]["ogv"].reshape(2, 2, 256, 2, 64) for c in range(NCORES)], 0)
    ossm = np.concatenate([R[c]["ossm"] for c in range(NCORES)], 0)
    return (yp.astype(np.float32), ys.astype(np.float32), ockv.astype(np.float32), okr.astype(np.float32),
            ogk.astype(np.float32), ogv.astype(np.float32), ossm.astype(np.float32))
```
